# Optimizing a Trainium2 kernel written in Bass

```python
import math
import jax
import jax.numpy as jnp
from jax import lax
import numpy as np

D_MODEL = 1024
BATCH = 2
SEQ = 16384
DEPTH = 2
DEC_BATCH = 4
DEC_SEQ = 4096
PAST_LEN = 128

GRID_W = 64
D_MIX = D_MODEL
N_MIXERS = 4
GROUP_W = D_MIX // N_MIXERS
HEAD_DIM = 64
GROUP_HEADS = GROUP_W // HEAD_DIM
EPS = 1e-6
SGU_CHUNK = 128
DN_HEADS = GROUP_HEADS
DN_DK = HEAD_DIM
DN_DV = HEAD_DIM
DN_CHUNK = 64
CONV_K = 5
DN_CONV_CH = DN_HEADS * (2 * DN_DK + DN_DV)
ATT_HEADS = GROUP_HEADS
ATT_KV_HEADS = 2
ATT_QBLOCK = 128
ROPE_THETA = 10000.0
ROPE_AXIS_DIM = HEAD_DIM // 2
ROPE_FREQS = ROPE_AXIS_DIM // 2
POOL_WINDOWS = (2, 4, 8, 16)
POOL_GROUP = GROUP_W // len(POOL_WINDOWS)

IN_WIDTHS = (
    GROUP_W, GROUP_W, GROUP_W,
    DN_HEADS * DN_DK, DN_HEADS * DN_DK, DN_HEADS * DN_DV,
    DN_HEADS * DN_DV, 2 * DN_HEADS, 2 * DN_HEADS,
    ATT_HEADS * HEAD_DIM, ATT_KV_HEADS * HEAD_DIM,
    ATT_KV_HEADS * HEAD_DIM, ATT_HEADS * HEAD_DIM,
    GROUP_W, GROUP_W,
)
IN_COLS = sum(IN_WIDTHS)

kernel_name = "hybrid_parallel_group_encoder"


def _rms(x):
    xf = x.astype(jnp.float32)
    return (xf * lax.rsqrt(jnp.mean(xf * xf, axis=-1, keepdims=True) + EPS)).astype(x.dtype)


def _rms_norm(x, w):
    return _rms(x) * w


def _l2norm(x):
    xf = x.astype(jnp.float32)
    return xf * lax.rsqrt(jnp.sum(xf * xf, axis=-1, keepdims=True) + EPS)


def _split_cols(proj):
    offsets = []
    acc = 0
    for width in IN_WIDTHS[:-1]:
        acc += width
        offsets.append(acc)
    return jnp.split(proj, offsets, axis=-1)


def _sgu_mixer(u, v, sgu_w, sgu_b):
    b_, s_, _ = u.shape
    nc = s_ // SGU_CHUNK
    v_n = _rms(v.reshape(b_, nc, SGU_CHUNK, GROUP_HEADS, HEAD_DIM))
    mixed = jnp.einsum("hij,bnjhd->bnihd", sgu_w, v_n) + sgu_b.T[:, :, None]
    out = u.reshape(b_, nc, SGU_CHUNK, GROUP_HEADS, HEAD_DIM) * mixed
    return out.reshape(b_, s_, GROUP_W)


def _centred_depthwise_conv(x, w):
    pad = CONV_K // 2
    s_ = x.shape[1]
    xp = jnp.pad(x, ((0, 0), (pad, pad), (0, 0)))
    return sum(xp[:, i:i + s_] * w[i] for i in range(CONV_K))


def _chunk_gated_delta_rule(q, k, v, g, beta):
    b_, s_, h_, dk = q.shape
    dv = v.shape[-1]
    n = s_ // DN_CHUNK

    def to_chunks(t):
        t = t.astype(jnp.float32).reshape(b_, n, DN_CHUNK, h_, *t.shape[3:])
        return jnp.moveaxis(t, 3, 2)

    q, k, v, g, beta = (to_chunks(t) for t in (q, k, v, g, beta))
    g = jnp.cumsum(g, axis=-1)
    idx = jnp.arange(DN_CHUNK)
    incl = idx[:, None] >= idx[None, :]
    strict = idx[:, None] > idx[None, :]
    decay = jnp.exp(jnp.where(incl, g[..., :, None] - g[..., None, :], -jnp.inf))
    k_beta = k * beta[..., None]
    v_beta = v * beta[..., None]
    lower = jnp.where(strict, jnp.einsum("bnhid,bnhjd->bnhij", k_beta, k) * decay, 0.0)
    eye = jnp.eye(DN_CHUNK, dtype=jnp.float32)
    t_mat = lax.linalg.triangular_solve(lower + eye, jnp.broadcast_to(eye, lower.shape),
                                        left_side=True, lower=True, unit_diagonal=True)
    u = t_mat @ v_beta
    w = t_mat @ (k_beta * jnp.exp(g)[..., None])
    intra = jnp.where(incl, jnp.einsum("bnhid,bnhjd->bnhij", q, k) * decay, 0.0)
    g_last = g[..., -1]
    q_dec = q * jnp.exp(g)[..., None]
    k_dec = k * jnp.exp(g_last[..., None] - g)[..., None]

    def step(state, xs):
        w_c, u_c, q_c, k_c, a_c, gl_c = xs
        v_new = u_c - w_c @ state
        out = q_c @ state + a_c @ v_new
        state = state * jnp.exp(gl_c)[..., None, None] + jnp.einsum("bhcd,bhce->bhde", k_c, v_new)
        return state, out

    xs = tuple(jnp.moveaxis(t, 1, 0) for t in (w, u, q_dec, k_dec, intra, g_last))
    state0 = jnp.zeros((b_, h_, dk, dv), jnp.float32)
    _, out = lax.scan(step, state0, xs)
    return jnp.transpose(out, (1, 0, 3, 2, 4)).reshape(b_, s_, h_, dv)


def _deltanet_mixer(q, k, v, beta_raw, alpha_raw, conv_w, a_log, dt_bias, norm_w):
    b_, s_, _ = q.shape
    qkv = jax.nn.silu(_centred_depthwise_conv(jnp.concatenate([q, k, v], axis=-1), conv_w))
    q, k, v = jnp.split(qkv, [DN_HEADS * DN_DK, 2 * DN_HEADS * DN_DK], axis=-1)
    q = _l2norm(q.reshape(b_, s_, DN_HEADS, DN_DK)) * (DN_DK ** -0.5)
    k = _l2norm(k.reshape(b_, s_, DN_HEADS, DN_DK))
    v = v.reshape(b_, s_, DN_HEADS, DN_DV)
    beta = jax.nn.sigmoid(beta_raw.astype(jnp.float32)).reshape(b_, s_, 2, DN_HEADS)
    g = -jnp.exp(a_log) * jax.nn.softplus(alpha_raw.astype(jnp.float32).reshape(b_, s_, 2, DN_HEADS) + dt_bias)
    o_fwd = _chunk_gated_delta_rule(q, k, v, g[:, :, 0], beta[:, :, 0])
    flip = lambda t: jnp.flip(t, axis=1)
    o_bwd = flip(_chunk_gated_delta_rule(flip(q), flip(k), flip(v), flip(g[:, :, 1]), flip(beta[:, :, 1])))
    o = _rms_norm(o_fwd + o_bwd, norm_w)
    return o.reshape(b_, s_, DN_HEADS * DN_DV)


def _axial_rope_tables(seq_len):
    n_rows = seq_len // GRID_W
    rows = jnp.repeat(jnp.arange(n_rows), GRID_W)
    cols = jnp.tile(jnp.arange(GRID_W), n_rows)
    inv_freq = jnp.power(ROPE_THETA, -2.0 * jnp.arange(ROPE_FREQS, dtype=jnp.float32) / ROPE_AXIS_DIM)
    ang = jnp.stack([rows, cols], axis=-1).astype(jnp.float32)[:, :, None] * inv_freq
    return jnp.cos(ang), jnp.sin(ang)


def _apply_axial_rope(x, cos, sin):
    b_, s_, h_, d_ = x.shape
    xr = x.reshape(b_, s_, h_, 2, 2, ROPE_FREQS)
    x1, x2 = xr[..., 0, :], xr[..., 1, :]
    c = cos[None, :, None]
    s = sin[None, :, None]
    out = jnp.stack([x1 * c - x2 * s, x2 * c + x1 * s], axis=-2)
    return out.reshape(b_, s_, h_, d_).astype(x.dtype)


def _attention_mixer(q, k, v, q_norm_w, k_norm_w, cos, sin):
    b_, s_, _ = q.shape
    grp = ATT_HEADS // ATT_KV_HEADS
    q = _apply_axial_rope(_rms_norm(q.reshape(b_, s_, ATT_HEADS, HEAD_DIM), q_norm_w), cos, sin)
    k = _apply_axial_rope(_rms_norm(k.reshape(b_, s_, ATT_KV_HEADS, HEAD_DIM), k_norm_w), cos, sin)
    v = v.reshape(b_, s_, ATT_KV_HEADS, HEAD_DIM)
    nb = s_ // ATT_QBLOCK
    qb = q.reshape(b_, nb, ATT_QBLOCK, ATT_KV_HEADS, grp, HEAD_DIM).transpose(1, 0, 2, 3, 4, 5)
    scale = HEAD_DIM ** -0.5

    def one_block(q_blk):
        s = jnp.einsum("bqkgd,bskd->bkgqs", q_blk, k, preferred_element_type=jnp.float32) * scale
        p = jax.nn.softmax(s, axis=-1)
        return jnp.einsum("bkgqs,bskd->bqkgd", p.astype(v.dtype), v)

    o = lax.map(one_block, qb)
    return o.transpose(1, 0, 2, 3, 4, 5).reshape(b_, s_, ATT_HEADS * HEAD_DIM)


def _pool_mixer(x, pool_w, pool_scale):
    b_, s_, _ = x.shape
    ng = len(POOL_WINDOWS)
    xf = x.astype(jnp.float32).reshape(b_, s_, ng, POOL_GROUP)
    csum = jnp.concatenate([jnp.zeros((b_, 1, ng, POOL_GROUP), jnp.float32), jnp.cumsum(xf, axis=1)], axis=1)
    t = jnp.arange(s_)
    diffs = []
    for gi, win in enumerate(POOL_WINDOWS):
        lo = jnp.clip(t - win // 2, 0, s_)
        hi = jnp.clip(t + win // 2, 0, s_)
        c_g = csum[:, :, gi]
        mean = (jnp.take(c_g, hi, axis=1) - jnp.take(c_g, lo, axis=1)) / (hi - lo).astype(jnp.float32)[None, :, None]
        diffs.append(mean - xf[:, :, gi])
    d = jnp.stack(diffs, axis=2)
    y = jnp.einsum("bsgc,gcd->bsgd", d, pool_w).reshape(b_, s_, GROUP_W) * pool_scale
    return y


def _layer(x, cos, sin, norm_w, w_in, sgu_w, sgu_b, conv_w, a_log, dt_bias, dn_norm_w,
           q_norm_w, k_norm_w, pool_w, pool_scale, w_out):
    h = _rms_norm(x, norm_w)
    proj = h @ w_in
    (a_u, a_v, a_z, b_q, b_k, b_v, b_z, b_beta, b_alpha,
     c_q, c_k, c_v, c_z, d_x, d_z) = _split_cols(proj)
    y_a = _sgu_mixer(a_u, a_v, sgu_w, sgu_b) * jax.nn.silu(a_z)
    y_b = _deltanet_mixer(b_q, b_k, b_v, b_beta, b_alpha, conv_w, a_log, dt_bias, dn_norm_w) * jax.nn.silu(b_z)
    y_c = _attention_mixer(c_q, c_k, c_v, q_norm_w, k_norm_w, cos, sin) * jax.nn.silu(c_z)
    y_d = _pool_mixer(d_x, pool_w, pool_scale) * jax.nn.silu(d_z)
    mix = jnp.concatenate([y_a.astype(x.dtype), y_b.astype(x.dtype), y_c.astype(x.dtype), y_d.astype(x.dtype)], axis=-1)
    return (x + mix @ w_out).astype(x.dtype)


def setup_inputs(seed: int = 0) -> dict:
    key = jax.random.key(seed)
    ks = jax.random.split(key, 16)
    f32 = jnp.float32

    def nrm(k, shape, scale):
        return jax.random.normal(k, shape, f32) * scale

    def gain(k, shape):
        return 1.0 + 0.02 * jax.random.normal(k, shape, f32)

    dt = jnp.exp(jax.random.uniform(ks[8], (DEPTH, 2, DN_HEADS), f32, math.log(1e-3), math.log(1e-1)))
    return {
        "x_prompt": nrm(ks[0], (BATCH, SEQ, D_MODEL), 1.0),
        "x_sample": nrm(ks[1], (DEC_BATCH, DEC_SEQ, D_MODEL), 1.0),
        "norm_w": gain(ks[2], (DEPTH, D_MODEL)),
        "w_in": nrm(ks[3], (DEPTH, D_MODEL, IN_COLS), D_MODEL ** -0.5),
        "sgu_w": nrm(ks[4], (DEPTH, GROUP_HEADS, SGU_CHUNK, SGU_CHUNK), SGU_CHUNK ** -0.5),
        "sgu_b": gain(ks[5], (DEPTH, GROUP_HEADS, SGU_CHUNK)),
        "conv_w": nrm(ks[6], (DEPTH, CONV_K, DN_CONV_CH), CONV_K ** -0.5),
        "a_log": jnp.log(jax.random.uniform(ks[7], (DEPTH, 2, DN_HEADS), f32, 1.0, 16.0)),
        "dt_bias": dt + jnp.log(-jnp.expm1(-dt)),
        "dn_norm_w": gain(ks[9], (DEPTH, DN_DV)),
        "q_norm_w": gain(ks[10], (DEPTH, HEAD_DIM)),
        "k_norm_w": gain(ks[11], (DEPTH, HEAD_DIM)),
        "pool_w": nrm(ks[12], (DEPTH, len(POOL_WINDOWS), POOL_GROUP, POOL_GROUP), POOL_GROUP ** -0.5),
        "pool_scale": gain(ks[13], (DEPTH, GROUP_W)),
        "w_out": nrm(ks[14], (DEPTH, D_MIX, D_MODEL), D_MIX ** -0.5),
    }


def reference(x_prompt, x_sample, norm_w, w_in, sgu_w, sgu_b, conv_w, a_log, dt_bias, dn_norm_w,
              q_norm_w, k_norm_w, pool_w, pool_scale, w_out):
    cos_p, sin_p = _axial_rope_tables(x_prompt.shape[1])
    cos_s, sin_s = _axial_rope_tables(x_sample.shape[1])
    y_prompt = x_prompt
    y_sample = x_sample
    for l in range(DEPTH):
        layer_params = (norm_w[l], w_in[l], sgu_w[l], sgu_b[l], conv_w[l], a_log[l], dt_bias[l],
                        dn_norm_w[l], q_norm_w[l], k_norm_w[l], pool_w[l], pool_scale[l], w_out[l])
        y_prompt = _layer(y_prompt, cos_p, sin_p, *layer_params)
        y_sample = _layer(y_sample, cos_s, sin_s, *layer_params)
    return (y_prompt, y_sample)
```

```python
import numpy as np
import ml_dtypes
from contextlib import ExitStack
import concourse.bass as bass
import concourse.mybir as mybir
from concourse.bass_utils import run_bass_kernel_spmd

F32 = mybir.dt.float32
BF16 = mybir.dt.bfloat16
AF = mybir.ActivationFunctionType
ALU = mybir.AluOpType
AX = mybir.AxisListType

D = 1024
NCOL = 3088
EPS = 1e-6
G_AU, G_AV, G_AZ, G_BQ, G_BK, G_BV, G_BZ, G_CQ, G_CK, G_CV, G_CZ, G_DX, G_DZ = \
    0, 2, 4, 6, 8, 10, 12, 14, 16, 17, 18, 20, 22
ORIG_OFF = dict(au=0, av=256, az=512, bq=768, bk=1024, bv=1280, bz=1536, bb=1792, ba=1800,
                cq=1808, ck=2064, cv=2192, cz=2320, dx=2576, dz=2832)
COL_PERM = np.concatenate([
    np.arange(0, 1792), np.arange(1808, 3088), np.arange(1792, 1808)])


class Buf:
    __slots__ = ("name", "w", "r")

    def __init__(self, name):
        self.name = name
        self.w = None
        self.r = {}


class K:
    ENG = ("pe", "act", "dve", "pool", "sp")

    def __init__(self, nc, stack):
        self.nc = nc
        self.sems = {}
        self.cnt = {}
        self.stack = stack
        for e in self.ENG:
            self._newsem(e)
        self.prog = {e: [] for e in self.ENG}
        self.waited = {e: {} for e in self.ENG}
        self.ninstr = 0

    def _newsem(self, key):
        self.sems[key] = self.stack.enter_context(self.nc.semaphore("s_" + key))
        self.cnt[key] = 0

    LIMIT = None

    def op(self, e, fn, reads=(), writes=(), dma=None):
        if K.LIMIT is not None and self.ninstr >= K.LIMIT:
            return
        need = {}

        def want(tok):
            if tok is None:
                return
            k, v = tok
            if k == "pe" and e == "pe" and dma is None:
                return
            if k not in self.ENG:
                v = self.cnt[k]
            if need.get(k, 0) < v:
                need[k] = v
        for b in reads:
            want(b.w)
        for b in writes:
            want(b.w)
            for k, v in b.r.items():
                want((k, v))
        waits = []
        wd = self.waited[e]
        for k, v in need.items():
            if wd.get(k, 0) < v:
                wd[k] = v
                waits.append((k, v))
        if dma is not None:
            if dma not in self.sems:
                self._newsem(dma)
            key, inc = dma, 16
        else:
            key, inc = e, 1
        self.cnt[key] += inc
        tok = (key, self.cnt[key])
        for b in reads:
            if b.r.get(key, 0) < tok[1]:
                b.r[key] = tok[1]
        for b in writes:
            b.w = tok
            b.r = {}
        self.prog[e].append((waits, fn, key, inc))
        self.ninstr += 1

    def barrier(self):
        tot = dict(self.cnt)
        for e in self.ENG:
            waits = []
            for k, v in tot.items():
                if v > 0 and self.waited[e].get(k, 0) < v:
                    self.waited[e][k] = v
                    waits.append((k, v))
            if waits:
                self.prog[e].append((waits, None, None, 0))

    def flush(self):
        nc = self.nc
        with nc.Block() as block:
            def run(e, eng):
                for waits, fn, key, inc in self.prog[e]:
                    for k, v in waits:
                        eng.wait_ge(self.sems[k], v)
                    if fn is not None:
                        fn(eng).then_inc(self.sems[key], inc)

            @block.tensor
            def _(eng):
                run("pe", eng)

            @block.scalar
            def _(eng):
                run("act", eng)

            @block.vector
            def _(eng):
                run("dve", eng)

            @block.gpsimd
            def _(eng):
                run("pool", eng)

            @block.sync
            def _(eng):
                run("sp", eng)
        self.prog = {e: [] for e in self.ENG}


class T:
    def __init__(self, t, name):
        self.t = t
        self.b = Buf(name)

    def __getitem__(self, idx):
        return self.t[idx]


_UID = [0]


def sb(nc, st, name, shape, dt):
    _UID[0] += 1
    nm = f"sb{_UID[0]}_{name}"
    return T(st.enter_context(nc.sbuf_tensor(nm, list(shape), dt)), nm)


def ps(nc, st, name, shape, dt):
    _UID[0] += 1
    nm = f"ps{_UID[0]}_{name}"
    return T(st.enter_context(nc.psum_tensor(nm, list(shape), dt)), nm)


def rope_tables(S):
    rows = np.repeat(np.arange(S // 64), 64)
    cols = np.tile(np.arange(64), S // 64)
    inv = np.power(np.float32(10000.0), -2.0 * np.arange(16, dtype=np.float32) / 32).astype(np.float32)
    ang = np.stack([rows, cols], -1).astype(np.float32)[:, :, None] * inv
    return np.cos(ang).astype(np.float32).reshape(S, 32), np.sin(ang).astype(np.float32).reshape(S, 32)


def build(seqs, depth, debug=False, phases="pfdao"):
    nc = bass.Bass("TRN2", target_bir_lowering=False)
    dr = {}

    def din(name, shape, dt=F32):
        dr[name] = nc.dram_tensor(name, list(shape), dt, kind="ExternalInput").ap()
        return dr[name]

    def dscr(name, shape, dt, out=False):
        kind = "ExternalOutput" if out else "Internal"
        dr[name] = nc.dram_tensor(name, list(shape), dt, kind=kind).ap()
        return dr[name]

    nseq = len(seqs)
    for i, S in enumerate(seqs):
        din(f"x{i}", [S, D])
        din(f"cos{i}", [S, 32])
        din(f"sin{i}", [S, 32])
        dscr(f"y{i}", [S, D], F32, out=True)
        dscr(f"y1_{i}", [S, D], F32)
        dscr(f"projT{i}", [3072, S], BF16, out=debug)
        dscr(f"qT{i}", [256, S], BF16, out=debug)
        dscr(f"kT{i}", [128, S], BF16, out=debug)
        dscr(f"va{i}", [S, 130], BF16, out=debug)
        dscr(f"bg{i}", [S, 16], F32, out=debug)
        dscr(f"mixT{i}", [1024, S], BF16, out=debug)
        dscr(f"convT{i}", [768, S], BF16, out=debug)
        dscr(f"of{i}", [S, 256], F32, out=debug)
        if debug:
            dscr(f"osum{i}", [S, 256], F32, out=True)
    din("w_in", [depth, 128, 8, NCOL])
    din("norm_w", [depth, 128, 8])
    din("w_out", [depth, 128, 8, D])
    din("qkw", [depth, 128, 6, 64])
    din("a_log", [depth, 128, 8])
    din("dt_bias", [depth, 128, 8])
    din("ident", [128, 128])
    din("sgu_wT", [depth, 128, 4, 128])
    din("sgu_bT", [depth, 128, 2, 128])
    din("conv_w", [depth, 128, 6, 5])
    din("pool_w", [depth, 128, 2, 128])
    din("pool_s", [depth, 128, 2])
    din("dn_w", [depth, 128, 4, 64])
    din("triU", [128, 128])
    din("triL", [128, 128])
    for i, S in enumerate(seqs):
        din(f"icnt{i}", [128, 2, S])

    with ExitStack() as top:
        kk = K(nc, top)
        for key in ("c0", "wl0", "wl1", "x0", "x1", "st0", "st1", "sq0", "sq1", "sp0", "sp1"):
            kk._newsem(key)
        with nc.Block() as blk0:
            @blk0.vector
            def _(eng):
                for key in kk.sems:
                    eng.sem_clear(kk.sems[key])
        ident_f = sb(nc, top, "ident_f", [128, 128], F32)
        ident = sb(nc, top, "ident_b", [128, 128], BF16)
        ones_f = sb(nc, top, "ones_f", [128, 128], F32)
        kk.op("sp", lambda e: e.dma_start(out=ident_f[:], in_=dr["ident"][:, :]), writes=[ident_f.b], dma="c0")
        kk.op("dve", lambda e: e.tensor_copy(out=ident[:], in_=ident_f[:]), reads=[ident_f.b], writes=[ident.b])
        kk.op("pool", lambda e: e.memset(ones_f[:], 1.0), writes=[ones_f.b])

        for l in range(depth):
            for si, S in enumerate(seqs):
                xin = dr[f"x{si}"] if l == 0 else dr[f"y1_{si}"]
                yout = dr[f"y{si}"] if l == depth - 1 else dr[f"y1_{si}"]
                for ph in phases:
                    if ph == "p":
                        phase_proj(nc, kk, dr, l, si, S, xin, ident, ident_f)
                    elif ph == "f":
                        phase_fm(nc, kk, dr, l, si, S, ident)
                    elif ph == "d":
                        phase_dn(nc, kk, dr, l, si, S, ident, ident_f, ones_f)
                    elif ph == "a":
                        phase_attn(nc, kk, dr, l, si, S, ones_f)
                    elif ph == "o":
                        phase_out(nc, kk, dr, l, si, S, xin, yout)
                    kk.barrier()
                    kk.flush()
    return nc


def load_w_bf16(nc, kk, st, name, src_ap, ncols, scale_t=None, chunk=1024):
    w = sb(nc, st, name, [128, 8, ncols], BF16)
    stg = [sb(nc, st, f"{name}_stg{i}", [128, chunk], F32) for i in range(2)]
    n = 0
    for kc in range(8):
        for c0 in range(0, ncols, chunk):
            cw = min(chunk, ncols - c0)
            s = stg[n % 2]
            kk.op("sp", lambda e, s=s, kc=kc, c0=c0, cw=cw: e.dma_start(out=s[:, 0:cw], in_=src_ap[:, kc, c0:c0 + cw]),
                  writes=[s.b], dma=f"wl{n % 2}")
            if scale_t is not None:
                kk.op("dve", lambda e, s=s, kc=kc, c0=c0, cw=cw: e.tensor_scalar(
                    out=w[:, kc, c0:c0 + cw], in0=s[:, 0:cw], scalar1=scale_t[:, kc:kc + 1], scalar2=1.0,
                    op0=ALU.mult, op1=ALU.mult), reads=[s.b, scale_t.b], writes=[w.b])
            else:
                eng = "dve" if n % 2 == 0 else "pool"
                kk.op(eng, lambda e, s=s, kc=kc, c0=c0, cw=cw: e.tensor_copy(out=w[:, kc, c0:c0 + cw], in_=s[:, 0:cw]),
                      reads=[s.b], writes=[w.b])
            n += 1
    return w


def phase_proj(nc, kk, dr, l, si, S, xin, ident, ident_f):
    with ExitStack() as st:
        nw = sb(nc, st, "nw", [128, 8], F32)
        kk.op("sp", lambda e: e.dma_start(out=nw[:], in_=dr["norm_w"][l]), writes=[nw.b], dma="c0")
        w = load_w_bf16(nc, kk, st, "w_in_sb", dr["w_in"][l], NCOL, scale_t=nw)
        qkw = sb(nc, st, "qkw", [128, 6, 64], F32)
        kk.op("sp", lambda e: e.dma_start(out=qkw[:], in_=dr["qkw"][l]), writes=[qkw.b], dma="c0")
        alog = sb(nc, st, "alog", [128, 8], F32)
        nA = sb(nc, st, "nA", [128, 8], F32)
        dtb = sb(nc, st, "dtb", [128, 8], F32)
        kk.op("sp", lambda e: e.dma_start(out=alog[:], in_=dr["a_log"][l]), writes=[alog.b], dma="c0")
        kk.op("sp", lambda e: e.dma_start(out=dtb[:], in_=dr["dt_bias"][l]), writes=[dtb.b], dma="c0")
        kk.op("act", lambda e: e.activation(out=nA[:], in_=alog[:], func=AF.Exp), reads=[alog.b], writes=[nA.b])
        kk.op("dve", lambda e: e.tensor_scalar(out=nA[:], in0=nA[:], scalar1=-1.0, scalar2=1.0, op0=ALU.mult, op1=ALU.mult),
              reads=[nA.b], writes=[nA.b])

        NS = 2
        xt = [sb(nc, st, f"xt{i}", [128, D], F32) for i in range(NS)]
        junk = sb(nc, st, "junk", [128, D], BF16)
        hb = [sb(nc, st, f"hb{i}", [128, D], BF16) for i in range(NS)]
        ssq = [sb(nc, st, f"ssq{i}", [128, 1], F32) for i in range(NS)]
        rstd = [sb(nc, st, f"rstd{i}", [128, 1], F32) for i in range(NS)]
        pT = [ps(nc, st, f"pT{i}", [128, 8, 128], BF16) for i in range(2)]
        hT = [sb(nc, st, f"hT{i}", [128, 8, 512], BF16) for i in range(2)]
        pacc = [ps(nc, st, f"pacc{i}", [128, 512], F32) for i in range(3)]
        ptk = [ps(nc, st, f"ptk{i}", [128, 512], F32) for i in range(1)]
        pbg = [ps(nc, st, f"pbg{i}", [128, 512], F32) for i in range(1)]
        ptr = ps(nc, st, "ptr", [128, 3, 128], BF16)
        stage = [sb(nc, st, f"stage{i}", [128, 24, 512], BF16) for i in range(2)]
        cs = [sb(nc, st, f"cs{i}", [128, 2, 32], F32) for i in range(NS)]
        qk = [sb(nc, st, f"qk{i}", [128, 6, 64], F32) for i in range(NS)]
        qsq = sb(nc, st, "qsq", [128, 6, 64], F32)
        qss = [sb(nc, st, f"qss{i}", [128, 6], F32) for i in range(NS)]
        qr = [sb(nc, st, f"qr{i}", [128, 6, 64], F32) for i in range(NS)]
        tmpa = sb(nc, st, "tmpa", [128, 6, 2, 16], F32)
        tmpb = sb(nc, st, "tmpb", [128, 6, 2, 16], F32)
        qkb = [sb(nc, st, f"qkb{i}", [128, 384], BF16) for i in range(NS)]
        qkT = [sb(nc, st, f"qkT{i}", [128, 3, 512], BF16) for i in range(2)]
        va = [sb(nc, st, f"va{i}", [128, 130], BF16) for i in range(NS)]
        bgt = [sb(nc, st, f"bgt{i}", [128, 16], F32) for i in range(NS)]
        t8 = [sb(nc, st, f"t8{i}", [128, 16], F32) for i in range(NS)]
        for v_ in va:
            kk.op("pool", lambda e, v_=v_: e.memset(v_[:], 1.0), writes=[v_.b])

        projT = dr[f"projT{si}"]
        nblk = S // 512
        ev = 0
        for tb in range(nblk):
            hTb = hT[tb % 2]
            qkTb = qkT[tb % 2]
            for t4 in range(4):
                ti = tb * 4 + t4
                s = ti % NS
                r0 = ti * 128
                x_, h_, sq_, rs_ = xt[s], hb[s], ssq[s], rstd[s]
                kk.op("sp", lambda e, x_=x_, r0=r0: e.dma_start(out=x_[:], in_=xin[r0:r0 + 128, :]),
                      writes=[x_.b], dma=f"x{s}")
                kk.op("sp", lambda e, c_=cs[s], r0=r0: e.dma_start(out=c_[:, 0, :], in_=dr[f"cos{si}"][r0:r0 + 128, :]),
                      writes=[cs[s].b], dma=f"x{s}")
                kk.op("sp", lambda e, c_=cs[s], r0=r0: e.dma_start(out=c_[:, 1, :], in_=dr[f"sin{si}"][r0:r0 + 128, :]),
                      writes=[cs[s].b], dma=f"x{s}")
                kk.op("act", lambda e, x_=x_, sq_=sq_: e.activation(out=junk[:], in_=x_[:], func=AF.Square, accum_out=sq_[:]),
                      reads=[x_.b], writes=[junk.b, sq_.b])
                kk.op("dve", lambda e, sq_=sq_, rs_=rs_: e.tensor_scalar(out=rs_[:], in0=sq_[:], scalar1=1.0 / D, scalar2=EPS,
                                                                         op0=ALU.mult, op1=ALU.add), reads=[sq_.b], writes=[rs_.b])
                kk.op("act", lambda e, rs_=rs_: e.activation(out=rs_[:], in_=rs_[:], func=AF.Ln), reads=[rs_.b], writes=[rs_.b])
                kk.op("act", lambda e, rs_=rs_: e.activation(out=rs_[:], in_=rs_[:], func=AF.Exp, scale=-0.5), reads=[rs_.b], writes=[rs_.b])
                kk.op("dve", lambda e, x_=x_, h_=h_, rs_=rs_: e.tensor_scalar(out=h_[:], in0=x_[:], scalar1=rs_[:, 0:1], scalar2=1.0,
                                                                              op0=ALU.mult, op1=ALU.mult),
                      reads=[x_.b, rs_.b], writes=[h_.b])
                p_ = pT[ti % 2]
                for kc in range(8):
                    kk.op("pe", lambda e, p_=p_, h_=h_, kc=kc: e.transpose(out=p_[:, kc, :], in_=h_[:, kc * 128:(kc + 1) * 128],
                                                                          identity=ident[:]),
                          reads=[h_.b, ident.b], writes=[p_.b])
                kk.op("act", lambda e, p_=p_, hTb=hTb, t4=t4: e.activation(out=hTb[:, :, t4 * 128:(t4 + 1) * 128], in_=p_[:],
                                                                           func=AF.Copy),
                      reads=[p_.b], writes=[hTb.b])
                pk = ptk[0]
                for kc in range(8):
                    kk.op("pe", lambda e, pk=pk, hTb=hTb, t4=t4, kc=kc: e.matmul(
                        pk[:], hTb[:, kc, t4 * 128:(t4 + 1) * 128], w[:, kc, G_CQ * 128:G_CQ * 128 + 512],
                        start=(kc == 0), stop=(kc == 7)), reads=[hTb.b, w.b], writes=[pk.b])
                pb_ = pbg[0]
                for kc in range(8):
                    kk.op("pe", lambda e, pb_=pb_, hTb=hTb, t4=t4, kc=kc: e.matmul(
                        pb_[:, 0:16], hTb[:, kc, t4 * 128:(t4 + 1) * 128], w[:, kc, 3072:3088],
                        start=(kc == 0), stop=(kc == 7)), reads=[hTb.b, w.b], writes=[pb_.b])
                q_, ss_, r_, qb_, va_, c_ = qk[s], qss[s], qr[s], qkb[s], va[s], cs[s]
                kk.op("dve", lambda e, q_=q_, pk=pk: e.tensor_copy(out=q_[:].rearrange("p a b -> p (a b)"), in_=pk[:, 0:384]),
                      reads=[pk.b], writes=[q_.b])
                kk.op("dve", lambda e, va_=va_, pk=pk: e.tensor_copy(
                    out=va_[:].rearrange("p (a b) -> p a b", a=2)[:, :, 0:64],
                    in_=pk[:, 384:512].rearrange("p (a b) -> p a b", a=2)), reads=[pk.b], writes=[va_.b])
                kk.op("pool", lambda e, va_=va_, r0=r0: e.dma_start(out=dr[f"va{si}"][r0:r0 + 128, :], in_=va_[:]),
                      reads=[va_.b], dma=f"st{s}")
                kk.op("dve", lambda e, q_=q_: e.tensor_tensor(out=qsq[:], in0=q_[:], in1=q_[:], op=ALU.mult),
                      reads=[q_.b], writes=[qsq.b])
                kk.op("dve", lambda e, ss_=ss_: e.tensor_reduce(out=ss_[:], in_=qsq[:], axis=AX.X, op=ALU.add),
                      reads=[qsq.b], writes=[ss_.b])
                kk.op("dve", lambda e, ss_=ss_: e.tensor_scalar(out=ss_[:], in0=ss_[:], scalar1=1.0 / 64, scalar2=EPS,
                                                                op0=ALU.mult, op1=ALU.add), reads=[ss_.b], writes=[ss_.b])
                kk.op("act", lambda e, ss_=ss_: e.activation(out=ss_[:], in_=ss_[:], func=AF.Ln), reads=[ss_.b], writes=[ss_.b])
                kk.op("act", lambda e, ss_=ss_: e.activation(out=ss_[:], in_=ss_[:], func=AF.Exp, scale=-0.5), reads=[ss_.b], writes=[ss_.b])
                kk.op("dve", lambda e, q_=q_, ss_=ss_: e.tensor_tensor(
                    out=q_[:], in0=q_[:], in1=ss_[:].unsqueeze(2).to_broadcast([128, 6, 64]), op=ALU.mult),
                    reads=[q_.b, ss_.b], writes=[q_.b])
                kk.op("pool", lambda e, q_=q_: e.tensor_tensor(out=q_[:], in0=q_[:], in1=qkw[:], op=ALU.mult),
                      reads=[q_.b, qkw.b], writes=[q_.b])
                def v5(t):
                    return t[:].rearrange("p h (a b f) -> p h a b f", a=2, b=2)
                cosb = lambda c_: c_[:, 0, :].rearrange("p (a f) -> p a f", a=2).unsqueeze(1).to_broadcast([128, 6, 2, 16])
                sinb = lambda c_: c_[:, 1, :].rearrange("p (a f) -> p a f", a=2).unsqueeze(1).to_broadcast([128, 6, 2, 16])
                kk.op("dve", lambda e, q_=q_, c_=c_: e.tensor_tensor(out=tmpa[:], in0=v5(q_)[:, :, :, 1, :], in1=sinb(c_), op=ALU.mult),
                      reads=[q_.b, c_.b], writes=[tmpa.b])
                kk.op("pool", lambda e, q_=q_, c_=c_: e.tensor_tensor(out=tmpb[:], in0=v5(q_)[:, :, :, 0, :], in1=sinb(c_), op=ALU.mult),
                      reads=[q_.b, c_.b], writes=[tmpb.b])
                kk.op("dve", lambda e, q_=q_, r_=r_, c_=c_: e.tensor_tensor(out=v5(r_)[:, :, :, 0, :], in0=v5(q_)[:, :, :, 0, :], in1=cosb(c_), op=ALU.mult),
                      reads=[q_.b, c_.b], writes=[r_.b])
                kk.op("pool", lambda e, q_=q_, r_=r_, c_=c_: e.tensor_tensor(out=v5(r_)[:, :, :, 1, :], in0=v5(q_)[:, :, :, 1, :], in1=cosb(c_), op=ALU.mult),
                      reads=[q_.b, c_.b], writes=[r_.b])
                kk.op("dve", lambda e, r_=r_: e.tensor_tensor(out=v5(r_)[:, :, :, 0, :], in0=v5(r_)[:, :, :, 0, :], in1=tmpa[:], op=ALU.subtract),
                      reads=[r_.b, tmpa.b], writes=[r_.b])
                kk.op("dve", lambda e, r_=r_: e.tensor_tensor(out=v5(r_)[:, :, :, 1, :], in0=v5(r_)[:, :, :, 1, :], in1=tmpb[:], op=ALU.add),
                      reads=[r_.b, tmpb.b], writes=[r_.b])
                kk.op("act", lambda e, r_=r_, qb_=qb_: e.activation(out=qb_[:], in_=r_[:].rearrange("p a b -> p (a b)"), func=AF.Copy),
                      reads=[r_.b], writes=[qb_.b])
                for j in range(3):
                    kk.op("pe", lambda e, qb_=qb_, j=j: e.transpose(out=ptr[:, j, :], in_=qb_[:, j * 128:(j + 1) * 128], identity=ident[:]),
                          reads=[qb_.b, ident.b], writes=[ptr.b])
                kk.op("dve", lambda e, qkTb=qkTb, t4=t4: e.tensor_copy(out=qkTb[:, :, t4 * 128:(t4 + 1) * 128], in_=ptr[:]),
                      reads=[ptr.b], writes=[qkTb.b])
                b_, t_ = bgt[s], t8[s]
                kk.op("dve", lambda e, t_=t_, pb_=pb_: e.tensor_tensor(out=t_[:, 8:16], in0=pb_[:, 8:16], in1=dtb[:], op=ALU.add),
                      reads=[pb_.b, dtb.b], writes=[t_.b])
                kk.op("act", lambda e, t_=t_, pb_=pb_: e.activation(out=t_[:, 0:8], in_=pb_[:, 0:8], func=AF.Exp, scale=-1.0),
                      reads=[pb_.b], writes=[t_.b])
                kk.op("act", lambda e, t_=t_: e.activation(out=t_[:, 8:16], in_=t_[:, 8:16], func=AF.Exp),
                      reads=[t_.b], writes=[t_.b])
                kk.op("act", lambda e, t_=t_: e.activation(out=t_[:, 8:16], in_=t_[:, 8:16], func=AF.Ln, bias=1.0),
                      reads=[t_.b], writes=[t_.b])
                kk.op("dve", lambda e, t_=t_: e.tensor_scalar(out=t_[:, 0:8], in0=t_[:, 0:8], scalar1=1.0, scalar2=1.0, op0=ALU.add, op1=ALU.mult),
                      reads=[t_.b], writes=[t_.b])
                kk.op("dve", lambda e, t_=t_, b_=b_: e.reciprocal(out=b_[:, 0:8], in_=t_[:, 0:8]), reads=[t_.b], writes=[b_.b])
                kk.op("dve", lambda e, t_=t_, b_=b_: e.tensor_tensor(out=b_[:, 8:16], in0=t_[:, 8:16], in1=nA[:], op=ALU.mult),
                      reads=[t_.b, nA.b], writes=[b_.b])
                kk.op("pool", lambda e, b_=b_, r0=r0: e.dma_start(out=dr[f"bg{si}"][r0:r0 + 128, :], in_=b_[:]),
                      reads=[b_.b], dma=f"st{s}")
            c0 = tb * 512
            kk.op("pool", lambda e, qkTb=qkTb, c0=c0: e.dma_start(
                out=dr[f"qT{si}"][:, c0:c0 + 512].rearrange("(j p) s -> p j s", p=128), in_=qkTb[:, 0:2, :]),
                reads=[qkTb.b], dma=f"sq{tb % 2}")
            kk.op("pool", lambda e, qkTb=qkTb, c0=c0: e.dma_start(out=dr[f"kT{si}"][:, c0:c0 + 512], in_=qkTb[:, 2, :]),
                  reads=[qkTb.b], dma=f"sq{tb % 2}")
            stg = stage[tb % 2]
            for g in range(24):
                pa = pacc[g % 3]
                for kc in range(8):
                    kk.op("pe", lambda e, pa=pa, g=g, kc=kc, hTb=hTb: e.matmul(
                        pa[:], w[:, kc, g * 128:(g + 1) * 128], hTb[:, kc, :], start=(kc == 0), stop=(kc == 7)),
                        reads=[w.b, hTb.b], writes=[pa.b])
                if ev % 2 == 0:
                    kk.op("act", lambda e, pa=pa, g=g, stg=stg: e.activation(out=stg[:, g, :], in_=pa[:], func=AF.Copy),
                          reads=[pa.b], writes=[stg.b])
                else:
                    kk.op("dve", lambda e, pa=pa, g=g, stg=stg: e.tensor_copy(out=stg[:, g, :], in_=pa[:]),
                          reads=[pa.b], writes=[stg.b])
                ev += 1
            for h2 in range(2):
                kk.op("pool", lambda e, stg=stg, c0=c0, h2=h2: e.dma_start(
                    out=projT[h2 * 1536:(h2 + 1) * 1536, c0:c0 + 512].rearrange("(g p) s -> p g s", p=128),
                    in_=stg[:, h2 * 12:(h2 + 1) * 12, :]), reads=[stg.b], dma=f"sp{tb % 2}")


def phase_out(nc, kk, dr, l, si, S, xin, yout):
    with ExitStack() as st:
        wo = load_w_bf16(nc, kk, st, "w_out_sb", dr["w_out"][l], D)
        NS = 2
        mt = [sb(nc, st, f"mt{i}", [128, 8, 128], BF16) for i in range(NS)]
        xt = [sb(nc, st, f"xo{i}", [128, D], F32) for i in range(NS)]
        yt = [sb(nc, st, f"yo{i}", [128, D], F32) for i in range(NS)]
        po = [ps(nc, st, f"po{i}", [128, 512], F32) for i in range(4)]
        mixT = dr[f"mixT{si}"]
        for ti in range(S // 128):
            s = ti % NS
            r0 = ti * 128
            m_, x_, y_ = mt[s], xt[s], yt[s]
            kk.op("sp", lambda e, m_=m_, r0=r0: e.dma_start(out=m_[:], in_=mixT[:, r0:r0 + 128].rearrange("(k p) s -> p k s", p=128)),
                  writes=[m_.b], dma=f"x{s}")
            kk.op("sp", lambda e, x_=x_, r0=r0: e.dma_start(out=x_[:], in_=xin[r0:r0 + 128, :]), writes=[x_.b], dma=f"x{s}")
            for g in range(2):
                p_ = po[(ti * 2 + g) % 4]
                for kc in range(8):
                    kk.op("pe", lambda e, p_=p_, m_=m_, kc=kc, g=g: e.matmul(
                        p_[:], m_[:, kc, :], wo[:, kc, g * 512:(g + 1) * 512], start=(kc == 0), stop=(kc == 7)),
                        reads=[m_.b, wo.b], writes=[p_.b])
                kk.op("dve", lambda e, p_=p_, x_=x_, y_=y_, g=g: e.tensor_tensor(
                    out=y_[:, g * 512:(g + 1) * 512], in0=p_[:], in1=x_[:, g * 512:(g + 1) * 512], op=ALU.add),
                    reads=[p_.b, x_.b], writes=[y_.b])
            kk.op("pool", lambda e, y_=y_, r0=r0: e.dma_start(out=yout[r0:r0 + 128, :], in_=y_[:]), reads=[y_.b], dma=f"st{s}")


def phase_attn(nc, kk, dr, l, si, S, ones_f):
    with ExitStack() as st:
        nkt = S // 128
        KT = sb(nc, st, "KT", [128, S], BF16)
        VA = sb(nc, st, "VA", [128, nkt, 130], BF16)
        for c in range(0, S, 2048):
            ce = min(c + 2048, S)
            kk.op("sp", lambda e, c=c, ce=ce: e.dma_start(out=KT[:, c:ce], in_=dr[f"kT{si}"][:, c:ce]), writes=[KT.b], dma="c0")
        for c in range(0, nkt, 16):
            ce = min(c + 16, nkt)
            kk.op("sp", lambda e, c=c, ce=ce: e.dma_start(out=VA[:, c:ce, :],
                                                          in_=dr[f"va{si}"][c * 128:ce * 128, :].rearrange("(t p) c -> p t c", p=128)),
                  writes=[VA.b], dma="c0")
        qt = [sb(nc, st, f"qt{i}", [128, 2, 512], BF16) for i in range(2)]
        zt = [sb(nc, st, f"zt{i}", [64, 4, 512], BF16) for i in range(2)]
        pss = [ps(nc, st, f"pss{i}", [128, 2, 512], F32) for i in range(2)]
        pex = [sb(nc, st, f"pex{i}", [128, 2, 512], BF16) for i in range(2)]
        pov = [ps(nc, st, f"pov{i}", [128, 512], F32) for i in range(2)]
        pbc = ps(nc, st, "pbc", [64, 512], F32)
        osb = [sb(nc, st, f"osb{i}", [128, 512], F32) for i in range(2)]
        rc = [sb(nc, st, f"rc{i}", [128, 512], F32) for i in range(2)]
        og = [sb(nc, st, f"og{i}", [64, 4, 512], BF16) for i in range(2)]
        it = 0
        for qb in range(S // 512):
            c0 = qb * 512
            q_ = qt[qb % 2]
            z_ = zt[qb % 2]
            og_ = og[qb % 2]
            for h in range(4):
                kv = h // 2
                kk.op("sp", lambda e, q_=q_, h=h, kv=kv, c0=c0: e.dma_start(
                    out=q_[kv * 64:(kv + 1) * 64, h % 2, :], in_=dr[f"qT{si}"][h * 64:(h + 1) * 64, c0:c0 + 512]),
                    writes=[q_.b], dma=f"x{qb % 2}")
            kk.op("sp", lambda e, z_=z_, c0=c0: e.dma_start(
                out=z_[:], in_=dr[f"projT{si}"][G_CZ * 128:(G_CZ + 2) * 128, c0:c0 + 512].rearrange("(h p) s -> p h s", p=64)),
                writes=[z_.b], dma=f"x{qb % 2}")
            kk.op("act", lambda e, z_=z_: e.activation(out=z_[:], in_=z_[:], func=AF.Silu), reads=[z_.b], writes=[z_.b])
            for h in range(4):
                kv = h // 2
                po_ = pov[h % 2]
                for kp in range(nkt // 2):
                    ps_ = pss[it % 2]
                    pe_ = pex[it % 2]
                    it += 1
                    for j in range(2):
                        kt = kp * 2 + j
                        kk.op("pe", lambda e, ps_=ps_, j=j, kt=kt, kv=kv, q_=q_, h=h: e.matmul(
                            ps_[:, j, :], KT[kv * 64:(kv + 1) * 64, kt * 128:(kt + 1) * 128], q_[kv * 64:(kv + 1) * 64, h % 2, :],
                            start=True, stop=True), reads=[KT.b, q_.b], writes=[ps_.b])
                    kk.op("act", lambda e, ps_=ps_, pe_=pe_: e.activation(out=pe_[:], in_=ps_[:], func=AF.Exp, scale=0.125),
                          reads=[ps_.b], writes=[pe_.b])
                    for j in range(2):
                        kt = kp * 2 + j
                        kk.op("pe", lambda e, po_=po_, pe_=pe_, j=j, kt=kt, kv=kv: e.matmul(
                            po_[0:65, :], VA[:, kt, kv * 65:(kv + 1) * 65], pe_[:, j, :],
                            start=(kt == 0), stop=(kt == nkt - 1)), reads=[VA.b, pe_.b], writes=[po_.b])
                o_ = osb[h % 2]
                r_ = rc[h % 2]
                kk.op("dve", lambda e, o_=o_, po_=po_: e.tensor_copy(out=o_[0:65, :], in_=po_[0:65, :]), reads=[po_.b], writes=[o_.b])
                kk.op("dve", lambda e, o_=o_, r_=r_: e.reciprocal(out=r_[64:65, :], in_=o_[64:65, :]), reads=[o_.b], writes=[r_.b])
                kk.op("pe", lambda e, r_=r_: e.matmul(pbc[:], ones_f[64:65, 0:64], r_[64:65, :], start=True, stop=True),
                      reads=[r_.b, ones_f.b], writes=[pbc.b])
                kk.op("dve", lambda e, o_=o_: e.tensor_tensor(out=o_[0:64, :], in0=o_[0:64, :], in1=pbc[:], op=ALU.mult),
                      reads=[o_.b, pbc.b], writes=[o_.b])
                kk.op("dve", lambda e, o_=o_, og_=og_, h=h, z_=z_: e.tensor_tensor(out=og_[:, h, :], in0=o_[0:64, :], in1=z_[:, h, :], op=ALU.mult),
                      reads=[o_.b, z_.b], writes=[og_.b])
            kk.op("pool", lambda e, og_=og_, c0=c0: e.dma_start(
                out=dr[f"mixT{si}"][512:768, c0:c0 + 512].rearrange("(h p) s -> p h s", p=64), in_=og_[:]),
                reads=[og_.b], dma=f"st{qb % 2}")


def phase_fm(nc, kk, dr, l, si, S, ident):
    with ExitStack() as st:
        projT = dr[f"projT{si}"]
        swf = sb(nc, st, "swf", [128, 4, 128], F32)
        sw = sb(nc, st, "sw", [128, 4, 128], BF16)
        sbT = sb(nc, st, "sbT", [128, 2, 128], F32)
        cw = sb(nc, st, "cw", [128, 6, 5], F32)
        pwf = sb(nc, st, "pwf", [128, 2, 128], F32)
        pw = sb(nc, st, "pw", [128, 2, 128], BF16)
        psc = sb(nc, st, "psc", [128, 2], F32)
        kk.op("sp", lambda e: e.dma_start(out=swf[:], in_=dr["sgu_wT"][l]), writes=[swf.b], dma="c0")
        kk.op("sp", lambda e: e.dma_start(out=sbT[:], in_=dr["sgu_bT"][l]), writes=[sbT.b], dma="c0")
        kk.op("sp", lambda e: e.dma_start(out=cw[:], in_=dr["conv_w"][l]), writes=[cw.b], dma="c0")
        kk.op("sp", lambda e: e.dma_start(out=pwf[:], in_=dr["pool_w"][l]), writes=[pwf.b], dma="c0")
        kk.op("sp", lambda e: e.dma_start(out=psc[:], in_=dr["pool_s"][l]), writes=[psc.b], dma="c0")
        kk.op("dve", lambda e: e.tensor_copy(out=sw[:], in_=swf[:]), reads=[swf.b], writes=[sw.b])
        kk.op("dve", lambda e: e.tensor_copy(out=pw[:], in_=pwf[:]), reads=[pwf.b], writes=[pw.b])

        NS = 2
        av = [sb(nc, st, f"av{i}", [128, 6, 512], BF16) for i in range(NS)]
        dxz = [sb(nc, st, f"dxz{i}", [128, 4, 528], BF16) for i in range(NS)]
        icn = [sb(nc, st, f"icn{i}", [128, 2, 512], F32) for i in range(NS)]
        bx = [sb(nc, st, f"bx{i}", [128, 6, 516], BF16) for i in range(NS)]
        pvt = ps(nc, st, "pvt", [128, 2, 128], BF16)
        vtok = sb(nc, st, "vtok", [128, 4, 64], F32)
        vsq = sb(nc, st, "vsq", [128, 4, 64], F32)
        vss = sb(nc, st, "vss", [128, 4], F32)
        vnm = [sb(nc, st, f"vnm{i}", [128, 4, 128], BF16) for i in range(2)]
        for v_ in vnm:
            kk.op("pool", lambda e, v_=v_: e.memset(v_[:], 0.0), writes=[v_.b])
        pm = [ps(nc, st, f"pm{i}", [128, 2, 512], F32) for i in range(1)]
        ta = sb(nc, st, "ta", [128, 2, 512], F32)
        sz = sb(nc, st, "sz", [128, 2, 512], BF16)
        ma = [sb(nc, st, f"ma{i}", [128, 2, 512], BF16) for i in range(NS)]
        ss = sb(nc, st, "ss", [128, 2, 528], F32)
        s2 = sb(nc, st, "s2", [128, 2, 528], F32)
        s4 = sb(nc, st, "s4", [128, 2, 528], F32)
        s8 = sb(nc, st, "s8", [128, 528], F32)
        dfb = sb(nc, st, "dfb", [128, 2, 512], BF16)
        ppl = ps(nc, st, "ppl", [128, 2, 512], F32)
        md = [sb(nc, st, f"md{i}", [128, 2, 512], BF16) for i in range(NS)]
        szd = sb(nc, st, "szd", [128, 2, 512], BF16)
        cacc = sb(nc, st, "cacc", [128, 512], F32)
        cvo = [sb(nc, st, f"cvo{i}", [128, 6, 512], BF16) for i in range(NS)]

        nblk = S // 512
        for tb in range(nblk):
            s = tb % NS
            c0 = tb * 512
            a_, d_, i_, b_ = av[s], dxz[s], icn[s], bx[s]
            kk.op("sp", lambda e, a_=a_, c0=c0: e.dma_start(out=a_[:], in_=projT[0:768, c0:c0 + 512].rearrange("(g p) s -> p g s", p=128)),
                  writes=[a_.b], dma=f"x{s}")
            lo = max(c0 - 8, 0)
            hi = min(c0 + 520, S)
            if lo > c0 - 8:
                kk.op("pool", lambda e, d_=d_: e.memset(d_[:, 0:2, 0:8], 0.0), writes=[d_.b])
            if hi < c0 + 520:
                kk.op("pool", lambda e, d_=d_: e.memset(d_[:, 0:2, 520:528], 0.0), writes=[d_.b])
            kk.op("sp", lambda e, d_=d_, lo=lo, hi=hi, c0=c0: e.dma_start(
                out=d_[:, 0:2, lo - (c0 - 8):hi - (c0 - 8)], in_=projT[G_DX * 128:(G_DX + 2) * 128, lo:hi].rearrange("(g p) s -> p g s", p=128)),
                writes=[d_.b], dma=f"x{s}")
            kk.op("sp", lambda e, d_=d_, c0=c0: e.dma_start(
                out=d_[:, 2:4, 0:512], in_=projT[G_DZ * 128:(G_DZ + 2) * 128, c0:c0 + 512].rearrange("(g p) s -> p g s", p=128)),
                writes=[d_.b], dma=f"x{s}")
            kk.op("sp", lambda e, i_=i_, c0=c0: e.dma_start(out=i_[:], in_=dr[f"icnt{si}"][:, :, c0:c0 + 512]), writes=[i_.b], dma=f"x{s}")
            lo2 = max(c0 - 2, 0)
            hi2 = min(c0 + 514, S)
            if lo2 > c0 - 2:
                kk.op("pool", lambda e, b_=b_: e.memset(b_[:, :, 0:2], 0.0), writes=[b_.b])
            if hi2 < c0 + 514:
                kk.op("pool", lambda e, b_=b_: e.memset(b_[:, :, 514:516], 0.0), writes=[b_.b])
            kk.op("sp", lambda e, b_=b_, lo2=lo2, hi2=hi2, c0=c0: e.dma_start(
                out=b_[:, :, lo2 - (c0 - 2):hi2 - (c0 - 2)], in_=projT[G_BQ * 128:(G_BQ + 6) * 128, lo2:hi2].rearrange("(g p) s -> p g s", p=128)),
                writes=[b_.b], dma=f"x{s}")

            pm_ = pm[0]
            for ch in range(4):
                vn_ = vnm[ch % 2]
                for j in range(2):
                    kk.op("pe", lambda e, a_=a_, j=j, ch=ch: e.transpose(out=pvt[:, j, :], in_=a_[:, 2 + j, ch * 128:(ch + 1) * 128], identity=ident[:]),
                          reads=[a_.b, ident.b], writes=[pvt.b])
                kk.op("dve", lambda e: e.tensor_copy(out=vtok[:].rearrange("p a b -> p (a b)"), in_=pvt[:].rearrange("p a b -> p (a b)")),
                      reads=[pvt.b], writes=[vtok.b])
                kk.op("dve", lambda e: e.tensor_tensor(out=vsq[:], in0=vtok[:], in1=vtok[:], op=ALU.mult), reads=[vtok.b], writes=[vsq.b])
                kk.op("dve", lambda e: e.tensor_reduce(out=vss[:], in_=vsq[:], axis=AX.X, op=ALU.add), reads=[vsq.b], writes=[vss.b])
                kk.op("dve", lambda e: e.tensor_scalar(out=vss[:], in0=vss[:], scalar1=1.0 / 64, scalar2=EPS, op0=ALU.mult, op1=ALU.add),
                      reads=[vss.b], writes=[vss.b])
                kk.op("act", lambda e: e.activation(out=vss[:], in_=vss[:], func=AF.Ln), reads=[vss.b], writes=[vss.b])
                kk.op("act", lambda e: e.activation(out=vss[:], in_=vss[:], func=AF.Exp, scale=-0.5), reads=[vss.b], writes=[vss.b])
                for h in range(4):
                    kk.op("dve", lambda e, vn_=vn_, h=h: e.tensor_scalar(
                        out=vn_[:, h, (h % 2) * 64:(h % 2) * 64 + 64], in0=vtok[:, h, :], scalar1=vss[:, h:h + 1], scalar2=1.0,
                        op0=ALU.mult, op1=ALU.mult), reads=[vtok.b, vss.b], writes=[vn_.b])
                for h in range(4):
                    kk.op("pe", lambda e, vn_=vn_, h=h, ch=ch, pm_=pm_: e.matmul(
                        pm_[:, h // 2, ch * 128:(ch + 1) * 128], vn_[:, h, :], sw[:, h, :], start=(h % 2 == 0), stop=(h % 2 == 1)),
                        reads=[vn_.b, sw.b], writes=[pm_.b])
            m_ = ma[s]
            kk.op("dve", lambda e, pm_=pm_: e.tensor_tensor(
                out=ta[:].rearrange("p a (c i) -> p a c i", c=4), in0=pm_[:].rearrange("p a (c i) -> p a c i", c=4),
                in1=sbT[:].unsqueeze(2).to_broadcast([128, 2, 4, 128]), op=ALU.add), reads=[pm_.b, sbT.b], writes=[ta.b])
            kk.op("act", lambda e, a_=a_: e.activation(out=sz[:], in_=a_[:, 4:6, :], func=AF.Silu), reads=[a_.b], writes=[sz.b])
            kk.op("pool", lambda e, a_=a_: e.tensor_tensor(out=ta[:], in0=ta[:], in1=a_[:, 0:2, :], op=ALU.mult), reads=[ta.b, a_.b], writes=[ta.b])
            kk.op("dve", lambda e, m_=m_: e.tensor_tensor(out=m_[:], in0=ta[:], in1=sz[:], op=ALU.mult), reads=[ta.b, sz.b], writes=[m_.b])
            kk.op("pool", lambda e, m_=m_, c0=c0: e.dma_start(out=dr[f"mixT{si}"][0:256, c0:c0 + 512].rearrange("(g p) s -> p g s", p=128), in_=m_[:]),
                  reads=[m_.b], dma=f"st{s}")

            X = d_
            kk.op("dve", lambda e, X=X: e.tensor_tensor(out=s2[:, :, 1:528], in0=X[:, 0:2, 0:527], in1=X[:, 0:2, 1:528], op=ALU.add),
                  reads=[X.b], writes=[s2.b])
            kk.op("pool", lambda e: e.tensor_tensor(out=s4[:, :, 2:527], in0=s2[:, :, 1:526], in1=s2[:, :, 3:528], op=ALU.add),
                  reads=[s2.b], writes=[s4.b])
            kk.op("dve", lambda e: e.tensor_tensor(out=s8[:, 4:525], in0=s4[:, 1, 2:523], in1=s4[:, 1, 6:527], op=ALU.add),
                  reads=[s4.b], writes=[s8.b])
            kk.op("pool", lambda e: e.tensor_copy(out=ss[0:64, 0, 8:520], in_=s2[0:64, 0, 8:520]), reads=[s2.b], writes=[ss.b])
            kk.op("pool", lambda e: e.tensor_copy(out=ss[64:128, 0, 8:520], in_=s4[64:128, 0, 8:520]), reads=[s4.b], writes=[ss.b])
            kk.op("dve", lambda e: e.tensor_copy(out=ss[0:64, 1, 8:520], in_=s8[0:64, 8:520]), reads=[s8.b], writes=[ss.b])
            kk.op("dve", lambda e: e.tensor_tensor(out=ss[64:128, 1, 8:520], in0=s8[64:128, 4:516], in1=s8[64:128, 12:524], op=ALU.add),
                  reads=[s8.b], writes=[ss.b])
            kk.op("dve", lambda e, i_=i_: e.tensor_tensor(out=ss[:, :, 8:520], in0=ss[:, :, 8:520], in1=i_[:], op=ALU.mult),
                  reads=[ss.b, i_.b], writes=[ss.b])
            kk.op("dve", lambda e, X=X: e.tensor_tensor(out=dfb[:], in0=ss[:, :, 8:520], in1=X[:, 0:2, 8:520], op=ALU.subtract),
                  reads=[ss.b, X.b], writes=[dfb.b])
            for ch in range(2):
                kk.op("pe", lambda e, ch=ch: e.matmul(ppl[:, ch, :], pw[:, ch, :], dfb[:, ch, :], start=True, stop=True),
                      reads=[pw.b, dfb.b], writes=[ppl.b])
            kk.op("act", lambda e, X=X: e.activation(out=szd[:], in_=X[:, 2:4, 0:512], func=AF.Silu), reads=[X.b], writes=[szd.b])
            o_ = md[s]
            for ch in range(2):
                kk.op("dve", lambda e, ch=ch, o_=o_: e.scalar_tensor_tensor(
                    out=o_[:, ch, :], in0=ppl[:, ch, :], scalar=psc[:, ch:ch + 1], in1=szd[:, ch, :], op0=ALU.mult, op1=ALU.mult),
                    reads=[ppl.b, psc.b, szd.b], writes=[o_.b])
            kk.op("pool", lambda e, o_=o_, c0=c0: e.dma_start(out=dr[f"mixT{si}"][768:1024, c0:c0 + 512].rearrange("(g p) s -> p g s", p=128), in_=o_[:]),
                  reads=[o_.b], dma=f"st{s}")

            co = cvo[s]
            for ch in range(6):
                kk.op("dve", lambda e, b_=b_, ch=ch: e.tensor_scalar(out=cacc[:], in0=b_[:, ch, 0:512], scalar1=cw[:, ch, 0:1], scalar2=1.0,
                                                                    op0=ALU.mult, op1=ALU.mult), reads=[b_.b, cw.b], writes=[cacc.b])
                for i in range(1, 5):
                    kk.op("dve", lambda e, b_=b_, ch=ch, i=i: e.scalar_tensor_tensor(
                        out=cacc[:], in0=b_[:, ch, i:i + 512], scalar=cw[:, ch, i:i + 1], in1=cacc[:], op0=ALU.mult, op1=ALU.add),
                        reads=[b_.b, cw.b, cacc.b], writes=[cacc.b])
                kk.op("act", lambda e, co=co, ch=ch: e.activation(out=co[:, ch, :], in_=cacc[:], func=AF.Silu), reads=[cacc.b], writes=[co.b])
            kk.op("pool", lambda e, co=co, c0=c0: e.dma_start(out=dr[f"convT{si}"][:, c0:c0 + 512].rearrange("(g p) s -> p g s", p=128), in_=co[:]),
                  reads=[co.b], dma=f"st{s}")


def bc(ap, shape, axis):
    return ap.unsqueeze(axis).to_broadcast(shape)


def phase_dn(nc, kk, dr, l, si, S, ident, ident_f, ones_f):
    with ExitStack() as st:
        NCH = S // 128
        tri = [sb(nc, st, f"tri{d}", [128, 128], F32) for d in range(2)]
        nmI = [sb(nc, st, f"nmI{d}", [128, 128], F32) for d in range(2)]
        nmS = [sb(nc, st, f"nmS{d}", [128, 128], F32) for d in range(2)]
        offd = sb(nc, st, "offd", [128, 128], F32)
        dnw = sb(nc, st, "dnw", [128, 4, 64], F32)
        kk.op("sp", lambda e: e.dma_start(out=tri[0][:], in_=dr["triU"][:, :]), writes=[tri[0].b], dma="c0")
        kk.op("sp", lambda e: e.dma_start(out=tri[1][:], in_=dr["triL"][:, :]), writes=[tri[1].b], dma="c0")
        kk.op("sp", lambda e: e.dma_start(out=dnw[:], in_=dr["dn_w"][l]), writes=[dnw.b], dma="c0")
        kk.op("dve", lambda e: e.tensor_scalar(out=offd[:], in0=ident_f[:], scalar1=-1.0, scalar2=1.0, op0=ALU.mult, op1=ALU.add),
              reads=[ident_f.b], writes=[offd.b])
        for d in range(2):
            kk.op("dve", lambda e, d=d: e.tensor_scalar(out=nmI[d][:], in0=tri[d][:], scalar1=-1.0, scalar2=1e30, op0=ALU.add, op1=ALU.mult),
                  reads=[tri[d].b], writes=[nmI[d].b])
            kk.op("dve", lambda e, d=d: e.tensor_tensor(out=nmS[d][:], in0=tri[d][:], in1=offd[:], op=ALU.mult),
                  reads=[tri[d].b, offd.b], writes=[nmS[d].b])
            kk.op("dve", lambda e, d=d: e.tensor_scalar(out=nmS[d][:], in0=nmS[d][:], scalar1=-1.0, scalar2=1e30, op0=ALU.add, op1=ALU.mult),
                  reads=[nmS[d].b], writes=[nmS[d].b])

        bkb = ps(nc, st, "bkb", [128, 1024], BF16)
        bk1 = ps(nc, st, "bk1", [128, 512], F32)
        bk2 = ps(nc, st, "bk2", [128, 512], F32)
        bk3 = ps(nc, st, "bk3", [128, 512], F32)
        bk4 = ps(nc, st, "bk4", [128, 512], F32)
        bk5 = ps(nc, st, "bk5", [128, 512], F32)
        bka = ps(nc, st, "bka", [128, 1024], F32)

        def v4(ap, n=4):
            return ap.rearrange("p (c i) -> p c i", c=n)

        cT = [sb(nc, st, f"cT{i}", [128, 6, 128], BF16) for i in range(2)]
        bgc = [sb(nc, st, f"bgc{i}", [128, 16], F32) for i in range(2)]
        ofl = [sb(nc, st, f"ofl{i}", [128, 256], F32) for i in range(2)]
        zb = [sb(nc, st, f"zb{i}", [128, 2, 128], BF16) for i in range(2)]
        tok = sb(nc, st, "tok", [128, 768], F32)
        sq = sb(nc, st, "dsq", [128, 512], F32)
        rs = sb(nc, st, "drs", [128, 8], F32)
        qkn = sb(nc, st, "qkn", [128, 8, 64], BF16)
        qkn32 = sb(nc, st, "qkn32", [128, 8, 64], F32)
        ek32 = sb(nc, st, "ek32", [128, 4, 64], F32)
        kd32 = sb(nc, st, "kd32", [128, 4, 64], F32)
        ekT32 = sb(nc, st, "ekT32", [64, 4, 128], F32)
        rb = sb(nc, st, "rb", [128, 4, 64], BF16)
        vnew32 = sb(nc, st, "vnew32", [128, 4, 64], F32)
        vb = sb(nc, st, "vb", [128, 256], BF16)
        qkT = sb(nc, st, "dqkT", [64, 8, 128], BF16)
        gs = sb(nc, st, "gs", [128, 8], F32)
        eg = sb(nc, st, "eg", [128, 4], F32)
        ekd = sb(nc, st, "ekd", [128, 4], F32)
        egl = sb(nc, st, "egl", [128, 4], F32)
        rg = sb(nc, st, "rg", [128, 4, 128], F32)
        dm = sb(nc, st, "dm", [128, 4, 128], F32)
        dmI = sb(nc, st, "dmI", [128, 4, 128], F32)
        dmS = sb(nc, st, "dmS", [128, 4, 128], F32)
        X = sb(nc, st, "X", [128, 4, 128], F32)
        AT = sb(nc, st, "AT", [128, 4, 128], BF16)
        YS = [sb(nc, st, f"YS{i}", [128, 4, 256], F32) for i in range(2)]
        YT = [sb(nc, st, f"YT{i}", [128, 4, 128], F32) for i in range(2)]
        db = sb(nc, st, "db", [128, 4, 128], F32)
        TTb = sb(nc, st, "TTb", [128, 4, 128], BF16)
        usb = sb(nc, st, "usb", [128, 4, 64], F32)
        ek = sb(nc, st, "ek", [128, 4, 64], BF16)
        kd = sb(nc, st, "kd", [128, 4, 64], BF16)
        qd = sb(nc, st, "qd", [128, 4, 64], BF16)
        wT = sb(nc, st, "wT", [64, 4, 128], BF16)
        qdT = sb(nc, st, "qdT", [64, 4, 128], BF16)
        vnew = sb(nc, st, "vnew", [128, 4, 64], BF16)
        Sf = sb(nc, st, "Sf", [64, 4, 64], F32)
        Sb = sb(nc, st, "Sb", [64, 4, 64], BF16)
        osb = [sb(nc, st, f"dosb{i}", [128, 256], F32) for i in range(2)]
        osq = sb(nc, st, "osq", [128, 256], F32)
        oss = sb(nc, st, "oss", [128, 4], F32)
        onb = sb(nc, st, "onb", [128, 256], BF16)
        omx = [sb(nc, st, f"omx{i}", [128, 2, 128], BF16) for i in range(2)]

        for d in range(2):
            if d == 1:
                kk.barrier()
            kk.op("pool", lambda e: e.memset(Sf[:], 0.0), writes=[Sf.b])
            kk.op("pool", lambda e: e.memset(Sb[:], 0.0), writes=[Sb.b])
            order = range(NCH) if d == 0 else range(NCH - 1, -1, -1)
            for n, ci in enumerate(order):
                s = n % 2
                r0 = ci * 128
                c_, g_ = cT[s], bgc[s]
                kk.op("sp", lambda e, c_=c_, r0=r0: e.dma_start(out=c_[:], in_=dr[f"convT{si}"][:, r0:r0 + 128].rearrange("(g p) s -> p g s", p=128)),
                      writes=[c_.b], dma=f"x{s}")
                kk.op("sp", lambda e, g_=g_, r0=r0: e.dma_start(out=g_[:], in_=dr[f"bg{si}"][r0:r0 + 128, :]), writes=[g_.b], dma=f"x{s}")
                if d == 1:
                    kk.op("sp", lambda e, o_=ofl[s], r0=r0: e.dma_start(out=o_[:], in_=dr[f"of{si}"][r0:r0 + 128, :]), writes=[ofl[s].b], dma=f"x{s}")
                    kk.op("sp", lambda e, z_=zb[s], r0=r0: e.dma_start(
                        out=z_[:], in_=dr[f"projT{si}"][G_BZ * 128:(G_BZ + 2) * 128, r0:r0 + 128].rearrange("(g p) s -> p g s", p=128)),
                        writes=[zb[s].b], dma=f"x{s}")
                gd = g_[:, 8 + 4 * d:12 + 4 * d]
                bd = g_[:, 4 * d:4 * d + 4]
                for j in range(6):
                    kk.op("pe", lambda e, c_=c_, j=j: e.transpose(out=bkb[:, j * 128:(j + 1) * 128], in_=c_[:, j, :], identity=ident[:]),
                          reads=[c_.b, ident.b], writes=[bkb.b])
                kk.op("dve", lambda e: e.tensor_copy(out=tok[:], in_=bkb[:, 0:768]), reads=[bkb.b], writes=[tok.b])
                kk.op("dve", lambda e: e.tensor_tensor(out=sq[:], in0=tok[:, 0:512], in1=tok[:, 0:512], op=ALU.mult), reads=[tok.b], writes=[sq.b])
                kk.op("dve", lambda e: e.tensor_reduce(out=rs[:], in_=v4(sq[:], 8), axis=AX.X, op=ALU.add), reads=[sq.b], writes=[rs.b])
                kk.op("dve", lambda e: e.tensor_scalar(out=rs[:], in0=rs[:], scalar1=EPS, scalar2=1.0, op0=ALU.add, op1=ALU.mult),
                      reads=[rs.b], writes=[rs.b])
                kk.op("act", lambda e: e.activation(out=rs[:], in_=rs[:], func=AF.Ln), reads=[rs.b], writes=[rs.b])
                kk.op("act", lambda e: e.activation(out=rs[:], in_=rs[:], func=AF.Exp, scale=-0.5), reads=[rs.b], writes=[rs.b])
                kk.op("dve", lambda e: e.tensor_scalar(out=rs[:, 0:4], in0=rs[:, 0:4], scalar1=0.125, scalar2=1.0, op0=ALU.mult, op1=ALU.mult),
                      reads=[rs.b], writes=[rs.b])
                kk.op("dve", lambda e: e.tensor_tensor(out=qkn32[:], in0=v4(tok[:, 0:512], 8), in1=bc(rs[:], [128, 8, 64], 2), op=ALU.mult),
                      reads=[tok.b, rs.b], writes=[qkn32.b])
                kk.op("pool", lambda e: e.tensor_copy(out=qkn[:], in_=qkn32[:]), reads=[qkn32.b], writes=[qkn.b])
                for j in range(8):
                    kk.op("pe", lambda e, j=j: e.transpose(out=bkb[0:64, j * 128:(j + 1) * 128], in_=qkn[:, j, :], identity=ident[:]),
                          reads=[qkn.b, ident.b], writes=[bkb.b])
                kk.op("act", lambda e: e.activation(out=qkT[:].rearrange("p a b -> p (a b)"), in_=bkb[0:64, 0:1024], func=AF.Copy),
                      reads=[bkb.b], writes=[qkT.b])
                kk.op("pe", lambda e, gd=gd, d=d: e.matmul(bk2[:, 0:4], tri[d][:], gd, start=True, stop=True), reads=[tri[d].b, g_.b], writes=[bk2.b])
                kk.op("pe", lambda e, gd=gd: e.matmul(bk2[:, 4:8], ones_f[:], gd, start=True, stop=True), reads=[ones_f.b, g_.b], writes=[bk2.b])
                kk.op("dve", lambda e: e.tensor_copy(out=gs[:], in_=bk2[:, 0:8]), reads=[bk2.b], writes=[gs.b])
                kk.op("act", lambda e: e.activation(out=eg[:], in_=gs[:, 0:4], func=AF.Exp), reads=[gs.b], writes=[eg.b])
                kk.op("act", lambda e: e.activation(out=egl[:], in_=gs[:, 4:8], func=AF.Exp), reads=[gs.b], writes=[egl.b])
                kk.op("dve", lambda e: e.tensor_tensor(out=ekd[:], in0=gs[:, 4:8], in1=gs[:, 0:4], op=ALU.subtract), reads=[gs.b], writes=[ekd.b])
                kk.op("act", lambda e: e.activation(out=ekd[:], in_=ekd[:], func=AF.Exp), reads=[ekd.b], writes=[ekd.b])
                kk.op("dve", lambda e, gd=gd, d=d: e.tensor_tensor(out=rg[:], in0=bc(tri[d][:], [128, 4, 128], 1), in1=bc(gd, [128, 4, 128], 2), op=ALU.mult),
                      reads=[tri[d].b, g_.b], writes=[rg.b])
                kk.op("pe", lambda e: e.matmul(bk3[:], ones_f[:], rg[:].rearrange("p a b -> p (a b)"), start=True, stop=True),
                      reads=[ones_f.b, rg.b], writes=[bk3.b])
                kk.op("dve", lambda e: e.tensor_tensor(out=dm[:], in0=v4(bk3[:]), in1=bc(gs[:, 0:4], [128, 4, 128], 2), op=ALU.subtract),
                      reads=[bk3.b, gs.b], writes=[dm.b])
                kk.op("dve", lambda e, d=d: e.tensor_tensor(out=dmI[:], in0=dm[:], in1=bc(nmI[d][:], [128, 4, 128], 1), op=ALU.add),
                      reads=[dm.b, nmI[d].b], writes=[dmI.b])
                kk.op("pool", lambda e, d=d: e.tensor_tensor(out=dmS[:], in0=dm[:], in1=bc(nmS[d][:], [128, 4, 128], 1), op=ALU.add),
                      reads=[dm.b, nmS[d].b], writes=[dmS.b])
                kk.op("act", lambda e: e.activation(out=dmI[:], in_=dmI[:], func=AF.Exp), reads=[dmI.b], writes=[dmI.b])
                kk.op("act", lambda e: e.activation(out=dmS[:], in_=dmS[:], func=AF.Exp), reads=[dmS.b], writes=[dmS.b])
                for h in range(4):
                    kTh = qkT[:, 4 + h, :]
                    qTh = qkT[:, h, :]
                    kk.op("pe", lambda e, h=h, kTh=kTh: e.matmul(bk4[:, h * 128:(h + 1) * 128], kTh, kTh, start=True, stop=True),
                          reads=[qkT.b], writes=[bk4.b])
                    kk.op("pe", lambda e, h=h, kTh=kTh, qTh=qTh: e.matmul(bk5[:, h * 128:(h + 1) * 128], kTh, qTh, start=True, stop=True),
                          reads=[qkT.b], writes=[bk5.b])
                for h in range(4):
                    kk.op("dve", lambda e, h=h, bd=bd: e.scalar_tensor_tensor(
                        out=X[:, h, :], in0=bk4[:, h * 128:(h + 1) * 128], scalar=bd[:, h:h + 1], in1=dmS[:, h, :], op0=ALU.mult, op1=ALU.mult),
                        reads=[bk4.b, g_.b, dmS.b], writes=[X.b])
                kk.op("dve", lambda e: e.tensor_tensor(out=AT[:], in0=v4(bk5[:]), in1=dmI[:], op=ALU.mult), reads=[bk5.b, dmI.b], writes=[AT.b])
                pbv = v4(bk4[:])
                for h in range(4):
                    kk.op("pe", lambda e, h=h: e.transpose(out=pbv[:, h, :], in_=X[:, h, :], identity=ident_f[:]),
                          reads=[X.b, ident_f.b], writes=[bk4.b])
                kk.op("dve", lambda e: e.tensor_copy(out=YT[1][:], in_=pbv), reads=[bk4.b], writes=[YT[1].b])
                pa = bka[:].rearrange("p (c i) -> p c i", c=4)
                for h in range(4):
                    kk.op("pe", lambda e, h=h: e.matmul(pa[:, h, 0:128], YT[1][:, h, :], X[:, h, :], start=True, stop=True),
                          reads=[YT[1].b, X.b], writes=[bka.b])
                    kk.op("pe", lambda e, h=h: e.matmul(pbv[:, h, :], X[:, h, :], YT[1][:, h, :], start=True, stop=True),
                          reads=[YT[1].b, X.b], writes=[bk4.b])
                kk.op("dve", lambda e: e.tensor_tensor(out=YS[0][:, :, 128:256], in0=bc(ident_f[:], [128, 4, 128], 1), in1=X[:], op=ALU.subtract),
                      reads=[ident_f.b, X.b], writes=[YS[0].b])
                kk.op("act", lambda e: e.activation(out=YS[0][:, :, 0:128], in_=pa[:, :, 0:128], func=AF.Identity), reads=[bka.b], writes=[YS[0].b])
                kk.op("dve", lambda e: e.tensor_copy(out=YT[0][:], in_=pbv), reads=[bk4.b], writes=[YT[0].b])
                cur = 0
                for k in range(1, 7):
                    ys, yt = YS[cur], YT[cur]
                    nys, nyt = YS[1 - cur], YT[1 - cur]
                    for h in range(4):
                        kk.op("pe", lambda e, h=h, ys=ys, yt=yt: e.matmul(pa[:, h, :], yt[:, h, :], ys[:, h, :], start=True, stop=True),
                              reads=[ys.b, yt.b], writes=[bka.b])
                        if k < 6:
                            kk.op("pe", lambda e, h=h, ys=ys, yt=yt: e.matmul(pbv[:, h, :], ys[:, h, 0:128], yt[:, h, :], start=True, stop=True),
                                  reads=[ys.b, yt.b], writes=[bk4.b])
                    kk.op("dve", lambda e, ys=ys, nys=nys: e.tensor_tensor(out=nys[:, :, 128:256], in0=pa[:, :, 128:256], in1=ys[:, :, 128:256], op=ALU.add),
                          reads=[bka.b, ys.b], writes=[nys.b])
                    if k < 6:
                        kk.op("act", lambda e, nys=nys: e.activation(out=nys[:, :, 0:128], in_=pa[:, :, 0:128], func=AF.Identity),
                              reads=[bka.b], writes=[nys.b])
                        kk.op("dve", lambda e, nyt=nyt: e.tensor_copy(out=nyt[:], in_=pbv), reads=[bk4.b], writes=[nyt.b])
                    cur = 1 - cur
                TT = YS[cur]
                kk.op("pool", lambda e, bd=bd: e.tensor_tensor(out=db[:], in0=bc(ident_f[:], [128, 4, 128], 1), in1=bc(bd, [128, 4, 128], 2), op=ALU.mult),
                      reads=[ident_f.b, g_.b], writes=[db.b])
                kk.op("pe", lambda e: e.matmul(bk3[:], ones_f[:], db[:].rearrange("p a b -> p (a b)"), start=True, stop=True),
                      reads=[ones_f.b, db.b], writes=[bk3.b])
                kk.op("dve", lambda e, TT=TT: e.tensor_tensor(out=TTb[:], in0=v4(bk3[:]), in1=TT[:, :, 128:256], op=ALU.mult),
                      reads=[bk3.b, TT.b], writes=[TTb.b])
                kk.op("dve", lambda e: e.tensor_tensor(out=ek32[:], in0=qkn32[:, 4:8, :], in1=bc(eg[:], [128, 4, 64], 2), op=ALU.mult),
                      reads=[qkn32.b, eg.b], writes=[ek32.b])
                kk.op("pool", lambda e: e.tensor_tensor(out=kd32[:], in0=qkn32[:, 4:8, :], in1=bc(ekd[:], [128, 4, 64], 2), op=ALU.mult),
                      reads=[qkn32.b, ekd.b], writes=[kd32.b])
                kk.op("pool", lambda e: e.tensor_tensor(out=qd[:], in0=qkn32[:, 0:4, :], in1=bc(eg[:], [128, 4, 64], 2), op=ALU.mult),
                      reads=[qkn32.b, eg.b], writes=[qd.b])
                pek = bk5[0:64, :].rearrange("p (c i) -> p c i", c=4)
                for h in range(4):
                    kk.op("pe", lambda e, h=h: e.transpose(out=pek[:, h, :], in_=ek32[:, h, :], identity=ident_f[:]),
                          reads=[ek32.b, ident_f.b], writes=[bk5.b])
                kk.op("dve", lambda e: e.tensor_copy(out=ekT32[:], in_=pek), reads=[bk5.b], writes=[ekT32.b])
                for h in range(4):
                    kk.op("pe", lambda e, h=h: e.transpose(out=bkb[0:64, h * 128:(h + 1) * 128], in_=qd[:, h, :], identity=ident[:]),
                          reads=[qd.b, ident.b], writes=[bkb.b])
                kk.op("dve", lambda e: e.tensor_copy(out=qdT[:].rearrange("p a b -> p (a b)"), in_=bkb[0:64, 0:512]), reads=[bkb.b], writes=[qdT.b])
                pws = bk1[:, 0:256].rearrange("p (c i) -> p c i", c=4)
                pout = bk1[:, 256:512].rearrange("p (c i) -> p c i", c=4)
                pds = bk3[0:64, 0:256].rearrange("p (c i) -> p c i", c=4)
                pv2 = bk2[:, 256:512].rearrange("p (c i) -> p c i", c=4)
                for h in range(4):
                    kk.op("pe", lambda e, h=h: e.matmul(pws[:, h, :], ekT32[:, h, :], Sf[:, h, :], start=True, stop=True),
                          reads=[ekT32.b, Sf.b], writes=[bk1.b])
                kk.op("dve", lambda e: e.tensor_tensor(out=rb[:], in0=v4(tok[:, 512:768]), in1=pws, op=ALU.subtract),
                      reads=[tok.b, bk1.b], writes=[rb.b])
                for h in range(4):
                    kk.op("pe", lambda e, h=h: e.matmul(pv2[:, h, :], TTb[:, h, :], rb[:, h, :], start=True, stop=True),
                          reads=[TTb.b, rb.b], writes=[bk2.b])
                kk.op("dve", lambda e: e.tensor_copy(out=vnew32[:], in_=pv2), reads=[bk2.b], writes=[vnew32.b])
                kk.op("dve", lambda e: e.tensor_copy(out=vnew[:], in_=pv2), reads=[bk2.b], writes=[vnew.b])
                for h in range(4):
                    kk.op("pe", lambda e, h=h: e.matmul(pout[:, h, :], qdT[:, h, :], Sb[:, h, :], start=True, stop=False),
                          reads=[qdT.b, Sb.b], writes=[bk1.b])
                    kk.op("pe", lambda e, h=h: e.matmul(pout[:, h, :], AT[:, h, :], vnew[:, h, :], start=False, stop=True),
                          reads=[AT.b, vnew.b], writes=[bk1.b])
                for h in range(4):
                    kk.op("pe", lambda e, h=h: e.matmul(pds[:, h, :], kd32[:, h, :], vnew32[:, h, :], start=True, stop=True),
                          reads=[kd32.b, vnew32.b], writes=[bk3.b])
                kk.op("dve", lambda e: e.tensor_tensor(out=Sf[:], in0=Sf[:], in1=bc(egl[0:64, :], [64, 4, 64], 2), op=ALU.mult),
                      reads=[Sf.b, egl.b], writes=[Sf.b])
                kk.op("dve", lambda e: e.tensor_tensor(out=Sf[:], in0=Sf[:], in1=pds, op=ALU.add), reads=[Sf.b, bk3.b], writes=[Sf.b])
                kk.op("act", lambda e: e.activation(out=Sb[:], in_=Sf[:], func=AF.Copy), reads=[Sf.b], writes=[Sb.b])
                o_ = osb[s]
                if d == 0:
                    kk.op("dve", lambda e, o_=o_: e.tensor_copy(out=o_[:], in_=bk1[:, 256:512]), reads=[bk1.b], writes=[o_.b])
                    kk.op("pool", lambda e, o_=o_, r0=r0: e.dma_start(out=dr[f"of{si}"][r0:r0 + 128, :], in_=o_[:]), reads=[o_.b], dma=f"st{s}")
                else:
                    kk.op("dve", lambda e, o_=o_, f_=ofl[s]: e.tensor_tensor(out=o_[:], in0=bk1[:, 256:512], in1=f_[:], op=ALU.add),
                          reads=[bk1.b, ofl[s].b], writes=[o_.b])
                    if f"osum{si}" in dr:
                        kk.op("pool", lambda e, o_=o_, r0=r0: e.dma_start(out=dr[f"osum{si}"][r0:r0 + 128, :], in_=o_[:]), reads=[o_.b], dma=f"st{s}")
                    kk.op("pool", lambda e, o_=o_: e.tensor_tensor(out=osq[:], in0=o_[:], in1=o_[:], op=ALU.mult), reads=[o_.b], writes=[osq.b])
                    kk.op("dve", lambda e: e.tensor_reduce(out=oss[:], in_=v4(osq[:]), axis=AX.X, op=ALU.add), reads=[osq.b], writes=[oss.b])
                    kk.op("dve", lambda e: e.tensor_scalar(out=oss[:], in0=oss[:], scalar1=1.0 / 64, scalar2=EPS, op0=ALU.mult, op1=ALU.add),
                          reads=[oss.b], writes=[oss.b])
                    kk.op("act", lambda e: e.activation(out=oss[:], in_=oss[:], func=AF.Ln), reads=[oss.b], writes=[oss.b])
                    kk.op("act", lambda e: e.activation(out=oss[:], in_=oss[:], func=AF.Exp, scale=-0.5), reads=[oss.b], writes=[oss.b])
                    kk.op("dve", lambda e, o_=o_: e.tensor_tensor(out=v4(o_[:]), in0=v4(o_[:]), in1=bc(oss[:], [128, 4, 64], 2), op=ALU.mult),
                          reads=[o_.b, oss.b], writes=[o_.b])
                    kk.op("pool", lambda e, o_=o_: e.tensor_tensor(out=onb[:], in0=o_[:], in1=dnw[:].rearrange("p a b -> p (a b)"), op=ALU.mult),
                          reads=[o_.b, dnw.b], writes=[onb.b])
                    for j in range(2):
                        kk.op("pe", lambda e, j=j: e.transpose(out=bkb[:, j * 128:(j + 1) * 128], in_=onb[:, j * 128:(j + 1) * 128], identity=ident[:]),
                              reads=[onb.b, ident.b], writes=[bkb.b])
                    z_ = zb[s]
                    m_ = omx[s]
                    kk.op("act", lambda e, z_=z_: e.activation(out=z_[:], in_=z_[:], func=AF.Silu), reads=[z_.b], writes=[z_.b])
                    kk.op("dve", lambda e, z_=z_, m_=m_: e.tensor_tensor(out=m_[:], in0=bkb[:, 0:256].rearrange("p (a b) -> p a b", a=2), in1=z_[:], op=ALU.mult),
                          reads=[bkb.b, z_.b], writes=[m_.b])
                    kk.op("pool", lambda e, m_=m_, r0=r0: e.dma_start(
                        out=dr[f"mixT{si}"][256:512, r0:r0 + 128].rearrange("(g p) s -> p g s", p=128), in_=m_[:]), reads=[m_.b], dma=f"st{s}")


POOL_WINDOWS = (2, 4, 8, 16)


def host_layout(seqs, depth, norm_w, w_in, sgu_w, sgu_b, conv_w, a_log, dt_bias, dn_norm_w,
                q_norm_w, k_norm_w, pool_w, pool_scale, w_out):
    f = np.float32
    m = {}
    m["w_in"] = np.ascontiguousarray(np.asarray(w_in, f).reshape(depth, 8, 128, NCOL).transpose(0, 2, 1, 3)[..., COL_PERM])
    m["norm_w"] = np.ascontiguousarray(np.asarray(norm_w, f).reshape(depth, 8, 128).transpose(0, 2, 1))
    m["w_out"] = np.ascontiguousarray(np.asarray(w_out, f).reshape(depth, 8, 128, D).transpose(0, 2, 1, 3))
    qk = np.concatenate([np.repeat(np.asarray(q_norm_w, f)[:, None, :], 4, 1), np.repeat(np.asarray(k_norm_w, f)[:, None, :], 2, 1)], 1)
    m["qkw"] = np.ascontiguousarray(np.broadcast_to(qk[:, None], (depth, 128, 6, 64)))
    m["a_log"] = np.ascontiguousarray(np.broadcast_to(np.asarray(a_log, f).reshape(depth, 1, 8), (depth, 128, 8)))
    m["dt_bias"] = np.ascontiguousarray(np.broadcast_to(np.asarray(dt_bias, f).reshape(depth, 1, 8), (depth, 128, 8)))
    m["ident"] = np.eye(128, dtype=f)
    m["sgu_wT"] = np.ascontiguousarray(np.asarray(sgu_w, f).transpose(0, 3, 1, 2))
    sb_ = np.asarray(sgu_b, f)
    sbT = np.zeros((depth, 128, 2, 128), f)
    for hp in range(2):
        for h2 in range(2):
            sbT[:, h2 * 64:(h2 + 1) * 64, hp, :] = sb_[:, hp * 2 + h2, None, :]
    m["sgu_bT"] = sbT
    m["conv_w"] = np.ascontiguousarray(np.asarray(conv_w, f).reshape(depth, 5, 6, 128).transpose(0, 3, 2, 1))
    pw = np.zeros((depth, 128, 2, 128), f)
    pwi = np.asarray(pool_w, f)
    for ch in range(2):
        for g2 in range(2):
            pw[:, g2 * 64:(g2 + 1) * 64, ch, g2 * 64:(g2 + 1) * 64] = pwi[:, ch * 2 + g2]
    m["pool_w"] = pw
    m["pool_s"] = np.ascontiguousarray(np.asarray(pool_scale, f).reshape(depth, 2, 128).transpose(0, 2, 1))
    m["dn_w"] = np.ascontiguousarray(np.broadcast_to(np.asarray(dn_norm_w, f)[:, None, None, :], (depth, 128, 4, 64)))
    k_ = np.arange(128)
    m["triU"] = (k_[:, None] <= k_[None, :]).astype(f)
    m["triL"] = (k_[:, None] >= k_[None, :]).astype(f)
    for i, S in enumerate(seqs):
        c, s_ = rope_tables(S)
        m[f"cos{i}"] = c
        m[f"sin{i}"] = s_
        t = np.arange(S)
        ic = np.zeros((128, 2, S), f)
        for g, win in enumerate(POOL_WINDOWS):
            lo = np.clip(t - win // 2, 0, S)
            hi = np.clip(t + win // 2, 0, S)
            ic[(g % 2) * 64:(g % 2) * 64 + 64, g // 2, :] = (1.0 / (hi - lo).astype(f))[None, :]
        m[f"icnt{i}"] = ic
    return m


_NC_CACHE = {}


def kernel(x_prompt, x_sample, norm_w, w_in, sgu_w, sgu_b, conv_w, a_log, dt_bias, dn_norm_w,
           q_norm_w, k_norm_w, pool_w, pool_scale, w_out):
    x_prompt = np.asarray(x_prompt, np.float32)
    x_sample = np.asarray(x_sample, np.float32)
    depth = int(np.asarray(w_in).shape[0])
    seqs = [x_prompt.shape[1], x_sample.shape[1]]
    key = (tuple(seqs), depth)
    if key not in _NC_CACHE:
        _NC_CACHE[key] = build(seqs, depth)
    nc = _NC_CACHE[key]
    common = host_layout(seqs, depth, norm_w, w_in, sgu_w, sgu_b, conv_w, a_log, dt_bias, dn_norm_w,
                         q_norm_w, k_norm_w, pool_w, pool_scale, w_out)
    nb_p, nb_s = x_prompt.shape[0], x_sample.shape[0]
    in_maps = []
    for c in range(8):
        mm = dict(common)
        mm["x0"] = np.ascontiguousarray(x_prompt[c % nb_p])
        mm["x1"] = np.ascontiguousarray(x_sample[c % nb_s])
        in_maps.append(mm)
    res = run_bass_kernel_spmd(nc, in_maps, core_ids=list(range(8)))
    yp = np.stack([np.asarray(res.results[b]["y0"], np.float32) for b in range(nb_p)], 0)
    ys = np.stack([np.asarray(res.results[b]["y1"], np.float32) for b in range(nb_s)], 0)
    return (yp, ys)
```

```python
import numpy as np
import ml_dtypes
from contextlib import ExitStack
import concourse.bass as bass
import concourse.mybir as mybir
from concourse.bass_utils import run_bass_kernel_spmd

F32 = mybir.dt.float32
BF16 = mybir.dt.bfloat16
AF = mybir.ActivationFunctionType
ALU = mybir.AluOpType
AX = mybir.AxisListType

D = 1024
NCOL = 3088
EPS = 1e-6
G_AU, G_AV, G_AZ, G_BQ, G_BK, G_BV, G_BZ, G_CQ, G_CK, G_CV, G_CZ, G_DX, G_DZ = \
    0, 2, 4, 6, 8, 10, 12, 14, 16, 17, 18, 20, 22
ORIG_OFF = dict(au=0, av=256, az=512, bq=768, bk=1024, bv=1280, bz=1536, bb=1792, ba=1800,
                cq=1808, ck=2064, cv=2192, cz=2320, dx=2576, dz=2832)
COL_PERM = np.concatenate([
    np.arange(0, 1792), np.arange(1808, 3088), np.arange(1792, 1808)])


class Buf:
    __slots__ = ("name", "w", "r")

    def __init__(self, name):
        self.name = name
        self.w = None
        self.r = {}


class K:
    ENG = ("pe", "act", "dve", "pool", "sp")

    def __init__(self, nc, stack):
        self.nc = nc
        self.sems = {}
        self.cnt = {}
        self.stack = stack
        for e in self.ENG:
            self._newsem(e)
        self.prog = {e: [] for e in self.ENG}
        self.waited = {e: {} for e in self.ENG}
        self.ninstr = 0

    def _newsem(self, key):
        self.sems[key] = self.stack.enter_context(self.nc.semaphore("s_" + key))
        self.cnt[key] = 0

    LIMIT = None

    def op(self, e, fn, reads=(), writes=(), dma=None):
        if K.LIMIT is not None and self.ninstr >= K.LIMIT:
            return
        need = {}

        def want(tok):
            if tok is None:
                return
            k, v = tok
            if k == "pe" and e == "pe" and dma is None:
                return
            if k not in self.ENG:
                v = self.cnt[k]
            if need.get(k, 0) < v:
                need[k] = v
        for b in reads:
            want(b.w)
        for b in writes:
            want(b.w)
            for k, v in b.r.items():
                want((k, v))
        waits = []
        wd = self.waited[e]
        for k, v in need.items():
            if wd.get(k, 0) < v:
                wd[k] = v
                waits.append((k, v))
        if dma is not None:
            if dma not in self.sems:
                self._newsem(dma)
            key, inc = dma, 16
        else:
            key, inc = e, 1
        self.cnt[key] += inc
        tok = (key, self.cnt[key])
        for b in reads:
            if b.r.get(key, 0) < tok[1]:
                b.r[key] = tok[1]
        for b in writes:
            b.w = tok
            b.r = {}
        self.prog[e].append((waits, fn, key, inc))
        self.ninstr += 1

    def barrier(self):
        tot = dict(self.cnt)
        for e in self.ENG:
            waits = []
            for k, v in tot.items():
                if v > 0 and self.waited[e].get(k, 0) < v:
                    self.waited[e][k] = v
                    waits.append((k, v))
            if waits:
                self.prog[e].append((waits, None, None, 0))

    def flush(self):
        nc = self.nc
        with nc.Block() as block:
            def run(e, eng):
                for waits, fn, key, inc in self.prog[e]:
                    for k, v in waits:
                        eng.wait_ge(self.sems[k], v)
                    if fn is not None:
                        fn(eng).then_inc(self.sems[key], inc)

            @block.tensor
            def _(eng):
                run("pe", eng)

            @block.scalar
            def _(eng):
                run("act", eng)

            @block.vector
            def _(eng):
                run("dve", eng)

            @block.gpsimd
            def _(eng):
                run("pool", eng)

            @block.sync
            def _(eng):
                run("sp", eng)
        self.prog = {e: [] for e in self.ENG}


class T:
    def __init__(self, t, name):
        self.t = t
        self.b = Buf(name)

    def __getitem__(self, idx):
        return self.t[idx]


_UID = [0]


def sb(nc, st, name, shape, dt):
    _UID[0] += 1
    nm = f"sb{_UID[0]}_{name}"
    return T(st.enter_context(nc.sbuf_tensor(nm, list(shape), dt)), nm)


def ps(nc, st, name, shape, dt):
    _UID[0] += 1
    nm = f"ps{_UID[0]}_{name}"
    return T(st.enter_context(nc.psum_tensor(nm, list(shape), dt)), nm)


def rope_tables(S):
    rows = np.repeat(np.arange(S // 64), 64)
    cols = np.tile(np.arange(64), S // 64)
    inv = np.power(np.float32(10000.0), -2.0 * np.arange(16, dtype=np.float32) / 32).astype(np.float32)
    ang = np.stack([rows, cols], -1).astype(np.float32)[:, :, None] * inv
    return np.cos(ang).astype(np.float32).reshape(S, 32), np.sin(ang).astype(np.float32).reshape(S, 32)


def build(seqs, depth, debug=False, phases="pfdao"):
    nc = bass.Bass("TRN2", target_bir_lowering=False)
    dr = {}

    def din(name, shape, dt=F32):
        dr[name] = nc.dram_tensor(name, list(shape), dt, kind="ExternalInput").ap()
        return dr[name]

    def dscr(name, shape, dt, out=False):
        kind = "ExternalOutput" if out else "Internal"
        dr[name] = nc.dram_tensor(name, list(shape), dt, kind=kind).ap()
        return dr[name]

    nseq = len(seqs)
    for i, S in enumerate(seqs):
        din(f"x{i}", [S, D])
        din(f"cos{i}", [S, 32])
        din(f"sin{i}", [S, 32])
        dscr(f"y{i}", [S, D], F32, out=True)
        dscr(f"y1_{i}", [S, D], F32)
        dscr(f"projT{i}", [3072, S], BF16, out=debug)
        dscr(f"qT{i}", [256, S], BF16, out=debug)
        dscr(f"kT{i}", [128, S], BF16, out=debug)
        dscr(f"va{i}", [S, 130], BF16, out=debug)
        dscr(f"bg{i}", [S, 16], F32, out=debug)
        dscr(f"mixT{i}", [1024, S], BF16, out=debug)
        dscr(f"convT{i}", [768, S], BF16, out=debug)
        dscr(f"of{i}", [S, 256], F32, out=debug)
        if debug:
            dscr(f"osum{i}", [S, 256], F32, out=True)
    din("w_in", [depth, 128, 8, NCOL])
    din("norm_w", [depth, 128, 8])
    din("w_out", [depth, 128, 8, D])
    din("qkw", [depth, 128, 6, 64])
    din("a_log", [depth, 128, 8])
    din("dt_bias", [depth, 128, 8])
    din("ident", [128, 128])
    din("sgu_wT", [depth, 128, 4, 128])
    din("sgu_bT", [depth, 128, 2, 128])
    din("conv_w", [depth, 128, 6, 5])
    din("pool_w", [depth, 128, 2, 128])
    din("pool_s", [depth, 128, 2])
    din("dn_w", [depth, 128, 4, 64])
    din("triU", [128, 128])
    din("triL", [128, 128])
    for i, S in enumerate(seqs):
        din(f"icnt{i}", [128, 2, S])

    with ExitStack() as top:
        kk = K(nc, top)
        for key in ("c0", "wl0", "wl1", "x0", "x1", "st0", "st1", "sq0", "sq1", "sp0", "sp1"):
            kk._newsem(key)
        with nc.Block() as blk0:
            @blk0.vector
            def _(eng):
                for key in kk.sems:
                    eng.sem_clear(kk.sems[key])
        ident_f = sb(nc, top, "ident_f", [128, 128], F32)
        ident = sb(nc, top, "ident_b", [128, 128], BF16)
        ones_f = sb(nc, top, "ones_f", [128, 128], F32)
        kk.op("sp", lambda e: e.dma_start(out=ident_f[:], in_=dr["ident"][:, :]), writes=[ident_f.b], dma="c0")
        kk.op("dve", lambda e: e.tensor_copy(out=ident[:], in_=ident_f[:]), reads=[ident_f.b], writes=[ident.b])
        kk.op("pool", lambda e: e.memset(ones_f[:], 1.0), writes=[ones_f.b])

        for l in range(depth):
            for si, S in enumerate(seqs):
                xin = dr[f"x{si}"] if l == 0 else dr[f"y1_{si}"]
                yout = dr[f"y{si}"] if l == depth - 1 else dr[f"y1_{si}"]
                for ph in phases:
                    if ph == "p":
                        phase_proj(nc, kk, dr, l, si, S, xin, ident, ident_f)
                    elif ph == "f":
                        phase_fm(nc, kk, dr, l, si, S, ident)
                    elif ph == "d":
                        phase_dn(nc, kk, dr, l, si, S, ident, ident_f, ones_f)
                    elif ph == "a":
                        phase_attn(nc, kk, dr, l, si, S, ones_f)
                    elif ph == "o":
                        phase_out(nc, kk, dr, l, si, S, xin, yout)
                    kk.barrier()
                    kk.flush()
    return nc


def load_w_bf16(nc, kk, st, name, src_ap, ncols, scale_t=None, chunk=1024):
    w = sb(nc, st, name, [128, 8, ncols], BF16)
    stg = [sb(nc, st, f"{name}_stg{i}", [128, chunk], F32) for i in range(2)]
    n = 0
    for kc in range(8):
        for c0 in range(0, ncols, chunk):
            cw = min(chunk, ncols - c0)
            s = stg[n % 2]
            kk.op("sp", lambda e, s=s, kc=kc, c0=c0, cw=cw: e.dma_start(out=s[:, 0:cw], in_=src_ap[:, kc, c0:c0 + cw]),
                  writes=[s.b], dma=f"wl{n % 2}")
            if scale_t is not None:
                kk.op("dve", lambda e, s=s, kc=kc, c0=c0, cw=cw: e.tensor_scalar(
                    out=w[:, kc, c0:c0 + cw], in0=s[:, 0:cw], scalar1=scale_t[:, kc:kc + 1], scalar2=1.0,
                    op0=ALU.mult, op1=ALU.mult), reads=[s.b, scale_t.b], writes=[w.b])
            else:
                eng = "dve" if n % 2 == 0 else "pool"
                kk.op(eng, lambda e, s=s, kc=kc, c0=c0, cw=cw: e.tensor_copy(out=w[:, kc, c0:c0 + cw], in_=s[:, 0:cw]),
                      reads=[s.b], writes=[w.b])
            n += 1
    return w


def phase_proj(nc, kk, dr, l, si, S, xin, ident, ident_f):
    with ExitStack() as st:
        nw = sb(nc, st, "nw", [128, 8], F32)
        kk.op("sp", lambda e: e.dma_start(out=nw[:], in_=dr["norm_w"][l]), writes=[nw.b], dma="c0")
        w = load_w_bf16(nc, kk, st, "w_in_sb", dr["w_in"][l], NCOL, scale_t=nw)
        qkw = sb(nc, st, "qkw", [128, 6, 64], F32)
        kk.op("sp", lambda e: e.dma_start(out=qkw[:], in_=dr["qkw"][l]), writes=[qkw.b], dma="c0")
        alog = sb(nc, st, "alog", [128, 8], F32)
        nA = sb(nc, st, "nA", [128, 8], F32)
        dtb = sb(nc, st, "dtb", [128, 8], F32)
        kk.op("sp", lambda e: e.dma_start(out=alog[:], in_=dr["a_log"][l]), writes=[alog.b], dma="c0")
        kk.op("sp", lambda e: e.dma_start(out=dtb[:], in_=dr["dt_bias"][l]), writes=[dtb.b], dma="c0")
        kk.op("act", lambda e: e.activation(out=nA[:], in_=alog[:], func=AF.Exp), reads=[alog.b], writes=[nA.b])
        kk.op("dve", lambda e: e.tensor_scalar(out=nA[:], in0=nA[:], scalar1=-1.0, scalar2=1.0, op0=ALU.mult, op1=ALU.mult),
              reads=[nA.b], writes=[nA.b])

        NS = 2
        xt = [sb(nc, st, f"xt{i}", [128, D], F32) for i in range(NS)]
        junk = sb(nc, st, "junk", [128, D], BF16)
        hb = [sb(nc, st, f"hb{i}", [128, D], BF16) for i in range(NS)]
        ssq = [sb(nc, st, f"ssq{i}", [128, 1], F32) for i in range(NS)]
        rstd = [sb(nc, st, f"rstd{i}", [128, 1], F32) for i in range(NS)]
        pT = [ps(nc, st, f"pT{i}", [128, 8, 128], BF16) for i in range(2)]
        hT = [sb(nc, st, f"hT{i}", [128, 8, 512], BF16) for i in range(2)]
        pacc = [ps(nc, st, f"pacc{i}", [128, 512], F32) for i in range(3)]
        ptk = [ps(nc, st, f"ptk{i}", [128, 512], F32) for i in range(1)]
        pbg = [ps(nc, st, f"pbg{i}", [128, 512], F32) for i in range(1)]
        ptr = ps(nc, st, "ptr", [128, 3, 128], BF16)
        stage = [sb(nc, st, f"stage{i}", [128, 24, 512], BF16) for i in range(2)]
        cs = [sb(nc, st, f"cs{i}", [128, 2, 32], F32) for i in range(NS)]
        qk = [sb(nc, st, f"qk{i}", [128, 6, 64], F32) for i in range(NS)]
        qsq = sb(nc, st, "qsq", [128, 6, 64], F32)
        qss = [sb(nc, st, f"qss{i}", [128, 6], F32) for i in range(NS)]
        qr = [sb(nc, st, f"qr{i}", [128, 6, 64], F32) for i in range(NS)]
        tmpa = sb(nc, st, "tmpa", [128, 6, 2, 16], F32)
        tmpb = sb(nc, st, "tmpb", [128, 6, 2, 16], F32)
        qkb = [sb(nc, st, f"qkb{i}", [128, 384], BF16) for i in range(NS)]
        qkT = [sb(nc, st, f"qkT{i}", [128, 3, 512], BF16) for i in range(2)]
        va = [sb(nc, st, f"va{i}", [128, 130], BF16) for i in range(NS)]
        bgt = [sb(nc, st, f"bgt{i}", [128, 16], F32) for i in range(NS)]
        t8 = [sb(nc, st, f"t8{i}", [128, 16], F32) for i in range(NS)]
        for v_ in va:
            kk.op("pool", lambda e, v_=v_: e.memset(v_[:], 1.0), writes=[v_.b])

        projT = dr[f"projT{si}"]
        nblk = S // 512
        ev = 0
        for tb in range(nblk):
            hTb = hT[tb % 2]
            qkTb = qkT[tb % 2]
            for t4 in range(4):
                ti = tb * 4 + t4
                s = ti % NS
                r0 = ti * 128
                x_, h_, sq_, rs_ = xt[s], hb[s], ssq[s], rstd[s]
                kk.op("sp", lambda e, x_=x_, r0=r0: e.dma_start(out=x_[:], in_=xin[r0:r0 + 128, :]),
                      writes=[x_.b], dma=f"x{s}")
                kk.op("sp", lambda e, c_=cs[s], r0=r0: e.dma_start(out=c_[:, 0, :], in_=dr[f"cos{si}"][r0:r0 + 128, :]),
                      writes=[cs[s].b], dma=f"x{s}")
                kk.op("sp", lambda e, c_=cs[s], r0=r0: e.dma_start(out=c_[:, 1, :], in_=dr[f"sin{si}"][r0:r0 + 128, :]),
                      writes=[cs[s].b], dma=f"x{s}")
                kk.op("act", lambda e, x_=x_, sq_=sq_: e.activation(out=junk[:], in_=x_[:], func=AF.Square, accum_out=sq_[:]),
                      reads=[x_.b], writes=[junk.b, sq_.b])
                kk.op("dve", lambda e, sq_=sq_, rs_=rs_: e.tensor_scalar(out=rs_[:], in0=sq_[:], scalar1=1.0 / D, scalar2=EPS,
                                                                         op0=ALU.mult, op1=ALU.add), reads=[sq_.b], writes=[rs_.b])
                kk.op("act", lambda e, rs_=rs_: e.activation(out=rs_[:], in_=rs_[:], func=AF.Ln), reads=[rs_.b], writes=[rs_.b])
                kk.op("act", lambda e, rs_=rs_: e.activation(out=rs_[:], in_=rs_[:], func=AF.Exp, scale=-0.5), reads=[rs_.b], writes=[rs_.b])
                kk.op("dve", lambda e, x_=x_, h_=h_, rs_=rs_: e.tensor_scalar(out=h_[:], in0=x_[:], scalar1=rs_[:, 0:1], scalar2=1.0,
                                                                              op0=ALU.mult, op1=ALU.mult),
                      reads=[x_.b, rs_.b], writes=[h_.b])
                p_ = pT[ti % 2]
                for kc in range(8):
                    kk.op("pe", lambda e, p_=p_, h_=h_, kc=kc: e.transpose(out=p_[:, kc, :], in_=h_[:, kc * 128:(kc + 1) * 128],
                                                                          identity=ident[:]),
                          reads=[h_.b, ident.b], writes=[p_.b])
                kk.op("act", lambda e, p_=p_, hTb=hTb, t4=t4: e.activation(out=hTb[:, :, t4 * 128:(t4 + 1) * 128], in_=p_[:],
                                                                           func=AF.Copy),
                      reads=[p_.b], writes=[hTb.b])
                pk = ptk[0]
                for kc in range(8):
                    kk.op("pe", lambda e, pk=pk, hTb=hTb, t4=t4, kc=kc: e.matmul(
                        pk[:], hTb[:, kc, t4 * 128:(t4 + 1) * 128], w[:, kc, G_CQ * 128:G_CQ * 128 + 512],
                        start=(kc == 0), stop=(kc == 7)), reads=[hTb.b, w.b], writes=[pk.b])
                pb_ = pbg[0]
                for kc in range(8):
                    kk.op("pe", lambda e, pb_=pb_, hTb=hTb, t4=t4, kc=kc: e.matmul(
                        pb_[:, 0:16], hTb[:, kc, t4 * 128:(t4 + 1) * 128], w[:, kc, 3072:3088],
                        start=(kc == 0), stop=(kc == 7)), reads=[hTb.b, w.b], writes=[pb_.b])
                q_, ss_, r_, qb_, va_, c_ = qk[s], qss[s], qr[s], qkb[s], va[s], cs[s]
                kk.op("dve", lambda e, q_=q_, pk=pk: e.tensor_copy(out=q_[:].rearrange("p a b -> p (a b)"), in_=pk[:, 0:384]),
                      reads=[pk.b], writes=[q_.b])
                kk.op("dve", lambda e, va_=va_, pk=pk: e.tensor_copy(
                    out=va_[:].rearrange("p (a b) -> p a b", a=2)[:, :, 0:64],
                    in_=pk[:, 384:512].rearrange("p (a b) -> p a b", a=2)), reads=[pk.b], writes=[va_.b])
                kk.op("pool", lambda e, va_=va_, r0=r0: e.dma_start(out=dr[f"va{si}"][r0:r0 + 128, :], in_=va_[:]),
                      reads=[va_.b], dma=f"st{s}")
                kk.op("dve", lambda e, q_=q_: e.tensor_tensor(out=qsq[:], in0=q_[:], in1=q_[:], op=ALU.mult),
                      reads=[q_.b], writes=[qsq.b])
                kk.op("dve", lambda e, ss_=ss_: e.tensor_reduce(out=ss_[:], in_=qsq[:], axis=AX.X, op=ALU.add),
                      reads=[qsq.b], writes=[ss_.b])
                kk.op("dve", lambda e, ss_=ss_: e.tensor_scalar(out=ss_[:], in0=ss_[:], scalar1=1.0 / 64, scalar2=EPS,
                                                                op0=ALU.mult, op1=ALU.add), reads=[ss_.b], writes=[ss_.b])
                kk.op("act", lambda e, ss_=ss_: e.activation(out=ss_[:], in_=ss_[:], func=AF.Ln), reads=[ss_.b], writes=[ss_.b])
                kk.op("act", lambda e, ss_=ss_: e.activation(out=ss_[:], in_=ss_[:], func=AF.Exp, scale=-0.5), reads=[ss_.b], writes=[ss_.b])
                kk.op("dve", lambda e, q_=q_, ss_=ss_: e.tensor_tensor(
                    out=q_[:], in0=q_[:], in1=ss_[:].unsqueeze(2).to_broadcast([128, 6, 64]), op=ALU.mult),
                    reads=[q_.b, ss_.b], writes=[q_.b])
                kk.op("pool", lambda e, q_=q_: e.tensor_tensor(out=q_[:], in0=q_[:], in1=qkw[:], op=ALU.mult),
                      reads=[q_.b, qkw.b], writes=[q_.b])
                def v5(t):
                    return t[:].rearrange("p h (a b f) -> p h a b f", a=2, b=2)
                cosb = lambda c_: c_[:, 0, :].rearrange("p (a f) -> p a f", a=2).unsqueeze(1).to_broadcast([128, 6, 2, 16])
                sinb = lambda c_: c_[:, 1, :].rearrange("p (a f) -> p a f", a=2).unsqueeze(1).to_broadcast([128, 6, 2, 16])
                kk.op("dve", lambda e, q_=q_, c_=c_: e.tensor_tensor(out=tmpa[:], in0=v5(q_)[:, :, :, 1, :], in1=sinb(c_), op=ALU.mult),
                      reads=[q_.b, c_.b], writes=[tmpa.b])
                kk.op("pool", lambda e, q_=q_, c_=c_: e.tensor_tensor(out=tmpb[:], in0=v5(q_)[:, :, :, 0, :], in1=sinb(c_), op=ALU.mult),
                      reads=[q_.b, c_.b], writes=[tmpb.b])
                kk.op("dve", lambda e, q_=q_, r_=r_, c_=c_: e.tensor_tensor(out=v5(r_)[:, :, :, 0, :], in0=v5(q_)[:, :, :, 0, :], in1=cosb(c_), op=ALU.mult),
                      reads=[q_.b, c_.b], writes=[r_.b])
                kk.op("pool", lambda e, q_=q_, r_=r_, c_=c_: e.tensor_tensor(out=v5(r_)[:, :, :, 1, :], in0=v5(q_)[:, :, :, 1, :], in1=cosb(c_), op=ALU.mult),
                      reads=[q_.b, c_.b], writes=[r_.b])
                kk.op("dve", lambda e, r_=r_: e.tensor_tensor(out=v5(r_)[:, :, :, 0, :], in0=v5(r_)[:, :, :, 0, :], in1=tmpa[:], op=ALU.subtract),
                      reads=[r_.b, tmpa.b], writes=[r_.b])
                kk.op("dve", lambda e, r_=r_: e.tensor_tensor(out=v5(r_)[:, :, :, 1, :], in0=v5(r_)[:, :, :, 1, :], in1=tmpb[:], op=ALU.add),
                      reads=[r_.b, tmpb.b], writes=[r_.b])
                kk.op("act", lambda e, r_=r_, qb_=qb_: e.activation(out=qb_[:], in_=r_[:].rearrange("p a b -> p (a b)"), func=AF.Copy),
                      reads=[r_.b], writes=[qb_.b])
                for j in range(3):
                    kk.op("pe", lambda e, qb_=qb_, j=j: e.transpose(out=ptr[:, j, :], in_=qb_[:, j * 128:(j + 1) * 128], identity=ident[:]),
                          reads=[qb_.b, ident.b], writes=[ptr.b])
                kk.op("dve", lambda e, qkTb=qkTb, t4=t4: e.tensor_copy(out=qkTb[:, :, t4 * 128:(t4 + 1) * 128], in_=ptr[:]),
                      reads=[ptr.b], writes=[qkTb.b])
                b_, t_ = bgt[s], t8[s]
                kk.op("dve", lambda e, t_=t_, pb_=pb_: e.tensor_tensor(out=t_[:, 8:16], in0=pb_[:, 8:16], in1=dtb[:], op=ALU.add),
                      reads=[pb_.b, dtb.b], writes=[t_.b])
                kk.op("act", lambda e, t_=t_, pb_=pb_: e.activation(out=t_[:, 0:8], in_=pb_[:, 0:8], func=AF.Exp, scale=-1.0),
                      reads=[pb_.b], writes=[t_.b])
                kk.op("act", lambda e, t_=t_: e.activation(out=t_[:, 8:16], in_=t_[:, 8:16], func=AF.Exp),
                      reads=[t_.b], writes=[t_.b])
                kk.op("act", lambda e, t_=t_: e.activation(out=t_[:, 8:16], in_=t_[:, 8:16], func=AF.Ln, bias=1.0),
                      reads=[t_.b], writes=[t_.b])
                kk.op("dve", lambda e, t_=t_: e.tensor_scalar(out=t_[:, 0:8], in0=t_[:, 0:8], scalar1=1.0, scalar2=1.0, op0=ALU.add, op1=ALU.mult),
                      reads=[t_.b], writes=[t_.b])
                kk.op("dve", lambda e, t_=t_, b_=b_: e.reciprocal(out=b_[:, 0:8], in_=t_[:, 0:8]), reads=[t_.b], writes=[b_.b])
                kk.op("dve", lambda e, t_=t_, b_=b_: e.tensor_tensor(out=b_[:, 8:16], in0=t_[:, 8:16], in1=nA[:], op=ALU.mult),
                      reads=[t_.b, nA.b], writes=[b_.b])
                kk.op("pool", lambda e, b_=b_, r0=r0: e.dma_start(out=dr[f"bg{si}"][r0:r0 + 128, :], in_=b_[:]),
                      reads=[b_.b], dma=f"st{s}")
            c0 = tb * 512
            kk.op("pool", lambda e, qkTb=qkTb, c0=c0: e.dma_start(
                out=dr[f"qT{si}"][:, c0:c0 + 512].rearrange("(j p) s -> p j s", p=128), in_=qkTb[:, 0:2, :]),
                reads=[qkTb.b], dma=f"sq{tb % 2}")
            kk.op("pool", lambda e, qkTb=qkTb, c0=c0: e.dma_start(out=dr[f"kT{si}"][:, c0:c0 + 512], in_=qkTb[:, 2, :]),
                  reads=[qkTb.b], dma=f"sq{tb % 2}")
            stg = stage[tb % 2]
            for g in range(24):
                pa = pacc[g % 3]
                for kc in range(8):
                    kk.op("pe", lambda e, pa=pa, g=g, kc=kc, hTb=hTb: e.matmul(
                        pa[:], w[:, kc, g * 128:(g + 1) * 128], hTb[:, kc, :], start=(kc == 0), stop=(kc == 7)),
                        reads=[w.b, hTb.b], writes=[pa.b])
                if ev % 2 == 0:
                    kk.op("act", lambda e, pa=pa, g=g, stg=stg: e.activation(out=stg[:, g, :], in_=pa[:], func=AF.Copy),
                          reads=[pa.b], writes=[stg.b])
                else:
                    kk.op("dve", lambda e, pa=pa, g=g, stg=stg: e.tensor_copy(out=stg[:, g, :], in_=pa[:]),
                          reads=[pa.b], writes=[stg.b])
                ev += 1
            for h2 in range(2):
                kk.op("pool", lambda e, stg=stg, c0=c0, h2=h2: e.dma_start(
                    out=projT[h2 * 1536:(h2 + 1) * 1536, c0:c0 + 512].rearrange("(g p) s -> p g s", p=128),
                    in_=stg[:, h2 * 12:(h2 + 1) * 12, :]), reads=[stg.b], dma=f"sp{tb % 2}")


def phase_out(nc, kk, dr, l, si, S, xin, yout):
    with ExitStack() as st:
        wo = load_w_bf16(nc, kk, st, "w_out_sb", dr["w_out"][l], D)
        NS = 2
        mt = [sb(nc, st, f"mt{i}", [128, 8, 128], BF16) for i in range(NS)]
        xt = [sb(nc, st, f"xo{i}", [128, D], F32) for i in range(NS)]
        yt = [sb(nc, st, f"yo{i}", [128, D], F32) for i in range(NS)]
        po = [ps(nc, st, f"po{i}", [128, 512], F32) for i in range(4)]
        mixT = dr[f"mixT{si}"]
        for ti in range(S // 128):
            s = ti % NS
            r0 = ti * 128
            m_, x_, y_ = mt[s], xt[s], yt[s]
            kk.op("sp", lambda e, m_=m_, r0=r0: e.dma_start(out=m_[:], in_=mixT[:, r0:r0 + 128].rearrange("(k p) s -> p k s", p=128)),
                  writes=[m_.b], dma=f"x{s}")
            kk.op("sp", lambda e, x_=x_, r0=r0: e.dma_start(out=x_[:], in_=xin[r0:r0 + 128, :]), writes=[x_.b], dma=f"x{s}")
            for g in range(2):
                p_ = po[(ti * 2 + g) % 4]
                for kc in range(8):
                    kk.op("pe", lambda e, p_=p_, m_=m_, kc=kc, g=g: e.matmul(
                        p_[:], m_[:, kc, :], wo[:, kc, g * 512:(g + 1) * 512], start=(kc == 0), stop=(kc == 7)),
                        reads=[m_.b, wo.b], writes=[p_.b])
                kk.op("dve", lambda e, p_=p_, x_=x_, y_=y_, g=g: e.tensor_tensor(
                    out=y_[:, g * 512:(g + 1) * 512], in0=p_[:], in1=x_[:, g * 512:(g + 1) * 512], op=ALU.add),
                    reads=[p_.b, x_.b], writes=[y_.b])
            kk.op("pool", lambda e, y_=y_, r0=r0: e.dma_start(out=yout[r0:r0 + 128, :], in_=y_[:]), reads=[y_.b], dma=f"st{s}")


def phase_attn(nc, kk, dr, l, si, S, ones_f):
    with ExitStack() as st:
        nkt = S // 128
        KT = sb(nc, st, "KT", [128, S], BF16)
        VA = sb(nc, st, "VA", [128, nkt, 130], BF16)
        for c in range(0, S, 2048):
            ce = min(c + 2048, S)
            kk.op("sp", lambda e, c=c, ce=ce: e.dma_start(out=KT[:, c:ce], in_=dr[f"kT{si}"][:, c:ce]), writes=[KT.b], dma="c0")
        for c in range(0, nkt, 16):
            ce = min(c + 16, nkt)
            kk.op("sp", lambda e, c=c, ce=ce: e.dma_start(out=VA[:, c:ce, :],
                                                          in_=dr[f"va{si}"][c * 128:ce * 128, :].rearrange("(t p) c -> p t c", p=128)),
                  writes=[VA.b], dma="c0")
        qt = [sb(nc, st, f"qt{i}", [128, 2, 512], BF16) for i in range(2)]
        zt = [sb(nc, st, f"zt{i}", [64, 4, 512], BF16) for i in range(2)]
        pss = [ps(nc, st, f"pss{i}", [128, 2, 512], F32) for i in range(2)]
        pex = [sb(nc, st, f"pex{i}", [128, 2, 512], BF16) for i in range(2)]
        pov = [ps(nc, st, f"pov{i}", [128, 512], F32) for i in range(2)]
        pbc = ps(nc, st, "pbc", [64, 512], F32)
        osb = [sb(nc, st, f"osb{i}", [128, 512], F32) for i in range(2)]
        rc = [sb(nc, st, f"rc{i}", [128, 512], F32) for i in range(2)]
        og = [sb(nc, st, f"og{i}", [64, 4, 512], BF16) for i in range(2)]
        it = 0
        for qb in range(S // 512):
            c0 = qb * 512
            q_ = qt[qb % 2]
            z_ = zt[qb % 2]
            og_ = og[qb % 2]
            for h in range(4):
                kv = h // 2
                kk.op("sp", lambda e, q_=q_, h=h, kv=kv, c0=c0: e.dma_start(
                    out=q_[kv * 64:(kv + 1) * 64, h % 2, :], in_=dr[f"qT{si}"][h * 64:(h + 1) * 64, c0:c0 + 512]),
                    writes=[q_.b], dma=f"x{qb % 2}")
            kk.op("sp", lambda e, z_=z_, c0=c0: e.dma_start(
                out=z_[:], in_=dr[f"projT{si}"][G_CZ * 128:(G_CZ + 2) * 128, c0:c0 + 512].rearrange("(h p) s -> p h s", p=64)),
                writes=[z_.b], dma=f"x{qb % 2}")
            kk.op("act", lambda e, z_=z_: e.activation(out=z_[:], in_=z_[:], func=AF.Silu), reads=[z_.b], writes=[z_.b])
            for h in range(4):
                kv = h // 2
                po_ = pov[h % 2]
                npair = nkt // 2
                slots = []
                for kp in range(npair):
                    slots.append((pss[it % 2], pex[it % 2]))
                    it += 1

                def emit_qk(kp):
                    ps_ = slots[kp][0]
                    for j in range(2):
                        kt = kp * 2 + j
                        kk.op("pe", lambda e, ps_=ps_, j=j, kt=kt, kv=kv, q_=q_, h=h: e.matmul(
                            ps_[:, j, :], KT[kv * 64:(kv + 1) * 64, kt * 128:(kt + 1) * 128], q_[kv * 64:(kv + 1) * 64, h % 2, :],
                            start=True, stop=True), reads=[KT.b, q_.b], writes=[ps_.b])

                emit_qk(0)
                for kp in range(npair):
                    ps_, pe_ = slots[kp]
                    if kp + 1 < npair:
                        emit_qk(kp + 1)
                    kk.op("act", lambda e, ps_=ps_, pe_=pe_: e.activation(out=pe_[:], in_=ps_[:], func=AF.Exp, scale=0.125),
                          reads=[ps_.b], writes=[pe_.b])
                    for j in range(2):
                        kt = kp * 2 + j
                        kk.op("pe", lambda e, po_=po_, pe_=pe_, j=j, kt=kt, kv=kv: e.matmul(
                            po_[0:65, :], VA[:, kt, kv * 65:(kv + 1) * 65], pe_[:, j, :],
                            start=(kt == 0), stop=(kt == nkt - 1)), reads=[VA.b, pe_.b], writes=[po_.b])
                o_ = osb[h % 2]
                r_ = rc[h % 2]
                kk.op("dve", lambda e, o_=o_, po_=po_: e.tensor_copy(out=o_[0:65, :], in_=po_[0:65, :]), reads=[po_.b], writes=[o_.b])
                kk.op("dve", lambda e, o_=o_, r_=r_: e.reciprocal(out=r_[64:65, :], in_=o_[64:65, :]), reads=[o_.b], writes=[r_.b])
                kk.op("pe", lambda e, r_=r_: e.matmul(pbc[:], ones_f[64:65, 0:64], r_[64:65, :], start=True, stop=True),
                      reads=[r_.b, ones_f.b], writes=[pbc.b])
                kk.op("dve", lambda e, o_=o_: e.tensor_tensor(out=o_[0:64, :], in0=o_[0:64, :], in1=pbc[:], op=ALU.mult),
                      reads=[o_.b, pbc.b], writes=[o_.b])
                kk.op("dve", lambda e, o_=o_, og_=og_, h=h, z_=z_: e.tensor_tensor(out=og_[:, h, :], in0=o_[0:64, :], in1=z_[:, h, :], op=ALU.mult),
                      reads=[o_.b, z_.b], writes=[og_.b])
            kk.op("pool", lambda e, og_=og_, c0=c0: e.dma_start(
                out=dr[f"mixT{si}"][512:768, c0:c0 + 512].rearrange("(h p) s -> p h s", p=64), in_=og_[:]),
                reads=[og_.b], dma=f"st{qb % 2}")


def phase_fm(nc, kk, dr, l, si, S, ident):
    with ExitStack() as st:
        projT = dr[f"projT{si}"]
        swf = sb(nc, st, "swf", [128, 4, 128], F32)
        sw = sb(nc, st, "sw", [128, 4, 128], BF16)
        sbT = sb(nc, st, "sbT", [128, 2, 128], F32)
        cw = sb(nc, st, "cw", [128, 6, 5], F32)
        pwf = sb(nc, st, "pwf", [128, 2, 128], F32)
        pw = sb(nc, st, "pw", [128, 2, 128], BF16)
        psc = sb(nc, st, "psc", [128, 2], F32)
        kk.op("sp", lambda e: e.dma_start(out=swf[:], in_=dr["sgu_wT"][l]), writes=[swf.b], dma="c0")
        kk.op("sp", lambda e: e.dma_start(out=sbT[:], in_=dr["sgu_bT"][l]), writes=[sbT.b], dma="c0")
        kk.op("sp", lambda e: e.dma_start(out=cw[:], in_=dr["conv_w"][l]), writes=[cw.b], dma="c0")
        kk.op("sp", lambda e: e.dma_start(out=pwf[:], in_=dr["pool_w"][l]), writes=[pwf.b], dma="c0")
        kk.op("sp", lambda e: e.dma_start(out=psc[:], in_=dr["pool_s"][l]), writes=[psc.b], dma="c0")
        kk.op("dve", lambda e: e.tensor_copy(out=sw[:], in_=swf[:]), reads=[swf.b], writes=[sw.b])
        kk.op("dve", lambda e: e.tensor_copy(out=pw[:], in_=pwf[:]), reads=[pwf.b], writes=[pw.b])

        NS = 2
        av = [sb(nc, st, f"av{i}", [128, 6, 512], BF16) for i in range(NS)]
        dxz = [sb(nc, st, f"dxz{i}", [128, 4, 528], BF16) for i in range(NS)]
        icn = [sb(nc, st, f"icn{i}", [128, 2, 512], F32) for i in range(NS)]
        bx = [sb(nc, st, f"bx{i}", [128, 6, 516], BF16) for i in range(NS)]
        pvt = ps(nc, st, "pvt", [128, 2, 128], BF16)
        vtok = sb(nc, st, "vtok", [128, 4, 64], F32)
        vsq = sb(nc, st, "vsq", [128, 4, 64], F32)
        vss = sb(nc, st, "vss", [128, 4], F32)
        vnm = [sb(nc, st, f"vnm{i}", [128, 4, 128], BF16) for i in range(2)]
        for v_ in vnm:
            kk.op("pool", lambda e, v_=v_: e.memset(v_[:], 0.0), writes=[v_.b])
        pm = [ps(nc, st, f"pm{i}", [128, 2, 512], F32) for i in range(1)]
        ta = sb(nc, st, "ta", [128, 2, 512], F32)
        sz = sb(nc, st, "sz", [128, 2, 512], BF16)
        ma = [sb(nc, st, f"ma{i}", [128, 2, 512], BF16) for i in range(NS)]
        ss = sb(nc, st, "ss", [128, 2, 528], F32)
        s2 = sb(nc, st, "s2", [128, 2, 528], F32)
        s4 = sb(nc, st, "s4", [128, 2, 528], F32)
        s8 = sb(nc, st, "s8", [128, 528], F32)
        dfb = sb(nc, st, "dfb", [128, 2, 512], BF16)
        ppl = ps(nc, st, "ppl", [128, 2, 512], F32)
        md = [sb(nc, st, f"md{i}", [128, 2, 512], BF16) for i in range(NS)]
        szd = sb(nc, st, "szd", [128, 2, 512], BF16)
        cacc = sb(nc, st, "cacc", [128, 512], F32)
        cvo = [sb(nc, st, f"cvo{i}", [128, 6, 512], BF16) for i in range(NS)]

        nblk = S // 512
        for tb in range(nblk):
            s = tb % NS
            c0 = tb * 512
            a_, d_, i_, b_ = av[s], dxz[s], icn[s], bx[s]
            kk.op("sp", lambda e, a_=a_, c0=c0: e.dma_start(out=a_[:], in_=projT[0:768, c0:c0 + 512].rearrange("(g p) s -> p g s", p=128)),
                  writes=[a_.b], dma=f"x{s}")
            lo = max(c0 - 8, 0)
            hi = min(c0 + 520, S)
            if lo > c0 - 8:
                kk.op("pool", lambda e, d_=d_: e.memset(d_[:, 0:2, 0:8], 0.0), writes=[d_.b])
            if hi < c0 + 520:
                kk.op("pool", lambda e, d_=d_: e.memset(d_[:, 0:2, 520:528], 0.0), writes=[d_.b])
            kk.op("sp", lambda e, d_=d_, lo=lo, hi=hi, c0=c0: e.dma_start(
                out=d_[:, 0:2, lo - (c0 - 8):hi - (c0 - 8)], in_=projT[G_DX * 128:(G_DX + 2) * 128, lo:hi].rearrange("(g p) s -> p g s", p=128)),
                writes=[d_.b], dma=f"x{s}")
            kk.op("sp", lambda e, d_=d_, c0=c0: e.dma_start(
                out=d_[:, 2:4, 0:512], in_=projT[G_DZ * 128:(G_DZ + 2) * 128, c0:c0 + 512].rearrange("(g p) s -> p g s", p=128)),
                writes=[d_.b], dma=f"x{s}")
            kk.op("sp", lambda e, i_=i_, c0=c0: e.dma_start(out=i_[:], in_=dr[f"icnt{si}"][:, :, c0:c0 + 512]), writes=[i_.b], dma=f"x{s}")
            lo2 = max(c0 - 2, 0)
            hi2 = min(c0 + 514, S)
            if lo2 > c0 - 2:
                kk.op("pool", lambda e, b_=b_: e.memset(b_[:, :, 0:2], 0.0), writes=[b_.b])
            if hi2 < c0 + 514:
                kk.op("pool", lambda e, b_=b_: e.memset(b_[:, :, 514:516], 0.0), writes=[b_.b])
            kk.op("sp", lambda e, b_=b_, lo2=lo2, hi2=hi2, c0=c0: e.dma_start(
                out=b_[:, :, lo2 - (c0 - 2):hi2 - (c0 - 2)], in_=projT[G_BQ * 128:(G_BQ + 6) * 128, lo2:hi2].rearrange("(g p) s -> p g s", p=128)),
                writes=[b_.b], dma=f"x{s}")

            pm_ = pm[0]
            for ch in range(4):
                vn_ = vnm[ch % 2]
                for j in range(2):
                    kk.op("pe", lambda e, a_=a_, j=j, ch=ch: e.transpose(out=pvt[:, j, :], in_=a_[:, 2 + j, ch * 128:(ch + 1) * 128], identity=ident[:]),
                          reads=[a_.b, ident.b], writes=[pvt.b])
                kk.op("dve", lambda e: e.tensor_copy(out=vtok[:].rearrange("p a b -> p (a b)"), in_=pvt[:].rearrange("p a b -> p (a b)")),
                      reads=[pvt.b], writes=[vtok.b])
                kk.op("dve", lambda e: e.tensor_tensor(out=vsq[:], in0=vtok[:], in1=vtok[:], op=ALU.mult), reads=[vtok.b], writes=[vsq.b])
                kk.op("dve", lambda e: e.tensor_reduce(out=vss[:], in_=vsq[:], axis=AX.X, op=ALU.add), reads=[vsq.b], writes=[vss.b])
                kk.op("dve", lambda e: e.tensor_scalar(out=vss[:], in0=vss[:], scalar1=1.0 / 64, scalar2=EPS, op0=ALU.mult, op1=ALU.add),
                      reads=[vss.b], writes=[vss.b])
                kk.op("act", lambda e: e.activation(out=vss[:], in_=vss[:], func=AF.Ln), reads=[vss.b], writes=[vss.b])
                kk.op("act", lambda e: e.activation(out=vss[:], in_=vss[:], func=AF.Exp, scale=-0.5), reads=[vss.b], writes=[vss.b])
                for h in range(4):
                    kk.op("dve", lambda e, vn_=vn_, h=h: e.tensor_scalar(
                        out=vn_[:, h, (h % 2) * 64:(h % 2) * 64 + 64], in0=vtok[:, h, :], scalar1=vss[:, h:h + 1], scalar2=1.0,
                        op0=ALU.mult, op1=ALU.mult), reads=[vtok.b, vss.b], writes=[vn_.b])
                for h in range(4):
                    kk.op("pe", lambda e, vn_=vn_, h=h, ch=ch, pm_=pm_: e.matmul(
                        pm_[:, h // 2, ch * 128:(ch + 1) * 128], vn_[:, h, :], sw[:, h, :], start=(h % 2 == 0), stop=(h % 2 == 1)),
                        reads=[vn_.b, sw.b], writes=[pm_.b])
            m_ = ma[s]
            kk.op("dve", lambda e, pm_=pm_: e.tensor_tensor(
                out=ta[:].rearrange("p a (c i) -> p a c i", c=4), in0=pm_[:].rearrange("p a (c i) -> p a c i", c=4),
                in1=sbT[:].unsqueeze(2).to_broadcast([128, 2, 4, 128]), op=ALU.add), reads=[pm_.b, sbT.b], writes=[ta.b])
            kk.op("act", lambda e, a_=a_: e.activation(out=sz[:], in_=a_[:, 4:6, :], func=AF.Silu), reads=[a_.b], writes=[sz.b])
            kk.op("pool", lambda e, a_=a_: e.tensor_tensor(out=ta[:], in0=ta[:], in1=a_[:, 0:2, :], op=ALU.mult), reads=[ta.b, a_.b], writes=[ta.b])
            kk.op("dve", lambda e, m_=m_: e.tensor_tensor(out=m_[:], in0=ta[:], in1=sz[:], op=ALU.mult), reads=[ta.b, sz.b], writes=[m_.b])
            kk.op("pool", lambda e, m_=m_, c0=c0: e.dma_start(out=dr[f"mixT{si}"][0:256, c0:c0 + 512].rearrange("(g p) s -> p g s", p=128), in_=m_[:]),
                  reads=[m_.b], dma=f"st{s}")

            X = d_
            kk.op("dve", lambda e, X=X: e.tensor_tensor(out=s2[:, :, 1:528], in0=X[:, 0:2, 0:527], in1=X[:, 0:2, 1:528], op=ALU.add),
                  reads=[X.b], writes=[s2.b])
            kk.op("pool", lambda e: e.tensor_tensor(out=s4[:, :, 2:527], in0=s2[:, :, 1:526], in1=s2[:, :, 3:528], op=ALU.add),
                  reads=[s2.b], writes=[s4.b])
            kk.op("dve", lambda e: e.tensor_tensor(out=s8[:, 4:525], in0=s4[:, 1, 2:523], in1=s4[:, 1, 6:527], op=ALU.add),
                  reads=[s4.b], writes=[s8.b])
            kk.op("pool", lambda e: e.tensor_copy(out=ss[0:64, 0, 8:520], in_=s2[0:64, 0, 8:520]), reads=[s2.b], writes=[ss.b])
            kk.op("pool", lambda e: e.tensor_copy(out=ss[64:128, 0, 8:520], in_=s4[64:128, 0, 8:520]), reads=[s4.b], writes=[ss.b])
            kk.op("dve", lambda e: e.tensor_copy(out=ss[0:64, 1, 8:520], in_=s8[0:64, 8:520]), reads=[s8.b], writes=[ss.b])
            kk.op("dve", lambda e: e.tensor_tensor(out=ss[64:128, 1, 8:520], in0=s8[64:128, 4:516], in1=s8[64:128, 12:524], op=ALU.add),
                  reads=[s8.b], writes=[ss.b])
            kk.op("dve", lambda e, i_=i_: e.tensor_tensor(out=ss[:, :, 8:520], in0=ss[:, :, 8:520], in1=i_[:], op=ALU.mult),
                  reads=[ss.b, i_.b], writes=[ss.b])
            kk.op("dve", lambda e, X=X: e.tensor_tensor(out=dfb[:], in0=ss[:, :, 8:520], in1=X[:, 0:2, 8:520], op=ALU.subtract),
                  reads=[ss.b, X.b], writes=[dfb.b])
            for ch in range(2):
                kk.op("pe", lambda e, ch=ch: e.matmul(ppl[:, ch, :], pw[:, ch, :], dfb[:, ch, :], start=True, stop=True),
                      reads=[pw.b, dfb.b], writes=[ppl.b])
            kk.op("act", lambda e, X=X: e.activation(out=szd[:], in_=X[:, 2:4, 0:512], func=AF.Silu), reads=[X.b], writes=[szd.b])
            o_ = md[s]
            for ch in range(2):
                kk.op("dve", lambda e, ch=ch, o_=o_: e.scalar_tensor_tensor(
                    out=o_[:, ch, :], in0=ppl[:, ch, :], scalar=psc[:, ch:ch + 1], in1=szd[:, ch, :], op0=ALU.mult, op1=ALU.mult),
                    reads=[ppl.b, psc.b, szd.b], writes=[o_.b])
            kk.op("pool", lambda e, o_=o_, c0=c0: e.dma_start(out=dr[f"mixT{si}"][768:1024, c0:c0 + 512].rearrange("(g p) s -> p g s", p=128), in_=o_[:]),
                  reads=[o_.b], dma=f"st{s}")

            co = cvo[s]
            for ch in range(6):
                kk.op("dve", lambda e, b_=b_, ch=ch: e.tensor_scalar(out=cacc[:], in0=b_[:, ch, 0:512], scalar1=cw[:, ch, 0:1], scalar2=1.0,
                                                                    op0=ALU.mult, op1=ALU.mult), reads=[b_.b, cw.b], writes=[cacc.b])
                for i in range(1, 5):
                    kk.op("dve", lambda e, b_=b_, ch=ch, i=i: e.scalar_tensor_tensor(
                        out=cacc[:], in0=b_[:, ch, i:i + 512], scalar=cw[:, ch, i:i + 1], in1=cacc[:], op0=ALU.mult, op1=ALU.add),
                        reads=[b_.b, cw.b, cacc.b], writes=[cacc.b])
                kk.op("act", lambda e, co=co, ch=ch: e.activation(out=co[:, ch, :], in_=cacc[:], func=AF.Silu), reads=[cacc.b], writes=[co.b])
            kk.op("pool", lambda e, co=co, c0=c0: e.dma_start(out=dr[f"convT{si}"][:, c0:c0 + 512].rearrange("(g p) s -> p g s", p=128), in_=co[:]),
                  reads=[co.b], dma=f"st{s}")


def bc(ap, shape, axis):
    return ap.unsqueeze(axis).to_broadcast(shape)


def phase_dn(nc, kk, dr, l, si, S, ident, ident_f, ones_f):
    with ExitStack() as st:
        NCH = S // 128
        tri = [sb(nc, st, f"tri{d}", [128, 128], F32) for d in range(2)]
        nmI = [sb(nc, st, f"nmI{d}", [128, 128], F32) for d in range(2)]
        nmS = [sb(nc, st, f"nmS{d}", [128, 128], F32) for d in range(2)]
        offd = sb(nc, st, "offd", [128, 128], F32)
        dnw = sb(nc, st, "dnw", [128, 4, 64], F32)
        kk.op("sp", lambda e: e.dma_start(out=tri[0][:], in_=dr["triU"][:, :]), writes=[tri[0].b], dma="c0")
        kk.op("sp", lambda e: e.dma_start(out=tri[1][:], in_=dr["triL"][:, :]), writes=[tri[1].b], dma="c0")
        kk.op("sp", lambda e: e.dma_start(out=dnw[:], in_=dr["dn_w"][l]), writes=[dnw.b], dma="c0")
        kk.op("dve", lambda e: e.tensor_scalar(out=offd[:], in0=ident_f[:], scalar1=-1.0, scalar2=1.0, op0=ALU.mult, op1=ALU.add),
              reads=[ident_f.b], writes=[offd.b])
        for d in range(2):
            kk.op("dve", lambda e, d=d: e.tensor_scalar(out=nmI[d][:], in0=tri[d][:], scalar1=-1.0, scalar2=1e30, op0=ALU.add, op1=ALU.mult),
                  reads=[tri[d].b], writes=[nmI[d].b])
            kk.op("dve", lambda e, d=d: e.tensor_tensor(out=nmS[d][:], in0=tri[d][:], in1=offd[:], op=ALU.mult),
                  reads=[tri[d].b, offd.b], writes=[nmS[d].b])
            kk.op("dve", lambda e, d=d: e.tensor_scalar(out=nmS[d][:], in0=nmS[d][:], scalar1=-1.0, scalar2=1e30, op0=ALU.add, op1=ALU.mult),
                  reads=[nmS[d].b], writes=[nmS[d].b])

        bkb = ps(nc, st, "bkb", [128, 1024], BF16)
        bk1 = ps(nc, st, "bk1", [128, 512], F32)
        bk2 = ps(nc, st, "bk2", [128, 512], F32)
        bk3 = ps(nc, st, "bk3", [128, 512], F32)
        bk4 = ps(nc, st, "bk4", [128, 512], F32)
        bk5 = ps(nc, st, "bk5", [128, 512], F32)
        bka = ps(nc, st, "bka", [128, 1024], F32)

        def v4(ap, n=4):
            return ap.rearrange("p (c i) -> p c i", c=n)

        cT = [sb(nc, st, f"cT{i}", [128, 6, 128], BF16) for i in range(2)]
        bgc = [sb(nc, st, f"bgc{i}", [128, 16], F32) for i in range(2)]
        ofl = [sb(nc, st, f"ofl{i}", [128, 256], F32) for i in range(2)]
        zb = [sb(nc, st, f"zb{i}", [128, 2, 128], BF16) for i in range(2)]
        tok = sb(nc, st, "tok", [128, 768], F32)
        sq = sb(nc, st, "dsq", [128, 512], F32)
        rs = sb(nc, st, "drs", [128, 8], F32)
        qkn = sb(nc, st, "qkn", [128, 8, 64], BF16)
        qkn32 = sb(nc, st, "qkn32", [128, 8, 64], F32)
        ek32 = sb(nc, st, "ek32", [128, 4, 64], F32)
        kd32 = sb(nc, st, "kd32", [128, 4, 64], F32)
        ekT32 = sb(nc, st, "ekT32", [64, 4, 128], F32)
        rb = sb(nc, st, "rb", [128, 4, 64], BF16)
        vnew32 = sb(nc, st, "vnew32", [128, 4, 64], F32)
        vb = sb(nc, st, "vb", [128, 256], BF16)
        qkT = sb(nc, st, "dqkT", [64, 8, 128], BF16)
        gs = sb(nc, st, "gs", [128, 8], F32)
        eg = sb(nc, st, "eg", [128, 4], F32)
        ekd = sb(nc, st, "ekd", [128, 4], F32)
        egl = sb(nc, st, "egl", [128, 4], F32)
        rg = sb(nc, st, "rg", [128, 4, 128], F32)
        dm = sb(nc, st, "dm", [128, 4, 128], F32)
        dmI = sb(nc, st, "dmI", [128, 4, 128], F32)
        dmS = sb(nc, st, "dmS", [128, 4, 128], F32)
        X = sb(nc, st, "X", [128, 4, 128], F32)
        AT = sb(nc, st, "AT", [128, 4, 128], BF16)
        YS = [sb(nc, st, f"YS{i}", [128, 4, 256], F32) for i in range(2)]
        YT = [sb(nc, st, f"YT{i}", [128, 4, 128], F32) for i in range(2)]
        db = sb(nc, st, "db", [128, 4, 128], F32)
        TTb = sb(nc, st, "TTb", [128, 4, 128], BF16)
        usb = sb(nc, st, "usb", [128, 4, 64], F32)
        ek = sb(nc, st, "ek", [128, 4, 64], BF16)
        kd = sb(nc, st, "kd", [128, 4, 64], BF16)
        qd = sb(nc, st, "qd", [128, 4, 64], BF16)
        wT = sb(nc, st, "wT", [64, 4, 128], BF16)
        qdT = sb(nc, st, "qdT", [64, 4, 128], BF16)
        vnew = sb(nc, st, "vnew", [128, 4, 64], BF16)
        Sf = sb(nc, st, "Sf", [64, 4, 64], F32)
        Sb = sb(nc, st, "Sb", [64, 4, 64], BF16)
        osb = [sb(nc, st, f"dosb{i}", [128, 256], F32) for i in range(2)]
        osq = sb(nc, st, "osq", [128, 256], F32)
        oss = sb(nc, st, "oss", [128, 4], F32)
        onb = sb(nc, st, "onb", [128, 256], BF16)
        omx = [sb(nc, st, f"omx{i}", [128, 2, 128], BF16) for i in range(2)]

        for d in range(2):
            if d == 1:
                kk.barrier()
            kk.op("pool", lambda e: e.memset(Sf[:], 0.0), writes=[Sf.b])
            kk.op("pool", lambda e: e.memset(Sb[:], 0.0), writes=[Sb.b])
            order = range(NCH) if d == 0 else range(NCH - 1, -1, -1)
            for n, ci in enumerate(order):
                s = n % 2
                r0 = ci * 128
                c_, g_ = cT[s], bgc[s]
                kk.op("sp", lambda e, c_=c_, r0=r0: e.dma_start(out=c_[:], in_=dr[f"convT{si}"][:, r0:r0 + 128].rearrange("(g p) s -> p g s", p=128)),
                      writes=[c_.b], dma=f"x{s}")
                kk.op("sp", lambda e, g_=g_, r0=r0: e.dma_start(out=g_[:], in_=dr[f"bg{si}"][r0:r0 + 128, :]), writes=[g_.b], dma=f"x{s}")
                if d == 1:
                    kk.op("sp", lambda e, o_=ofl[s], r0=r0: e.dma_start(out=o_[:], in_=dr[f"of{si}"][r0:r0 + 128, :]), writes=[ofl[s].b], dma=f"x{s}")
                    kk.op("sp", lambda e, z_=zb[s], r0=r0: e.dma_start(
                        out=z_[:], in_=dr[f"projT{si}"][G_BZ * 128:(G_BZ + 2) * 128, r0:r0 + 128].rearrange("(g p) s -> p g s", p=128)),
                        writes=[zb[s].b], dma=f"x{s}")
                gd = g_[:, 8 + 4 * d:12 + 4 * d]
                bd = g_[:, 4 * d:4 * d + 4]
                for j in range(6):
                    kk.op("pe", lambda e, c_=c_, j=j: e.transpose(out=bkb[:, j * 128:(j + 1) * 128], in_=c_[:, j, :], identity=ident[:]),
                          reads=[c_.b, ident.b], writes=[bkb.b])
                kk.op("dve", lambda e: e.tensor_copy(out=tok[:], in_=bkb[:, 0:768]), reads=[bkb.b], writes=[tok.b])
                kk.op("dve", lambda e: e.tensor_tensor(out=sq[:], in0=tok[:, 0:512], in1=tok[:, 0:512], op=ALU.mult), reads=[tok.b], writes=[sq.b])
                kk.op("dve", lambda e: e.tensor_reduce(out=rs[:], in_=v4(sq[:], 8), axis=AX.X, op=ALU.add), reads=[sq.b], writes=[rs.b])
                kk.op("dve", lambda e: e.tensor_scalar(out=rs[:], in0=rs[:], scalar1=EPS, scalar2=1.0, op0=ALU.add, op1=ALU.mult),
                      reads=[rs.b], writes=[rs.b])
                kk.op("act", lambda e: e.activation(out=rs[:], in_=rs[:], func=AF.Ln), reads=[rs.b], writes=[rs.b])
                kk.op("act", lambda e: e.activation(out=rs[:], in_=rs[:], func=AF.Exp, scale=-0.5), reads=[rs.b], writes=[rs.b])
                kk.op("dve", lambda e: e.tensor_scalar(out=rs[:, 0:4], in0=rs[:, 0:4], scalar1=0.125, scalar2=1.0, op0=ALU.mult, op1=ALU.mult),
                      reads=[rs.b], writes=[rs.b])
                kk.op("dve", lambda e: e.tensor_tensor(out=qkn32[:], in0=v4(tok[:, 0:512], 8), in1=bc(rs[:], [128, 8, 64], 2), op=ALU.mult),
                      reads=[tok.b, rs.b], writes=[qkn32.b])
                kk.op("pool", lambda e: e.tensor_copy(out=qkn[:], in_=qkn32[:]), reads=[qkn32.b], writes=[qkn.b])
                for j in range(8):
                    kk.op("pe", lambda e, j=j: e.transpose(out=bkb[0:64, j * 128:(j + 1) * 128], in_=qkn[:, j, :], identity=ident[:]),
                          reads=[qkn.b, ident.b], writes=[bkb.b])
                kk.op("act", lambda e: e.activation(out=qkT[:].rearrange("p a b -> p (a b)"), in_=bkb[0:64, 0:1024], func=AF.Copy),
                      reads=[bkb.b], writes=[qkT.b])
                kk.op("pe", lambda e, gd=gd, d=d: e.matmul(bk2[:, 0:4], tri[d][:], gd, start=True, stop=True), reads=[tri[d].b, g_.b], writes=[bk2.b])
                kk.op("pe", lambda e, gd=gd: e.matmul(bk2[:, 4:8], ones_f[:], gd, start=True, stop=True), reads=[ones_f.b, g_.b], writes=[bk2.b])
                kk.op("dve", lambda e: e.tensor_copy(out=gs[:], in_=bk2[:, 0:8]), reads=[bk2.b], writes=[gs.b])
                kk.op("act", lambda e: e.activation(out=eg[:], in_=gs[:, 0:4], func=AF.Exp), reads=[gs.b], writes=[eg.b])
                kk.op("act", lambda e: e.activation(out=egl[:], in_=gs[:, 4:8], func=AF.Exp), reads=[gs.b], writes=[egl.b])
                kk.op("dve", lambda e: e.tensor_tensor(out=ekd[:], in0=gs[:, 4:8], in1=gs[:, 0:4], op=ALU.subtract), reads=[gs.b], writes=[ekd.b])
                kk.op("act", lambda e: e.activation(out=ekd[:], in_=ekd[:], func=AF.Exp), reads=[ekd.b], writes=[ekd.b])
                kk.op("dve", lambda e, gd=gd, d=d: e.tensor_tensor(out=rg[:], in0=bc(tri[d][:], [128, 4, 128], 1), in1=bc(gd, [128, 4, 128], 2), op=ALU.mult),
                      reads=[tri[d].b, g_.b], writes=[rg.b])
                kk.op("pe", lambda e: e.matmul(bk3[:], ones_f[:], rg[:].rearrange("p a b -> p (a b)"), start=True, stop=True),
                      reads=[ones_f.b, rg.b], writes=[bk3.b])
                kk.op("dve", lambda e: e.tensor_tensor(out=dm[:], in0=v4(bk3[:]), in1=bc(gs[:, 0:4], [128, 4, 128], 2), op=ALU.subtract),
                      reads=[bk3.b, gs.b], writes=[dm.b])
                kk.op("dve", lambda e, d=d: e.tensor_tensor(out=dmI[:], in0=dm[:], in1=bc(nmI[d][:], [128, 4, 128], 1), op=ALU.add),
                      reads=[dm.b, nmI[d].b], writes=[dmI.b])
                kk.op("pool", lambda e, d=d: e.tensor_tensor(out=dmS[:], in0=dm[:], in1=bc(nmS[d][:], [128, 4, 128], 1), op=ALU.add),
                      reads=[dm.b, nmS[d].b], writes=[dmS.b])
                kk.op("act", lambda e: e.activation(out=dmI[:], in_=dmI[:], func=AF.Exp), reads=[dmI.b], writes=[dmI.b])
                kk.op("act", lambda e: e.activation(out=dmS[:], in_=dmS[:], func=AF.Exp), reads=[dmS.b], writes=[dmS.b])
                for h in range(4):
                    kTh = qkT[:, 4 + h, :]
                    qTh = qkT[:, h, :]
                    kk.op("pe", lambda e, h=h, kTh=kTh: e.matmul(bk4[:, h * 128:(h + 1) * 128], kTh, kTh, start=True, stop=True),
                          reads=[qkT.b], writes=[bk4.b])
                    kk.op("pe", lambda e, h=h, kTh=kTh, qTh=qTh: e.matmul(bk5[:, h * 128:(h + 1) * 128], kTh, qTh, start=True, stop=True),
                          reads=[qkT.b], writes=[bk5.b])
                for h in range(4):
                    kk.op("dve", lambda e, h=h, bd=bd: e.scalar_tensor_tensor(
                        out=X[:, h, :], in0=bk4[:, h * 128:(h + 1) * 128], scalar=bd[:, h:h + 1], in1=dmS[:, h, :], op0=ALU.mult, op1=ALU.mult),
                        reads=[bk4.b, g_.b, dmS.b], writes=[X.b])
                kk.op("dve", lambda e: e.tensor_tensor(out=AT[:], in0=v4(bk5[:]), in1=dmI[:], op=ALU.mult), reads=[bk5.b, dmI.b], writes=[AT.b])
                pbv = v4(bk4[:])
                for h in range(4):
                    kk.op("pe", lambda e, h=h: e.transpose(out=pbv[:, h, :], in_=X[:, h, :], identity=ident_f[:]),
                          reads=[X.b, ident_f.b], writes=[bk4.b])
                kk.op("dve", lambda e: e.tensor_copy(out=YT[1][:], in_=pbv), reads=[bk4.b], writes=[YT[1].b])
                pa = bka[:].rearrange("p (c i) -> p c i", c=4)
                for h in range(4):
                    kk.op("pe", lambda e, h=h: e.matmul(pa[:, h, 0:128], YT[1][:, h, :], X[:, h, :], start=True, stop=True),
                          reads=[YT[1].b, X.b], writes=[bka.b])
                    kk.op("pe", lambda e, h=h: e.matmul(pbv[:, h, :], X[:, h, :], YT[1][:, h, :], start=True, stop=True),
                          reads=[YT[1].b, X.b], writes=[bk4.b])
                kk.op("dve", lambda e: e.tensor_tensor(out=YS[0][:, :, 128:256], in0=bc(ident_f[:], [128, 4, 128], 1), in1=X[:], op=ALU.subtract),
                      reads=[ident_f.b, X.b], writes=[YS[0].b])
                kk.op("act", lambda e: e.activation(out=YS[0][:, :, 0:128], in_=pa[:, :, 0:128], func=AF.Identity), reads=[bka.b], writes=[YS[0].b])
                kk.op("dve", lambda e: e.tensor_copy(out=YT[0][:], in_=pbv), reads=[bk4.b], writes=[YT[0].b])
                cur = 0
                for k in range(1, 7):
                    ys, yt = YS[cur], YT[cur]
                    nys, nyt = YS[1 - cur], YT[1 - cur]
                    for h in range(4):
                        kk.op("pe", lambda e, h=h, ys=ys, yt=yt: e.matmul(pa[:, h, :], yt[:, h, :], ys[:, h, :], start=True, stop=True),
                              reads=[ys.b, yt.b], writes=[bka.b])
                        if k < 6:
                            kk.op("pe", lambda e, h=h, ys=ys, yt=yt: e.matmul(pbv[:, h, :], ys[:, h, 0:128], yt[:, h, :], start=True, stop=True),
                                  reads=[ys.b, yt.b], writes=[bk4.b])
                    kk.op("dve", lambda e, ys=ys, nys=nys: e.tensor_tensor(out=nys[:, :, 128:256], in0=pa[:, :, 128:256], in1=ys[:, :, 128:256], op=ALU.add),
                          reads=[bka.b, ys.b], writes=[nys.b])
                    if k < 6:
                        kk.op("act", lambda e, nys=nys: e.activation(out=nys[:, :, 0:128], in_=pa[:, :, 0:128], func=AF.Identity),
                              reads=[bka.b], writes=[nys.b])
                        kk.op("dve", lambda e, nyt=nyt: e.tensor_copy(out=nyt[:], in_=pbv), reads=[bk4.b], writes=[nyt.b])
                    cur = 1 - cur
                TT = YS[cur]
                kk.op("pool", lambda e, bd=bd: e.tensor_tensor(out=db[:], in0=bc(ident_f[:], [128, 4, 128], 1), in1=bc(bd, [128, 4, 128], 2), op=ALU.mult),
                      reads=[ident_f.b, g_.b], writes=[db.b])
                kk.op("pe", lambda e: e.matmul(bk3[:], ones_f[:], db[:].rearrange("p a b -> p (a b)"), start=True, stop=True),
                      reads=[ones_f.b, db.b], writes=[bk3.b])
                kk.op("dve", lambda e, TT=TT: e.tensor_tensor(out=TTb[:], in0=v4(bk3[:]), in1=TT[:, :, 128:256], op=ALU.mult),
                      reads=[bk3.b, TT.b], writes=[TTb.b])
                kk.op("dve", lambda e: e.tensor_tensor(out=ek32[:], in0=qkn32[:, 4:8, :], in1=bc(eg[:], [128, 4, 64], 2), op=ALU.mult),
                      reads=[qkn32.b, eg.b], writes=[ek32.b])
                kk.op("pool", lambda e: e.tensor_tensor(out=kd32[:], in0=qkn32[:, 4:8, :], in1=bc(ekd[:], [128, 4, 64], 2), op=ALU.mult),
                      reads=[qkn32.b, ekd.b], writes=[kd32.b])
                kk.op("pool", lambda e: e.tensor_tensor(out=qd[:], in0=qkn32[:, 0:4, :], in1=bc(eg[:], [128, 4, 64], 2), op=ALU.mult),
                      reads=[qkn32.b, eg.b], writes=[qd.b])
                pek = bk5[0:64, :].rearrange("p (c i) -> p c i", c=4)
                for h in range(4):
                    kk.op("pe", lambda e, h=h: e.transpose(out=pek[:, h, :], in_=ek32[:, h, :], identity=ident_f[:]),
                          reads=[ek32.b, ident_f.b], writes=[bk5.b])
                kk.op("dve", lambda e: e.tensor_copy(out=ekT32[:], in_=pek), reads=[bk5.b], writes=[ekT32.b])
                for h in range(4):
                    kk.op("pe", lambda e, h=h: e.transpose(out=bkb[0:64, h * 128:(h + 1) * 128], in_=qd[:, h, :], identity=ident[:]),
                          reads=[qd.b, ident.b], writes=[bkb.b])
                kk.op("dve", lambda e: e.tensor_copy(out=qdT[:].rearrange("p a b -> p (a b)"), in_=bkb[0:64, 0:512]), reads=[bkb.b], writes=[qdT.b])
                pws = bk1[:, 0:256].rearrange("p (c i) -> p c i", c=4)
                pout = bk1[:, 256:512].rearrange("p (c i) -> p c i", c=4)
                pds = bk3[0:64, 0:256].rearrange("p (c i) -> p c i", c=4)
                pv2 = bk2[:, 256:512].rearrange("p (c i) -> p c i", c=4)
                for h in range(4):
                    kk.op("pe", lambda e, h=h: e.matmul(pws[:, h, :], ekT32[:, h, :], Sf[:, h, :], start=True, stop=True),
                          reads=[ekT32.b, Sf.b], writes=[bk1.b])
                kk.op("dve", lambda e: e.tensor_tensor(out=rb[:], in0=v4(tok[:, 512:768]), in1=pws, op=ALU.subtract),
                      reads=[tok.b, bk1.b], writes=[rb.b])
                for h in range(4):
                    kk.op("pe", lambda e, h=h: e.matmul(pv2[:, h, :], TTb[:, h, :], rb[:, h, :], start=True, stop=True),
                          reads=[TTb.b, rb.b], writes=[bk2.b])
                kk.op("dve", lambda e: e.tensor_copy(out=vnew32[:], in_=pv2), reads=[bk2.b], writes=[vnew32.b])
                kk.op("dve", lambda e: e.tensor_copy(out=vnew[:], in_=pv2), reads=[bk2.b], writes=[vnew.b])
                for h in range(4):
                    kk.op("pe", lambda e, h=h: e.matmul(pout[:, h, :], qdT[:, h, :], Sb[:, h, :], start=True, stop=False),
                          reads=[qdT.b, Sb.b], writes=[bk1.b])
                    kk.op("pe", lambda e, h=h: e.matmul(pout[:, h, :], AT[:, h, :], vnew[:, h, :], start=False, stop=True),
                          reads=[AT.b, vnew.b], writes=[bk1.b])
                for h in range(4):
                    kk.op("pe", lambda e, h=h: e.matmul(pds[:, h, :], kd32[:, h, :], vnew32[:, h, :], start=True, stop=True),
                          reads=[kd32.b, vnew32.b], writes=[bk3.b])
                kk.op("dve", lambda e: e.tensor_tensor(out=Sf[:], in0=Sf[:], in1=bc(egl[0:64, :], [64, 4, 64], 2), op=ALU.mult),
                      reads=[Sf.b, egl.b], writes=[Sf.b])
                kk.op("dve", lambda e: e.tensor_tensor(out=Sf[:], in0=Sf[:], in1=pds, op=ALU.add), reads=[Sf.b, bk3.b], writes=[Sf.b])
                kk.op("act", lambda e: e.activation(out=Sb[:], in_=Sf[:], func=AF.Copy), reads=[Sf.b], writes=[Sb.b])
                o_ = osb[s]
                if d == 0:
                    kk.op("dve", lambda e, o_=o_: e.tensor_copy(out=o_[:], in_=bk1[:, 256:512]), reads=[bk1.b], writes=[o_.b])
                    kk.op("pool", lambda e, o_=o_, r0=r0: e.dma_start(out=dr[f"of{si}"][r0:r0 + 128, :], in_=o_[:]), reads=[o_.b], dma=f"st{s}")
                else:
                    kk.op("dve", lambda e, o_=o_, f_=ofl[s]: e.tensor_tensor(out=o_[:], in0=bk1[:, 256:512], in1=f_[:], op=ALU.add),
                          reads=[bk1.b, ofl[s].b], writes=[o_.b])
                    if f"osum{si}" in dr:
                        kk.op("pool", lambda e, o_=o_, r0=r0: e.dma_start(out=dr[f"osum{si}"][r0:r0 + 128, :], in_=o_[:]), reads=[o_.b], dma=f"st{s}")
                    kk.op("pool", lambda e, o_=o_: e.tensor_tensor(out=osq[:], in0=o_[:], in1=o_[:], op=ALU.mult), reads=[o_.b], writes=[osq.b])
                    kk.op("dve", lambda e: e.tensor_reduce(out=oss[:], in_=v4(osq[:]), axis=AX.X, op=ALU.add), reads=[osq.b], writes=[oss.b])
                    kk.op("dve", lambda e: e.tensor_scalar(out=oss[:], in0=oss[:], scalar1=1.0 / 64, scalar2=EPS, op0=ALU.mult, op1=ALU.add),
                          reads=[oss.b], writes=[oss.b])
                    kk.op("act", lambda e: e.activation(out=oss[:], in_=oss[:], func=AF.Ln), reads=[oss.b], writes=[oss.b])
                    kk.op("act", lambda e: e.activation(out=oss[:], in_=oss[:], func=AF.Exp, scale=-0.5), reads=[oss.b], writes=[oss.b])
                    kk.op("dve", lambda e, o_=o_: e.tensor_tensor(out=v4(o_[:]), in0=v4(o_[:]), in1=bc(oss[:], [128, 4, 64], 2), op=ALU.mult),
                          reads=[o_.b, oss.b], writes=[o_.b])
                    kk.op("pool", lambda e, o_=o_: e.tensor_tensor(out=onb[:], in0=o_[:], in1=dnw[:].rearrange("p a b -> p (a b)"), op=ALU.mult),
                          reads=[o_.b, dnw.b], writes=[onb.b])
                    for j in range(2):
                        kk.op("pe", lambda e, j=j: e.transpose(out=bkb[:, j * 128:(j + 1) * 128], in_=onb[:, j * 128:(j + 1) * 128], identity=ident[:]),
                              reads=[onb.b, ident.b], writes=[bkb.b])
                    z_ = zb[s]
                    m_ = omx[s]
                    kk.op("act", lambda e, z_=z_: e.activation(out=z_[:], in_=z_[:], func=AF.Silu), reads=[z_.b], writes=[z_.b])
                    kk.op("dve", lambda e, z_=z_, m_=m_: e.tensor_tensor(out=m_[:], in0=bkb[:, 0:256].rearrange("p (a b) -> p a b", a=2), in1=z_[:], op=ALU.mult),
                          reads=[bkb.b, z_.b], writes=[m_.b])
                    kk.op("pool", lambda e, m_=m_, r0=r0: e.dma_start(
                        out=dr[f"mixT{si}"][256:512, r0:r0 + 128].rearrange("(g p) s -> p g s", p=128), in_=m_[:]), reads=[m_.b], dma=f"st{s}")


POOL_WINDOWS = (2, 4, 8, 16)


def host_layout(seqs, depth, norm_w, w_in, sgu_w, sgu_b, conv_w, a_log, dt_bias, dn_norm_w,
                q_norm_w, k_norm_w, pool_w, pool_scale, w_out):
    f = np.float32
    m = {}
    m["w_in"] = np.ascontiguousarray(np.asarray(w_in, f).reshape(depth, 8, 128, NCOL).transpose(0, 2, 1, 3)[..., COL_PERM])
    m["norm_w"] = np.ascontiguousarray(np.asarray(norm_w, f).reshape(depth, 8, 128).transpose(0, 2, 1))
    m["w_out"] = np.ascontiguousarray(np.asarray(w_out, f).reshape(depth, 8, 128, D).transpose(0, 2, 1, 3))
    qk = np.concatenate([np.repeat(np.asarray(q_norm_w, f)[:, None, :], 4, 1), np.repeat(np.asarray(k_norm_w, f)[:, None, :], 2, 1)], 1)
    m["qkw"] = np.ascontiguousarray(np.broadcast_to(qk[:, None], (depth, 128, 6, 64)))
    m["a_log"] = np.ascontiguousarray(np.broadcast_to(np.asarray(a_log, f).reshape(depth, 1, 8), (depth, 128, 8)))
    m["dt_bias"] = np.ascontiguousarray(np.broadcast_to(np.asarray(dt_bias, f).reshape(depth, 1, 8), (depth, 128, 8)))
    m["ident"] = np.eye(128, dtype=f)
    m["sgu_wT"] = np.ascontiguousarray(np.asarray(sgu_w, f).transpose(0, 3, 1, 2))
    sb_ = np.asarray(sgu_b, f)
    sbT = np.zeros((depth, 128, 2, 128), f)
    for hp in range(2):
        for h2 in range(2):
            sbT[:, h2 * 64:(h2 + 1) * 64, hp, :] = sb_[:, hp * 2 + h2, None, :]
    m["sgu_bT"] = sbT
    m["conv_w"] = np.ascontiguousarray(np.asarray(conv_w, f).reshape(depth, 5, 6, 128).transpose(0, 3, 2, 1))
    pw = np.zeros((depth, 128, 2, 128), f)
    pwi = np.asarray(pool_w, f)
    for ch in range(2):
        for g2 in range(2):
            pw[:, g2 * 64:(g2 + 1) * 64, ch, g2 * 64:(g2 + 1) * 64] = pwi[:, ch * 2 + g2]
    m["pool_w"] = pw
    m["pool_s"] = np.ascontiguousarray(np.asarray(pool_scale, f).reshape(depth, 2, 128).transpose(0, 2, 1))
    m["dn_w"] = np.ascontiguousarray(np.broadcast_to(np.asarray(dn_norm_w, f)[:, None, None, :], (depth, 128, 4, 64)))
    k_ = np.arange(128)
    m["triU"] = (k_[:, None] <= k_[None, :]).astype(f)
    m["triL"] = (k_[:, None] >= k_[None, :]).astype(f)
    for i, S in enumerate(seqs):
        c, s_ = rope_tables(S)
        m[f"cos{i}"] = c
        m[f"sin{i}"] = s_
        t = np.arange(S)
        ic = np.zeros((128, 2, S), f)
        for g, win in enumerate(POOL_WINDOWS):
            lo = np.clip(t - win // 2, 0, S)
            hi = np.clip(t + win // 2, 0, S)
            ic[(g % 2) * 64:(g % 2) * 64 + 64, g // 2, :] = (1.0 / (hi - lo).astype(f))[None, :]
        m[f"icnt{i}"] = ic
    return m


_NC_CACHE = {}


def kernel(x_prompt, x_sample, norm_w, w_in, sgu_w, sgu_b, conv_w, a_log, dt_bias, dn_norm_w,
           q_norm_w, k_norm_w, pool_w, pool_scale, w_out):
    x_prompt = np.asarray(x_prompt, np.float32)
    x_sample = np.asarray(x_sample, np.float32)
    depth = int(np.asarray(w_in).shape[0])
    seqs = [x_prompt.shape[1], x_sample.shape[1]]
    key = (tuple(seqs), depth)
    if key not in _NC_CACHE:
        _NC_CACHE[key] = build(seqs, depth)
    nc = _NC_CACHE[key]
    common = host_layout(seqs, depth, norm_w, w_in, sgu_w, sgu_b, conv_w, a_log, dt_bias, dn_norm_w,
                         q_norm_w, k_norm_w, pool_w, pool_scale, w_out)
    nb_p, nb_s = x_prompt.shape[0], x_sample.shape[0]
    in_maps = []
    for c in range(8):
        mm = dict(common)
        mm["x0"] = np.ascontiguousarray(x_prompt[c % nb_p])
        mm["x1"] = np.ascontiguousarray(x_sample[c % nb_s])
        in_maps.append(mm)
    res = run_bass_kernel_spmd(nc, in_maps, core_ids=list(range(8)))
    yp = np.stack([np.asarray(res.results[b]["y0"], np.float32) for b in range(nb_p)], 0)
    ys = np.stack([np.asarray(res.results[b]["y1"], np.float32) for b in range(nb_s)], 0)
    return (yp, ys)
```

```python
import numpy as np
import ml_dtypes
from contextlib import ExitStack
import concourse.bass as bass
import concourse.mybir as mybir
from concourse.bass_utils import run_bass_kernel_spmd

F32 = mybir.dt.float32
BF16 = mybir.dt.bfloat16
AF = mybir.ActivationFunctionType
ALU = mybir.AluOpType
AX = mybir.AxisListType

D = 1024
NCOL = 3088
EPS = 1e-6
G_AU, G_AV, G_AZ, G_BQ, G_BK, G_BV, G_BZ, G_CQ, G_CK, G_CV, G_CZ, G_DX, G_DZ = \
    0, 2, 4, 6, 8, 10, 12, 14, 16, 17, 18, 20, 22
ORIG_OFF = dict(au=0, av=256, az=512, bq=768, bk=1024, bv=1280, bz=1536, bb=1792, ba=1800,
                cq=1808, ck=2064, cv=2192, cz=2320, dx=2576, dz=2832)
COL_PERM = np.concatenate([
    np.arange(0, 1792), np.arange(1808, 3088), np.arange(1792, 1808)])


class Buf:
    __slots__ = ("name", "w", "r")

    def __init__(self, name):
        self.name = name
        self.w = None
        self.r = {}


class K:
    ENG = ("pe", "act", "dve", "pool", "sp")

    def __init__(self, nc, stack):
        self.nc = nc
        self.sems = {}
        self.cnt = {}
        self.stack = stack
        for e in self.ENG:
            self._newsem(e)
        self.prog = {e: [] for e in self.ENG}
        self.waited = {e: {} for e in self.ENG}
        self.ninstr = 0

    def _newsem(self, key):
        self.sems[key] = self.stack.enter_context(self.nc.semaphore("s_" + key))
        self.cnt[key] = 0

    LIMIT = None

    def op(self, e, fn, reads=(), writes=(), dma=None):
        if K.LIMIT is not None and self.ninstr >= K.LIMIT:
            return
        need = {}

        def want(tok):
            if tok is None:
                return
            k, v = tok
            if k == "pe" and e == "pe" and dma is None:
                return
            if k not in self.ENG:
                v = self.cnt[k]
            if need.get(k, 0) < v:
                need[k] = v
        for b in reads:
            want(b.w)
        for b in writes:
            want(b.w)
            for k, v in b.r.items():
                want((k, v))
        waits = []
        wd = self.waited[e]
        for k, v in need.items():
            if wd.get(k, 0) < v:
                wd[k] = v
                waits.append((k, v))
        if dma is not None:
            if dma not in self.sems:
                self._newsem(dma)
            key, inc = dma, 16
        else:
            key, inc = e, 1
        self.cnt[key] += inc
        tok = (key, self.cnt[key])
        for b in reads:
            if b.r.get(key, 0) < tok[1]:
                b.r[key] = tok[1]
        for b in writes:
            b.w = tok
            b.r = {}
        self.prog[e].append((waits, fn, key, inc))
        self.ninstr += 1

    def barrier(self):
        tot = dict(self.cnt)
        for e in self.ENG:
            waits = []
            for k, v in tot.items():
                if v > 0 and self.waited[e].get(k, 0) < v:
                    self.waited[e][k] = v
                    waits.append((k, v))
            if waits:
                self.prog[e].append((waits, None, None, 0))

    def flush(self):
        nc = self.nc
        _DYN.clear()
        with nc.Block() as block:
            def run(e, eng):
                for waits, fn, key, inc in self.prog[e]:
                    for k, v in waits:
                        eng.wait_ge(self.sems[k], v)
                    if fn is not None:
                        fn(eng).then_inc(self.sems[key], inc)

            @block.tensor
            def _(eng):
                run("pe", eng)

            @block.scalar
            def _(eng):
                run("act", eng)

            @block.vector
            def _(eng):
                run("dve", eng)

            @block.gpsimd
            def _(eng):
                run("pool", eng)

            @block.sync
            def _(eng):
                run("sp", eng)
        self.prog = {e: [] for e in self.ENG}


class T:
    def __init__(self, t, name):
        self.t = t
        self.b = Buf(name)

    def __getitem__(self, idx):
        return self.t[idx]


_UID = [0]


def sb(nc, st, name, shape, dt):
    _UID[0] += 1
    nm = f"sb{_UID[0]}_{name}"
    return T(st.enter_context(nc.sbuf_tensor(nm, list(shape), dt)), nm)


def ps(nc, st, name, shape, dt):
    _UID[0] += 1
    nm = f"ps{_UID[0]}_{name}"
    return T(st.enter_context(nc.psum_tensor(nm, list(shape), dt)), nm)


def rope_tables(S):
    rows = np.repeat(np.arange(S // 64), 64)
    cols = np.tile(np.arange(64), S // 64)
    inv = np.power(np.float32(10000.0), -2.0 * np.arange(16, dtype=np.float32) / 32).astype(np.float32)
    ang = np.stack([rows, cols], -1).astype(np.float32)[:, :, None] * inv
    return np.cos(ang).astype(np.float32).reshape(S, 32), np.sin(ang).astype(np.float32).reshape(S, 32)


def build(seqs, depth, debug=False, phases="pfdao", divs=(2, 4), groups=(4, 2)):
    nc = bass.Bass("TRN2", target_bir_lowering=False)
    dr = {}

    def din(name, shape, dt=F32):
        dr[name] = nc.dram_tensor(name, list(shape), dt, kind="ExternalInput").ap()
        return dr[name]

    def dscr(name, shape, dt, out=False):
        kind = "ExternalOutput" if out else "Internal"
        dr[name] = nc.dram_tensor(name, list(shape), dt, kind=kind).ap()
        return dr[name]

    nseq = len(seqs)
    for i, S in enumerate(seqs):
        din(f"x{i}", [S, D])
        din(f"cos{i}", [S, 32])
        din(f"sin{i}", [S, 32])
        dscr(f"y{i}", [S // groups[i], D], F32, out=True)
        dscr(f"y1_{i}", [S, D], F32)
        dscr(f"projT{i}", [3072, S], BF16, out=debug)
        dscr(f"qT{i}", [256, S], BF16, out=debug)
        dscr(f"kT{i}", [128, S], BF16, out=debug)
        dscr(f"va{i}", [S, 130], BF16, out=debug)
        dscr(f"bg{i}", [S, 16], F32, out=debug)
        dscr(f"mixT{i}", [1024, S], BF16, out=debug)
        dscr(f"convT{i}", [768, S], BF16, out=debug)
        dscr(f"of{i}", [S, 256], F32, out=debug)
        npi = S // groups[i]
        dscr(f"qTp{i}", [256, npi], BF16)
        dscr(f"zTp{i}", [256, npi], BF16)
        dscr(f"mixTp{i}", [1024, npi], BF16)
        dscr(f"xp{i}", [npi, D], F32)
        if debug:
            dscr(f"osum{i}", [S, 256], F32, out=True)
    din("w_in", [depth, 128, 8, NCOL])
    din("norm_w", [depth, 128, 8])
    din("w_out", [depth, 128, 8, D])
    din("qkw", [depth, 128, 6, 64])
    din("a_log", [depth, 128, 8])
    din("dt_bias", [depth, 128, 8])
    din("ident", [128, 128])
    din("sgu_wT", [depth, 128, 4, 128])
    din("sgu_bT", [depth, 128, 2, 128])
    din("conv_w", [depth, 128, 6, 5])
    din("pool_w", [depth, 128, 2, 128])
    din("pool_s", [depth, 128, 2])
    din("dn_w", [depth, 128, 4, 64])
    din("triU", [128, 128])
    din("triL", [128, 128])
    for i, S in enumerate(seqs):
        din(f"icnt{i}", [128, 2, S])

    with ExitStack() as top:
        kk = K(nc, top)
        for key in ("c0", "wl0", "wl1", "x0", "x1", "st0", "st1", "sq0", "sq1", "sp0", "sp1"):
            kk._newsem(key)
        with nc.Block() as blk0:
            @blk0.vector
            def _(eng):
                for key in kk.sems:
                    eng.sem_clear(kk.sems[key])
        ident_f = sb(nc, top, "ident_f", [128, 128], F32)
        ident = sb(nc, top, "ident_b", [128, 128], BF16)
        ones_f = sb(nc, top, "ones_f", [128, 128], F32)
        kk.op("sp", lambda e: e.dma_start(out=ident_f[:], in_=dr["ident"][:, :]), writes=[ident_f.b], dma="c0")
        kk.op("dve", lambda e: e.tensor_copy(out=ident[:], in_=ident_f[:]), reads=[ident_f.b], writes=[ident.b])
        kk.op("pool", lambda e: e.memset(ones_f[:], 1.0), writes=[ones_f.b])

        for l in range(depth):
            for si, S in enumerate(seqs):
                xin = dr[f"x{si}"] if l == 0 else dr[f"y1_{si}"]
                last = (l == depth - 1)
                yout = dr[f"y{si}"] if last else dr[f"y1_{si}"]
                part = (S // groups[si], divs[si]) if last else None
                for ph in phases:
                    if ph == "p":
                        phase_proj(nc, kk, dr, l, si, S, xin, ident, ident_f)
                    elif ph == "f":
                        phase_fm(nc, kk, dr, l, si, S, ident)
                    elif ph == "d":
                        phase_dn(nc, kk, dr, l, si, S, ident, ident_f, ones_f)
                    elif ph == "a":
                        if part is not None:
                            phase_part(nc, kk, dr, si, S, xin, part)
                            kk.barrier()
                            kk.flush()
                        phase_attn(nc, kk, dr, l, si, S, ones_f, part)
                    elif ph == "o":
                        phase_out(nc, kk, dr, l, si, S, xin, yout, part)
                    kk.barrier()
                    kk.flush()
    return nc


def load_w_bf16(nc, kk, st, name, src_ap, ncols, scale_t=None, chunk=1024):
    w = sb(nc, st, name, [128, 8, ncols], BF16)
    stg = [sb(nc, st, f"{name}_stg{i}", [128, chunk], F32) for i in range(2)]
    n = 0
    for kc in range(8):
        for c0 in range(0, ncols, chunk):
            cw = min(chunk, ncols - c0)
            s = stg[n % 2]
            kk.op("sp", lambda e, s=s, kc=kc, c0=c0, cw=cw: e.dma_start(out=s[:, 0:cw], in_=src_ap[:, kc, c0:c0 + cw]),
                  writes=[s.b], dma=f"wl{n % 2}")
            if scale_t is not None:
                kk.op("dve", lambda e, s=s, kc=kc, c0=c0, cw=cw: e.tensor_scalar(
                    out=w[:, kc, c0:c0 + cw], in0=s[:, 0:cw], scalar1=scale_t[:, kc:kc + 1], scalar2=1.0,
                    op0=ALU.mult, op1=ALU.mult), reads=[s.b, scale_t.b], writes=[w.b])
            else:
                eng = "dve" if n % 2 == 0 else "pool"
                kk.op(eng, lambda e, s=s, kc=kc, c0=c0, cw=cw: e.tensor_copy(out=w[:, kc, c0:c0 + cw], in_=s[:, 0:cw]),
                      reads=[s.b], writes=[w.b])
            n += 1
    return w


def phase_proj(nc, kk, dr, l, si, S, xin, ident, ident_f):
    with ExitStack() as st:
        nw = sb(nc, st, "nw", [128, 8], F32)
        kk.op("sp", lambda e: e.dma_start(out=nw[:], in_=dr["norm_w"][l]), writes=[nw.b], dma="c0")
        w = load_w_bf16(nc, kk, st, "w_in_sb", dr["w_in"][l], NCOL, scale_t=nw)
        qkw = sb(nc, st, "qkw", [128, 6, 64], F32)
        kk.op("sp", lambda e: e.dma_start(out=qkw[:], in_=dr["qkw"][l]), writes=[qkw.b], dma="c0")
        alog = sb(nc, st, "alog", [128, 8], F32)
        nA = sb(nc, st, "nA", [128, 8], F32)
        dtb = sb(nc, st, "dtb", [128, 8], F32)
        kk.op("sp", lambda e: e.dma_start(out=alog[:], in_=dr["a_log"][l]), writes=[alog.b], dma="c0")
        kk.op("sp", lambda e: e.dma_start(out=dtb[:], in_=dr["dt_bias"][l]), writes=[dtb.b], dma="c0")
        kk.op("act", lambda e: e.activation(out=nA[:], in_=alog[:], func=AF.Exp), reads=[alog.b], writes=[nA.b])
        kk.op("dve", lambda e: e.tensor_scalar(out=nA[:], in0=nA[:], scalar1=-1.0, scalar2=1.0, op0=ALU.mult, op1=ALU.mult),
              reads=[nA.b], writes=[nA.b])

        NS = 2
        xt = [sb(nc, st, f"xt{i}", [128, D], F32) for i in range(NS)]
        junk = sb(nc, st, "junk", [128, D], BF16)
        hb = [sb(nc, st, f"hb{i}", [128, D], BF16) for i in range(NS)]
        ssq = [sb(nc, st, f"ssq{i}", [128, 1], F32) for i in range(NS)]
        rstd = [sb(nc, st, f"rstd{i}", [128, 1], F32) for i in range(NS)]
        pT = [ps(nc, st, f"pT{i}", [128, 8, 128], BF16) for i in range(2)]
        hT = [sb(nc, st, f"hT{i}", [128, 8, 512], BF16) for i in range(2)]
        pacc = [ps(nc, st, f"pacc{i}", [128, 512], F32) for i in range(3)]
        ptk = [ps(nc, st, f"ptk{i}", [128, 512], F32) for i in range(1)]
        pbg = [ps(nc, st, f"pbg{i}", [128, 512], F32) for i in range(1)]
        ptr = ps(nc, st, "ptr", [128, 3, 128], BF16)
        stage = [sb(nc, st, f"stage{i}", [128, 24, 512], BF16) for i in range(2)]
        cs = [sb(nc, st, f"cs{i}", [128, 2, 32], F32) for i in range(NS)]
        qk = [sb(nc, st, f"qk{i}", [128, 6, 64], F32) for i in range(NS)]
        qsq = sb(nc, st, "qsq", [128, 6, 64], F32)
        qss = [sb(nc, st, f"qss{i}", [128, 6], F32) for i in range(NS)]
        qr = [sb(nc, st, f"qr{i}", [128, 6, 64], F32) for i in range(NS)]
        tmpa = sb(nc, st, "tmpa", [128, 6, 2, 16], F32)
        tmpb = sb(nc, st, "tmpb", [128, 6, 2, 16], F32)
        qkb = [sb(nc, st, f"qkb{i}", [128, 384], BF16) for i in range(NS)]
        qkT = [sb(nc, st, f"qkT{i}", [128, 3, 512], BF16) for i in range(2)]
        va = [sb(nc, st, f"va{i}", [128, 130], BF16) for i in range(NS)]
        bgt = [sb(nc, st, f"bgt{i}", [128, 16], F32) for i in range(NS)]
        t8 = [sb(nc, st, f"t8{i}", [128, 16], F32) for i in range(NS)]
        for v_ in va:
            kk.op("pool", lambda e, v_=v_: e.memset(v_[:], 1.0), writes=[v_.b])

        projT = dr[f"projT{si}"]
        nblk = S // 512
        ev = 0
        for tb in range(nblk):
            hTb = hT[tb % 2]
            qkTb = qkT[tb % 2]
            for t4 in range(4):
                ti = tb * 4 + t4
                s = ti % NS
                r0 = ti * 128
                x_, h_, sq_, rs_ = xt[s], hb[s], ssq[s], rstd[s]
                kk.op("sp", lambda e, x_=x_, r0=r0: e.dma_start(out=x_[:], in_=xin[r0:r0 + 128, :]),
                      writes=[x_.b], dma=f"x{s}")
                kk.op("sp", lambda e, c_=cs[s], r0=r0: e.dma_start(out=c_[:, 0, :], in_=dr[f"cos{si}"][r0:r0 + 128, :]),
                      writes=[cs[s].b], dma=f"x{s}")
                kk.op("sp", lambda e, c_=cs[s], r0=r0: e.dma_start(out=c_[:, 1, :], in_=dr[f"sin{si}"][r0:r0 + 128, :]),
                      writes=[cs[s].b], dma=f"x{s}")
                kk.op("act", lambda e, x_=x_, sq_=sq_: e.activation(out=junk[:], in_=x_[:], func=AF.Square, accum_out=sq_[:]),
                      reads=[x_.b], writes=[junk.b, sq_.b])
                kk.op("dve", lambda e, sq_=sq_, rs_=rs_: e.tensor_scalar(out=rs_[:], in0=sq_[:], scalar1=1.0 / D, scalar2=EPS,
                                                                         op0=ALU.mult, op1=ALU.add), reads=[sq_.b], writes=[rs_.b])
                kk.op("act", lambda e, rs_=rs_: e.activation(out=rs_[:], in_=rs_[:], func=AF.Ln), reads=[rs_.b], writes=[rs_.b])
                kk.op("act", lambda e, rs_=rs_: e.activation(out=rs_[:], in_=rs_[:], func=AF.Exp, scale=-0.5), reads=[rs_.b], writes=[rs_.b])
                kk.op("dve", lambda e, x_=x_, h_=h_, rs_=rs_: e.tensor_scalar(out=h_[:], in0=x_[:], scalar1=rs_[:, 0:1], scalar2=1.0,
                                                                              op0=ALU.mult, op1=ALU.mult),
                      reads=[x_.b, rs_.b], writes=[h_.b])
                p_ = pT[ti % 2]
                for kc in range(8):
                    kk.op("pe", lambda e, p_=p_, h_=h_, kc=kc: e.transpose(out=p_[:, kc, :], in_=h_[:, kc * 128:(kc + 1) * 128],
                                                                          identity=ident[:]),
                          reads=[h_.b, ident.b], writes=[p_.b])
                kk.op("act", lambda e, p_=p_, hTb=hTb, t4=t4: e.activation(out=hTb[:, :, t4 * 128:(t4 + 1) * 128], in_=p_[:],
                                                                           func=AF.Identity),
                      reads=[p_.b], writes=[hTb.b])
                pk = ptk[0]
                for kc in range(8):
                    kk.op("pe", lambda e, pk=pk, hTb=hTb, t4=t4, kc=kc: e.matmul(
                        pk[:], hTb[:, kc, t4 * 128:(t4 + 1) * 128], w[:, kc, G_CQ * 128:G_CQ * 128 + 512],
                        start=(kc == 0), stop=(kc == 7)), reads=[hTb.b, w.b], writes=[pk.b])
                pb_ = pbg[0]
                for kc in range(8):
                    kk.op("pe", lambda e, pb_=pb_, hTb=hTb, t4=t4, kc=kc: e.matmul(
                        pb_[:, 0:16], hTb[:, kc, t4 * 128:(t4 + 1) * 128], w[:, kc, 3072:3088],
                        start=(kc == 0), stop=(kc == 7)), reads=[hTb.b, w.b], writes=[pb_.b])
                q_, ss_, r_, qb_, va_, c_ = qk[s], qss[s], qr[s], qkb[s], va[s], cs[s]
                kk.op("dve", lambda e, q_=q_, pk=pk: e.tensor_copy(out=q_[:].rearrange("p a b -> p (a b)"), in_=pk[:, 0:384]),
                      reads=[pk.b], writes=[q_.b])
                kk.op("dve", lambda e, va_=va_, pk=pk: e.tensor_copy(
                    out=va_[:].rearrange("p (a b) -> p a b", a=2)[:, :, 0:64],
                    in_=pk[:, 384:512].rearrange("p (a b) -> p a b", a=2)), reads=[pk.b], writes=[va_.b])
                kk.op("pool", lambda e, va_=va_, r0=r0: e.dma_start(out=dr[f"va{si}"][r0:r0 + 128, :], in_=va_[:]),
                      reads=[va_.b], dma=f"st{s}")
                kk.op("dve", lambda e, q_=q_: e.tensor_tensor(out=qsq[:], in0=q_[:], in1=q_[:], op=ALU.mult),
                      reads=[q_.b], writes=[qsq.b])
                kk.op("dve", lambda e, ss_=ss_: e.tensor_reduce(out=ss_[:], in_=qsq[:], axis=AX.X, op=ALU.add),
                      reads=[qsq.b], writes=[ss_.b])
                kk.op("dve", lambda e, ss_=ss_: e.tensor_scalar(out=ss_[:], in0=ss_[:], scalar1=1.0 / 64, scalar2=EPS,
                                                                op0=ALU.mult, op1=ALU.add), reads=[ss_.b], writes=[ss_.b])
                kk.op("act", lambda e, ss_=ss_: e.activation(out=ss_[:], in_=ss_[:], func=AF.Ln), reads=[ss_.b], writes=[ss_.b])
                kk.op("act", lambda e, ss_=ss_: e.activation(out=ss_[:], in_=ss_[:], func=AF.Exp, scale=-0.5), reads=[ss_.b], writes=[ss_.b])
                kk.op("dve", lambda e, q_=q_, ss_=ss_: e.tensor_tensor(
                    out=q_[:], in0=q_[:], in1=ss_[:].unsqueeze(2).to_broadcast([128, 6, 64]), op=ALU.mult),
                    reads=[q_.b, ss_.b], writes=[q_.b])
                kk.op("pool", lambda e, q_=q_: e.tensor_tensor(out=q_[:], in0=q_[:], in1=qkw[:], op=ALU.mult),
                      reads=[q_.b, qkw.b], writes=[q_.b])
                def v5(t):
                    return t[:].rearrange("p h (a b f) -> p h a b f", a=2, b=2)
                cosb = lambda c_: c_[:, 0, :].rearrange("p (a f) -> p a f", a=2).unsqueeze(1).to_broadcast([128, 6, 2, 16])
                sinb = lambda c_: c_[:, 1, :].rearrange("p (a f) -> p a f", a=2).unsqueeze(1).to_broadcast([128, 6, 2, 16])
                kk.op("dve", lambda e, q_=q_, c_=c_: e.tensor_tensor(out=tmpa[:], in0=v5(q_)[:, :, :, 1, :], in1=sinb(c_), op=ALU.mult),
                      reads=[q_.b, c_.b], writes=[tmpa.b])
                kk.op("pool", lambda e, q_=q_, c_=c_: e.tensor_tensor(out=tmpb[:], in0=v5(q_)[:, :, :, 0, :], in1=sinb(c_), op=ALU.mult),
                      reads=[q_.b, c_.b], writes=[tmpb.b])
                kk.op("dve", lambda e, q_=q_, r_=r_, c_=c_: e.tensor_tensor(out=v5(r_)[:, :, :, 0, :], in0=v5(q_)[:, :, :, 0, :], in1=cosb(c_), op=ALU.mult),
                      reads=[q_.b, c_.b], writes=[r_.b])
                kk.op("pool", lambda e, q_=q_, r_=r_, c_=c_: e.tensor_tensor(out=v5(r_)[:, :, :, 1, :], in0=v5(q_)[:, :, :, 1, :], in1=cosb(c_), op=ALU.mult),
                      reads=[q_.b, c_.b], writes=[r_.b])
                kk.op("dve", lambda e, r_=r_: e.tensor_tensor(out=v5(r_)[:, :, :, 0, :], in0=v5(r_)[:, :, :, 0, :], in1=tmpa[:], op=ALU.subtract),
                      reads=[r_.b, tmpa.b], writes=[r_.b])
                kk.op("dve", lambda e, r_=r_: e.tensor_tensor(out=v5(r_)[:, :, :, 1, :], in0=v5(r_)[:, :, :, 1, :], in1=tmpb[:], op=ALU.add),
                      reads=[r_.b, tmpb.b], writes=[r_.b])
                kk.op("act", lambda e, r_=r_, qb_=qb_: e.activation(out=qb_[:], in_=r_[:].rearrange("p a b -> p (a b)"), func=AF.Identity),
                      reads=[r_.b], writes=[qb_.b])
                for j in range(3):
                    kk.op("pe", lambda e, qb_=qb_, j=j: e.transpose(out=ptr[:, j, :], in_=qb_[:, j * 128:(j + 1) * 128], identity=ident[:]),
                          reads=[qb_.b, ident.b], writes=[ptr.b])
                kk.op("dve", lambda e, qkTb=qkTb, t4=t4: e.tensor_copy(out=qkTb[:, :, t4 * 128:(t4 + 1) * 128], in_=ptr[:]),
                      reads=[ptr.b], writes=[qkTb.b])
                b_, t_ = bgt[s], t8[s]
                kk.op("dve", lambda e, t_=t_, pb_=pb_: e.tensor_tensor(out=t_[:, 8:16], in0=pb_[:, 8:16], in1=dtb[:], op=ALU.add),
                      reads=[pb_.b, dtb.b], writes=[t_.b])
                kk.op("act", lambda e, t_=t_, pb_=pb_: e.activation(out=t_[:, 0:8], in_=pb_[:, 0:8], func=AF.Exp, scale=-1.0),
                      reads=[pb_.b], writes=[t_.b])
                kk.op("act", lambda e, t_=t_: e.activation(out=t_[:, 8:16], in_=t_[:, 8:16], func=AF.Exp),
                      reads=[t_.b], writes=[t_.b])
                kk.op("act", lambda e, t_=t_: e.activation(out=t_[:, 8:16], in_=t_[:, 8:16], func=AF.Ln, bias=1.0),
                      reads=[t_.b], writes=[t_.b])
                kk.op("dve", lambda e, t_=t_: e.tensor_scalar(out=t_[:, 0:8], in0=t_[:, 0:8], scalar1=1.0, scalar2=1.0, op0=ALU.add, op1=ALU.mult),
                      reads=[t_.b], writes=[t_.b])
                kk.op("dve", lambda e, t_=t_, b_=b_: e.reciprocal(out=b_[:, 0:8], in_=t_[:, 0:8]), reads=[t_.b], writes=[b_.b])
                kk.op("dve", lambda e, t_=t_, b_=b_: e.tensor_tensor(out=b_[:, 8:16], in0=t_[:, 8:16], in1=nA[:], op=ALU.mult),
                      reads=[t_.b, nA.b], writes=[b_.b])
                kk.op("pool", lambda e, b_=b_, r0=r0: e.dma_start(out=dr[f"bg{si}"][r0:r0 + 128, :], in_=b_[:]),
                      reads=[b_.b], dma=f"st{s}")
            c0 = tb * 512
            kk.op("pool", lambda e, qkTb=qkTb, c0=c0: e.dma_start(
                out=dr[f"qT{si}"][:, c0:c0 + 512].rearrange("(j p) s -> p j s", p=128), in_=qkTb[:, 0:2, :]),
                reads=[qkTb.b], dma=f"sq{tb % 2}")
            kk.op("pool", lambda e, qkTb=qkTb, c0=c0: e.dma_start(out=dr[f"kT{si}"][:, c0:c0 + 512], in_=qkTb[:, 2, :]),
                  reads=[qkTb.b], dma=f"sq{tb % 2}")
            stg = stage[tb % 2]
            for g in range(24):
                pa = pacc[g % 3]
                for kc in range(8):
                    kk.op("pe", lambda e, pa=pa, g=g, kc=kc, hTb=hTb: e.matmul(
                        pa[:], w[:, kc, g * 128:(g + 1) * 128], hTb[:, kc, :], start=(kc == 0), stop=(kc == 7)),
                        reads=[w.b, hTb.b], writes=[pa.b])
                if ev % 2 == 0:
                    kk.op("act", lambda e, pa=pa, g=g, stg=stg: e.activation(out=stg[:, g, :], in_=pa[:], func=AF.Identity),
                          reads=[pa.b], writes=[stg.b])
                else:
                    kk.op("dve", lambda e, pa=pa, g=g, stg=stg: e.tensor_copy(out=stg[:, g, :], in_=pa[:]),
                          reads=[pa.b], writes=[stg.b])
                ev += 1
            for h2 in range(2):
                kk.op("pool", lambda e, stg=stg, c0=c0, h2=h2: e.dma_start(
                    out=projT[h2 * 1536:(h2 + 1) * 1536, c0:c0 + 512].rearrange("(g p) s -> p g s", p=128),
                    in_=stg[:, h2 * 12:(h2 + 1) * 12, :]), reads=[stg.b], dma=f"sp{tb % 2}")


_DYN = {}


def dyn(e, part, base, n):
    if part is None:
        return slice(base, base + n)
    npart, div = part
    key = (id(e), div, npart)
    if key not in _DYN:
        _DYN[key] = e.snap((e.partition_id() // div) * npart)
    return bass.ds(_DYN[key] + base, n)


def phase_part(nc, kk, dr, si, S, xin, part):
    npart, div = part
    dummy = Buf("partcopy")

    def off(e):
        return e.snap((e.partition_id() // div) * npart)

    if si == 0:
        qe, xe, me = "sp", "sp", "pool"
    else:
        qe, xe, me = "pool", "act", "pool"
    kk.op(qe, lambda e: e.dma_start(out=dr[f"qTp{si}"][:, :], in_=dr[f"qT{si}"][:, bass.ds(off(e), npart)]), writes=[dummy], dma="c0")
    kk.op(qe, lambda e: e.dma_start(out=dr[f"zTp{si}"][:, :], in_=dr[f"projT{si}"][G_CZ * 128:(G_CZ + 2) * 128, bass.ds(off(e), npart)]),
          writes=[dummy], dma="c0")
    kk.op(me, lambda e: e.dma_start(out=dr[f"mixTp{si}"][0:512, :], in_=dr[f"mixT{si}"][0:512, bass.ds(off(e), npart)]), writes=[dummy], dma="c0")
    kk.op(me, lambda e: e.dma_start(out=dr[f"mixTp{si}"][768:1024, :], in_=dr[f"mixT{si}"][768:1024, bass.ds(off(e), npart)]),
          writes=[dummy], dma="c0")
    step = 1024
    for r in range(0, npart, step):
        n = min(step, npart - r)
        kk.op(xe, lambda e, r=r, n=n: e.dma_start(out=dr[f"xp{si}"][r:r + n, :], in_=xin[bass.ds(off(e) + r, n), :]), writes=[dummy], dma="c0")


def phase_out(nc, kk, dr, l, si, S, xin, yout, part=None):
    with ExitStack() as st:
        ntok = S if part is None else part[0]
        if part is not None:
            xin = dr[f"xp{si}"]
        wo = load_w_bf16(nc, kk, st, "w_out_sb", dr["w_out"][l], D)
        NS = 2
        mt = [sb(nc, st, f"mt{i}", [128, 8, 128], BF16) for i in range(NS)]
        xt = [sb(nc, st, f"xo{i}", [128, D], F32) for i in range(NS)]
        yt = [sb(nc, st, f"yo{i}", [128, D], F32) for i in range(NS)]
        po = [ps(nc, st, f"po{i}", [128, 512], F32) for i in range(4)]
        mixT = dr[f"mixT{si}"] if part is None else dr[f"mixTp{si}"]
        for ti in range(ntok // 128):
            s = ti % NS
            r0 = ti * 128
            m_, x_, y_ = mt[s], xt[s], yt[s]
            kk.op("sp", lambda e, m_=m_, r0=r0: e.dma_start(out=m_[:], in_=mixT[:, r0:r0 + 128].rearrange("(k p) s -> p k s", p=128)),
                  writes=[m_.b], dma=f"x{s}")
            kk.op("sp", lambda e, x_=x_, r0=r0: e.dma_start(out=x_[:], in_=xin[r0:r0 + 128, :]), writes=[x_.b], dma=f"x{s}")
            for g in range(2):
                p_ = po[(ti * 2 + g) % 4]
                for kc in range(8):
                    kk.op("pe", lambda e, p_=p_, m_=m_, kc=kc, g=g: e.matmul(
                        p_[:], m_[:, kc, :], wo[:, kc, g * 512:(g + 1) * 512], start=(kc == 0), stop=(kc == 7)),
                        reads=[m_.b, wo.b], writes=[p_.b])
                kk.op("dve", lambda e, p_=p_, x_=x_, y_=y_, g=g: e.tensor_tensor(
                    out=y_[:, g * 512:(g + 1) * 512], in0=p_[:], in1=x_[:, g * 512:(g + 1) * 512], op=ALU.add),
                    reads=[p_.b, x_.b], writes=[y_.b])
            kk.op("pool", lambda e, y_=y_, r0=r0: e.dma_start(out=yout[r0:r0 + 128, :], in_=y_[:]), reads=[y_.b], dma=f"st{s}")


def phase_attn(nc, kk, dr, l, si, S, ones_f, part=None):
    with ExitStack() as st:
        nkt = S // 128
        KT = sb(nc, st, "KT", [128, S], BF16)
        VA = sb(nc, st, "VA", [128, nkt, 130], BF16)
        for c in range(0, S, 2048):
            ce = min(c + 2048, S)
            kk.op("sp", lambda e, c=c, ce=ce: e.dma_start(out=KT[:, c:ce], in_=dr[f"kT{si}"][:, c:ce]), writes=[KT.b], dma="c0")
        for c in range(0, nkt, 16):
            ce = min(c + 16, nkt)
            kk.op("sp", lambda e, c=c, ce=ce: e.dma_start(out=VA[:, c:ce, :],
                                                          in_=dr[f"va{si}"][c * 128:ce * 128, :].rearrange("(t p) c -> p t c", p=128)),
                  writes=[VA.b], dma="c0")
        qt = [sb(nc, st, f"qt{i}", [128, 2, 512], BF16) for i in range(2)]
        zt = [sb(nc, st, f"zt{i}", [64, 4, 512], BF16) for i in range(2)]
        pss = [ps(nc, st, f"pss{i}", [128, 2, 512], F32) for i in range(2)]
        pex = [sb(nc, st, f"pex{i}", [128, 2, 512], BF16) for i in range(2)]
        pov = [ps(nc, st, f"pov{i}", [128, 512], F32) for i in range(2)]
        pbc = ps(nc, st, "pbc", [64, 512], F32)
        osb = [sb(nc, st, f"osb{i}", [128, 512], F32) for i in range(2)]
        rc = [sb(nc, st, f"rc{i}", [128, 512], F32) for i in range(2)]
        og = [sb(nc, st, f"og{i}", [64, 4, 512], BF16) for i in range(2)]
        it = 0
        nq = S if part is None else part[0]
        qsrc = dr[f"qT{si}"] if part is None else dr[f"qTp{si}"]
        zsrc = dr[f"projT{si}"][G_CZ * 128:(G_CZ + 2) * 128, :] if part is None else dr[f"zTp{si}"]
        mixdst = dr[f"mixT{si}"] if part is None else dr[f"mixTp{si}"]
        for qb in range(nq // 512):
            c0 = qb * 512
            q_ = qt[qb % 2]
            z_ = zt[qb % 2]
            og_ = og[qb % 2]
            for h in range(4):
                kv = h // 2
                kk.op("sp", lambda e, q_=q_, h=h, kv=kv, c0=c0: e.dma_start(
                    out=q_[kv * 64:(kv + 1) * 64, h % 2, :], in_=qsrc[h * 64:(h + 1) * 64, c0:c0 + 512]),
                    writes=[q_.b], dma=f"x{qb % 2}")
            kk.op("sp", lambda e, z_=z_, c0=c0: e.dma_start(
                out=z_[:], in_=zsrc[:, c0:c0 + 512].rearrange("(h p) s -> p h s", p=64)),
                writes=[z_.b], dma=f"x{qb % 2}")
            kk.op("act", lambda e, z_=z_: e.activation(out=z_[:], in_=z_[:], func=AF.Silu), reads=[z_.b], writes=[z_.b])
            for h in range(4):
                kv = h // 2
                po_ = pov[h % 2]
                npair = nkt // 2
                slots = []
                for kp in range(npair):
                    slots.append((pss[it % 2], pex[it % 2]))
                    it += 1

                def emit_qk(kp):
                    ps_ = slots[kp][0]
                    for j in range(2):
                        kt = kp * 2 + j
                        kk.op("pe", lambda e, ps_=ps_, j=j, kt=kt, kv=kv, q_=q_, h=h: e.matmul(
                            ps_[:, j, :], KT[kv * 64:(kv + 1) * 64, kt * 128:(kt + 1) * 128], q_[kv * 64:(kv + 1) * 64, h % 2, :],
                            start=True, stop=True), reads=[KT.b, q_.b], writes=[ps_.b])

                emit_qk(0)
                for kp in range(npair):
                    ps_, pe_ = slots[kp]
                    if kp + 1 < npair:
                        emit_qk(kp + 1)
                    kk.op("act", lambda e, ps_=ps_, pe_=pe_: e.activation(out=pe_[:], in_=ps_[:], func=AF.Exp, scale=0.125),
                          reads=[ps_.b], writes=[pe_.b])
                    for j in range(2):
                        kt = kp * 2 + j
                        kk.op("pe", lambda e, po_=po_, pe_=pe_, j=j, kt=kt, kv=kv: e.matmul(
                            po_[0:65, :], VA[:, kt, kv * 65:(kv + 1) * 65], pe_[:, j, :],
                            start=(kt == 0), stop=(kt == nkt - 1)), reads=[VA.b, pe_.b], writes=[po_.b])
                o_ = osb[h % 2]
                r_ = rc[h % 2]
                kk.op("dve", lambda e, o_=o_, po_=po_: e.tensor_copy(out=o_[0:65, :], in_=po_[0:65, :]), reads=[po_.b], writes=[o_.b])
                kk.op("dve", lambda e, o_=o_, r_=r_: e.reciprocal(out=r_[64:65, :], in_=o_[64:65, :]), reads=[o_.b], writes=[r_.b])
                kk.op("pe", lambda e, r_=r_: e.matmul(pbc[:], ones_f[64:65, 0:64], r_[64:65, :], start=True, stop=True),
                      reads=[r_.b, ones_f.b], writes=[pbc.b])
                kk.op("dve", lambda e, o_=o_: e.tensor_tensor(out=o_[0:64, :], in0=o_[0:64, :], in1=pbc[:], op=ALU.mult),
                      reads=[o_.b, pbc.b], writes=[o_.b])
                kk.op("dve", lambda e, o_=o_, og_=og_, h=h, z_=z_: e.tensor_tensor(out=og_[:, h, :], in0=o_[0:64, :], in1=z_[:, h, :], op=ALU.mult),
                      reads=[o_.b, z_.b], writes=[og_.b])
            kk.op("pool", lambda e, og_=og_, c0=c0: e.dma_start(
                out=mixdst[512:768, c0:c0 + 512].rearrange("(h p) s -> p h s", p=64), in_=og_[:]),
                reads=[og_.b], dma=f"st{qb % 2}")


def phase_fm(nc, kk, dr, l, si, S, ident):
    with ExitStack() as st:
        projT = dr[f"projT{si}"]
        swf = sb(nc, st, "swf", [128, 4, 128], F32)
        sw = sb(nc, st, "sw", [128, 4, 128], BF16)
        sbT = sb(nc, st, "sbT", [128, 2, 128], F32)
        cw = sb(nc, st, "cw", [128, 6, 5], F32)
        pwf = sb(nc, st, "pwf", [128, 2, 128], F32)
        pw = sb(nc, st, "pw", [128, 2, 128], BF16)
        psc = sb(nc, st, "psc", [128, 2], F32)
        kk.op("sp", lambda e: e.dma_start(out=swf[:], in_=dr["sgu_wT"][l]), writes=[swf.b], dma="c0")
        kk.op("sp", lambda e: e.dma_start(out=sbT[:], in_=dr["sgu_bT"][l]), writes=[sbT.b], dma="c0")
        kk.op("sp", lambda e: e.dma_start(out=cw[:], in_=dr["conv_w"][l]), writes=[cw.b], dma="c0")
        kk.op("sp", lambda e: e.dma_start(out=pwf[:], in_=dr["pool_w"][l]), writes=[pwf.b], dma="c0")
        kk.op("sp", lambda e: e.dma_start(out=psc[:], in_=dr["pool_s"][l]), writes=[psc.b], dma="c0")
        kk.op("dve", lambda e: e.tensor_copy(out=sw[:], in_=swf[:]), reads=[swf.b], writes=[sw.b])
        kk.op("dve", lambda e: e.tensor_copy(out=pw[:], in_=pwf[:]), reads=[pwf.b], writes=[pw.b])

        NS = 2
        av = [sb(nc, st, f"av{i}", [128, 6, 512], BF16) for i in range(NS)]
        dxz = [sb(nc, st, f"dxz{i}", [128, 4, 528], BF16) for i in range(NS)]
        icn = [sb(nc, st, f"icn{i}", [128, 2, 512], F32) for i in range(NS)]
        bx = [sb(nc, st, f"bx{i}", [128, 6, 516], BF16) for i in range(NS)]
        pvt = ps(nc, st, "pvt", [128, 2, 128], BF16)
        vtok = sb(nc, st, "vtok", [128, 4, 64], F32)
        vsq = sb(nc, st, "vsq", [128, 4, 64], F32)
        vss = sb(nc, st, "vss", [128, 4], F32)
        vnm = [sb(nc, st, f"vnm{i}", [128, 4, 128], BF16) for i in range(2)]
        for v_ in vnm:
            kk.op("pool", lambda e, v_=v_: e.memset(v_[:], 0.0), writes=[v_.b])
        pm = [ps(nc, st, f"pm{i}", [128, 2, 512], F32) for i in range(1)]
        ta = sb(nc, st, "ta", [128, 2, 512], F32)
        sz = sb(nc, st, "sz", [128, 2, 512], BF16)
        ma = [sb(nc, st, f"ma{i}", [128, 2, 512], BF16) for i in range(NS)]
        ss = sb(nc, st, "ss", [128, 2, 528], F32)
        s2 = sb(nc, st, "s2", [128, 2, 528], F32)
        s4 = sb(nc, st, "s4", [128, 2, 528], F32)
        s8 = sb(nc, st, "s8", [128, 528], F32)
        dfb = sb(nc, st, "dfb", [128, 2, 512], BF16)
        ppl = ps(nc, st, "ppl", [128, 2, 512], F32)
        md = [sb(nc, st, f"md{i}", [128, 2, 512], BF16) for i in range(NS)]
        szd = sb(nc, st, "szd", [128, 2, 512], BF16)
        cacc = sb(nc, st, "cacc", [128, 512], F32)
        cvo = [sb(nc, st, f"cvo{i}", [128, 6, 512], BF16) for i in range(NS)]

        nblk = S // 512
        for tb in range(nblk):
            s = tb % NS
            c0 = tb * 512
            a_, d_, i_, b_ = av[s], dxz[s], icn[s], bx[s]
            kk.op("sp", lambda e, a_=a_, c0=c0: e.dma_start(out=a_[:], in_=projT[0:768, c0:c0 + 512].rearrange("(g p) s -> p g s", p=128)),
                  writes=[a_.b], dma=f"x{s}")
            lo = max(c0 - 8, 0)
            hi = min(c0 + 520, S)
            if lo > c0 - 8:
                kk.op("pool", lambda e, d_=d_: e.memset(d_[:, 0:2, 0:8], 0.0), writes=[d_.b])
            if hi < c0 + 520:
                kk.op("pool", lambda e, d_=d_: e.memset(d_[:, 0:2, 520:528], 0.0), writes=[d_.b])
            kk.op("sp", lambda e, d_=d_, lo=lo, hi=hi, c0=c0: e.dma_start(
                out=d_[:, 0:2, lo - (c0 - 8):hi - (c0 - 8)], in_=projT[G_DX * 128:(G_DX + 2) * 128, lo:hi].rearrange("(g p) s -> p g s", p=128)),
                writes=[d_.b], dma=f"x{s}")
            kk.op("sp", lambda e, d_=d_, c0=c0: e.dma_start(
                out=d_[:, 2:4, 0:512], in_=projT[G_DZ * 128:(G_DZ + 2) * 128, c0:c0 + 512].rearrange("(g p) s -> p g s", p=128)),
                writes=[d_.b], dma=f"x{s}")
            kk.op("sp", lambda e, i_=i_, c0=c0: e.dma_start(out=i_[:], in_=dr[f"icnt{si}"][:, :, c0:c0 + 512]), writes=[i_.b], dma=f"x{s}")
            lo2 = max(c0 - 2, 0)
            hi2 = min(c0 + 514, S)
            if lo2 > c0 - 2:
                kk.op("pool", lambda e, b_=b_: e.memset(b_[:, :, 0:2], 0.0), writes=[b_.b])
            if hi2 < c0 + 514:
                kk.op("pool", lambda e, b_=b_: e.memset(b_[:, :, 514:516], 0.0), writes=[b_.b])
            kk.op("sp", lambda e, b_=b_, lo2=lo2, hi2=hi2, c0=c0: e.dma_start(
                out=b_[:, :, lo2 - (c0 - 2):hi2 - (c0 - 2)], in_=projT[G_BQ * 128:(G_BQ + 6) * 128, lo2:hi2].rearrange("(g p) s -> p g s", p=128)),
                writes=[b_.b], dma=f"x{s}")

            pm_ = pm[0]
            for ch in range(4):
                vn_ = vnm[ch % 2]
                for j in range(2):
                    kk.op("pe", lambda e, a_=a_, j=j, ch=ch: e.transpose(out=pvt[:, j, :], in_=a_[:, 2 + j, ch * 128:(ch + 1) * 128], identity=ident[:]),
                          reads=[a_.b, ident.b], writes=[pvt.b])
                kk.op("dve", lambda e: e.tensor_copy(out=vtok[:].rearrange("p a b -> p (a b)"), in_=pvt[:].rearrange("p a b -> p (a b)")),
                      reads=[pvt.b], writes=[vtok.b])
                kk.op("dve", lambda e: e.tensor_tensor(out=vsq[:], in0=vtok[:], in1=vtok[:], op=ALU.mult), reads=[vtok.b], writes=[vsq.b])
                kk.op("dve", lambda e: e.tensor_reduce(out=vss[:], in_=vsq[:], axis=AX.X, op=ALU.add), reads=[vsq.b], writes=[vss.b])
                kk.op("dve", lambda e: e.tensor_scalar(out=vss[:], in0=vss[:], scalar1=1.0 / 64, scalar2=EPS, op0=ALU.mult, op1=ALU.add),
                      reads=[vss.b], writes=[vss.b])
                kk.op("act", lambda e: e.activation(out=vss[:], in_=vss[:], func=AF.Ln), reads=[vss.b], writes=[vss.b])
                kk.op("act", lambda e: e.activation(out=vss[:], in_=vss[:], func=AF.Exp, scale=-0.5), reads=[vss.b], writes=[vss.b])
                for h in range(4):
                    kk.op("dve", lambda e, vn_=vn_, h=h: e.tensor_scalar(
                        out=vn_[:, h, (h % 2) * 64:(h % 2) * 64 + 64], in0=vtok[:, h, :], scalar1=vss[:, h:h + 1], scalar2=1.0,
                        op0=ALU.mult, op1=ALU.mult), reads=[vtok.b, vss.b], writes=[vn_.b])
                for h in range(4):
                    kk.op("pe", lambda e, vn_=vn_, h=h, ch=ch, pm_=pm_: e.matmul(
                        pm_[:, h // 2, ch * 128:(ch + 1) * 128], vn_[:, h, :], sw[:, h, :], start=(h % 2 == 0), stop=(h % 2 == 1)),
                        reads=[vn_.b, sw.b], writes=[pm_.b])
            m_ = ma[s]
            kk.op("dve", lambda e, pm_=pm_: e.tensor_tensor(
                out=ta[:].rearrange("p a (c i) -> p a c i", c=4), in0=pm_[:].rearrange("p a (c i) -> p a c i", c=4),
                in1=sbT[:].unsqueeze(2).to_broadcast([128, 2, 4, 128]), op=ALU.add), reads=[pm_.b, sbT.b], writes=[ta.b])
            kk.op("act", lambda e, a_=a_: e.activation(out=sz[:], in_=a_[:, 4:6, :], func=AF.Silu), reads=[a_.b], writes=[sz.b])
            kk.op("pool", lambda e, a_=a_: e.tensor_tensor(out=ta[:], in0=ta[:], in1=a_[:, 0:2, :], op=ALU.mult), reads=[ta.b, a_.b], writes=[ta.b])
            kk.op("dve", lambda e, m_=m_: e.tensor_tensor(out=m_[:], in0=ta[:], in1=sz[:], op=ALU.mult), reads=[ta.b, sz.b], writes=[m_.b])
            kk.op("pool", lambda e, m_=m_, c0=c0: e.dma_start(out=dr[f"mixT{si}"][0:256, c0:c0 + 512].rearrange("(g p) s -> p g s", p=128), in_=m_[:]),
                  reads=[m_.b], dma=f"st{s}")

            X = d_
            kk.op("dve", lambda e, X=X: e.tensor_tensor(out=s2[:, :, 1:528], in0=X[:, 0:2, 0:527], in1=X[:, 0:2, 1:528], op=ALU.add),
                  reads=[X.b], writes=[s2.b])
            kk.op("pool", lambda e: e.tensor_tensor(out=s4[:, :, 2:527], in0=s2[:, :, 1:526], in1=s2[:, :, 3:528], op=ALU.add),
                  reads=[s2.b], writes=[s4.b])
            kk.op("dve", lambda e: e.tensor_tensor(out=s8[:, 4:525], in0=s4[:, 1, 2:523], in1=s4[:, 1, 6:527], op=ALU.add),
                  reads=[s4.b], writes=[s8.b])
            kk.op("pool", lambda e: e.tensor_copy(out=ss[0:64, 0, 8:520], in_=s2[0:64, 0, 8:520]), reads=[s2.b], writes=[ss.b])
            kk.op("pool", lambda e: e.tensor_copy(out=ss[64:128, 0, 8:520], in_=s4[64:128, 0, 8:520]), reads=[s4.b], writes=[ss.b])
            kk.op("dve", lambda e: e.tensor_copy(out=ss[0:64, 1, 8:520], in_=s8[0:64, 8:520]), reads=[s8.b], writes=[ss.b])
            kk.op("dve", lambda e: e.tensor_tensor(out=ss[64:128, 1, 8:520], in0=s8[64:128, 4:516], in1=s8[64:128, 12:524], op=ALU.add),
                  reads=[s8.b], writes=[ss.b])
            kk.op("dve", lambda e, i_=i_: e.tensor_tensor(out=ss[:, :, 8:520], in0=ss[:, :, 8:520], in1=i_[:], op=ALU.mult),
                  reads=[ss.b, i_.b], writes=[ss.b])
            kk.op("dve", lambda e, X=X: e.tensor_tensor(out=dfb[:], in0=ss[:, :, 8:520], in1=X[:, 0:2, 8:520], op=ALU.subtract),
                  reads=[ss.b, X.b], writes=[dfb.b])
            for ch in range(2):
                kk.op("pe", lambda e, ch=ch: e.matmul(ppl[:, ch, :], pw[:, ch, :], dfb[:, ch, :], start=True, stop=True),
                      reads=[pw.b, dfb.b], writes=[ppl.b])
            kk.op("act", lambda e, X=X: e.activation(out=szd[:], in_=X[:, 2:4, 0:512], func=AF.Silu), reads=[X.b], writes=[szd.b])
            o_ = md[s]
            for ch in range(2):
                kk.op("dve", lambda e, ch=ch, o_=o_: e.scalar_tensor_tensor(
                    out=o_[:, ch, :], in0=ppl[:, ch, :], scalar=psc[:, ch:ch + 1], in1=szd[:, ch, :], op0=ALU.mult, op1=ALU.mult),
                    reads=[ppl.b, psc.b, szd.b], writes=[o_.b])
            kk.op("pool", lambda e, o_=o_, c0=c0: e.dma_start(out=dr[f"mixT{si}"][768:1024, c0:c0 + 512].rearrange("(g p) s -> p g s", p=128), in_=o_[:]),
                  reads=[o_.b], dma=f"st{s}")

            co = cvo[s]
            for ch in range(6):
                kk.op("dve", lambda e, b_=b_, ch=ch: e.tensor_scalar(out=cacc[:], in0=b_[:, ch, 0:512], scalar1=cw[:, ch, 0:1], scalar2=1.0,
                                                                    op0=ALU.mult, op1=ALU.mult), reads=[b_.b, cw.b], writes=[cacc.b])
                for i in range(1, 5):
                    kk.op("dve", lambda e, b_=b_, ch=ch, i=i: e.scalar_tensor_tensor(
                        out=cacc[:], in0=b_[:, ch, i:i + 512], scalar=cw[:, ch, i:i + 1], in1=cacc[:], op0=ALU.mult, op1=ALU.add),
                        reads=[b_.b, cw.b, cacc.b], writes=[cacc.b])
                kk.op("act", lambda e, co=co, ch=ch: e.activation(out=co[:, ch, :], in_=cacc[:], func=AF.Silu), reads=[cacc.b], writes=[co.b])
            kk.op("pool", lambda e, co=co, c0=c0: e.dma_start(out=dr[f"convT{si}"][:, c0:c0 + 512].rearrange("(g p) s -> p g s", p=128), in_=co[:]),
                  reads=[co.b], dma=f"st{s}")


def bc(ap, shape, axis):
    return ap.unsqueeze(axis).to_broadcast(shape)


def phase_dn(nc, kk, dr, l, si, S, ident, ident_f, ones_f):
    with ExitStack() as st:
        NCH = S // 128
        tri = [sb(nc, st, f"tri{d}", [128, 128], F32) for d in range(2)]
        nmI = [sb(nc, st, f"nmI{d}", [128, 128], F32) for d in range(2)]
        nmS = [sb(nc, st, f"nmS{d}", [128, 128], F32) for d in range(2)]
        offd = sb(nc, st, "offd", [128, 128], F32)
        dnw = sb(nc, st, "dnw", [128, 4, 64], F32)
        kk.op("sp", lambda e: e.dma_start(out=tri[0][:], in_=dr["triU"][:, :]), writes=[tri[0].b], dma="c0")
        kk.op("sp", lambda e: e.dma_start(out=tri[1][:], in_=dr["triL"][:, :]), writes=[tri[1].b], dma="c0")
        kk.op("sp", lambda e: e.dma_start(out=dnw[:], in_=dr["dn_w"][l]), writes=[dnw.b], dma="c0")
        kk.op("dve", lambda e: e.tensor_scalar(out=offd[:], in0=ident_f[:], scalar1=-1.0, scalar2=1.0, op0=ALU.mult, op1=ALU.add),
              reads=[ident_f.b], writes=[offd.b])
        for d in range(2):
            kk.op("dve", lambda e, d=d: e.tensor_scalar(out=nmI[d][:], in0=tri[d][:], scalar1=-1.0, scalar2=1e30, op0=ALU.add, op1=ALU.mult),
                  reads=[tri[d].b], writes=[nmI[d].b])
            kk.op("dve", lambda e, d=d: e.tensor_tensor(out=nmS[d][:], in0=tri[d][:], in1=offd[:], op=ALU.mult),
                  reads=[tri[d].b, offd.b], writes=[nmS[d].b])
            kk.op("dve", lambda e, d=d: e.tensor_scalar(out=nmS[d][:], in0=nmS[d][:], scalar1=-1.0, scalar2=1e30, op0=ALU.add, op1=ALU.mult),
                  reads=[nmS[d].b], writes=[nmS[d].b])

        bkb = ps(nc, st, "bkb", [128, 1024], BF16)
        bk1 = ps(nc, st, "bk1", [128, 512], F32)
        bk2 = ps(nc, st, "bk2", [128, 512], F32)
        bk3 = ps(nc, st, "bk3", [128, 512], F32)
        bk4 = ps(nc, st, "bk4", [128, 512], F32)
        bk5 = ps(nc, st, "bk5", [128, 512], F32)
        bka = ps(nc, st, "bka", [128, 1024], F32)

        def v4(ap, n=4):
            return ap.rearrange("p (c i) -> p c i", c=n)

        cT = [sb(nc, st, f"cT{i}", [128, 6, 128], BF16) for i in range(2)]
        bgc = [sb(nc, st, f"bgc{i}", [128, 16], F32) for i in range(2)]
        ofl = [sb(nc, st, f"ofl{i}", [128, 256], F32) for i in range(2)]
        zb = [sb(nc, st, f"zb{i}", [128, 2, 128], BF16) for i in range(2)]
        tok = sb(nc, st, "tok", [128, 768], F32)
        sq = sb(nc, st, "dsq", [128, 512], F32)
        rs = sb(nc, st, "drs", [128, 8], F32)
        qkn = sb(nc, st, "qkn", [128, 8, 64], BF16)
        qkn32 = sb(nc, st, "qkn32", [128, 8, 64], F32)
        ek32 = sb(nc, st, "ek32", [128, 4, 64], F32)
        kd32 = sb(nc, st, "kd32", [128, 4, 64], F32)
        ekT32 = sb(nc, st, "ekT32", [64, 4, 128], F32)
        rb = sb(nc, st, "rb", [128, 4, 64], BF16)
        vnew32 = sb(nc, st, "vnew32", [128, 4, 64], F32)
        vb = sb(nc, st, "vb", [128, 256], BF16)
        qkT = sb(nc, st, "dqkT", [64, 8, 128], BF16)
        gs = sb(nc, st, "gs", [128, 8], F32)
        eg = sb(nc, st, "eg", [128, 4], F32)
        ekd = sb(nc, st, "ekd", [128, 4], F32)
        egl = sb(nc, st, "egl", [128, 4], F32)
        rg = sb(nc, st, "rg", [128, 4, 128], F32)
        dm = sb(nc, st, "dm", [128, 4, 128], F32)
        dmI = sb(nc, st, "dmI", [128, 4, 128], F32)
        dmS = sb(nc, st, "dmS", [128, 4, 128], F32)
        X = sb(nc, st, "X", [128, 4, 128], F32)
        AT = sb(nc, st, "AT", [128, 4, 128], BF16)
        YS = [sb(nc, st, f"YS{i}", [128, 4, 256], F32) for i in range(2)]
        YT = [sb(nc, st, f"YT{i}", [128, 4, 128], F32) for i in range(2)]
        db = sb(nc, st, "db", [128, 4, 128], F32)
        TTb = sb(nc, st, "TTb", [128, 4, 128], BF16)
        usb = sb(nc, st, "usb", [128, 4, 64], F32)
        ek = sb(nc, st, "ek", [128, 4, 64], BF16)
        kd = sb(nc, st, "kd", [128, 4, 64], BF16)
        qd = sb(nc, st, "qd", [128, 4, 64], BF16)
        wT = sb(nc, st, "wT", [64, 4, 128], BF16)
        qdT = sb(nc, st, "qdT", [64, 4, 128], BF16)
        vnew = sb(nc, st, "vnew", [128, 4, 64], BF16)
        Sf = sb(nc, st, "Sf", [64, 4, 64], F32)
        Sb = sb(nc, st, "Sb", [64, 4, 64], BF16)
        osb = [sb(nc, st, f"dosb{i}", [128, 256], F32) for i in range(2)]
        osq = sb(nc, st, "osq", [128, 256], F32)
        oss = sb(nc, st, "oss", [128, 4], F32)
        onb = sb(nc, st, "onb", [128, 256], BF16)
        omx = [sb(nc, st, f"omx{i}", [128, 2, 128], BF16) for i in range(2)]

        for d in range(2):
            if d == 1:
                kk.barrier()
            kk.op("pool", lambda e: e.memset(Sf[:], 0.0), writes=[Sf.b])
            kk.op("pool", lambda e: e.memset(Sb[:], 0.0), writes=[Sb.b])
            order = range(NCH) if d == 0 else range(NCH - 1, -1, -1)
            for n, ci in enumerate(order):
                s = n % 2
                r0 = ci * 128
                c_, g_ = cT[s], bgc[s]
                kk.op("sp", lambda e, c_=c_, r0=r0: e.dma_start(out=c_[:], in_=dr[f"convT{si}"][:, r0:r0 + 128].rearrange("(g p) s -> p g s", p=128)),
                      writes=[c_.b], dma=f"x{s}")
                kk.op("sp", lambda e, g_=g_, r0=r0: e.dma_start(out=g_[:], in_=dr[f"bg{si}"][r0:r0 + 128, :]), writes=[g_.b], dma=f"x{s}")
                if d == 1:
                    kk.op("sp", lambda e, o_=ofl[s], r0=r0: e.dma_start(out=o_[:], in_=dr[f"of{si}"][r0:r0 + 128, :]), writes=[ofl[s].b], dma=f"x{s}")
                    kk.op("sp", lambda e, z_=zb[s], r0=r0: e.dma_start(
                        out=z_[:], in_=dr[f"projT{si}"][G_BZ * 128:(G_BZ + 2) * 128, r0:r0 + 128].rearrange("(g p) s -> p g s", p=128)),
                        writes=[zb[s].b], dma=f"x{s}")
                gd = g_[:, 8 + 4 * d:12 + 4 * d]
                bd = g_[:, 4 * d:4 * d + 4]
                for j in range(6):
                    kk.op("pe", lambda e, c_=c_, j=j: e.transpose(out=bkb[:, j * 128:(j + 1) * 128], in_=c_[:, j, :], identity=ident[:]),
                          reads=[c_.b, ident.b], writes=[bkb.b])
                kk.op("dve", lambda e: e.tensor_copy(out=tok[:], in_=bkb[:, 0:768]), reads=[bkb.b], writes=[tok.b])
                kk.op("dve", lambda e: e.tensor_tensor(out=sq[:], in0=tok[:, 0:512], in1=tok[:, 0:512], op=ALU.mult), reads=[tok.b], writes=[sq.b])
                kk.op("dve", lambda e: e.tensor_reduce(out=rs[:], in_=v4(sq[:], 8), axis=AX.X, op=ALU.add), reads=[sq.b], writes=[rs.b])
                kk.op("dve", lambda e: e.tensor_scalar(out=rs[:], in0=rs[:], scalar1=EPS, scalar2=1.0, op0=ALU.add, op1=ALU.mult),
                      reads=[rs.b], writes=[rs.b])
                kk.op("act", lambda e: e.activation(out=rs[:], in_=rs[:], func=AF.Ln), reads=[rs.b], writes=[rs.b])
                kk.op("act", lambda e: e.activation(out=rs[:], in_=rs[:], func=AF.Exp, scale=-0.5), reads=[rs.b], writes=[rs.b])
                kk.op("dve", lambda e: e.tensor_scalar(out=rs[:, 0:4], in0=rs[:, 0:4], scalar1=0.125, scalar2=1.0, op0=ALU.mult, op1=ALU.mult),
                      reads=[rs.b], writes=[rs.b])
                kk.op("dve", lambda e: e.tensor_tensor(out=qkn32[:], in0=v4(tok[:, 0:512], 8), in1=bc(rs[:], [128, 8, 64], 2), op=ALU.mult),
                      reads=[tok.b, rs.b], writes=[qkn32.b])
                kk.op("pool", lambda e: e.tensor_copy(out=qkn[:], in_=qkn32[:]), reads=[qkn32.b], writes=[qkn.b])
                for j in range(8):
                    kk.op("pe", lambda e, j=j: e.transpose(out=bkb[0:64, j * 128:(j + 1) * 128], in_=qkn[:, j, :], identity=ident[:]),
                          reads=[qkn.b, ident.b], writes=[bkb.b])
                kk.op("act", lambda e: e.activation(out=qkT[:].rearrange("p a b -> p (a b)"), in_=bkb[0:64, 0:1024], func=AF.Identity),
                      reads=[bkb.b], writes=[qkT.b])
                kk.op("pe", lambda e, gd=gd, d=d: e.matmul(bk2[:, 0:4], tri[d][:], gd, start=True, stop=True), reads=[tri[d].b, g_.b], writes=[bk2.b])
                kk.op("pe", lambda e, gd=gd: e.matmul(bk2[:, 4:8], ones_f[:], gd, start=True, stop=True), reads=[ones_f.b, g_.b], writes=[bk2.b])
                kk.op("dve", lambda e: e.tensor_copy(out=gs[:], in_=bk2[:, 0:8]), reads=[bk2.b], writes=[gs.b])
                kk.op("act", lambda e: e.activation(out=eg[:], in_=gs[:, 0:4], func=AF.Exp), reads=[gs.b], writes=[eg.b])
                kk.op("act", lambda e: e.activation(out=egl[:], in_=gs[:, 4:8], func=AF.Exp), reads=[gs.b], writes=[egl.b])
                kk.op("dve", lambda e: e.tensor_tensor(out=ekd[:], in0=gs[:, 4:8], in1=gs[:, 0:4], op=ALU.subtract), reads=[gs.b], writes=[ekd.b])
                kk.op("act", lambda e: e.activation(out=ekd[:], in_=ekd[:], func=AF.Exp), reads=[ekd.b], writes=[ekd.b])
                kk.op("dve", lambda e, gd=gd, d=d: e.tensor_tensor(out=rg[:], in0=bc(tri[d][:], [128, 4, 128], 1), in1=bc(gd, [128, 4, 128], 2), op=ALU.mult),
                      reads=[tri[d].b, g_.b], writes=[rg.b])
                kk.op("pe", lambda e: e.matmul(bk3[:], ones_f[:], rg[:].rearrange("p a b -> p (a b)"), start=True, stop=True),
                      reads=[ones_f.b, rg.b], writes=[bk3.b])
                kk.op("dve", lambda e: e.tensor_tensor(out=dm[:], in0=v4(bk3[:]), in1=bc(gs[:, 0:4], [128, 4, 128], 2), op=ALU.subtract),
                      reads=[bk3.b, gs.b], writes=[dm.b])
                kk.op("dve", lambda e, d=d: e.tensor_tensor(out=dmI[:], in0=dm[:], in1=bc(nmI[d][:], [128, 4, 128], 1), op=ALU.add),
                      reads=[dm.b, nmI[d].b], writes=[dmI.b])
                kk.op("pool", lambda e, d=d: e.tensor_tensor(out=dmS[:], in0=dm[:], in1=bc(nmS[d][:], [128, 4, 128], 1), op=ALU.add),
                      reads=[dm.b, nmS[d].b], writes=[dmS.b])
                kk.op("act", lambda e: e.activation(out=dmI[:], in_=dmI[:], func=AF.Exp), reads=[dmI.b], writes=[dmI.b])
                kk.op("act", lambda e: e.activation(out=dmS[:], in_=dmS[:], func=AF.Exp), reads=[dmS.b], writes=[dmS.b])
                for h in range(4):
                    kTh = qkT[:, 4 + h, :]
                    qTh = qkT[:, h, :]
                    kk.op("pe", lambda e, h=h, kTh=kTh: e.matmul(bk4[:, h * 128:(h + 1) * 128], kTh, kTh, start=True, stop=True),
                          reads=[qkT.b], writes=[bk4.b])
                    kk.op("pe", lambda e, h=h, kTh=kTh, qTh=qTh: e.matmul(bk5[:, h * 128:(h + 1) * 128], kTh, qTh, start=True, stop=True),
                          reads=[qkT.b], writes=[bk5.b])
                for h in range(4):
                    kk.op("dve", lambda e, h=h, bd=bd: e.scalar_tensor_tensor(
                        out=X[:, h, :], in0=bk4[:, h * 128:(h + 1) * 128], scalar=bd[:, h:h + 1], in1=dmS[:, h, :], op0=ALU.mult, op1=ALU.mult),
                        reads=[bk4.b, g_.b, dmS.b], writes=[X.b])
                kk.op("dve", lambda e: e.tensor_tensor(out=AT[:], in0=v4(bk5[:]), in1=dmI[:], op=ALU.mult), reads=[bk5.b, dmI.b], writes=[AT.b])
                pbv = v4(bk4[:])
                for h in range(4):
                    kk.op("pe", lambda e, h=h: e.transpose(out=pbv[:, h, :], in_=X[:, h, :], identity=ident_f[:]),
                          reads=[X.b, ident_f.b], writes=[bk4.b])
                kk.op("dve", lambda e: e.tensor_copy(out=YT[1][:], in_=pbv), reads=[bk4.b], writes=[YT[1].b])
                pa = bka[:].rearrange("p (c i) -> p c i", c=4)
                for h in range(4):
                    kk.op("pe", lambda e, h=h: e.matmul(pa[:, h, 0:128], YT[1][:, h, :], X[:, h, :], start=True, stop=True),
                          reads=[YT[1].b, X.b], writes=[bka.b])
                    kk.op("pe", lambda e, h=h: e.matmul(pbv[:, h, :], X[:, h, :], YT[1][:, h, :], start=True, stop=True),
                          reads=[YT[1].b, X.b], writes=[bk4.b])
                kk.op("dve", lambda e: e.tensor_tensor(out=YS[0][:, :, 128:256], in0=bc(ident_f[:], [128, 4, 128], 1), in1=X[:], op=ALU.subtract),
                      reads=[ident_f.b, X.b], writes=[YS[0].b])
                kk.op("act", lambda e: e.activation(out=YS[0][:, :, 0:128], in_=pa[:, :, 0:128], func=AF.Identity), reads=[bka.b], writes=[YS[0].b])
                kk.op("dve", lambda e: e.tensor_copy(out=YT[0][:], in_=pbv), reads=[bk4.b], writes=[YT[0].b])
                cur = 0
                for k in range(1, 7):
                    ys, yt = YS[cur], YT[cur]
                    nys, nyt = YS[1 - cur], YT[1 - cur]
                    for h in range(4):
                        kk.op("pe", lambda e, h=h, ys=ys, yt=yt: e.matmul(pa[:, h, :], yt[:, h, :], ys[:, h, :], start=True, stop=True),
                              reads=[ys.b, yt.b], writes=[bka.b])
                        if k < 6:
                            kk.op("pe", lambda e, h=h, ys=ys, yt=yt: e.matmul(pbv[:, h, :], ys[:, h, 0:128], yt[:, h, :], start=True, stop=True),
                                  reads=[ys.b, yt.b], writes=[bk4.b])
                    kk.op("dve", lambda e, ys=ys, nys=nys: e.tensor_tensor(out=nys[:, :, 128:256], in0=pa[:, :, 128:256], in1=ys[:, :, 128:256], op=ALU.add),
                          reads=[bka.b, ys.b], writes=[nys.b])
                    if k < 6:
                        kk.op("act", lambda e, nys=nys: e.activation(out=nys[:, :, 0:128], in_=pa[:, :, 0:128], func=AF.Identity),
                              reads=[bka.b], writes=[nys.b])
                        kk.op("dve", lambda e, nyt=nyt: e.tensor_copy(out=nyt[:], in_=pbv), reads=[bk4.b], writes=[nyt.b])
                    cur = 1 - cur
                TT = YS[cur]
                kk.op("pool", lambda e, bd=bd: e.tensor_tensor(out=db[:], in0=bc(ident_f[:], [128, 4, 128], 1), in1=bc(bd, [128, 4, 128], 2), op=ALU.mult),
                      reads=[ident_f.b, g_.b], writes=[db.b])
                kk.op("pe", lambda e: e.matmul(bk3[:], ones_f[:], db[:].rearrange("p a b -> p (a b)"), start=True, stop=True),
                      reads=[ones_f.b, db.b], writes=[bk3.b])
                kk.op("dve", lambda e, TT=TT: e.tensor_tensor(out=TTb[:], in0=v4(bk3[:]), in1=TT[:, :, 128:256], op=ALU.mult),
                      reads=[bk3.b, TT.b], writes=[TTb.b])
                kk.op("dve", lambda e: e.tensor_tensor(out=ek32[:], in0=qkn32[:, 4:8, :], in1=bc(eg[:], [128, 4, 64], 2), op=ALU.mult),
                      reads=[qkn32.b, eg.b], writes=[ek32.b])
                kk.op("pool", lambda e: e.tensor_tensor(out=kd32[:], in0=qkn32[:, 4:8, :], in1=bc(ekd[:], [128, 4, 64], 2), op=ALU.mult),
                      reads=[qkn32.b, ekd.b], writes=[kd32.b])
                kk.op("pool", lambda e: e.tensor_tensor(out=qd[:], in0=qkn32[:, 0:4, :], in1=bc(eg[:], [128, 4, 64], 2), op=ALU.mult),
                      reads=[qkn32.b, eg.b], writes=[qd.b])
                pek = bk5[0:64, :].rearrange("p (c i) -> p c i", c=4)
                for h in range(4):
                    kk.op("pe", lambda e, h=h: e.transpose(out=pek[:, h, :], in_=ek32[:, h, :], identity=ident_f[:]),
                          reads=[ek32.b, ident_f.b], writes=[bk5.b])
                kk.op("dve", lambda e: e.tensor_copy(out=ekT32[:], in_=pek), reads=[bk5.b], writes=[ekT32.b])
                for h in range(4):
                    kk.op("pe", lambda e, h=h: e.transpose(out=bkb[0:64, h * 128:(h + 1) * 128], in_=qd[:, h, :], identity=ident[:]),
                          reads=[qd.b, ident.b], writes=[bkb.b])
                kk.op("dve", lambda e: e.tensor_copy(out=qdT[:].rearrange("p a b -> p (a b)"), in_=bkb[0:64, 0:512]), reads=[bkb.b], writes=[qdT.b])
                pws = bk1[:, 0:256].rearrange("p (c i) -> p c i", c=4)
                pout = bk1[:, 256:512].rearrange("p (c i) -> p c i", c=4)
                pds = bk3[0:64, 0:256].rearrange("p (c i) -> p c i", c=4)
                pv2 = bk2[:, 256:512].rearrange("p (c i) -> p c i", c=4)
                for h in range(4):
                    kk.op("pe", lambda e, h=h: e.matmul(pws[:, h, :], ekT32[:, h, :], Sf[:, h, :], start=True, stop=True),
                          reads=[ekT32.b, Sf.b], writes=[bk1.b])
                kk.op("dve", lambda e: e.tensor_tensor(out=rb[:], in0=v4(tok[:, 512:768]), in1=pws, op=ALU.subtract),
                      reads=[tok.b, bk1.b], writes=[rb.b])
                for h in range(4):
                    kk.op("pe", lambda e, h=h: e.matmul(pv2[:, h, :], TTb[:, h, :], rb[:, h, :], start=True, stop=True),
                          reads=[TTb.b, rb.b], writes=[bk2.b])
                kk.op("dve", lambda e: e.tensor_copy(out=vnew32[:], in_=pv2), reads=[bk2.b], writes=[vnew32.b])
                kk.op("dve", lambda e: e.tensor_copy(out=vnew[:], in_=pv2), reads=[bk2.b], writes=[vnew.b])
                for h in range(4):
                    kk.op("pe", lambda e, h=h: e.matmul(pout[:, h, :], qdT[:, h, :], Sb[:, h, :], start=True, stop=False),
                          reads=[qdT.b, Sb.b], writes=[bk1.b])
                    kk.op("pe", lambda e, h=h: e.matmul(pout[:, h, :], AT[:, h, :], vnew[:, h, :], start=False, stop=True),
                          reads=[AT.b, vnew.b], writes=[bk1.b])
                for h in range(4):
                    kk.op("pe", lambda e, h=h: e.matmul(pds[:, h, :], kd32[:, h, :], vnew32[:, h, :], start=True, stop=True),
                          reads=[kd32.b, vnew32.b], writes=[bk3.b])
                kk.op("dve", lambda e: e.tensor_tensor(out=Sf[:], in0=Sf[:], in1=bc(egl[0:64, :], [64, 4, 64], 2), op=ALU.mult),
                      reads=[Sf.b, egl.b], writes=[Sf.b])
                kk.op("dve", lambda e: e.tensor_tensor(out=Sf[:], in0=Sf[:], in1=pds, op=ALU.add), reads=[Sf.b, bk3.b], writes=[Sf.b])
                kk.op("act", lambda e: e.activation(out=Sb[:], in_=Sf[:], func=AF.Identity), reads=[Sf.b], writes=[Sb.b])
                o_ = osb[s]
                if d == 0:
                    kk.op("dve", lambda e, o_=o_: e.tensor_copy(out=o_[:], in_=bk1[:, 256:512]), reads=[bk1.b], writes=[o_.b])
                    kk.op("pool", lambda e, o_=o_, r0=r0: e.dma_start(out=dr[f"of{si}"][r0:r0 + 128, :], in_=o_[:]), reads=[o_.b], dma=f"st{s}")
                else:
                    kk.op("dve", lambda e, o_=o_, f_=ofl[s]: e.tensor_tensor(out=o_[:], in0=bk1[:, 256:512], in1=f_[:], op=ALU.add),
                          reads=[bk1.b, ofl[s].b], writes=[o_.b])
                    if f"osum{si}" in dr:
                        kk.op("pool", lambda e, o_=o_, r0=r0: e.dma_start(out=dr[f"osum{si}"][r0:r0 + 128, :], in_=o_[:]), reads=[o_.b], dma=f"st{s}")
                    kk.op("pool", lambda e, o_=o_: e.tensor_tensor(out=osq[:], in0=o_[:], in1=o_[:], op=ALU.mult), reads=[o_.b], writes=[osq.b])
                    kk.op("dve", lambda e: e.tensor_reduce(out=oss[:], in_=v4(osq[:]), axis=AX.X, op=ALU.add), reads=[osq.b], writes=[oss.b])
                    kk.op("dve", lambda e: e.tensor_scalar(out=oss[:], in0=oss[:], scalar1=1.0 / 64, scalar2=EPS, op0=ALU.mult, op1=ALU.add),
                          reads=[oss.b], writes=[oss.b])
                    kk.op("act", lambda e: e.activation(out=oss[:], in_=oss[:], func=AF.Ln), reads=[oss.b], writes=[oss.b])
                    kk.op("act", lambda e: e.activation(out=oss[:], in_=oss[:], func=AF.Exp, scale=-0.5), reads=[oss.b], writes=[oss.b])
                    kk.op("dve", lambda e, o_=o_: e.tensor_tensor(out=v4(o_[:]), in0=v4(o_[:]), in1=bc(oss[:], [128, 4, 64], 2), op=ALU.mult),
                          reads=[o_.b, oss.b], writes=[o_.b])
                    kk.op("pool", lambda e, o_=o_: e.tensor_tensor(out=onb[:], in0=o_[:], in1=dnw[:].rearrange("p a b -> p (a b)"), op=ALU.mult),
                          reads=[o_.b, dnw.b], writes=[onb.b])
                    for j in range(2):
                        kk.op("pe", lambda e, j=j: e.transpose(out=bkb[:, j * 128:(j + 1) * 128], in_=onb[:, j * 128:(j + 1) * 128], identity=ident[:]),
                              reads=[onb.b, ident.b], writes=[bkb.b])
                    z_ = zb[s]
                    m_ = omx[s]
                    kk.op("act", lambda e, z_=z_: e.activation(out=z_[:], in_=z_[:], func=AF.Silu), reads=[z_.b], writes=[z_.b])
                    kk.op("dve", lambda e, z_=z_, m_=m_: e.tensor_tensor(out=m_[:], in0=bkb[:, 0:256].rearrange("p (a b) -> p a b", a=2), in1=z_[:], op=ALU.mult),
                          reads=[bkb.b, z_.b], writes=[m_.b])
                    kk.op("pool", lambda e, m_=m_, r0=r0: e.dma_start(
                        out=dr[f"mixT{si}"][256:512, r0:r0 + 128].rearrange("(g p) s -> p g s", p=128), in_=m_[:]), reads=[m_.b], dma=f"st{s}")


POOL_WINDOWS = (2, 4, 8, 16)


def host_layout(seqs, depth, norm_w, w_in, sgu_w, sgu_b, conv_w, a_log, dt_bias, dn_norm_w,
                q_norm_w, k_norm_w, pool_w, pool_scale, w_out):
    f = np.float32
    m = {}
    m["w_in"] = np.ascontiguousarray(np.asarray(w_in, f).reshape(depth, 8, 128, NCOL).transpose(0, 2, 1, 3)[..., COL_PERM])
    m["norm_w"] = np.ascontiguousarray(np.asarray(norm_w, f).reshape(depth, 8, 128).transpose(0, 2, 1))
    m["w_out"] = np.ascontiguousarray(np.asarray(w_out, f).reshape(depth, 8, 128, D).transpose(0, 2, 1, 3))
    qk = np.concatenate([np.repeat(np.asarray(q_norm_w, f)[:, None, :], 4, 1), np.repeat(np.asarray(k_norm_w, f)[:, None, :], 2, 1)], 1)
    m["qkw"] = np.ascontiguousarray(np.broadcast_to(qk[:, None], (depth, 128, 6, 64)))
    m["a_log"] = np.ascontiguousarray(np.broadcast_to(np.asarray(a_log, f).reshape(depth, 1, 8), (depth, 128, 8)))
    m["dt_bias"] = np.ascontiguousarray(np.broadcast_to(np.asarray(dt_bias, f).reshape(depth, 1, 8), (depth, 128, 8)))
    m["ident"] = np.eye(128, dtype=f)
    m["sgu_wT"] = np.ascontiguousarray(np.asarray(sgu_w, f).transpose(0, 3, 1, 2))
    sb_ = np.asarray(sgu_b, f)
    sbT = np.zeros((depth, 128, 2, 128), f)
    for hp in range(2):
        for h2 in range(2):
            sbT[:, h2 * 64:(h2 + 1) * 64, hp, :] = sb_[:, hp * 2 + h2, None, :]
    m["sgu_bT"] = sbT
    m["conv_w"] = np.ascontiguousarray(np.asarray(conv_w, f).reshape(depth, 5, 6, 128).transpose(0, 3, 2, 1))
    pw = np.zeros((depth, 128, 2, 128), f)
    pwi = np.asarray(pool_w, f)
    for ch in range(2):
        for g2 in range(2):
            pw[:, g2 * 64:(g2 + 1) * 64, ch, g2 * 64:(g2 + 1) * 64] = pwi[:, ch * 2 + g2]
    m["pool_w"] = pw
    m["pool_s"] = np.ascontiguousarray(np.asarray(pool_scale, f).reshape(depth, 2, 128).transpose(0, 2, 1))
    m["dn_w"] = np.ascontiguousarray(np.broadcast_to(np.asarray(dn_norm_w, f)[:, None, None, :], (depth, 128, 4, 64)))
    k_ = np.arange(128)
    m["triU"] = (k_[:, None] <= k_[None, :]).astype(f)
    m["triL"] = (k_[:, None] >= k_[None, :]).astype(f)
    for i, S in enumerate(seqs):
        c, s_ = rope_tables(S)
        m[f"cos{i}"] = c
        m[f"sin{i}"] = s_
        t = np.arange(S)
        ic = np.zeros((128, 2, S), f)
        for g, win in enumerate(POOL_WINDOWS):
            lo = np.clip(t - win // 2, 0, S)
            hi = np.clip(t + win // 2, 0, S)
            ic[(g % 2) * 64:(g % 2) * 64 + 64, g // 2, :] = (1.0 / (hi - lo).astype(f))[None, :]
        m[f"icnt{i}"] = ic
    return m


_NC_CACHE = {}


def kernel(x_prompt, x_sample, norm_w, w_in, sgu_w, sgu_b, conv_w, a_log, dt_bias, dn_norm_w,
           q_norm_w, k_norm_w, pool_w, pool_scale, w_out):
    x_prompt = np.asarray(x_prompt, np.float32)
    x_sample = np.asarray(x_sample, np.float32)
    depth = int(np.asarray(w_in).shape[0])
    seqs = [x_prompt.shape[1], x_sample.shape[1]]
    key = (tuple(seqs), depth)
    if key not in _NC_CACHE:
        _NC_CACHE[key] = build(seqs, depth, divs=(x_prompt.shape[0], x_sample.shape[0]),
                               groups=(8 // x_prompt.shape[0], 8 // x_sample.shape[0]))
    nc = _NC_CACHE[key]
    common = host_layout(seqs, depth, norm_w, w_in, sgu_w, sgu_b, conv_w, a_log, dt_bias, dn_norm_w,
                         q_norm_w, k_norm_w, pool_w, pool_scale, w_out)
    nb_p, nb_s = x_prompt.shape[0], x_sample.shape[0]
    in_maps = []
    for c in range(8):
        mm = dict(common)
        mm["x0"] = np.ascontiguousarray(x_prompt[c % nb_p])
        mm["x1"] = np.ascontiguousarray(x_sample[c % nb_s])
        in_maps.append(mm)
    res = run_bass_kernel_spmd(nc, in_maps, core_ids=list(range(8)))
    yp = np.zeros(x_prompt.shape, np.float32)
    ys = np.zeros(x_sample.shape, np.float32)
    gp, gs = 8 // nb_p, 8 // nb_s
    np_, ns_ = seqs[0] // gp, seqs[1] // gs
    for c in range(8):
        yp[c % nb_p, (c // nb_p) * np_:(c // nb_p + 1) * np_] = np.asarray(res.results[c]["y0"], np.float32)
        ys[c % nb_s, (c // nb_s) * ns_:(c // nb_s + 1) * ns_] = np.asarray(res.results[c]["y1"], np.float32)
    return (yp, ys)
```

```python
import numpy as np
import ml_dtypes
from contextlib import ExitStack
import concourse.bass as bass
import concourse.mybir as mybir
from concourse.bass_utils import run_bass_kernel_spmd

F32 = mybir.dt.float32
BF16 = mybir.dt.bfloat16
AF = mybir.ActivationFunctionType
ALU = mybir.AluOpType
AX = mybir.AxisListType

D = 1024
NCOL = 3088
EPS = 1e-6
G_AU, G_AV, G_AZ, G_BQ, G_BK, G_BV, G_BZ, G_CQ, G_CK, G_CV, G_CZ, G_DX, G_DZ = \
    0, 2, 4, 6, 8, 10, 12, 14, 16, 17, 18, 20, 22
ORIG_OFF = dict(au=0, av=256, az=512, bq=768, bk=1024, bv=1280, bz=1536, bb=1792, ba=1800,
                cq=1808, ck=2064, cv=2192, cz=2320, dx=2576, dz=2832)
COL_PERM = np.concatenate([
    np.arange(0, 1792), np.arange(1808, 3088), np.arange(1792, 1808)])


class Buf:
    __slots__ = ("name", "w", "r")

    def __init__(self, name):
        self.name = name
        self.w = None
        self.r = {}


class K:
    ENG = ("pe", "act", "dve", "pool", "sp")

    def __init__(self, nc, stack):
        self.nc = nc
        self.sems = {}
        self.cnt = {}
        self.stack = stack
        for e in self.ENG:
            self._newsem(e)
        self.prog = {e: [] for e in self.ENG}
        self.waited = {e: {} for e in self.ENG}
        self.ninstr = 0

    def _newsem(self, key):
        self.sems[key] = self.stack.enter_context(self.nc.semaphore("s_" + key))
        self.cnt[key] = 0

    LIMIT = None

    def op(self, e, fn, reads=(), writes=(), dma=None):
        if K.LIMIT is not None and self.ninstr >= K.LIMIT:
            return
        need = {}

        def want(tok):
            if tok is None:
                return
            k, v = tok
            if k == "pe" and e == "pe" and dma is None:
                return
            if k not in self.ENG:
                v = self.cnt[k]
            if need.get(k, 0) < v:
                need[k] = v
        for b in reads:
            want(b.w)
        for b in writes:
            want(b.w)
            for k, v in b.r.items():
                want((k, v))
        waits = []
        wd = self.waited[e]
        for k, v in need.items():
            if wd.get(k, 0) < v:
                wd[k] = v
                waits.append((k, v))
        if dma is not None:
            if dma not in self.sems:
                self._newsem(dma)
            key, inc = dma, 16
        else:
            key, inc = e, 1
        self.cnt[key] += inc
        tok = (key, self.cnt[key])
        for b in reads:
            if b.r.get(key, 0) < tok[1]:
                b.r[key] = tok[1]
        for b in writes:
            b.w = tok
            b.r = {}
        self.prog[e].append((waits, fn, key, inc))
        self.ninstr += 1

    def barrier(self):
        tot = dict(self.cnt)
        for e in self.ENG:
            waits = []
            for k, v in tot.items():
                if v > 0 and self.waited[e].get(k, 0) < v:
                    self.waited[e][k] = v
                    waits.append((k, v))
            if waits:
                self.prog[e].append((waits, None, None, 0))

    def flush(self):
        nc = self.nc
        _DYN.clear()
        with nc.Block() as block:
            def run(e, eng):
                for waits, fn, key, inc in self.prog[e]:
                    for k, v in waits:
                        eng.wait_ge(self.sems[k], v)
                    if fn is not None:
                        fn(eng).then_inc(self.sems[key], inc)

            @block.tensor
            def _(eng):
                run("pe", eng)

            @block.scalar
            def _(eng):
                run("act", eng)

            @block.vector
            def _(eng):
                run("dve", eng)

            @block.gpsimd
            def _(eng):
                run("pool", eng)

            @block.sync
            def _(eng):
                run("sp", eng)
        self.prog = {e: [] for e in self.ENG}


class T:
    def __init__(self, t, name):
        self.t = t
        self.b = Buf(name)

    def __getitem__(self, idx):
        return self.t[idx]


_UID = [0]


def sb(nc, st, name, shape, dt):
    _UID[0] += 1
    nm = f"sb{_UID[0]}_{name}"
    return T(st.enter_context(nc.sbuf_tensor(nm, list(shape), dt)), nm)


def ps(nc, st, name, shape, dt):
    _UID[0] += 1
    nm = f"ps{_UID[0]}_{name}"
    return T(st.enter_context(nc.psum_tensor(nm, list(shape), dt)), nm)


def rope_tables(S):
    rows = np.repeat(np.arange(S // 64), 64)
    cols = np.tile(np.arange(64), S // 64)
    inv = np.power(np.float32(10000.0), -2.0 * np.arange(16, dtype=np.float32) / 32).astype(np.float32)
    ang = np.stack([rows, cols], -1).astype(np.float32)[:, :, None] * inv
    return np.cos(ang).astype(np.float32).reshape(S, 32), np.sin(ang).astype(np.float32).reshape(S, 32)


def build(seqs, depth, debug=False, phases="pfdao", divs=(2, 4), groups=(4, 2)):
    nc = bass.Bass("TRN2", target_bir_lowering=False)
    dr = {}

    def din(name, shape, dt=F32):
        dr[name] = nc.dram_tensor(name, list(shape), dt, kind="ExternalInput").ap()
        return dr[name]

    def dscr(name, shape, dt, out=False):
        kind = "ExternalOutput" if out else "Internal"
        dr[name] = nc.dram_tensor(name, list(shape), dt, kind=kind).ap()
        return dr[name]

    nseq = len(seqs)
    for i, S in enumerate(seqs):
        din(f"x{i}", [S, D])
        din(f"cos{i}", [S, 32])
        din(f"sin{i}", [S, 32])
        dscr(f"y{i}", [S // groups[i], D], F32, out=True)
        dscr(f"y1_{i}", [S, D], F32)
        dscr(f"projT{i}", [3072, S], BF16, out=debug)
        dscr(f"qT{i}", [256, S], BF16, out=debug)
        dscr(f"kT{i}", [128, S], BF16, out=debug)
        dscr(f"va{i}", [S, 130], BF16, out=debug)
        dscr(f"bg{i}", [S, 16], F32, out=debug)
        dscr(f"mixT{i}", [1024, S], BF16, out=debug)
        dscr(f"convT{i}", [768, S], BF16, out=debug)
        dscr(f"of{i}", [S, 256], F32, out=debug)
        npi = S // groups[i]
        dscr(f"qTp{i}", [256, npi], BF16)
        dscr(f"zTp{i}", [256, npi], BF16)
        dscr(f"mixTp{i}", [1024, npi], BF16)
        dscr(f"xp{i}", [npi, D], F32)
        if debug:
            dscr(f"osum{i}", [S, 256], F32, out=True)
    din("w_in", [depth, 128, 8, NCOL])
    din("norm_w", [depth, 128, 8])
    din("w_out", [depth, 128, 8, D])
    din("qkw", [depth, 128, 6, 64])
    din("a_log", [depth, 128, 8])
    din("dt_bias", [depth, 128, 8])
    din("ident", [128, 128])
    din("sgu_wT", [depth, 128, 4, 128])
    din("sgu_bT", [depth, 128, 2, 128])
    din("conv_w", [depth, 128, 6, 5])
    din("pool_w", [depth, 128, 2, 128])
    din("pool_s", [depth, 128, 2])
    din("dn_w", [depth, 128, 4, 64])
    din("triU", [128, 128])
    din("triL", [128, 128])
    for i, S in enumerate(seqs):
        din(f"icnt{i}", [128, 2, S])

    with ExitStack() as top:
        kk = K(nc, top)
        for key in ("c0", "wl0", "wl1", "x0", "x1", "st0", "st1", "sq0", "sq1", "sp0", "sp1"):
            kk._newsem(key)
        with nc.Block() as blk0:
            @blk0.vector
            def _(eng):
                for key in kk.sems:
                    eng.sem_clear(kk.sems[key])
        ident_f = sb(nc, top, "ident_f", [128, 128], F32)
        ident = sb(nc, top, "ident_b", [128, 128], BF16)
        ones_f = sb(nc, top, "ones_f", [128, 128], F32)
        kk.op("sp", lambda e: e.dma_start(out=ident_f[:], in_=dr["ident"][:, :]), writes=[ident_f.b], dma="c0")
        kk.op("dve", lambda e: e.tensor_copy(out=ident[:], in_=ident_f[:]), reads=[ident_f.b], writes=[ident.b])
        kk.op("pool", lambda e: e.memset(ones_f[:], 1.0), writes=[ones_f.b])

        for l in range(depth):
            for si, S in enumerate(seqs):
                xin = dr[f"x{si}"] if l == 0 else dr[f"y1_{si}"]
                last = (l == depth - 1)
                yout = dr[f"y{si}"] if last else dr[f"y1_{si}"]
                part = (S // groups[si], divs[si]) if last else None
                for ph in phases:
                    if ph == "p":
                        phase_proj(nc, kk, dr, l, si, S, xin, ident, ident_f)
                    elif ph == "f":
                        phase_fm(nc, kk, dr, l, si, S, ident)
                    elif ph == "d":
                        phase_dn(nc, kk, dr, l, si, S, ident, ident_f, ones_f)
                    elif ph == "a":
                        if part is not None:
                            phase_part(nc, kk, dr, si, S, xin, part)
                            kk.barrier()
                            kk.flush()
                        phase_attn(nc, kk, dr, l, si, S, ones_f, part)
                    elif ph == "o":
                        phase_out(nc, kk, dr, l, si, S, xin, yout, part)
                    kk.barrier()
                    kk.flush()
    return nc


def load_w_bf16(nc, kk, st, name, src_ap, ncols, scale_t=None, chunk=1024):
    w = sb(nc, st, name, [128, 8, ncols], BF16)
    stg = [sb(nc, st, f"{name}_stg{i}", [128, chunk], F32) for i in range(2)]
    n = 0
    for kc in range(8):
        for c0 in range(0, ncols, chunk):
            cw = min(chunk, ncols - c0)
            s = stg[n % 2]
            kk.op("sp", lambda e, s=s, kc=kc, c0=c0, cw=cw: e.dma_start(out=s[:, 0:cw], in_=src_ap[:, kc, c0:c0 + cw]),
                  writes=[s.b], dma=f"wl{n % 2}")
            if scale_t is not None:
                kk.op("dve", lambda e, s=s, kc=kc, c0=c0, cw=cw: e.tensor_scalar(
                    out=w[:, kc, c0:c0 + cw], in0=s[:, 0:cw], scalar1=scale_t[:, kc:kc + 1], scalar2=1.0,
                    op0=ALU.mult, op1=ALU.mult), reads=[s.b, scale_t.b], writes=[w.b])
            else:
                eng = "dve" if n % 2 == 0 else "pool"
                kk.op(eng, lambda e, s=s, kc=kc, c0=c0, cw=cw: e.tensor_copy(out=w[:, kc, c0:c0 + cw], in_=s[:, 0:cw]),
                      reads=[s.b], writes=[w.b])
            n += 1
    return w


def phase_proj(nc, kk, dr, l, si, S, xin, ident, ident_f):
    with ExitStack() as st:
        nw = sb(nc, st, "nw", [128, 8], F32)
        kk.op("sp", lambda e: e.dma_start(out=nw[:], in_=dr["norm_w"][l]), writes=[nw.b], dma="c0")
        w = load_w_bf16(nc, kk, st, "w_in_sb", dr["w_in"][l], NCOL, scale_t=nw)
        qkw = sb(nc, st, "qkw", [128, 6, 64], F32)
        kk.op("sp", lambda e: e.dma_start(out=qkw[:], in_=dr["qkw"][l]), writes=[qkw.b], dma="c0")
        alog = sb(nc, st, "alog", [128, 8], F32)
        nA = sb(nc, st, "nA", [128, 8], F32)
        dtb = sb(nc, st, "dtb", [128, 8], F32)
        kk.op("sp", lambda e: e.dma_start(out=alog[:], in_=dr["a_log"][l]), writes=[alog.b], dma="c0")
        kk.op("sp", lambda e: e.dma_start(out=dtb[:], in_=dr["dt_bias"][l]), writes=[dtb.b], dma="c0")
        kk.op("act", lambda e: e.activation(out=nA[:], in_=alog[:], func=AF.Exp), reads=[alog.b], writes=[nA.b])
        kk.op("dve", lambda e: e.tensor_scalar(out=nA[:], in0=nA[:], scalar1=-1.0, scalar2=1.0, op0=ALU.mult, op1=ALU.mult),
              reads=[nA.b], writes=[nA.b])

        NS = 2
        xt = [sb(nc, st, f"xt{i}", [128, D], F32) for i in range(NS)]
        junk = sb(nc, st, "junk", [128, D], BF16)
        hb = [sb(nc, st, f"hb{i}", [128, D], BF16) for i in range(NS)]
        ssq = [sb(nc, st, f"ssq{i}", [128, 1], F32) for i in range(NS)]
        rstd = [sb(nc, st, f"rstd{i}", [128, 1], F32) for i in range(NS)]
        pT = [ps(nc, st, f"pT{i}", [128, 8, 128], BF16) for i in range(2)]
        hT = [sb(nc, st, f"hT{i}", [128, 8, 512], BF16) for i in range(2)]
        pacc = [ps(nc, st, f"pacc{i}", [128, 512], F32) for i in range(3)]
        ptk = [ps(nc, st, f"ptk{i}", [128, 512], F32) for i in range(1)]
        pbg = [ps(nc, st, f"pbg{i}", [128, 512], F32) for i in range(1)]
        ptr = ps(nc, st, "ptr", [128, 3, 128], BF16)
        stage = [sb(nc, st, f"stage{i}", [128, 24, 512], BF16) for i in range(2)]
        cs = [sb(nc, st, f"cs{i}", [128, 2, 32], F32) for i in range(NS)]
        qk = [sb(nc, st, f"qk{i}", [128, 6, 64], F32) for i in range(NS)]
        qsq = sb(nc, st, "qsq", [128, 6, 64], F32)
        qss = [sb(nc, st, f"qss{i}", [128, 6], F32) for i in range(NS)]
        qr = [sb(nc, st, f"qr{i}", [128, 6, 64], F32) for i in range(NS)]
        tmpa = sb(nc, st, "tmpa", [128, 6, 2, 16], F32)
        tmpb = sb(nc, st, "tmpb", [128, 6, 2, 16], F32)
        qkb = [sb(nc, st, f"qkb{i}", [128, 384], BF16) for i in range(NS)]
        qkT = [sb(nc, st, f"qkT{i}", [128, 3, 512], BF16) for i in range(2)]
        va = [sb(nc, st, f"va{i}", [128, 130], BF16) for i in range(NS)]
        bgt = [sb(nc, st, f"bgt{i}", [128, 16], F32) for i in range(NS)]
        t8 = [sb(nc, st, f"t8{i}", [128, 16], F32) for i in range(NS)]
        for v_ in va:
            kk.op("pool", lambda e, v_=v_: e.memset(v_[:], 1.0), writes=[v_.b])

        projT = dr[f"projT{si}"]
        nblk = S // 512
        ev = 0
        for tb in range(nblk):
            hTb = hT[tb % 2]
            qkTb = qkT[tb % 2]
            for t4 in range(4):
                ti = tb * 4 + t4
                s = ti % NS
                r0 = ti * 128
                x_, h_, sq_, rs_ = xt[s], hb[s], ssq[s], rstd[s]
                kk.op("sp", lambda e, x_=x_, r0=r0: e.dma_start(out=x_[:], in_=xin[r0:r0 + 128, :]),
                      writes=[x_.b], dma=f"x{s}")
                kk.op("sp", lambda e, c_=cs[s], r0=r0: e.dma_start(out=c_[:, 0, :], in_=dr[f"cos{si}"][r0:r0 + 128, :]),
                      writes=[cs[s].b], dma=f"x{s}")
                kk.op("sp", lambda e, c_=cs[s], r0=r0: e.dma_start(out=c_[:, 1, :], in_=dr[f"sin{si}"][r0:r0 + 128, :]),
                      writes=[cs[s].b], dma=f"x{s}")
                kk.op("act", lambda e, x_=x_, sq_=sq_: e.activation(out=junk[:], in_=x_[:], func=AF.Square, accum_out=sq_[:]),
                      reads=[x_.b], writes=[junk.b, sq_.b])
                kk.op("dve", lambda e, sq_=sq_, rs_=rs_: e.tensor_scalar(out=rs_[:], in0=sq_[:], scalar1=1.0 / D, scalar2=EPS,
                                                                         op0=ALU.mult, op1=ALU.add), reads=[sq_.b], writes=[rs_.b])
                kk.op("act", lambda e, rs_=rs_: e.activation(out=rs_[:], in_=rs_[:], func=AF.Ln), reads=[rs_.b], writes=[rs_.b])
                kk.op("act", lambda e, rs_=rs_: e.activation(out=rs_[:], in_=rs_[:], func=AF.Exp, scale=-0.5), reads=[rs_.b], writes=[rs_.b])
                kk.op("dve", lambda e, x_=x_, h_=h_, rs_=rs_: e.tensor_scalar(out=h_[:], in0=x_[:], scalar1=rs_[:, 0:1], scalar2=1.0,
                                                                              op0=ALU.mult, op1=ALU.mult),
                      reads=[x_.b, rs_.b], writes=[h_.b])
                p_ = pT[ti % 2]
                for kc in range(8):
                    kk.op("pe", lambda e, p_=p_, h_=h_, kc=kc: e.transpose(out=p_[:, kc, :], in_=h_[:, kc * 128:(kc + 1) * 128],
                                                                          identity=ident[:]),
                          reads=[h_.b, ident.b], writes=[p_.b])
                kk.op("act", lambda e, p_=p_, hTb=hTb, t4=t4: e.activation(out=hTb[:, :, t4 * 128:(t4 + 1) * 128], in_=p_[:],
                                                                           func=AF.Identity),
                      reads=[p_.b], writes=[hTb.b])
                pk = ptk[0]
                for kc in range(8):
                    kk.op("pe", lambda e, pk=pk, hTb=hTb, t4=t4, kc=kc: e.matmul(
                        pk[:], hTb[:, kc, t4 * 128:(t4 + 1) * 128], w[:, kc, G_CQ * 128:G_CQ * 128 + 512],
                        start=(kc == 0), stop=(kc == 7)), reads=[hTb.b, w.b], writes=[pk.b])
                pb_ = pbg[0]
                for kc in range(8):
                    kk.op("pe", lambda e, pb_=pb_, hTb=hTb, t4=t4, kc=kc: e.matmul(
                        pb_[:, 0:16], hTb[:, kc, t4 * 128:(t4 + 1) * 128], w[:, kc, 3072:3088],
                        start=(kc == 0), stop=(kc == 7)), reads=[hTb.b, w.b], writes=[pb_.b])
                q_, ss_, r_, qb_, va_, c_ = qk[s], qss[s], qr[s], qkb[s], va[s], cs[s]
                kk.op("dve", lambda e, q_=q_, pk=pk: e.tensor_copy(out=q_[:].rearrange("p a b -> p (a b)"), in_=pk[:, 0:384]),
                      reads=[pk.b], writes=[q_.b])
                kk.op("dve", lambda e, va_=va_, pk=pk: e.tensor_copy(
                    out=va_[:].rearrange("p (a b) -> p a b", a=2)[:, :, 0:64],
                    in_=pk[:, 384:512].rearrange("p (a b) -> p a b", a=2)), reads=[pk.b], writes=[va_.b])
                kk.op("pool", lambda e, va_=va_, r0=r0: e.dma_start(out=dr[f"va{si}"][r0:r0 + 128, :], in_=va_[:]),
                      reads=[va_.b], dma=f"st{s}")
                kk.op("dve", lambda e, q_=q_: e.tensor_tensor(out=qsq[:], in0=q_[:], in1=q_[:], op=ALU.mult),
                      reads=[q_.b], writes=[qsq.b])
                kk.op("dve", lambda e, ss_=ss_: e.tensor_reduce(out=ss_[:], in_=qsq[:], axis=AX.X, op=ALU.add),
                      reads=[qsq.b], writes=[ss_.b])
                kk.op("dve", lambda e, ss_=ss_: e.tensor_scalar(out=ss_[:], in0=ss_[:], scalar1=1.0 / 64, scalar2=EPS,
                                                                op0=ALU.mult, op1=ALU.add), reads=[ss_.b], writes=[ss_.b])
                kk.op("act", lambda e, ss_=ss_: e.activation(out=ss_[:], in_=ss_[:], func=AF.Ln), reads=[ss_.b], writes=[ss_.b])
                kk.op("act", lambda e, ss_=ss_: e.activation(out=ss_[:], in_=ss_[:], func=AF.Exp, scale=-0.5), reads=[ss_.b], writes=[ss_.b])
                kk.op("dve", lambda e, q_=q_, ss_=ss_: e.tensor_tensor(
                    out=q_[:], in0=q_[:], in1=ss_[:].unsqueeze(2).to_broadcast([128, 6, 64]), op=ALU.mult),
                    reads=[q_.b, ss_.b], writes=[q_.b])
                kk.op("pool", lambda e, q_=q_: e.tensor_tensor(out=q_[:], in0=q_[:], in1=qkw[:], op=ALU.mult),
                      reads=[q_.b, qkw.b], writes=[q_.b])
                def v5(t):
                    return t[:].rearrange("p h (a b f) -> p h a b f", a=2, b=2)
                cosb = lambda c_: c_[:, 0, :].rearrange("p (a f) -> p a f", a=2).unsqueeze(1).to_broadcast([128, 6, 2, 16])
                sinb = lambda c_: c_[:, 1, :].rearrange("p (a f) -> p a f", a=2).unsqueeze(1).to_broadcast([128, 6, 2, 16])
                kk.op("dve", lambda e, q_=q_, c_=c_: e.tensor_tensor(out=tmpa[:], in0=v5(q_)[:, :, :, 1, :], in1=sinb(c_), op=ALU.mult),
                      reads=[q_.b, c_.b], writes=[tmpa.b])
                kk.op("pool", lambda e, q_=q_, c_=c_: e.tensor_tensor(out=tmpb[:], in0=v5(q_)[:, :, :, 0, :], in1=sinb(c_), op=ALU.mult),
                      reads=[q_.b, c_.b], writes=[tmpb.b])
                kk.op("dve", lambda e, q_=q_, r_=r_, c_=c_: e.tensor_tensor(out=v5(r_)[:, :, :, 0, :], in0=v5(q_)[:, :, :, 0, :], in1=cosb(c_), op=ALU.mult),
                      reads=[q_.b, c_.b], writes=[r_.b])
                kk.op("pool", lambda e, q_=q_, r_=r_, c_=c_: e.tensor_tensor(out=v5(r_)[:, :, :, 1, :], in0=v5(q_)[:, :, :, 1, :], in1=cosb(c_), op=ALU.mult),
                      reads=[q_.b, c_.b], writes=[r_.b])
                kk.op("dve", lambda e, r_=r_: e.tensor_tensor(out=v5(r_)[:, :, :, 0, :], in0=v5(r_)[:, :, :, 0, :], in1=tmpa[:], op=ALU.subtract),
                      reads=[r_.b, tmpa.b], writes=[r_.b])
                kk.op("dve", lambda e, r_=r_: e.tensor_tensor(out=v5(r_)[:, :, :, 1, :], in0=v5(r_)[:, :, :, 1, :], in1=tmpb[:], op=ALU.add),
                      reads=[r_.b, tmpb.b], writes=[r_.b])
                kk.op("act", lambda e, r_=r_, qb_=qb_: e.activation(out=qb_[:], in_=r_[:].rearrange("p a b -> p (a b)"), func=AF.Identity),
                      reads=[r_.b], writes=[qb_.b])
                for j in range(3):
                    kk.op("pe", lambda e, qb_=qb_, j=j: e.transpose(out=ptr[:, j, :], in_=qb_[:, j * 128:(j + 1) * 128], identity=ident[:]),
                          reads=[qb_.b, ident.b], writes=[ptr.b])
                kk.op("dve", lambda e, qkTb=qkTb, t4=t4: e.tensor_copy(out=qkTb[:, :, t4 * 128:(t4 + 1) * 128], in_=ptr[:]),
                      reads=[ptr.b], writes=[qkTb.b])
                b_, t_ = bgt[s], t8[s]
                kk.op("dve", lambda e, t_=t_, pb_=pb_: e.tensor_tensor(out=t_[:, 8:16], in0=pb_[:, 8:16], in1=dtb[:], op=ALU.add),
                      reads=[pb_.b, dtb.b], writes=[t_.b])
                kk.op("act", lambda e, t_=t_, pb_=pb_: e.activation(out=t_[:, 0:8], in_=pb_[:, 0:8], func=AF.Exp, scale=-1.0),
                      reads=[pb_.b], writes=[t_.b])
                kk.op("act", lambda e, t_=t_: e.activation(out=t_[:, 8:16], in_=t_[:, 8:16], func=AF.Exp),
                      reads=[t_.b], writes=[t_.b])
                kk.op("act", lambda e, t_=t_: e.activation(out=t_[:, 8:16], in_=t_[:, 8:16], func=AF.Ln, bias=1.0),
                      reads=[t_.b], writes=[t_.b])
                kk.op("dve", lambda e, t_=t_: e.tensor_scalar(out=t_[:, 0:8], in0=t_[:, 0:8], scalar1=1.0, scalar2=1.0, op0=ALU.add, op1=ALU.mult),
                      reads=[t_.b], writes=[t_.b])
                kk.op("dve", lambda e, t_=t_, b_=b_: e.reciprocal(out=b_[:, 0:8], in_=t_[:, 0:8]), reads=[t_.b], writes=[b_.b])
                kk.op("dve", lambda e, t_=t_, b_=b_: e.tensor_tensor(out=b_[:, 8:16], in0=t_[:, 8:16], in1=nA[:], op=ALU.mult),
                      reads=[t_.b, nA.b], writes=[b_.b])
                kk.op("pool", lambda e, b_=b_, r0=r0: e.dma_start(out=dr[f"bg{si}"][r0:r0 + 128, :], in_=b_[:]),
                      reads=[b_.b], dma=f"st{s}")
            c0 = tb * 512
            kk.op("pool", lambda e, qkTb=qkTb, c0=c0: e.dma_start(
                out=dr[f"qT{si}"][:, c0:c0 + 512].rearrange("(j p) s -> p j s", p=128), in_=qkTb[:, 0:2, :]),
                reads=[qkTb.b], dma=f"sq{tb % 2}")
            kk.op("pool", lambda e, qkTb=qkTb, c0=c0: e.dma_start(out=dr[f"kT{si}"][:, c0:c0 + 512], in_=qkTb[:, 2, :]),
                  reads=[qkTb.b], dma=f"sq{tb % 2}")
            stg = stage[tb % 2]
            for g in range(24):
                pa = pacc[g % 3]
                for kc in range(8):
                    kk.op("pe", lambda e, pa=pa, g=g, kc=kc, hTb=hTb: e.matmul(
                        pa[:], w[:, kc, g * 128:(g + 1) * 128], hTb[:, kc, :], start=(kc == 0), stop=(kc == 7)),
                        reads=[w.b, hTb.b], writes=[pa.b])
                if ev % 2 == 0:
                    kk.op("act", lambda e, pa=pa, g=g, stg=stg: e.activation(out=stg[:, g, :], in_=pa[:], func=AF.Identity),
                          reads=[pa.b], writes=[stg.b])
                else:
                    kk.op("dve", lambda e, pa=pa, g=g, stg=stg: e.tensor_copy(out=stg[:, g, :], in_=pa[:]),
                          reads=[pa.b], writes=[stg.b])
                ev += 1
            for h2 in range(2):
                kk.op("pool", lambda e, stg=stg, c0=c0, h2=h2: e.dma_start(
                    out=projT[h2 * 1536:(h2 + 1) * 1536, c0:c0 + 512].rearrange("(g p) s -> p g s", p=128),
                    in_=stg[:, h2 * 12:(h2 + 1) * 12, :]), reads=[stg.b], dma=f"sp{tb % 2}")


_DYN = {}


def dyn(e, part, base, n):
    if part is None:
        return slice(base, base + n)
    npart, div = part
    key = (id(e), div, npart)
    if key not in _DYN:
        _DYN[key] = e.snap((e.partition_id() // div) * npart)
    return bass.ds(_DYN[key] + base, n)


def phase_part(nc, kk, dr, si, S, xin, part):
    npart, div = part
    dummy = Buf("partcopy")

    def off(e):
        return e.snap((e.partition_id() // div) * npart)

    if si == 0:
        qe, xe, me = "sp", "sp", "pool"
    else:
        qe, xe, me = "pool", "act", "pool"
    kk.op(qe, lambda e: e.dma_start(out=dr[f"qTp{si}"][:, :], in_=dr[f"qT{si}"][:, bass.ds(off(e), npart)]), writes=[dummy], dma="c0")
    kk.op(qe, lambda e: e.dma_start(out=dr[f"zTp{si}"][:, :], in_=dr[f"projT{si}"][G_CZ * 128:(G_CZ + 2) * 128, bass.ds(off(e), npart)]),
          writes=[dummy], dma="c0")
    kk.op(me, lambda e: e.dma_start(out=dr[f"mixTp{si}"][0:512, :], in_=dr[f"mixT{si}"][0:512, bass.ds(off(e), npart)]), writes=[dummy], dma="c0")
    kk.op(me, lambda e: e.dma_start(out=dr[f"mixTp{si}"][768:1024, :], in_=dr[f"mixT{si}"][768:1024, bass.ds(off(e), npart)]),
          writes=[dummy], dma="c0")
    step = 1024
    for r in range(0, npart, step):
        n = min(step, npart - r)
        kk.op(xe, lambda e, r=r, n=n: e.dma_start(out=dr[f"xp{si}"][r:r + n, :], in_=xin[bass.ds(off(e) + r, n), :]), writes=[dummy], dma="c0")


def phase_out(nc, kk, dr, l, si, S, xin, yout, part=None):
    with ExitStack() as st:
        ntok = S if part is None else part[0]
        if part is not None:
            xin = dr[f"xp{si}"]
        wo = load_w_bf16(nc, kk, st, "w_out_sb", dr["w_out"][l], D)
        NS = 2
        mt = [sb(nc, st, f"mt{i}", [128, 8, 128], BF16) for i in range(NS)]
        xt = [sb(nc, st, f"xo{i}", [128, D], F32) for i in range(NS)]
        yt = [sb(nc, st, f"yo{i}", [128, D], F32) for i in range(NS)]
        po = [ps(nc, st, f"po{i}", [128, 512], F32) for i in range(4)]
        mixT = dr[f"mixT{si}"] if part is None else dr[f"mixTp{si}"]
        for ti in range(ntok // 128):
            s = ti % NS
            r0 = ti * 128
            m_, x_, y_ = mt[s], xt[s], yt[s]
            kk.op("sp", lambda e, m_=m_, r0=r0: e.dma_start(out=m_[:], in_=mixT[:, r0:r0 + 128].rearrange("(k p) s -> p k s", p=128)),
                  writes=[m_.b], dma=f"x{s}")
            kk.op("sp", lambda e, x_=x_, r0=r0: e.dma_start(out=x_[:], in_=xin[r0:r0 + 128, :]), writes=[x_.b], dma=f"x{s}")
            for g in range(2):
                p_ = po[(ti * 2 + g) % 4]
                for kc in range(8):
                    kk.op("pe", lambda e, p_=p_, m_=m_, kc=kc, g=g: e.matmul(
                        p_[:], m_[:, kc, :], wo[:, kc, g * 512:(g + 1) * 512], start=(kc == 0), stop=(kc == 7)),
                        reads=[m_.b, wo.b], writes=[p_.b])
                kk.op("dve", lambda e, p_=p_, x_=x_, y_=y_, g=g: e.tensor_tensor(
                    out=y_[:, g * 512:(g + 1) * 512], in0=p_[:], in1=x_[:, g * 512:(g + 1) * 512], op=ALU.add),
                    reads=[p_.b, x_.b], writes=[y_.b])
            kk.op("pool", lambda e, y_=y_, r0=r0: e.dma_start(out=yout[r0:r0 + 128, :], in_=y_[:]), reads=[y_.b], dma=f"st{s}")


def phase_attn(nc, kk, dr, l, si, S, ones_f, part=None):
    with ExitStack() as st:
        nkt = S // 128
        KT = sb(nc, st, "KT", [128, S], BF16)
        VA = sb(nc, st, "VA", [128, nkt, 130], BF16)
        for c in range(0, S, 2048):
            ce = min(c + 2048, S)
            kk.op("sp", lambda e, c=c, ce=ce: e.dma_start(out=KT[:, c:ce], in_=dr[f"kT{si}"][:, c:ce]), writes=[KT.b], dma="c0")
        for c in range(0, nkt, 16):
            ce = min(c + 16, nkt)
            kk.op("sp", lambda e, c=c, ce=ce: e.dma_start(out=VA[:, c:ce, :],
                                                          in_=dr[f"va{si}"][c * 128:ce * 128, :].rearrange("(t p) c -> p t c", p=128)),
                  writes=[VA.b], dma="c0")
        qt = [sb(nc, st, f"qt{i}", [128, 2, 512], BF16) for i in range(2)]
        zt = [sb(nc, st, f"zt{i}", [64, 4, 512], BF16) for i in range(2)]
        pss = [ps(nc, st, f"pss{i}", [128, 2, 512], F32) for i in range(2)]
        pex = [sb(nc, st, f"pex{i}", [128, 2, 512], BF16) for i in range(2)]
        pov = [ps(nc, st, f"pov{i}", [128, 512], F32) for i in range(2)]
        pbc = ps(nc, st, "pbc", [64, 512], F32)
        osb = [sb(nc, st, f"osb{i}", [128, 512], F32) for i in range(2)]
        rc = [sb(nc, st, f"rc{i}", [128, 512], F32) for i in range(2)]
        og = [sb(nc, st, f"og{i}", [64, 4, 512], BF16) for i in range(2)]
        it = 0
        nq = S if part is None else part[0]
        qsrc = dr[f"qT{si}"] if part is None else dr[f"qTp{si}"]
        zsrc = dr[f"projT{si}"][G_CZ * 128:(G_CZ + 2) * 128, :] if part is None else dr[f"zTp{si}"]
        mixdst = dr[f"mixT{si}"] if part is None else dr[f"mixTp{si}"]
        for qb in range(nq // 512):
            c0 = qb * 512
            q_ = qt[qb % 2]
            z_ = zt[qb % 2]
            og_ = og[qb % 2]
            for h in range(4):
                kv = h // 2
                kk.op("sp", lambda e, q_=q_, h=h, kv=kv, c0=c0: e.dma_start(
                    out=q_[kv * 64:(kv + 1) * 64, h % 2, :], in_=qsrc[h * 64:(h + 1) * 64, c0:c0 + 512]),
                    writes=[q_.b], dma=f"x{qb % 2}")
            kk.op("sp", lambda e, z_=z_, c0=c0: e.dma_start(
                out=z_[:], in_=zsrc[:, c0:c0 + 512].rearrange("(h p) s -> p h s", p=64)),
                writes=[z_.b], dma=f"x{qb % 2}")
            kk.op("act", lambda e, z_=z_: e.activation(out=z_[:], in_=z_[:], func=AF.Silu), reads=[z_.b], writes=[z_.b])
            for h in range(4):
                kv = h // 2
                po_ = pov[h % 2]
                npair = nkt // 2
                slots = []
                for kp in range(npair):
                    slots.append((pss[it % 2], pex[it % 2]))
                    it += 1

                def emit_qk(kp):
                    ps_ = slots[kp][0]
                    for j in range(2):
                        kt = kp * 2 + j
                        kk.op("pe", lambda e, ps_=ps_, j=j, kt=kt, kv=kv, q_=q_, h=h: e.matmul(
                            ps_[:, j, :], KT[kv * 64:(kv + 1) * 64, kt * 128:(kt + 1) * 128], q_[kv * 64:(kv + 1) * 64, h % 2, :],
                            start=True, stop=True), reads=[KT.b, q_.b], writes=[ps_.b])

                emit_qk(0)
                for kp in range(npair):
                    ps_, pe_ = slots[kp]
                    if kp + 1 < npair:
                        emit_qk(kp + 1)
                    kk.op("act", lambda e, ps_=ps_, pe_=pe_: e.activation(out=pe_[:], in_=ps_[:], func=AF.Exp, scale=0.125),
                          reads=[ps_.b], writes=[pe_.b])
                    for j in range(2):
                        kt = kp * 2 + j
                        kk.op("pe", lambda e, po_=po_, pe_=pe_, j=j, kt=kt, kv=kv: e.matmul(
                            po_[0:65, :], VA[:, kt, kv * 65:(kv + 1) * 65], pe_[:, j, :],
                            start=(kt == 0), stop=(kt == nkt - 1)), reads=[VA.b, pe_.b], writes=[po_.b])
                o_ = osb[h % 2]
                r_ = rc[h % 2]
                kk.op("dve", lambda e, o_=o_, po_=po_: e.tensor_copy(out=o_[0:65, :], in_=po_[0:65, :]), reads=[po_.b], writes=[o_.b])
                kk.op("dve", lambda e, o_=o_, r_=r_: e.reciprocal(out=r_[64:65, :], in_=o_[64:65, :]), reads=[o_.b], writes=[r_.b])
                kk.op("pe", lambda e, r_=r_: e.matmul(pbc[:], ones_f[64:65, 0:64], r_[64:65, :], start=True, stop=True),
                      reads=[r_.b, ones_f.b], writes=[pbc.b])
                kk.op("dve", lambda e, o_=o_: e.tensor_tensor(out=o_[0:64, :], in0=o_[0:64, :], in1=pbc[:], op=ALU.mult),
                      reads=[o_.b, pbc.b], writes=[o_.b])
                kk.op("dve", lambda e, o_=o_, og_=og_, h=h, z_=z_: e.tensor_tensor(out=og_[:, h, :], in0=o_[0:64, :], in1=z_[:, h, :], op=ALU.mult),
                      reads=[o_.b, z_.b], writes=[og_.b])
            kk.op("pool", lambda e, og_=og_, c0=c0: e.dma_start(
                out=mixdst[512:768, c0:c0 + 512].rearrange("(h p) s -> p h s", p=64), in_=og_[:]),
                reads=[og_.b], dma=f"st{qb % 2}")


def phase_fm(nc, kk, dr, l, si, S, ident):
    with ExitStack() as st:
        projT = dr[f"projT{si}"]
        swf = sb(nc, st, "swf", [128, 4, 128], F32)
        sw = sb(nc, st, "sw", [128, 4, 128], BF16)
        sbT = sb(nc, st, "sbT", [128, 2, 128], F32)
        cw = sb(nc, st, "cw", [128, 6, 5], F32)
        pwf = sb(nc, st, "pwf", [128, 2, 128], F32)
        pw = sb(nc, st, "pw", [128, 2, 128], BF16)
        psc = sb(nc, st, "psc", [128, 2], F32)
        kk.op("sp", lambda e: e.dma_start(out=swf[:], in_=dr["sgu_wT"][l]), writes=[swf.b], dma="c0")
        kk.op("sp", lambda e: e.dma_start(out=sbT[:], in_=dr["sgu_bT"][l]), writes=[sbT.b], dma="c0")
        kk.op("sp", lambda e: e.dma_start(out=cw[:], in_=dr["conv_w"][l]), writes=[cw.b], dma="c0")
        kk.op("sp", lambda e: e.dma_start(out=pwf[:], in_=dr["pool_w"][l]), writes=[pwf.b], dma="c0")
        kk.op("sp", lambda e: e.dma_start(out=psc[:], in_=dr["pool_s"][l]), writes=[psc.b], dma="c0")
        kk.op("dve", lambda e: e.tensor_copy(out=sw[:], in_=swf[:]), reads=[swf.b], writes=[sw.b])
        kk.op("dve", lambda e: e.tensor_copy(out=pw[:], in_=pwf[:]), reads=[pwf.b], writes=[pw.b])

        NS = 2
        av = [sb(nc, st, f"av{i}", [128, 6, 512], BF16) for i in range(NS)]
        dxz = [sb(nc, st, f"dxz{i}", [128, 4, 528], BF16) for i in range(NS)]
        icn = [sb(nc, st, f"icn{i}", [128, 2, 512], F32) for i in range(NS)]
        bx = [sb(nc, st, f"bx{i}", [128, 6, 516], BF16) for i in range(NS)]
        pvt = ps(nc, st, "pvt", [128, 2, 128], BF16)
        vtok = sb(nc, st, "vtok", [128, 4, 64], F32)
        vsq = sb(nc, st, "vsq", [128, 4, 64], F32)
        vss = sb(nc, st, "vss", [128, 4], F32)
        vnm = [sb(nc, st, f"vnm{i}", [128, 4, 128], BF16) for i in range(2)]
        for v_ in vnm:
            kk.op("pool", lambda e, v_=v_: e.memset(v_[:], 0.0), writes=[v_.b])
        pm = [ps(nc, st, f"pm{i}", [128, 2, 512], F32) for i in range(1)]
        ta = sb(nc, st, "ta", [128, 2, 512], F32)
        sz = sb(nc, st, "sz", [128, 2, 512], BF16)
        ma = [sb(nc, st, f"ma{i}", [128, 2, 512], BF16) for i in range(NS)]
        ss = sb(nc, st, "ss", [128, 2, 528], F32)
        s2 = sb(nc, st, "s2", [128, 2, 528], F32)
        s4 = sb(nc, st, "s4", [128, 2, 528], F32)
        s8 = sb(nc, st, "s8", [128, 528], F32)
        dfb = sb(nc, st, "dfb", [128, 2, 512], BF16)
        ppl = ps(nc, st, "ppl", [128, 2, 512], F32)
        md = [sb(nc, st, f"md{i}", [128, 2, 512], BF16) for i in range(NS)]
        szd = sb(nc, st, "szd", [128, 2, 512], BF16)
        cacc = sb(nc, st, "cacc", [128, 512], F32)
        cvo = [sb(nc, st, f"cvo{i}", [128, 6, 512], BF16) for i in range(NS)]

        nblk = S // 512
        for tb in range(nblk):
            s = tb % NS
            c0 = tb * 512
            a_, d_, i_, b_ = av[s], dxz[s], icn[s], bx[s]
            kk.op("sp", lambda e, a_=a_, c0=c0: e.dma_start(out=a_[:], in_=projT[0:768, c0:c0 + 512].rearrange("(g p) s -> p g s", p=128)),
                  writes=[a_.b], dma=f"x{s}")
            lo = max(c0 - 8, 0)
            hi = min(c0 + 520, S)
            if lo > c0 - 8:
                kk.op("pool", lambda e, d_=d_: e.memset(d_[:, 0:2, 0:8], 0.0), writes=[d_.b])
            if hi < c0 + 520:
                kk.op("pool", lambda e, d_=d_: e.memset(d_[:, 0:2, 520:528], 0.0), writes=[d_.b])
            kk.op("sp", lambda e, d_=d_, lo=lo, hi=hi, c0=c0: e.dma_start(
                out=d_[:, 0:2, lo - (c0 - 8):hi - (c0 - 8)], in_=projT[G_DX * 128:(G_DX + 2) * 128, lo:hi].rearrange("(g p) s -> p g s", p=128)),
                writes=[d_.b], dma=f"x{s}")
            kk.op("sp", lambda e, d_=d_, c0=c0: e.dma_start(
                out=d_[:, 2:4, 0:512], in_=projT[G_DZ * 128:(G_DZ + 2) * 128, c0:c0 + 512].rearrange("(g p) s -> p g s", p=128)),
                writes=[d_.b], dma=f"x{s}")
            kk.op("sp", lambda e, i_=i_, c0=c0: e.dma_start(out=i_[:], in_=dr[f"icnt{si}"][:, :, c0:c0 + 512]), writes=[i_.b], dma=f"x{s}")
            lo2 = max(c0 - 2, 0)
            hi2 = min(c0 + 514, S)
            if lo2 > c0 - 2:
                kk.op("pool", lambda e, b_=b_: e.memset(b_[:, :, 0:2], 0.0), writes=[b_.b])
            if hi2 < c0 + 514:
                kk.op("pool", lambda e, b_=b_: e.memset(b_[:, :, 514:516], 0.0), writes=[b_.b])
            kk.op("sp", lambda e, b_=b_, lo2=lo2, hi2=hi2, c0=c0: e.dma_start(
                out=b_[:, :, lo2 - (c0 - 2):hi2 - (c0 - 2)], in_=projT[G_BQ * 128:(G_BQ + 6) * 128, lo2:hi2].rearrange("(g p) s -> p g s", p=128)),
                writes=[b_.b], dma=f"x{s}")

            pm_ = pm[0]
            for ch in range(4):
                vn_ = vnm[ch % 2]
                for j in range(2):
                    kk.op("pe", lambda e, a_=a_, j=j, ch=ch: e.transpose(out=pvt[:, j, :], in_=a_[:, 2 + j, ch * 128:(ch + 1) * 128], identity=ident[:]),
                          reads=[a_.b, ident.b], writes=[pvt.b])
                kk.op("dve", lambda e: e.tensor_copy(out=vtok[:].rearrange("p a b -> p (a b)"), in_=pvt[:].rearrange("p a b -> p (a b)")),
                      reads=[pvt.b], writes=[vtok.b])
                kk.op("dve", lambda e: e.tensor_tensor(out=vsq[:], in0=vtok[:], in1=vtok[:], op=ALU.mult), reads=[vtok.b], writes=[vsq.b])
                kk.op("dve", lambda e: e.tensor_reduce(out=vss[:], in_=vsq[:], axis=AX.X, op=ALU.add), reads=[vsq.b], writes=[vss.b])
                kk.op("dve", lambda e: e.tensor_scalar(out=vss[:], in0=vss[:], scalar1=1.0 / 64, scalar2=EPS, op0=ALU.mult, op1=ALU.add),
                      reads=[vss.b], writes=[vss.b])
                kk.op("act", lambda e: e.activation(out=vss[:], in_=vss[:], func=AF.Ln), reads=[vss.b], writes=[vss.b])
                kk.op("act", lambda e: e.activation(out=vss[:], in_=vss[:], func=AF.Exp, scale=-0.5), reads=[vss.b], writes=[vss.b])
                for h in range(4):
                    kk.op("dve", lambda e, vn_=vn_, h=h: e.tensor_scalar(
                        out=vn_[:, h, (h % 2) * 64:(h % 2) * 64 + 64], in0=vtok[:, h, :], scalar1=vss[:, h:h + 1], scalar2=1.0,
                        op0=ALU.mult, op1=ALU.mult), reads=[vtok.b, vss.b], writes=[vn_.b])
                for h in range(4):
                    kk.op("pe", lambda e, vn_=vn_, h=h, ch=ch, pm_=pm_: e.matmul(
                        pm_[:, h // 2, ch * 128:(ch + 1) * 128], vn_[:, h, :], sw[:, h, :], start=(h % 2 == 0), stop=(h % 2 == 1)),
                        reads=[vn_.b, sw.b], writes=[pm_.b])
            m_ = ma[s]
            kk.op("dve", lambda e, pm_=pm_: e.tensor_tensor(
                out=ta[:].rearrange("p a (c i) -> p a c i", c=4), in0=pm_[:].rearrange("p a (c i) -> p a c i", c=4),
                in1=sbT[:].unsqueeze(2).to_broadcast([128, 2, 4, 128]), op=ALU.add), reads=[pm_.b, sbT.b], writes=[ta.b])
            kk.op("act", lambda e, a_=a_: e.activation(out=sz[:], in_=a_[:, 4:6, :], func=AF.Silu), reads=[a_.b], writes=[sz.b])
            kk.op("pool", lambda e, a_=a_: e.tensor_tensor(out=ta[:], in0=ta[:], in1=a_[:, 0:2, :], op=ALU.mult), reads=[ta.b, a_.b], writes=[ta.b])
            kk.op("dve", lambda e, m_=m_: e.tensor_tensor(out=m_[:], in0=ta[:], in1=sz[:], op=ALU.mult), reads=[ta.b, sz.b], writes=[m_.b])
            kk.op("pool", lambda e, m_=m_, c0=c0: e.dma_start(out=dr[f"mixT{si}"][0:256, c0:c0 + 512].rearrange("(g p) s -> p g s", p=128), in_=m_[:]),
                  reads=[m_.b], dma=f"st{s}")

            X = d_
            kk.op("dve", lambda e, X=X: e.tensor_tensor(out=s2[:, :, 1:528], in0=X[:, 0:2, 0:527], in1=X[:, 0:2, 1:528], op=ALU.add),
                  reads=[X.b], writes=[s2.b])
            kk.op("pool", lambda e: e.tensor_tensor(out=s4[:, :, 2:527], in0=s2[:, :, 1:526], in1=s2[:, :, 3:528], op=ALU.add),
                  reads=[s2.b], writes=[s4.b])
            kk.op("dve", lambda e: e.tensor_tensor(out=s8[:, 4:525], in0=s4[:, 1, 2:523], in1=s4[:, 1, 6:527], op=ALU.add),
                  reads=[s4.b], writes=[s8.b])
            kk.op("pool", lambda e: e.tensor_copy(out=ss[0:64, 0, 8:520], in_=s2[0:64, 0, 8:520]), reads=[s2.b], writes=[ss.b])
            kk.op("pool", lambda e: e.tensor_copy(out=ss[64:128, 0, 8:520], in_=s4[64:128, 0, 8:520]), reads=[s4.b], writes=[ss.b])
            kk.op("dve", lambda e: e.tensor_copy(out=ss[0:64, 1, 8:520], in_=s8[0:64, 8:520]), reads=[s8.b], writes=[ss.b])
            kk.op("dve", lambda e: e.tensor_tensor(out=ss[64:128, 1, 8:520], in0=s8[64:128, 4:516], in1=s8[64:128, 12:524], op=ALU.add),
                  reads=[s8.b], writes=[ss.b])
            kk.op("dve", lambda e, i_=i_: e.tensor_tensor(out=ss[:, :, 8:520], in0=ss[:, :, 8:520], in1=i_[:], op=ALU.mult),
                  reads=[ss.b, i_.b], writes=[ss.b])
            kk.op("dve", lambda e, X=X: e.tensor_tensor(out=dfb[:], in0=ss[:, :, 8:520], in1=X[:, 0:2, 8:520], op=ALU.subtract),
                  reads=[ss.b, X.b], writes=[dfb.b])
            for ch in range(2):
                kk.op("pe", lambda e, ch=ch: e.matmul(ppl[:, ch, :], pw[:, ch, :], dfb[:, ch, :], start=True, stop=True),
                      reads=[pw.b, dfb.b], writes=[ppl.b])
            kk.op("act", lambda e, X=X: e.activation(out=szd[:], in_=X[:, 2:4, 0:512], func=AF.Silu), reads=[X.b], writes=[szd.b])
            o_ = md[s]
            for ch in range(2):
                kk.op("dve", lambda e, ch=ch, o_=o_: e.scalar_tensor_tensor(
                    out=o_[:, ch, :], in0=ppl[:, ch, :], scalar=psc[:, ch:ch + 1], in1=szd[:, ch, :], op0=ALU.mult, op1=ALU.mult),
                    reads=[ppl.b, psc.b, szd.b], writes=[o_.b])
            kk.op("pool", lambda e, o_=o_, c0=c0: e.dma_start(out=dr[f"mixT{si}"][768:1024, c0:c0 + 512].rearrange("(g p) s -> p g s", p=128), in_=o_[:]),
                  reads=[o_.b], dma=f"st{s}")

            co = cvo[s]
            for ch in range(6):
                kk.op("dve", lambda e, b_=b_, ch=ch: e.tensor_scalar(out=cacc[:], in0=b_[:, ch, 0:512], scalar1=cw[:, ch, 0:1], scalar2=1.0,
                                                                    op0=ALU.mult, op1=ALU.mult), reads=[b_.b, cw.b], writes=[cacc.b])
                for i in range(1, 5):
                    kk.op("dve", lambda e, b_=b_, ch=ch, i=i: e.scalar_tensor_tensor(
                        out=cacc[:], in0=b_[:, ch, i:i + 512], scalar=cw[:, ch, i:i + 1], in1=cacc[:], op0=ALU.mult, op1=ALU.add),
                        reads=[b_.b, cw.b, cacc.b], writes=[cacc.b])
                kk.op("act", lambda e, co=co, ch=ch: e.activation(out=co[:, ch, :], in_=cacc[:], func=AF.Silu), reads=[cacc.b], writes=[co.b])
            kk.op("pool", lambda e, co=co, c0=c0: e.dma_start(out=dr[f"convT{si}"][:, c0:c0 + 512].rearrange("(g p) s -> p g s", p=128), in_=co[:]),
                  reads=[co.b], dma=f"st{s}")


def bc(ap, shape, axis):
    return ap.unsqueeze(axis).to_broadcast(shape)


def phase_dn(nc, kk, dr, l, si, S, ident, ident_f, ones_f):
    with ExitStack() as st:
        NCH = S // 128
        tri = [sb(nc, st, f"tri{d}", [128, 128], F32) for d in range(2)]
        nmI = [sb(nc, st, f"nmI{d}", [128, 128], F32) for d in range(2)]
        nmS = [sb(nc, st, f"nmS{d}", [128, 128], F32) for d in range(2)]
        offd = sb(nc, st, "offd", [128, 128], F32)
        dnw = sb(nc, st, "dnw", [128, 4, 64], F32)
        kk.op("sp", lambda e: e.dma_start(out=tri[0][:], in_=dr["triU"][:, :]), writes=[tri[0].b], dma="c0")
        kk.op("sp", lambda e: e.dma_start(out=tri[1][:], in_=dr["triL"][:, :]), writes=[tri[1].b], dma="c0")
        kk.op("sp", lambda e: e.dma_start(out=dnw[:], in_=dr["dn_w"][l]), writes=[dnw.b], dma="c0")
        kk.op("dve", lambda e: e.tensor_scalar(out=offd[:], in0=ident_f[:], scalar1=-1.0, scalar2=1.0, op0=ALU.mult, op1=ALU.add),
              reads=[ident_f.b], writes=[offd.b])
        for d in range(2):
            kk.op("dve", lambda e, d=d: e.tensor_scalar(out=nmI[d][:], in0=tri[d][:], scalar1=-1.0, scalar2=1e30, op0=ALU.add, op1=ALU.mult),
                  reads=[tri[d].b], writes=[nmI[d].b])
            kk.op("dve", lambda e, d=d: e.tensor_tensor(out=nmS[d][:], in0=tri[d][:], in1=offd[:], op=ALU.mult),
                  reads=[tri[d].b, offd.b], writes=[nmS[d].b])
            kk.op("dve", lambda e, d=d: e.tensor_scalar(out=nmS[d][:], in0=nmS[d][:], scalar1=-1.0, scalar2=1e30, op0=ALU.add, op1=ALU.mult),
                  reads=[nmS[d].b], writes=[nmS[d].b])

        bkb = ps(nc, st, "bkb", [128, 1024], BF16)
        bk1 = ps(nc, st, "bk1", [128, 512], F32)
        bk2 = ps(nc, st, "bk2", [128, 512], F32)
        bk3 = ps(nc, st, "bk3", [128, 512], F32)
        bk4 = ps(nc, st, "bk4", [128, 512], F32)
        bk5 = ps(nc, st, "bk5", [128, 512], F32)
        bka = ps(nc, st, "bka", [128, 1024], F32)

        def v4(ap, n=4):
            return ap.rearrange("p (c i) -> p c i", c=n)

        cT = [sb(nc, st, f"cT{i}", [128, 6, 128], BF16) for i in range(2)]
        bgc = [sb(nc, st, f"bgc{i}", [128, 16], F32) for i in range(2)]
        ofl = [sb(nc, st, f"ofl{i}", [128, 256], F32) for i in range(2)]
        zb = [sb(nc, st, f"zb{i}", [128, 2, 128], BF16) for i in range(2)]
        tokL = [sb(nc, st, f"tok{i}", [128, 768], F32) for i in range(2)]
        sq = sb(nc, st, "dsq", [128, 512], F32)
        rs = sb(nc, st, "drs", [128, 8], F32)
        qkn = sb(nc, st, "qkn", [128, 8, 64], BF16)
        qkn32 = sb(nc, st, "qkn32", [128, 8, 64], F32)
        ek32 = sb(nc, st, "ek32", [128, 4, 64], F32)
        kd32L = [sb(nc, st, f"kd32{i}", [128, 4, 64], F32) for i in range(2)]
        ekT32L = [sb(nc, st, f"ekT32{i}", [64, 4, 128], F32) for i in range(2)]
        rb = sb(nc, st, "rb", [128, 4, 64], BF16)
        vnew32 = sb(nc, st, "vnew32", [128, 4, 64], F32)
        vb = sb(nc, st, "vb", [128, 256], BF16)
        qkT = sb(nc, st, "dqkT", [64, 8, 128], BF16)
        gs = sb(nc, st, "gs", [128, 8], F32)
        eg = sb(nc, st, "eg", [128, 4], F32)
        ekd = sb(nc, st, "ekd", [128, 4], F32)
        eglL = [sb(nc, st, f"egl{i}", [128, 4], F32) for i in range(2)]
        rg = sb(nc, st, "rg", [128, 4, 128], F32)
        dm = sb(nc, st, "dm", [128, 4, 128], F32)
        dmI = sb(nc, st, "dmI", [128, 4, 128], F32)
        dmS = sb(nc, st, "dmS", [128, 4, 128], F32)
        X = sb(nc, st, "X", [128, 4, 128], F32)
        ATL = [sb(nc, st, f"AT{i}", [128, 4, 128], BF16) for i in range(2)]
        YS = [sb(nc, st, f"YS{i}", [128, 4, 256], F32) for i in range(2)]
        YT = [sb(nc, st, f"YT{i}", [128, 4, 128], F32) for i in range(2)]
        db = sb(nc, st, "db", [128, 4, 128], F32)
        TTbL = [sb(nc, st, f"TTb{i}", [128, 4, 128], BF16) for i in range(2)]
        usb = sb(nc, st, "usb", [128, 4, 64], F32)
        ek = sb(nc, st, "ek", [128, 4, 64], BF16)
        kd = sb(nc, st, "kd", [128, 4, 64], BF16)
        qd = sb(nc, st, "qd", [128, 4, 64], BF16)
        wT = sb(nc, st, "wT", [64, 4, 128], BF16)
        qdTL = [sb(nc, st, f"qdT{i}", [64, 4, 128], BF16) for i in range(2)]
        vnew = sb(nc, st, "vnew", [128, 4, 64], BF16)
        Sf = sb(nc, st, "Sf", [64, 4, 64], F32)
        Sb = sb(nc, st, "Sb", [64, 4, 64], BF16)
        osb = [sb(nc, st, f"dosb{i}", [128, 256], F32) for i in range(2)]
        osq = sb(nc, st, "osq", [128, 256], F32)
        oss = sb(nc, st, "oss", [128, 4], F32)
        onb = sb(nc, st, "onb", [128, 256], BF16)
        omx = [sb(nc, st, f"omx{i}", [128, 2, 128], BF16) for i in range(2)]

        for d in range(2):
            if d == 1:
                kk.barrier()
            kk.op("pool", lambda e: e.memset(Sf[:], 0.0), writes=[Sf.b])
            kk.op("pool", lambda e: e.memset(Sb[:], 0.0), writes=[Sb.b])
            order = list(range(NCH)) if d == 0 else list(range(NCH - 1, -1, -1))

            def prep(n, ci, d=d):
                s = n % 2
                tok, AT, TTb, qdT, egl, kd32, ekT32 = tokL[s], ATL[s], TTbL[s], qdTL[s], eglL[s], kd32L[s], ekT32L[s]
                r0 = ci * 128
                c_, g_ = cT[s], bgc[s]
                yield
                kk.op("sp", lambda e, c_=c_, r0=r0: e.dma_start(out=c_[:], in_=dr[f"convT{si}"][:, r0:r0 + 128].rearrange("(g p) s -> p g s", p=128)),
                      writes=[c_.b], dma=f"x{s}")
                yield
                kk.op("sp", lambda e, g_=g_, r0=r0: e.dma_start(out=g_[:], in_=dr[f"bg{si}"][r0:r0 + 128, :]), writes=[g_.b], dma=f"x{s}")
                if d == 1:
                    kk.op("sp", lambda e, o_=ofl[s], r0=r0: e.dma_start(out=o_[:], in_=dr[f"of{si}"][r0:r0 + 128, :]), writes=[ofl[s].b], dma=f"x{s}")
                    kk.op("sp", lambda e, z_=zb[s], r0=r0: e.dma_start(
                        out=z_[:], in_=dr[f"projT{si}"][G_BZ * 128:(G_BZ + 2) * 128, r0:r0 + 128].rearrange("(g p) s -> p g s", p=128)),
                        writes=[zb[s].b], dma=f"x{s}")
                gd = g_[:, 8 + 4 * d:12 + 4 * d]
                bd = g_[:, 4 * d:4 * d + 4]
                yield
                for j in range(6):
                    kk.op("pe", lambda e, c_=c_, j=j: e.transpose(out=bkb[:, j * 128:(j + 1) * 128], in_=c_[:, j, :], identity=ident[:]),
                          reads=[c_.b, ident.b], writes=[bkb.b])
                yield
                kk.op("dve", lambda e: e.tensor_copy(out=tok[:], in_=bkb[:, 0:768]), reads=[bkb.b], writes=[tok.b])
                yield
                kk.op("dve", lambda e: e.tensor_tensor(out=sq[:], in0=tok[:, 0:512], in1=tok[:, 0:512], op=ALU.mult), reads=[tok.b], writes=[sq.b])
                yield
                kk.op("dve", lambda e: e.tensor_reduce(out=rs[:], in_=v4(sq[:], 8), axis=AX.X, op=ALU.add), reads=[sq.b], writes=[rs.b])
                yield
                kk.op("dve", lambda e: e.tensor_scalar(out=rs[:], in0=rs[:], scalar1=EPS, scalar2=1.0, op0=ALU.add, op1=ALU.mult),
                      reads=[rs.b], writes=[rs.b])
                yield
                kk.op("act", lambda e: e.activation(out=rs[:], in_=rs[:], func=AF.Ln), reads=[rs.b], writes=[rs.b])
                yield
                kk.op("act", lambda e: e.activation(out=rs[:], in_=rs[:], func=AF.Exp, scale=-0.5), reads=[rs.b], writes=[rs.b])
                yield
                kk.op("dve", lambda e: e.tensor_scalar(out=rs[:, 0:4], in0=rs[:, 0:4], scalar1=0.125, scalar2=1.0, op0=ALU.mult, op1=ALU.mult),
                      reads=[rs.b], writes=[rs.b])
                yield
                kk.op("dve", lambda e: e.tensor_tensor(out=qkn32[:], in0=v4(tok[:, 0:512], 8), in1=bc(rs[:], [128, 8, 64], 2), op=ALU.mult),
                      reads=[tok.b, rs.b], writes=[qkn32.b])
                yield
                kk.op("pool", lambda e: e.tensor_copy(out=qkn[:], in_=qkn32[:]), reads=[qkn32.b], writes=[qkn.b])
                yield
                for j in range(8):
                    kk.op("pe", lambda e, j=j: e.transpose(out=bkb[0:64, j * 128:(j + 1) * 128], in_=qkn[:, j, :], identity=ident[:]),
                          reads=[qkn.b, ident.b], writes=[bkb.b])
                yield
                kk.op("act", lambda e: e.activation(out=qkT[:].rearrange("p a b -> p (a b)"), in_=bkb[0:64, 0:1024], func=AF.Identity),
                      reads=[bkb.b], writes=[qkT.b])
                yield
                kk.op("pe", lambda e, gd=gd, d=d: e.matmul(bk4[:, 0:4], tri[d][:], gd, start=True, stop=True), reads=[tri[d].b, g_.b], writes=[bk4.b])
                yield
                kk.op("pe", lambda e, gd=gd: e.matmul(bk4[:, 4:8], ones_f[:], gd, start=True, stop=True), reads=[ones_f.b, g_.b], writes=[bk4.b])
                yield
                kk.op("dve", lambda e: e.tensor_copy(out=gs[:], in_=bk4[:, 0:8]), reads=[bk4.b], writes=[gs.b])
                yield
                kk.op("act", lambda e: e.activation(out=eg[:], in_=gs[:, 0:4], func=AF.Exp), reads=[gs.b], writes=[eg.b])
                yield
                kk.op("act", lambda e: e.activation(out=egl[:], in_=gs[:, 4:8], func=AF.Exp), reads=[gs.b], writes=[egl.b])
                yield
                kk.op("dve", lambda e: e.tensor_tensor(out=ekd[:], in0=gs[:, 4:8], in1=gs[:, 0:4], op=ALU.subtract), reads=[gs.b], writes=[ekd.b])
                yield
                kk.op("act", lambda e: e.activation(out=ekd[:], in_=ekd[:], func=AF.Exp), reads=[ekd.b], writes=[ekd.b])
                yield
                kk.op("dve", lambda e, gd=gd, d=d: e.tensor_tensor(out=rg[:], in0=bc(tri[d][:], [128, 4, 128], 1), in1=bc(gd, [128, 4, 128], 2), op=ALU.mult),
                      reads=[tri[d].b, g_.b], writes=[rg.b])
                yield
                kk.op("pe", lambda e: e.matmul(bk3[:], ones_f[:], rg[:].rearrange("p a b -> p (a b)"), start=True, stop=True),
                      reads=[ones_f.b, rg.b], writes=[bk3.b])
                yield
                kk.op("dve", lambda e: e.tensor_tensor(out=dm[:], in0=v4(bk3[:]), in1=bc(gs[:, 0:4], [128, 4, 128], 2), op=ALU.subtract),
                      reads=[bk3.b, gs.b], writes=[dm.b])
                yield
                kk.op("dve", lambda e, d=d: e.tensor_tensor(out=dmI[:], in0=dm[:], in1=bc(nmI[d][:], [128, 4, 128], 1), op=ALU.add),
                      reads=[dm.b, nmI[d].b], writes=[dmI.b])
                yield
                kk.op("pool", lambda e, d=d: e.tensor_tensor(out=dmS[:], in0=dm[:], in1=bc(nmS[d][:], [128, 4, 128], 1), op=ALU.add),
                      reads=[dm.b, nmS[d].b], writes=[dmS.b])
                yield
                kk.op("act", lambda e: e.activation(out=dmI[:], in_=dmI[:], func=AF.Exp), reads=[dmI.b], writes=[dmI.b])
                yield
                kk.op("act", lambda e: e.activation(out=dmS[:], in_=dmS[:], func=AF.Exp), reads=[dmS.b], writes=[dmS.b])
                yield
                for h in range(4):
                    kTh = qkT[:, 4 + h, :]
                    qTh = qkT[:, h, :]
                    kk.op("pe", lambda e, h=h, kTh=kTh: e.matmul(bk4[:, h * 128:(h + 1) * 128], kTh, kTh, start=True, stop=True),
                          reads=[qkT.b], writes=[bk4.b])
                    kk.op("pe", lambda e, h=h, kTh=kTh, qTh=qTh: e.matmul(bk5[:, h * 128:(h + 1) * 128], kTh, qTh, start=True, stop=True),
                          reads=[qkT.b], writes=[bk5.b])
                yield
                for h in range(4):
                    kk.op("dve", lambda e, h=h, bd=bd: e.scalar_tensor_tensor(
                        out=X[:, h, :], in0=bk4[:, h * 128:(h + 1) * 128], scalar=bd[:, h:h + 1], in1=dmS[:, h, :], op0=ALU.mult, op1=ALU.mult),
                        reads=[bk4.b, g_.b, dmS.b], writes=[X.b])
                yield
                kk.op("dve", lambda e: e.tensor_tensor(out=AT[:], in0=v4(bk5[:]), in1=dmI[:], op=ALU.mult), reads=[bk5.b, dmI.b], writes=[AT.b])
                pbv = v4(bk4[:])
                yield
                for h in range(4):
                    kk.op("pe", lambda e, h=h: e.transpose(out=pbv[:, h, :], in_=X[:, h, :], identity=ident_f[:]),
                          reads=[X.b, ident_f.b], writes=[bk4.b])
                yield
                kk.op("dve", lambda e: e.tensor_copy(out=YT[1][:], in_=pbv), reads=[bk4.b], writes=[YT[1].b])
                pa = bka[:].rearrange("p (c i) -> p c i", c=4)
                yield
                for h in range(4):
                    kk.op("pe", lambda e, h=h: e.matmul(pa[:, h, 0:128], YT[1][:, h, :], X[:, h, :], start=True, stop=True),
                          reads=[YT[1].b, X.b], writes=[bka.b])
                    kk.op("pe", lambda e, h=h: e.matmul(pbv[:, h, :], X[:, h, :], YT[1][:, h, :], start=True, stop=True),
                          reads=[YT[1].b, X.b], writes=[bk4.b])
                yield
                kk.op("dve", lambda e: e.tensor_tensor(out=YS[0][:, :, 128:256], in0=bc(ident_f[:], [128, 4, 128], 1), in1=X[:], op=ALU.subtract),
                      reads=[ident_f.b, X.b], writes=[YS[0].b])
                yield
                kk.op("act", lambda e: e.activation(out=YS[0][:, :, 0:128], in_=pa[:, :, 0:128], func=AF.Identity), reads=[bka.b], writes=[YS[0].b])
                yield
                kk.op("dve", lambda e: e.tensor_copy(out=YT[0][:], in_=pbv), reads=[bk4.b], writes=[YT[0].b])
                cur = 0
                yield
                for k in range(1, 7):
                    ys, yt = YS[cur], YT[cur]
                    nys, nyt = YS[1 - cur], YT[1 - cur]
                    for h in range(4):
                        kk.op("pe", lambda e, h=h, ys=ys, yt=yt: e.matmul(pa[:, h, :], yt[:, h, :], ys[:, h, :], start=True, stop=True),
                              reads=[ys.b, yt.b], writes=[bka.b])
                        if k < 6:
                            kk.op("pe", lambda e, h=h, ys=ys, yt=yt: e.matmul(pbv[:, h, :], ys[:, h, 0:128], yt[:, h, :], start=True, stop=True),
                                  reads=[ys.b, yt.b], writes=[bk4.b])
                    kk.op("dve", lambda e, ys=ys, nys=nys: e.tensor_tensor(out=nys[:, :, 128:256], in0=pa[:, :, 128:256], in1=ys[:, :, 128:256], op=ALU.add),
                          reads=[bka.b, ys.b], writes=[nys.b])
                    if k < 6:
                        kk.op("act", lambda e, nys=nys: e.activation(out=nys[:, :, 0:128], in_=pa[:, :, 0:128], func=AF.Identity),
                              reads=[bka.b], writes=[nys.b])
                        kk.op("dve", lambda e, nyt=nyt: e.tensor_copy(out=nyt[:], in_=pbv), reads=[bk4.b], writes=[nyt.b])
                    cur = 1 - cur
                TT = YS[cur]
                yield
                kk.op("pool", lambda e, bd=bd: e.tensor_tensor(out=db[:], in0=bc(ident_f[:], [128, 4, 128], 1), in1=bc(bd, [128, 4, 128], 2), op=ALU.mult),
                      reads=[ident_f.b, g_.b], writes=[db.b])
                yield
                kk.op("pe", lambda e: e.matmul(bk3[:], ones_f[:], db[:].rearrange("p a b -> p (a b)"), start=True, stop=True),
                      reads=[ones_f.b, db.b], writes=[bk3.b])
                yield
                kk.op("dve", lambda e, TT=TT: e.tensor_tensor(out=TTb[:], in0=v4(bk3[:]), in1=TT[:, :, 128:256], op=ALU.mult),
                      reads=[bk3.b, TT.b], writes=[TTb.b])
                yield
                kk.op("dve", lambda e: e.tensor_tensor(out=ek32[:], in0=qkn32[:, 4:8, :], in1=bc(eg[:], [128, 4, 64], 2), op=ALU.mult),
                      reads=[qkn32.b, eg.b], writes=[ek32.b])
                yield
                kk.op("pool", lambda e: e.tensor_tensor(out=kd32[:], in0=qkn32[:, 4:8, :], in1=bc(ekd[:], [128, 4, 64], 2), op=ALU.mult),
                      reads=[qkn32.b, ekd.b], writes=[kd32.b])
                yield
                kk.op("pool", lambda e: e.tensor_tensor(out=qd[:], in0=qkn32[:, 0:4, :], in1=bc(eg[:], [128, 4, 64], 2), op=ALU.mult),
                      reads=[qkn32.b, eg.b], writes=[qd.b])
                pek = bk5[0:64, :].rearrange("p (c i) -> p c i", c=4)
                yield
                for h in range(4):
                    kk.op("pe", lambda e, h=h: e.transpose(out=pek[:, h, :], in_=ek32[:, h, :], identity=ident_f[:]),
                          reads=[ek32.b, ident_f.b], writes=[bk5.b])
                yield
                kk.op("dve", lambda e: e.tensor_copy(out=ekT32[:], in_=pek), reads=[bk5.b], writes=[ekT32.b])
                yield
                for h in range(4):
                    kk.op("pe", lambda e, h=h: e.transpose(out=bkb[0:64, h * 128:(h + 1) * 128], in_=qd[:, h, :], identity=ident[:]),
                          reads=[qd.b, ident.b], writes=[bkb.b])
                yield
                kk.op("dve", lambda e: e.tensor_copy(out=qdT[:].rearrange("p a b -> p (a b)"), in_=bkb[0:64, 0:512]), reads=[bkb.b], writes=[qdT.b])

            def scan(n, ci, d=d):
                s = n % 2
                r0 = ci * 128
                tok, AT, TTb, qdT, egl, kd32, ekT32 = tokL[s], ATL[s], TTbL[s], qdTL[s], eglL[s], kd32L[s], ekT32L[s]
                pws = bk1[:, 0:256].rearrange("p (c i) -> p c i", c=4)
                pout = bk1[:, 256:512].rearrange("p (c i) -> p c i", c=4)
                pds = bk2[0:64, 0:256].rearrange("p (c i) -> p c i", c=4)
                pv2 = bk2[:, 256:512].rearrange("p (c i) -> p c i", c=4)
                yield
                for h in range(4):
                    kk.op("pe", lambda e, h=h: e.matmul(pws[:, h, :], ekT32[:, h, :], Sf[:, h, :], start=True, stop=True),
                          reads=[ekT32.b, Sf.b], writes=[bk1.b])
                yield
                kk.op("dve", lambda e: e.tensor_tensor(out=rb[:], in0=v4(tok[:, 512:768]), in1=pws, op=ALU.subtract),
                      reads=[tok.b, bk1.b], writes=[rb.b])
                yield
                for h in range(4):
                    kk.op("pe", lambda e, h=h: e.matmul(pv2[:, h, :], TTb[:, h, :], rb[:, h, :], start=True, stop=True),
                          reads=[TTb.b, rb.b], writes=[bk2.b])
                yield
                kk.op("dve", lambda e: e.tensor_copy(out=vnew32[:], in_=pv2), reads=[bk2.b], writes=[vnew32.b])
                yield
                kk.op("dve", lambda e: e.tensor_copy(out=vnew[:], in_=pv2), reads=[bk2.b], writes=[vnew.b])
                yield
                for h in range(4):
                    kk.op("pe", lambda e, h=h: e.matmul(pout[:, h, :], qdT[:, h, :], Sb[:, h, :], start=True, stop=False),
                          reads=[qdT.b, Sb.b], writes=[bk1.b])
                    kk.op("pe", lambda e, h=h: e.matmul(pout[:, h, :], AT[:, h, :], vnew[:, h, :], start=False, stop=True),
                          reads=[AT.b, vnew.b], writes=[bk1.b])
                yield
                for h in range(4):
                    kk.op("pe", lambda e, h=h: e.matmul(pds[:, h, :], kd32[:, h, :], vnew32[:, h, :], start=True, stop=True),
                          reads=[kd32.b, vnew32.b], writes=[bk2.b])
                yield
                kk.op("dve", lambda e: e.tensor_tensor(out=Sf[:], in0=Sf[:], in1=bc(egl[0:64, :], [64, 4, 64], 2), op=ALU.mult),
                      reads=[Sf.b, egl.b], writes=[Sf.b])
                yield
                kk.op("dve", lambda e: e.tensor_tensor(out=Sf[:], in0=Sf[:], in1=pds, op=ALU.add), reads=[Sf.b, bk2.b], writes=[Sf.b])
                yield
                kk.op("act", lambda e: e.activation(out=Sb[:], in_=Sf[:], func=AF.Identity), reads=[Sf.b], writes=[Sb.b])
                o_ = osb[s]
                if d == 0:
                    kk.op("dve", lambda e, o_=o_: e.tensor_copy(out=o_[:], in_=bk1[:, 256:512]), reads=[bk1.b], writes=[o_.b])
                    kk.op("pool", lambda e, o_=o_, r0=r0: e.dma_start(out=dr[f"of{si}"][r0:r0 + 128, :], in_=o_[:]), reads=[o_.b], dma=f"st{s}")
                else:
                    kk.op("dve", lambda e, o_=o_, f_=ofl[s]: e.tensor_tensor(out=o_[:], in0=bk1[:, 256:512], in1=f_[:], op=ALU.add),
                          reads=[bk1.b, ofl[s].b], writes=[o_.b])
                    if f"osum{si}" in dr:
                        kk.op("pool", lambda e, o_=o_, r0=r0: e.dma_start(out=dr[f"osum{si}"][r0:r0 + 128, :], in_=o_[:]), reads=[o_.b], dma=f"st{s}")
                    kk.op("pool", lambda e, o_=o_: e.tensor_tensor(out=osq[:], in0=o_[:], in1=o_[:], op=ALU.mult), reads=[o_.b], writes=[osq.b])
                    kk.op("dve", lambda e: e.tensor_reduce(out=oss[:], in_=v4(osq[:]), axis=AX.X, op=ALU.add), reads=[osq.b], writes=[oss.b])
                    kk.op("dve", lambda e: e.tensor_scalar(out=oss[:], in0=oss[:], scalar1=1.0 / 64, scalar2=EPS, op0=ALU.mult, op1=ALU.add),
                          reads=[oss.b], writes=[oss.b])
                    kk.op("act", lambda e: e.activation(out=oss[:], in_=oss[:], func=AF.Ln), reads=[oss.b], writes=[oss.b])
                    kk.op("act", lambda e: e.activation(out=oss[:], in_=oss[:], func=AF.Exp, scale=-0.5), reads=[oss.b], writes=[oss.b])
                    kk.op("dve", lambda e, o_=o_: e.tensor_tensor(out=v4(o_[:]), in0=v4(o_[:]), in1=bc(oss[:], [128, 4, 64], 2), op=ALU.mult),
                          reads=[o_.b, oss.b], writes=[o_.b])
                    kk.op("pool", lambda e, o_=o_: e.tensor_tensor(out=onb[:], in0=o_[:], in1=dnw[:].rearrange("p a b -> p (a b)"), op=ALU.mult),
                          reads=[o_.b, dnw.b], writes=[onb.b])
                    for j in range(2):
                        kk.op("pe", lambda e, j=j: e.transpose(out=bkb[:, j * 128:(j + 1) * 128], in_=onb[:, j * 128:(j + 1) * 128], identity=ident[:]),
                              reads=[onb.b, ident.b], writes=[bkb.b])
                    z_ = zb[s]
                    m_ = omx[s]
                    kk.op("act", lambda e, z_=z_: e.activation(out=z_[:], in_=z_[:], func=AF.Silu), reads=[z_.b], writes=[z_.b])
                    kk.op("dve", lambda e, z_=z_, m_=m_: e.tensor_tensor(out=m_[:], in0=bkb[:, 0:256].rearrange("p (a b) -> p a b", a=2), in1=z_[:], op=ALU.mult),
                          reads=[bkb.b, z_.b], writes=[m_.b])
                    kk.op("pool", lambda e, m_=m_, r0=r0: e.dma_start(
                        out=dr[f"mixT{si}"][256:512, r0:r0 + 128].rearrange("(g p) s -> p g s", p=128), in_=m_[:]), reads=[m_.b], dma=f"st{s}")


            for _ in prep(0, order[0]):
                pass
            for n in range(NCH):
                gsc = scan(n, order[n])
                gpr = prep(n + 1, order[n + 1]) if n + 1 < NCH else iter(())
                alive_s = alive_p = True
                while alive_s or alive_p:
                    for _ in range(3):
                        if alive_p and next(gpr, "END") == "END":
                            alive_p = False
                    if alive_s and next(gsc, "END") == "END":
                        alive_s = False


POOL_WINDOWS = (2, 4, 8, 16)


def host_layout(seqs, depth, norm_w, w_in, sgu_w, sgu_b, conv_w, a_log, dt_bias, dn_norm_w,
                q_norm_w, k_norm_w, pool_w, pool_scale, w_out):
    f = np.float32
    m = {}
    m["w_in"] = np.ascontiguousarray(np.asarray(w_in, f).reshape(depth, 8, 128, NCOL).transpose(0, 2, 1, 3)[..., COL_PERM])
    m["norm_w"] = np.ascontiguousarray(np.asarray(norm_w, f).reshape(depth, 8, 128).transpose(0, 2, 1))
    m["w_out"] = np.ascontiguousarray(np.asarray(w_out, f).reshape(depth, 8, 128, D).transpose(0, 2, 1, 3))
    qk = np.concatenate([np.repeat(np.asarray(q_norm_w, f)[:, None, :], 4, 1), np.repeat(np.asarray(k_norm_w, f)[:, None, :], 2, 1)], 1)
    m["qkw"] = np.ascontiguousarray(np.broadcast_to(qk[:, None], (depth, 128, 6, 64)))
    m["a_log"] = np.ascontiguousarray(np.broadcast_to(np.asarray(a_log, f).reshape(depth, 1, 8), (depth, 128, 8)))
    m["dt_bias"] = np.ascontiguousarray(np.broadcast_to(np.asarray(dt_bias, f).reshape(depth, 1, 8), (depth, 128, 8)))
    m["ident"] = np.eye(128, dtype=f)
    m["sgu_wT"] = np.ascontiguousarray(np.asarray(sgu_w, f).transpose(0, 3, 1, 2))
    sb_ = np.asarray(sgu_b, f)
    sbT = np.zeros((depth, 128, 2, 128), f)
    for hp in range(2):
        for h2 in range(2):
            sbT[:, h2 * 64:(h2 + 1) * 64, hp, :] = sb_[:, hp * 2 + h2, None, :]
    m["sgu_bT"] = sbT
    m["conv_w"] = np.ascontiguousarray(np.asarray(conv_w, f).reshape(depth, 5, 6, 128).transpose(0, 3, 2, 1))
    pw = np.zeros((depth, 128, 2, 128), f)
    pwi = np.asarray(pool_w, f)
    for ch in range(2):
        for g2 in range(2):
            pw[:, g2 * 64:(g2 + 1) * 64, ch, g2 * 64:(g2 + 1) * 64] = pwi[:, ch * 2 + g2]
    m["pool_w"] = pw
    m["pool_s"] = np.ascontiguousarray(np.asarray(pool_scale, f).reshape(depth, 2, 128).transpose(0, 2, 1))
    m["dn_w"] = np.ascontiguousarray(np.broadcast_to(np.asarray(dn_norm_w, f)[:, None, None, :], (depth, 128, 4, 64)))
    k_ = np.arange(128)
    m["triU"] = (k_[:, None] <= k_[None, :]).astype(f)
    m["triL"] = (k_[:, None] >= k_[None, :]).astype(f)
    for i, S in enumerate(seqs):
        c, s_ = rope_tables(S)
        m[f"cos{i}"] = c
        m[f"sin{i}"] = s_
        t = np.arange(S)
        ic = np.zeros((128, 2, S), f)
        for g, win in enumerate(POOL_WINDOWS):
            lo = np.clip(t - win // 2, 0, S)
            hi = np.clip(t + win // 2, 0, S)
            ic[(g % 2) * 64:(g % 2) * 64 + 64, g // 2, :] = (1.0 / (hi - lo).astype(f))[None, :]
        m[f"icnt{i}"] = ic
    return m


_NC_CACHE = {}


def kernel(x_prompt, x_sample, norm_w, w_in, sgu_w, sgu_b, conv_w, a_log, dt_bias, dn_norm_w,
           q_norm_w, k_norm_w, pool_w, pool_scale, w_out):
    x_prompt = np.asarray(x_prompt, np.float32)
    x_sample = np.asarray(x_sample, np.float32)
    depth = int(np.asarray(w_in).shape[0])
    seqs = [x_prompt.shape[1], x_sample.shape[1]]
    key = (tuple(seqs), depth)
    if key not in _NC_CACHE:
        _NC_CACHE[key] = build(seqs, depth, divs=(x_prompt.shape[0], x_sample.shape[0]),
                               groups=(8 // x_prompt.shape[0], 8 // x_sample.shape[0]))
    nc = _NC_CACHE[key]
    common = host_layout(seqs, depth, norm_w, w_in, sgu_w, sgu_b, conv_w, a_log, dt_bias, dn_norm_w,
                         q_norm_w, k_norm_w, pool_w, pool_scale, w_out)
    nb_p, nb_s = x_prompt.shape[0], x_sample.shape[0]
    in_maps = []
    for c in range(8):
        mm = dict(common)
        mm["x0"] = np.ascontiguousarray(x_prompt[c % nb_p])
        mm["x1"] = np.ascontiguousarray(x_sample[c % nb_s])
        in_maps.append(mm)
    res = run_bass_kernel_spmd(nc, in_maps, core_ids=list(range(8)))
    yp = np.zeros(x_prompt.shape, np.float32)
    ys = np.zeros(x_sample.shape, np.float32)
    gp, gs = 8 // nb_p, 8 // nb_s
    np_, ns_ = seqs[0] // gp, seqs[1] // gs
    for c in range(8):
        yp[c % nb_p, (c // nb_p) * np_:(c // nb_p + 1) * np_] = np.asarray(res.results[c]["y0"], np.float32)
        ys[c % nb_s, (c // nb_s) * ns_:(c // nb_s + 1) * ns_] = np.asarray(res.results[c]["y1"], np.float32)
    return (yp, ys)
```

```python
import numpy as np
import ml_dtypes
from contextlib import ExitStack
import concourse.bass as bass
import concourse.mybir as mybir
from concourse.bass_utils import run_bass_kernel_spmd

F32 = mybir.dt.float32
F32R = mybir.dt.float32r
BF16 = mybir.dt.bfloat16
AF = mybir.ActivationFunctionType
ALU = mybir.AluOpType
AX = mybir.AxisListType

D = 1024
NCOL = 3088
EPS = 1e-6
G_AU, G_AV, G_AZ, G_BQ, G_BK, G_BV, G_BZ, G_CQ, G_CK, G_CV, G_CZ, G_DX, G_DZ = \
    0, 2, 4, 6, 8, 10, 12, 14, 16, 17, 18, 20, 22
ORIG_OFF = dict(au=0, av=256, az=512, bq=768, bk=1024, bv=1280, bz=1536, bb=1792, ba=1800,
                cq=1808, ck=2064, cv=2192, cz=2320, dx=2576, dz=2832)
COL_PERM = np.concatenate([
    np.arange(0, 1792), np.arange(1808, 3088), np.arange(1792, 1808)])


class Buf:
    __slots__ = ("name", "w", "r")

    def __init__(self, name):
        self.name = name
        self.w = None
        self.r = {}


class K:
    ENG = ("pe", "act", "dve", "pool", "sp")

    def __init__(self, nc, stack):
        self.nc = nc
        self.sems = {}
        self.cnt = {}
        self.stack = stack
        for e in self.ENG:
            self._newsem(e)
        self.prog = {e: [] for e in self.ENG}
        self.waited = {e: {} for e in self.ENG}
        self.ninstr = 0

    def _newsem(self, key):
        self.sems[key] = self.stack.enter_context(self.nc.semaphore("s_" + key))
        self.cnt[key] = 0

    LIMIT = None

    def op(self, e, fn, reads=(), writes=(), dma=None):
        if K.LIMIT is not None and self.ninstr >= K.LIMIT:
            return
        need = {}

        def want(tok):
            if tok is None:
                return
            k, v = tok
            if k == "pe" and e == "pe" and dma is None:
                return
            if k not in self.ENG:
                v = self.cnt[k]
            if need.get(k, 0) < v:
                need[k] = v
        for b in reads:
            want(b.w)
        for b in writes:
            want(b.w)
            for k, v in b.r.items():
                want((k, v))
        waits = []
        wd = self.waited[e]
        for k, v in need.items():
            if wd.get(k, 0) < v:
                wd[k] = v
                waits.append((k, v))
        if dma is not None:
            if dma not in self.sems:
                self._newsem(dma)
            key, inc = dma, 16
        else:
            key, inc = e, 1
        self.cnt[key] += inc
        tok = (key, self.cnt[key])
        for b in reads:
            if b.r.get(key, 0) < tok[1]:
                b.r[key] = tok[1]
        for b in writes:
            b.w = tok
            b.r = {}
        self.prog[e].append((waits, fn, key, inc))
        self.ninstr += 1

    def barrier(self):
        tot = dict(self.cnt)
        for e in self.ENG:
            waits = []
            for k, v in tot.items():
                if v > 0 and self.waited[e].get(k, 0) < v:
                    self.waited[e][k] = v
                    waits.append((k, v))
            if waits:
                self.prog[e].append((waits, None, None, 0))

    def flush(self):
        nc = self.nc
        _DYN.clear()
        with nc.Block() as block:
            def run(e, eng):
                for waits, fn, key, inc in self.prog[e]:
                    for k, v in waits:
                        eng.wait_ge(self.sems[k], v)
                    if fn is not None:
                        fn(eng).then_inc(self.sems[key], inc)

            @block.tensor
            def _(eng):
                run("pe", eng)

            @block.scalar
            def _(eng):
                run("act", eng)

            @block.vector
            def _(eng):
                run("dve", eng)

            @block.gpsimd
            def _(eng):
                run("pool", eng)

            @block.sync
            def _(eng):
                run("sp", eng)
        self.prog = {e: [] for e in self.ENG}


class T:
    def __init__(self, t, name):
        self.t = t
        self.b = Buf(name)

    def __getitem__(self, idx):
        return self.t[idx]


_UID = [0]


def sb(nc, st, name, shape, dt):
    _UID[0] += 1
    nm = f"sb{_UID[0]}_{name}"
    return T(st.enter_context(nc.sbuf_tensor(nm, list(shape), dt)), nm)


def ps(nc, st, name, shape, dt):
    _UID[0] += 1
    nm = f"ps{_UID[0]}_{name}"
    return T(st.enter_context(nc.psum_tensor(nm, list(shape), dt)), nm)


def rope_tables(S):
    rows = np.repeat(np.arange(S // 64), 64)
    cols = np.tile(np.arange(64), S // 64)
    inv = np.power(np.float32(10000.0), -2.0 * np.arange(16, dtype=np.float32) / 32).astype(np.float32)
    ang = np.stack([rows, cols], -1).astype(np.float32)[:, :, None] * inv
    return np.cos(ang).astype(np.float32).reshape(S, 32), np.sin(ang).astype(np.float32).reshape(S, 32)


def build(seqs, depth, debug=False, phases="pfdao", divs=(2, 4), groups=(4, 2)):
    nc = bass.Bass("TRN2", target_bir_lowering=False)
    dr = {}

    def din(name, shape, dt=F32):
        dr[name] = nc.dram_tensor(name, list(shape), dt, kind="ExternalInput").ap()
        return dr[name]

    def dscr(name, shape, dt, out=False):
        kind = "ExternalOutput" if out else "Internal"
        dr[name] = nc.dram_tensor(name, list(shape), dt, kind=kind).ap()
        return dr[name]

    nseq = len(seqs)
    for i, S in enumerate(seqs):
        din(f"x{i}", [S, D])
        din(f"cos{i}", [S, 32])
        din(f"sin{i}", [S, 32])
        dscr(f"y{i}", [S // groups[i], D], F32, out=True)
        dscr(f"y1_{i}", [S, D], F32)
        dscr(f"projT{i}", [3072, S], BF16, out=debug)
        dscr(f"qT{i}", [256, S], BF16, out=debug)
        dscr(f"kT{i}", [128, S], BF16, out=debug)
        dscr(f"va{i}", [S, 130], BF16, out=debug)
        dscr(f"bg{i}", [S, 16], F32, out=debug)
        dscr(f"mixT{i}", [1024, S], BF16, out=debug)
        dscr(f"convT{i}", [768, S], BF16, out=debug)
        dscr(f"of{i}", [S, 256], F32, out=debug)
        npi = S // groups[i]
        dscr(f"qTp{i}", [256, npi], BF16)
        dscr(f"zTp{i}", [256, npi], BF16)
        dscr(f"mixTp{i}", [1024, npi], BF16)
        dscr(f"xp{i}", [npi, D], F32)
        if debug:
            dscr(f"osum{i}", [S, 256], F32, out=True)
    din("w_in", [depth, 128, 8, NCOL])
    din("norm_w", [depth, 128, 8])
    din("w_out", [depth, 128, 8, D])
    din("qkw", [depth, 128, 6, 64])
    din("a_log", [depth, 128, 8])
    din("dt_bias", [depth, 128, 8])
    din("ident", [128, 128])
    din("sgu_wT", [depth, 128, 4, 128])
    din("sgu_bT", [depth, 128, 2, 128])
    din("conv_w", [depth, 128, 6, 5])
    din("pool_w", [depth, 128, 2, 128])
    din("pool_s", [depth, 128, 2])
    din("dn_w", [depth, 128, 4, 64])
    din("triU", [128, 128])
    din("triL", [128, 128])
    for i, S in enumerate(seqs):
        din(f"icnt{i}", [128, 2, S])

    with ExitStack() as top:
        kk = K(nc, top)
        for key in ("c0", "wl0", "wl1", "x0", "x1", "st0", "st1", "sq0", "sq1", "sp0", "sp1"):
            kk._newsem(key)
        with nc.Block() as blk0:
            @blk0.vector
            def _(eng):
                for key in kk.sems:
                    eng.sem_clear(kk.sems[key])
        ident_f = sb(nc, top, "ident_f", [128, 128], F32)
        ident = sb(nc, top, "ident_b", [128, 128], BF16)
        ones_f = sb(nc, top, "ones_f", [128, 128], F32)
        kk.op("sp", lambda e: e.dma_start(out=ident_f[:], in_=dr["ident"][:, :]), writes=[ident_f.b], dma="c0")
        kk.op("dve", lambda e: e.tensor_copy(out=ident[:], in_=ident_f[:]), reads=[ident_f.b], writes=[ident.b])
        kk.op("pool", lambda e: e.memset(ones_f[:], 1.0), writes=[ones_f.b])

        for l in range(depth):
            for si, S in enumerate(seqs):
                xin = dr[f"x{si}"] if l == 0 else dr[f"y1_{si}"]
                last = (l == depth - 1)
                yout = dr[f"y{si}"] if last else dr[f"y1_{si}"]
                part = (S // groups[si], divs[si]) if last else None
                for ph in phases:
                    if ph == "p":
                        phase_proj(nc, kk, dr, l, si, S, xin, ident, ident_f)
                    elif ph == "f":
                        phase_fm(nc, kk, dr, l, si, S, ident)
                    elif ph == "d":
                        phase_dn(nc, kk, dr, l, si, S, ident, ident_f, ones_f)
                    elif ph == "a":
                        if part is not None:
                            phase_part(nc, kk, dr, si, S, xin, part)
                            kk.barrier()
                            kk.flush()
                        phase_attn(nc, kk, dr, l, si, S, ones_f, part)
                    elif ph == "o":
                        phase_out(nc, kk, dr, l, si, S, xin, yout, part)
                    kk.barrier()
                    kk.flush()
    return nc


def load_w_bf16(nc, kk, st, name, src_ap, ncols, scale_t=None, chunk=1024):
    w = sb(nc, st, name, [128, 8, ncols], BF16)
    stg = [sb(nc, st, f"{name}_stg{i}", [128, chunk], F32) for i in range(2)]
    n = 0
    for kc in range(8):
        for c0 in range(0, ncols, chunk):
            cw = min(chunk, ncols - c0)
            s = stg[n % 2]
            kk.op("sp", lambda e, s=s, kc=kc, c0=c0, cw=cw: e.dma_start(out=s[:, 0:cw], in_=src_ap[:, kc, c0:c0 + cw]),
                  writes=[s.b], dma=f"wl{n % 2}")
            if scale_t is not None:
                kk.op("dve", lambda e, s=s, kc=kc, c0=c0, cw=cw: e.tensor_scalar(
                    out=w[:, kc, c0:c0 + cw], in0=s[:, 0:cw], scalar1=scale_t[:, kc:kc + 1], scalar2=1.0,
                    op0=ALU.mult, op1=ALU.mult), reads=[s.b, scale_t.b], writes=[w.b])
            else:
                eng = "dve" if n % 2 == 0 else "pool"
                kk.op(eng, lambda e, s=s, kc=kc, c0=c0, cw=cw: e.tensor_copy(out=w[:, kc, c0:c0 + cw], in_=s[:, 0:cw]),
                      reads=[s.b], writes=[w.b])
            n += 1
    return w


def phase_proj(nc, kk, dr, l, si, S, xin, ident, ident_f):
    with ExitStack() as st:
        nw = sb(nc, st, "nw", [128, 8], F32)
        kk.op("sp", lambda e: e.dma_start(out=nw[:], in_=dr["norm_w"][l]), writes=[nw.b], dma="c0")
        w = load_w_bf16(nc, kk, st, "w_in_sb", dr["w_in"][l], NCOL, scale_t=nw)
        qkw = sb(nc, st, "qkw", [128, 6, 64], F32)
        kk.op("sp", lambda e: e.dma_start(out=qkw[:], in_=dr["qkw"][l]), writes=[qkw.b], dma="c0")
        alog = sb(nc, st, "alog", [128, 8], F32)
        nA = sb(nc, st, "nA", [128, 8], F32)
        dtb = sb(nc, st, "dtb", [128, 8], F32)
        kk.op("sp", lambda e: e.dma_start(out=alog[:], in_=dr["a_log"][l]), writes=[alog.b], dma="c0")
        kk.op("sp", lambda e: e.dma_start(out=dtb[:], in_=dr["dt_bias"][l]), writes=[dtb.b], dma="c0")
        kk.op("act", lambda e: e.activation(out=nA[:], in_=alog[:], func=AF.Exp), reads=[alog.b], writes=[nA.b])
        kk.op("dve", lambda e: e.tensor_scalar(out=nA[:], in0=nA[:], scalar1=-1.0, scalar2=1.0, op0=ALU.mult, op1=ALU.mult),
              reads=[nA.b], writes=[nA.b])

        NS = 2
        xt = [sb(nc, st, f"xt{i}", [128, D], F32) for i in range(NS)]
        junk = sb(nc, st, "junk", [128, D], BF16)
        hb = [sb(nc, st, f"hb{i}", [128, D], BF16) for i in range(NS)]
        ssq = [sb(nc, st, f"ssq{i}", [128, 1], F32) for i in range(NS)]
        rstd = [sb(nc, st, f"rstd{i}", [128, 1], F32) for i in range(NS)]
        pT = [ps(nc, st, f"pT{i}", [128, 8, 128], BF16) for i in range(2)]
        hT = [sb(nc, st, f"hT{i}", [128, 8, 512], BF16) for i in range(2)]
        pacc = [ps(nc, st, f"pacc{i}", [128, 512], F32) for i in range(3)]
        ptk = [ps(nc, st, f"ptk{i}", [128, 512], F32) for i in range(1)]
        pbg = [ps(nc, st, f"pbg{i}", [128, 512], F32) for i in range(1)]
        ptr = ps(nc, st, "ptr", [128, 3, 128], BF16)
        stage = [sb(nc, st, f"stage{i}", [128, 24, 512], BF16) for i in range(2)]
        cs = [sb(nc, st, f"cs{i}", [128, 2, 32], F32) for i in range(NS)]
        qk = [sb(nc, st, f"qk{i}", [128, 6, 64], F32) for i in range(NS)]
        qsq = sb(nc, st, "qsq", [128, 6, 64], F32)
        qss = [sb(nc, st, f"qss{i}", [128, 6], F32) for i in range(NS)]
        qr = [sb(nc, st, f"qr{i}", [128, 6, 64], F32) for i in range(NS)]
        tmpa = sb(nc, st, "tmpa", [128, 6, 2, 16], F32)
        tmpb = sb(nc, st, "tmpb", [128, 6, 2, 16], F32)
        qkb = [sb(nc, st, f"qkb{i}", [128, 384], BF16) for i in range(NS)]
        qkT = [sb(nc, st, f"qkT{i}", [128, 3, 512], BF16) for i in range(2)]
        va = [sb(nc, st, f"va{i}", [128, 130], BF16) for i in range(NS)]
        bgt = [sb(nc, st, f"bgt{i}", [128, 16], F32) for i in range(NS)]
        t8 = [sb(nc, st, f"t8{i}", [128, 16], F32) for i in range(NS)]
        for v_ in va:
            kk.op("pool", lambda e, v_=v_: e.memset(v_[:], 1.0), writes=[v_.b])

        projT = dr[f"projT{si}"]
        nblk = S // 512
        ev = 0
        for tb in range(nblk):
            hTb = hT[tb % 2]
            qkTb = qkT[tb % 2]
            for t4 in range(4):
                ti = tb * 4 + t4
                s = ti % NS
                r0 = ti * 128
                x_, h_, sq_, rs_ = xt[s], hb[s], ssq[s], rstd[s]
                kk.op("sp", lambda e, x_=x_, r0=r0: e.dma_start(out=x_[:], in_=xin[r0:r0 + 128, :]),
                      writes=[x_.b], dma=f"x{s}")
                kk.op("sp", lambda e, c_=cs[s], r0=r0: e.dma_start(out=c_[:, 0, :], in_=dr[f"cos{si}"][r0:r0 + 128, :]),
                      writes=[cs[s].b], dma=f"x{s}")
                kk.op("sp", lambda e, c_=cs[s], r0=r0: e.dma_start(out=c_[:, 1, :], in_=dr[f"sin{si}"][r0:r0 + 128, :]),
                      writes=[cs[s].b], dma=f"x{s}")
                kk.op("act", lambda e, x_=x_, sq_=sq_: e.activation(out=junk[:], in_=x_[:], func=AF.Square, accum_out=sq_[:]),
                      reads=[x_.b], writes=[junk.b, sq_.b])
                kk.op("dve", lambda e, sq_=sq_, rs_=rs_: e.tensor_scalar(out=rs_[:], in0=sq_[:], scalar1=1.0 / D, scalar2=EPS,
                                                                         op0=ALU.mult, op1=ALU.add), reads=[sq_.b], writes=[rs_.b])
                kk.op("act", lambda e, rs_=rs_: e.activation(out=rs_[:], in_=rs_[:], func=AF.Ln), reads=[rs_.b], writes=[rs_.b])
                kk.op("act", lambda e, rs_=rs_: e.activation(out=rs_[:], in_=rs_[:], func=AF.Exp, scale=-0.5), reads=[rs_.b], writes=[rs_.b])
                kk.op("dve", lambda e, x_=x_, h_=h_, rs_=rs_: e.tensor_scalar(out=h_[:], in0=x_[:], scalar1=rs_[:, 0:1], scalar2=1.0,
                                                                              op0=ALU.mult, op1=ALU.mult),
                      reads=[x_.b, rs_.b], writes=[h_.b])
                p_ = pT[ti % 2]
                for kc in range(8):
                    kk.op("pe", lambda e, p_=p_, h_=h_, kc=kc: e.transpose(out=p_[:, kc, :], in_=h_[:, kc * 128:(kc + 1) * 128],
                                                                          identity=ident[:]),
                          reads=[h_.b, ident.b], writes=[p_.b])
                kk.op("act", lambda e, p_=p_, hTb=hTb, t4=t4: e.activation(out=hTb[:, :, t4 * 128:(t4 + 1) * 128], in_=p_[:],
                                                                           func=AF.Identity),
                      reads=[p_.b], writes=[hTb.b])
                pk = ptk[0]
                for kc in range(8):
                    kk.op("pe", lambda e, pk=pk, hTb=hTb, t4=t4, kc=kc: e.matmul(
                        pk[:], hTb[:, kc, t4 * 128:(t4 + 1) * 128], w[:, kc, G_CQ * 128:G_CQ * 128 + 512],
                        start=(kc == 0), stop=(kc == 7)), reads=[hTb.b, w.b], writes=[pk.b])
                pb_ = pbg[0]
                for kc in range(8):
                    kk.op("pe", lambda e, pb_=pb_, hTb=hTb, t4=t4, kc=kc: e.matmul(
                        pb_[:, 0:16], hTb[:, kc, t4 * 128:(t4 + 1) * 128], w[:, kc, 3072:3088],
                        start=(kc == 0), stop=(kc == 7)), reads=[hTb.b, w.b], writes=[pb_.b])
                q_, ss_, r_, qb_, va_, c_ = qk[s], qss[s], qr[s], qkb[s], va[s], cs[s]
                kk.op("dve", lambda e, q_=q_, pk=pk: e.tensor_copy(out=q_[:].rearrange("p a b -> p (a b)"), in_=pk[:, 0:384]),
                      reads=[pk.b], writes=[q_.b])
                kk.op("dve", lambda e, va_=va_, pk=pk: e.tensor_copy(
                    out=va_[:].rearrange("p (a b) -> p a b", a=2)[:, :, 0:64],
                    in_=pk[:, 384:512].rearrange("p (a b) -> p a b", a=2)), reads=[pk.b], writes=[va_.b])
                kk.op("pool", lambda e, va_=va_, r0=r0: e.dma_start(out=dr[f"va{si}"][r0:r0 + 128, :], in_=va_[:]),
                      reads=[va_.b], dma=f"st{s}")
                kk.op("dve", lambda e, q_=q_: e.tensor_tensor(out=qsq[:], in0=q_[:], in1=q_[:], op=ALU.mult),
                      reads=[q_.b], writes=[qsq.b])
                kk.op("dve", lambda e, ss_=ss_: e.tensor_reduce(out=ss_[:], in_=qsq[:], axis=AX.X, op=ALU.add),
                      reads=[qsq.b], writes=[ss_.b])
                kk.op("dve", lambda e, ss_=ss_: e.tensor_scalar(out=ss_[:], in0=ss_[:], scalar1=1.0 / 64, scalar2=EPS,
                                                                op0=ALU.mult, op1=ALU.add), reads=[ss_.b], writes=[ss_.b])
                kk.op("act", lambda e, ss_=ss_: e.activation(out=ss_[:], in_=ss_[:], func=AF.Ln), reads=[ss_.b], writes=[ss_.b])
                kk.op("act", lambda e, ss_=ss_: e.activation(out=ss_[:], in_=ss_[:], func=AF.Exp, scale=-0.5), reads=[ss_.b], writes=[ss_.b])
                kk.op("dve", lambda e, q_=q_, ss_=ss_: e.tensor_tensor(
                    out=q_[:], in0=q_[:], in1=ss_[:].unsqueeze(2).to_broadcast([128, 6, 64]), op=ALU.mult),
                    reads=[q_.b, ss_.b], writes=[q_.b])
                kk.op("pool", lambda e, q_=q_: e.tensor_tensor(out=q_[:], in0=q_[:], in1=qkw[:], op=ALU.mult),
                      reads=[q_.b, qkw.b], writes=[q_.b])
                def v5(t):
                    return t[:].rearrange("p h (a b f) -> p h a b f", a=2, b=2)
                cosb = lambda c_: c_[:, 0, :].rearrange("p (a f) -> p a f", a=2).unsqueeze(1).to_broadcast([128, 6, 2, 16])
                sinb = lambda c_: c_[:, 1, :].rearrange("p (a f) -> p a f", a=2).unsqueeze(1).to_broadcast([128, 6, 2, 16])
                kk.op("dve", lambda e, q_=q_, c_=c_: e.tensor_tensor(out=tmpa[:], in0=v5(q_)[:, :, :, 1, :], in1=sinb(c_), op=ALU.mult),
                      reads=[q_.b, c_.b], writes=[tmpa.b])
                kk.op("pool", lambda e, q_=q_, c_=c_: e.tensor_tensor(out=tmpb[:], in0=v5(q_)[:, :, :, 0, :], in1=sinb(c_), op=ALU.mult),
                      reads=[q_.b, c_.b], writes=[tmpb.b])
                kk.op("dve", lambda e, q_=q_, r_=r_, c_=c_: e.tensor_tensor(out=v5(r_)[:, :, :, 0, :], in0=v5(q_)[:, :, :, 0, :], in1=cosb(c_), op=ALU.mult),
                      reads=[q_.b, c_.b], writes=[r_.b])
                kk.op("pool", lambda e, q_=q_, r_=r_, c_=c_: e.tensor_tensor(out=v5(r_)[:, :, :, 1, :], in0=v5(q_)[:, :, :, 1, :], in1=cosb(c_), op=ALU.mult),
                      reads=[q_.b, c_.b], writes=[r_.b])
                kk.op("dve", lambda e, r_=r_: e.tensor_tensor(out=v5(r_)[:, :, :, 0, :], in0=v5(r_)[:, :, :, 0, :], in1=tmpa[:], op=ALU.subtract),
                      reads=[r_.b, tmpa.b], writes=[r_.b])
                kk.op("dve", lambda e, r_=r_: e.tensor_tensor(out=v5(r_)[:, :, :, 1, :], in0=v5(r_)[:, :, :, 1, :], in1=tmpb[:], op=ALU.add),
                      reads=[r_.b, tmpb.b], writes=[r_.b])
                kk.op("act", lambda e, r_=r_, qb_=qb_: e.activation(out=qb_[:], in_=r_[:].rearrange("p a b -> p (a b)"), func=AF.Identity),
                      reads=[r_.b], writes=[qb_.b])
                for j in range(3):
                    kk.op("pe", lambda e, qb_=qb_, j=j: e.transpose(out=ptr[:, j, :], in_=qb_[:, j * 128:(j + 1) * 128], identity=ident[:]),
                          reads=[qb_.b, ident.b], writes=[ptr.b])
                kk.op("dve", lambda e, qkTb=qkTb, t4=t4: e.tensor_copy(out=qkTb[:, :, t4 * 128:(t4 + 1) * 128], in_=ptr[:]),
                      reads=[ptr.b], writes=[qkTb.b])
                b_, t_ = bgt[s], t8[s]
                kk.op("dve", lambda e, t_=t_, pb_=pb_: e.tensor_tensor(out=t_[:, 8:16], in0=pb_[:, 8:16], in1=dtb[:], op=ALU.add),
                      reads=[pb_.b, dtb.b], writes=[t_.b])
                kk.op("act", lambda e, t_=t_, pb_=pb_: e.activation(out=t_[:, 0:8], in_=pb_[:, 0:8], func=AF.Exp, scale=-1.0),
                      reads=[pb_.b], writes=[t_.b])
                kk.op("act", lambda e, t_=t_: e.activation(out=t_[:, 8:16], in_=t_[:, 8:16], func=AF.Exp),
                      reads=[t_.b], writes=[t_.b])
                kk.op("act", lambda e, t_=t_: e.activation(out=t_[:, 8:16], in_=t_[:, 8:16], func=AF.Ln, bias=1.0),
                      reads=[t_.b], writes=[t_.b])
                kk.op("dve", lambda e, t_=t_: e.tensor_scalar(out=t_[:, 0:8], in0=t_[:, 0:8], scalar1=1.0, scalar2=1.0, op0=ALU.add, op1=ALU.mult),
                      reads=[t_.b], writes=[t_.b])
                kk.op("dve", lambda e, t_=t_, b_=b_: e.reciprocal(out=b_[:, 0:8], in_=t_[:, 0:8]), reads=[t_.b], writes=[b_.b])
                kk.op("dve", lambda e, t_=t_, b_=b_: e.tensor_tensor(out=b_[:, 8:16], in0=t_[:, 8:16], in1=nA[:], op=ALU.mult),
                      reads=[t_.b, nA.b], writes=[b_.b])
                kk.op("pool", lambda e, b_=b_, r0=r0: e.dma_start(out=dr[f"bg{si}"][r0:r0 + 128, :], in_=b_[:]),
                      reads=[b_.b], dma=f"st{s}")
            c0 = tb * 512
            kk.op("pool", lambda e, qkTb=qkTb, c0=c0: e.dma_start(
                out=dr[f"qT{si}"][:, c0:c0 + 512].rearrange("(j p) s -> p j s", p=128), in_=qkTb[:, 0:2, :]),
                reads=[qkTb.b], dma=f"sq{tb % 2}")
            kk.op("pool", lambda e, qkTb=qkTb, c0=c0: e.dma_start(out=dr[f"kT{si}"][:, c0:c0 + 512], in_=qkTb[:, 2, :]),
                  reads=[qkTb.b], dma=f"sq{tb % 2}")
            stg = stage[tb % 2]
            for g in range(24):
                pa = pacc[g % 3]
                for kc in range(8):
                    kk.op("pe", lambda e, pa=pa, g=g, kc=kc, hTb=hTb: e.matmul(
                        pa[:], w[:, kc, g * 128:(g + 1) * 128], hTb[:, kc, :], start=(kc == 0), stop=(kc == 7)),
                        reads=[w.b, hTb.b], writes=[pa.b])
                if ev % 2 == 0:
                    kk.op("act", lambda e, pa=pa, g=g, stg=stg: e.activation(out=stg[:, g, :], in_=pa[:], func=AF.Identity),
                          reads=[pa.b], writes=[stg.b])
                else:
                    kk.op("dve", lambda e, pa=pa, g=g, stg=stg: e.tensor_copy(out=stg[:, g, :], in_=pa[:]),
                          reads=[pa.b], writes=[stg.b])
                ev += 1
            for h2 in range(2):
                kk.op("pool", lambda e, stg=stg, c0=c0, h2=h2: e.dma_start(
                    out=projT[h2 * 1536:(h2 + 1) * 1536, c0:c0 + 512].rearrange("(g p) s -> p g s", p=128),
                    in_=stg[:, h2 * 12:(h2 + 1) * 12, :]), reads=[stg.b], dma=f"sp{tb % 2}")


_DYN = {}


def dyn(e, part, base, n):
    if part is None:
        return slice(base, base + n)
    npart, div = part
    key = (id(e), div, npart)
    if key not in _DYN:
        _DYN[key] = e.snap((e.partition_id() // div) * npart)
    return bass.ds(_DYN[key] + base, n)


def phase_part(nc, kk, dr, si, S, xin, part):
    npart, div = part
    dummy = Buf("partcopy")

    def off(e):
        return e.snap((e.partition_id() // div) * npart)

    if si == 0:
        qe, xe, me = "sp", "sp", "pool"
    else:
        qe, xe, me = "pool", "act", "pool"
    kk.op(qe, lambda e: e.dma_start(out=dr[f"qTp{si}"][:, :], in_=dr[f"qT{si}"][:, bass.ds(off(e), npart)]), writes=[dummy], dma="c0")
    kk.op(qe, lambda e: e.dma_start(out=dr[f"zTp{si}"][:, :], in_=dr[f"projT{si}"][G_CZ * 128:(G_CZ + 2) * 128, bass.ds(off(e), npart)]),
          writes=[dummy], dma="c0")
    kk.op(me, lambda e: e.dma_start(out=dr[f"mixTp{si}"][0:512, :], in_=dr[f"mixT{si}"][0:512, bass.ds(off(e), npart)]), writes=[dummy], dma="c0")
    kk.op(me, lambda e: e.dma_start(out=dr[f"mixTp{si}"][768:1024, :], in_=dr[f"mixT{si}"][768:1024, bass.ds(off(e), npart)]),
          writes=[dummy], dma="c0")
    step = 1024
    for r in range(0, npart, step):
        n = min(step, npart - r)
        kk.op(xe, lambda e, r=r, n=n: e.dma_start(out=dr[f"xp{si}"][r:r + n, :], in_=xin[bass.ds(off(e) + r, n), :]), writes=[dummy], dma="c0")


def phase_out(nc, kk, dr, l, si, S, xin, yout, part=None):
    with ExitStack() as st:
        ntok = S if part is None else part[0]
        if part is not None:
            xin = dr[f"xp{si}"]
        wo = load_w_bf16(nc, kk, st, "w_out_sb", dr["w_out"][l], D)
        NS = 2
        mt = [sb(nc, st, f"mt{i}", [128, 8, 128], BF16) for i in range(NS)]
        xt = [sb(nc, st, f"xo{i}", [128, D], F32) for i in range(NS)]
        yt = [sb(nc, st, f"yo{i}", [128, D], F32) for i in range(NS)]
        po = [ps(nc, st, f"po{i}", [128, 512], F32) for i in range(4)]
        mixT = dr[f"mixT{si}"] if part is None else dr[f"mixTp{si}"]
        for ti in range(ntok // 128):
            s = ti % NS
            r0 = ti * 128
            m_, x_, y_ = mt[s], xt[s], yt[s]
            kk.op("sp", lambda e, m_=m_, r0=r0: e.dma_start(out=m_[:], in_=mixT[:, r0:r0 + 128].rearrange("(k p) s -> p k s", p=128)),
                  writes=[m_.b], dma=f"x{s}")
            kk.op("sp", lambda e, x_=x_, r0=r0: e.dma_start(out=x_[:], in_=xin[r0:r0 + 128, :]), writes=[x_.b], dma=f"x{s}")
            for g in range(2):
                p_ = po[(ti * 2 + g) % 4]
                for kc in range(8):
                    kk.op("pe", lambda e, p_=p_, m_=m_, kc=kc, g=g: e.matmul(
                        p_[:], m_[:, kc, :], wo[:, kc, g * 512:(g + 1) * 512], start=(kc == 0), stop=(kc == 7)),
                        reads=[m_.b, wo.b], writes=[p_.b])
                kk.op("dve", lambda e, p_=p_, x_=x_, y_=y_, g=g: e.tensor_tensor(
                    out=y_[:, g * 512:(g + 1) * 512], in0=p_[:], in1=x_[:, g * 512:(g + 1) * 512], op=ALU.add),
                    reads=[p_.b, x_.b], writes=[y_.b])
            kk.op("pool", lambda e, y_=y_, r0=r0: e.dma_start(out=yout[r0:r0 + 128, :], in_=y_[:]), reads=[y_.b], dma=f"st{s}")


def phase_attn(nc, kk, dr, l, si, S, ones_f, part=None):
    with ExitStack() as st:
        nkt = S // 128
        KT = sb(nc, st, "KT", [128, S], BF16)
        VA = sb(nc, st, "VA", [128, nkt, 130], BF16)
        for c in range(0, S, 2048):
            ce = min(c + 2048, S)
            kk.op("sp", lambda e, c=c, ce=ce: e.dma_start(out=KT[:, c:ce], in_=dr[f"kT{si}"][:, c:ce]), writes=[KT.b], dma="c0")
        for c in range(0, nkt, 16):
            ce = min(c + 16, nkt)
            kk.op("sp", lambda e, c=c, ce=ce: e.dma_start(out=VA[:, c:ce, :],
                                                          in_=dr[f"va{si}"][c * 128:ce * 128, :].rearrange("(t p) c -> p t c", p=128)),
                  writes=[VA.b], dma="c0")
        qt = [sb(nc, st, f"qt{i}", [128, 2, 512], BF16) for i in range(2)]
        zt = [sb(nc, st, f"zt{i}", [64, 4, 512], BF16) for i in range(2)]
        pss = [ps(nc, st, f"pss{i}", [128, 2, 512], F32) for i in range(2)]
        pex = [sb(nc, st, f"pex{i}", [128, 2, 512], BF16) for i in range(2)]
        pov = [ps(nc, st, f"pov{i}", [128, 512], F32) for i in range(2)]
        pbc = ps(nc, st, "pbc", [64, 512], F32)
        osb = [sb(nc, st, f"osb{i}", [128, 512], F32) for i in range(2)]
        rc = [sb(nc, st, f"rc{i}", [128, 512], F32) for i in range(2)]
        og = [sb(nc, st, f"og{i}", [64, 4, 512], BF16) for i in range(2)]
        it = 0
        nq = S if part is None else part[0]
        qsrc = dr[f"qT{si}"] if part is None else dr[f"qTp{si}"]
        zsrc = dr[f"projT{si}"][G_CZ * 128:(G_CZ + 2) * 128, :] if part is None else dr[f"zTp{si}"]
        mixdst = dr[f"mixT{si}"] if part is None else dr[f"mixTp{si}"]
        for qb in range(nq // 512):
            c0 = qb * 512
            q_ = qt[qb % 2]
            z_ = zt[qb % 2]
            og_ = og[qb % 2]
            for h in range(4):
                kv = h // 2
                kk.op("sp", lambda e, q_=q_, h=h, kv=kv, c0=c0: e.dma_start(
                    out=q_[kv * 64:(kv + 1) * 64, h % 2, :], in_=qsrc[h * 64:(h + 1) * 64, c0:c0 + 512]),
                    writes=[q_.b], dma=f"x{qb % 2}")
            kk.op("sp", lambda e, z_=z_, c0=c0: e.dma_start(
                out=z_[:], in_=zsrc[:, c0:c0 + 512].rearrange("(h p) s -> p h s", p=64)),
                writes=[z_.b], dma=f"x{qb % 2}")
            kk.op("act", lambda e, z_=z_: e.activation(out=z_[:], in_=z_[:], func=AF.Silu), reads=[z_.b], writes=[z_.b])
            for h in range(4):
                kv = h // 2
                po_ = pov[h % 2]
                npair = nkt // 2
                slots = []
                for kp in range(npair):
                    slots.append((pss[it % 2], pex[it % 2]))
                    it += 1

                def emit_qk(kp):
                    ps_ = slots[kp][0]
                    for j in range(2):
                        kt = kp * 2 + j
                        kk.op("pe", lambda e, ps_=ps_, j=j, kt=kt, kv=kv, q_=q_, h=h: e.matmul(
                            ps_[:, j, :], KT[kv * 64:(kv + 1) * 64, kt * 128:(kt + 1) * 128], q_[kv * 64:(kv + 1) * 64, h % 2, :],
                            start=True, stop=True), reads=[KT.b, q_.b], writes=[ps_.b])

                emit_qk(0)
                for kp in range(npair):
                    ps_, pe_ = slots[kp]
                    if kp + 1 < npair:
                        emit_qk(kp + 1)
                    kk.op("act", lambda e, ps_=ps_, pe_=pe_: e.activation(out=pe_[:], in_=ps_[:], func=AF.Exp, scale=0.125),
                          reads=[ps_.b], writes=[pe_.b])
                    for j in range(2):
                        kt = kp * 2 + j
                        kk.op("pe", lambda e, po_=po_, pe_=pe_, j=j, kt=kt, kv=kv: e.matmul(
                            po_[0:65, :], VA[:, kt, kv * 65:(kv + 1) * 65], pe_[:, j, :],
                            start=(kt == 0), stop=(kt == nkt - 1)), reads=[VA.b, pe_.b], writes=[po_.b])
                o_ = osb[h % 2]
                r_ = rc[h % 2]
                kk.op("dve", lambda e, o_=o_, po_=po_: e.tensor_copy(out=o_[0:65, :], in_=po_[0:65, :]), reads=[po_.b], writes=[o_.b])
                kk.op("dve", lambda e, o_=o_, r_=r_: e.reciprocal(out=r_[64:65, :], in_=o_[64:65, :]), reads=[o_.b], writes=[r_.b])
                kk.op("pe", lambda e, r_=r_: e.matmul(pbc[:], ones_f[64:65, 0:64], r_[64:65, :], start=True, stop=True),
                      reads=[r_.b, ones_f.b], writes=[pbc.b])
                kk.op("dve", lambda e, o_=o_: e.tensor_tensor(out=o_[0:64, :], in0=o_[0:64, :], in1=pbc[:], op=ALU.mult),
                      reads=[o_.b, pbc.b], writes=[o_.b])
                kk.op("dve", lambda e, o_=o_, og_=og_, h=h, z_=z_: e.tensor_tensor(out=og_[:, h, :], in0=o_[0:64, :], in1=z_[:, h, :], op=ALU.mult),
                      reads=[o_.b, z_.b], writes=[og_.b])
            kk.op("pool", lambda e, og_=og_, c0=c0: e.dma_start(
                out=mixdst[512:768, c0:c0 + 512].rearrange("(h p) s -> p h s", p=64), in_=og_[:]),
                reads=[og_.b], dma=f"st{qb % 2}")


def phase_fm(nc, kk, dr, l, si, S, ident):
    with ExitStack() as st:
        projT = dr[f"projT{si}"]
        swf = sb(nc, st, "swf", [128, 4, 128], F32)
        sw = sb(nc, st, "sw", [128, 4, 128], BF16)
        sbT = sb(nc, st, "sbT", [128, 2, 128], F32)
        cw = sb(nc, st, "cw", [128, 6, 5], F32)
        pwf = sb(nc, st, "pwf", [128, 2, 128], F32)
        pw = sb(nc, st, "pw", [128, 2, 128], BF16)
        psc = sb(nc, st, "psc", [128, 2], F32)
        kk.op("sp", lambda e: e.dma_start(out=swf[:], in_=dr["sgu_wT"][l]), writes=[swf.b], dma="c0")
        kk.op("sp", lambda e: e.dma_start(out=sbT[:], in_=dr["sgu_bT"][l]), writes=[sbT.b], dma="c0")
        kk.op("sp", lambda e: e.dma_start(out=cw[:], in_=dr["conv_w"][l]), writes=[cw.b], dma="c0")
        kk.op("sp", lambda e: e.dma_start(out=pwf[:], in_=dr["pool_w"][l]), writes=[pwf.b], dma="c0")
        kk.op("sp", lambda e: e.dma_start(out=psc[:], in_=dr["pool_s"][l]), writes=[psc.b], dma="c0")
        kk.op("dve", lambda e: e.tensor_copy(out=sw[:], in_=swf[:]), reads=[swf.b], writes=[sw.b])
        kk.op("dve", lambda e: e.tensor_copy(out=pw[:], in_=pwf[:]), reads=[pwf.b], writes=[pw.b])

        NS = 2
        av = [sb(nc, st, f"av{i}", [128, 6, 512], BF16) for i in range(NS)]
        dxz = [sb(nc, st, f"dxz{i}", [128, 4, 528], BF16) for i in range(NS)]
        icn = [sb(nc, st, f"icn{i}", [128, 2, 512], F32) for i in range(NS)]
        bx = [sb(nc, st, f"bx{i}", [128, 6, 516], BF16) for i in range(NS)]
        pvt = ps(nc, st, "pvt", [128, 2, 128], BF16)
        vtok = sb(nc, st, "vtok", [128, 4, 64], F32)
        vsq = sb(nc, st, "vsq", [128, 4, 64], F32)
        vss = sb(nc, st, "vss", [128, 4], F32)
        vnm = [sb(nc, st, f"vnm{i}", [128, 4, 128], BF16) for i in range(2)]
        for v_ in vnm:
            kk.op("pool", lambda e, v_=v_: e.memset(v_[:], 0.0), writes=[v_.b])
        pm = [ps(nc, st, f"pm{i}", [128, 2, 512], F32) for i in range(1)]
        ta = sb(nc, st, "ta", [128, 2, 512], F32)
        sz = sb(nc, st, "sz", [128, 2, 512], BF16)
        ma = [sb(nc, st, f"ma{i}", [128, 2, 512], BF16) for i in range(NS)]
        ss = sb(nc, st, "ss", [128, 2, 528], F32)
        s2 = sb(nc, st, "s2", [128, 2, 528], F32)
        s4 = sb(nc, st, "s4", [128, 2, 528], F32)
        s8 = sb(nc, st, "s8", [128, 528], F32)
        dfb = sb(nc, st, "dfb", [128, 2, 512], BF16)
        ppl = ps(nc, st, "ppl", [128, 2, 512], F32)
        md = [sb(nc, st, f"md{i}", [128, 2, 512], BF16) for i in range(NS)]
        szd = sb(nc, st, "szd", [128, 2, 512], BF16)
        cacc = sb(nc, st, "cacc", [128, 512], F32)
        cvo = [sb(nc, st, f"cvo{i}", [128, 6, 512], BF16) for i in range(NS)]

        nblk = S // 512
        for tb in range(nblk):
            s = tb % NS
            c0 = tb * 512
            a_, d_, i_, b_ = av[s], dxz[s], icn[s], bx[s]
            kk.op("sp", lambda e, a_=a_, c0=c0: e.dma_start(out=a_[:], in_=projT[0:768, c0:c0 + 512].rearrange("(g p) s -> p g s", p=128)),
                  writes=[a_.b], dma=f"x{s}")
            lo = max(c0 - 8, 0)
            hi = min(c0 + 520, S)
            if lo > c0 - 8:
                kk.op("pool", lambda e, d_=d_: e.memset(d_[:, 0:2, 0:8], 0.0), writes=[d_.b])
            if hi < c0 + 520:
                kk.op("pool", lambda e, d_=d_: e.memset(d_[:, 0:2, 520:528], 0.0), writes=[d_.b])
            kk.op("sp", lambda e, d_=d_, lo=lo, hi=hi, c0=c0: e.dma_start(
                out=d_[:, 0:2, lo - (c0 - 8):hi - (c0 - 8)], in_=projT[G_DX * 128:(G_DX + 2) * 128, lo:hi].rearrange("(g p) s -> p g s", p=128)),
                writes=[d_.b], dma=f"x{s}")
            kk.op("sp", lambda e, d_=d_, c0=c0: e.dma_start(
                out=d_[:, 2:4, 0:512], in_=projT[G_DZ * 128:(G_DZ + 2) * 128, c0:c0 + 512].rearrange("(g p) s -> p g s", p=128)),
                writes=[d_.b], dma=f"x{s}")
            kk.op("sp", lambda e, i_=i_, c0=c0: e.dma_start(out=i_[:], in_=dr[f"icnt{si}"][:, :, c0:c0 + 512]), writes=[i_.b], dma=f"x{s}")
            lo2 = max(c0 - 2, 0)
            hi2 = min(c0 + 514, S)
            if lo2 > c0 - 2:
                kk.op("pool", lambda e, b_=b_: e.memset(b_[:, :, 0:2], 0.0), writes=[b_.b])
            if hi2 < c0 + 514:
                kk.op("pool", lambda e, b_=b_: e.memset(b_[:, :, 514:516], 0.0), writes=[b_.b])
            kk.op("sp", lambda e, b_=b_, lo2=lo2, hi2=hi2, c0=c0: e.dma_start(
                out=b_[:, :, lo2 - (c0 - 2):hi2 - (c0 - 2)], in_=projT[G_BQ * 128:(G_BQ + 6) * 128, lo2:hi2].rearrange("(g p) s -> p g s", p=128)),
                writes=[b_.b], dma=f"x{s}")

            pm_ = pm[0]
            for ch in range(4):
                vn_ = vnm[ch % 2]
                for j in range(2):
                    kk.op("pe", lambda e, a_=a_, j=j, ch=ch: e.transpose(out=pvt[:, j, :], in_=a_[:, 2 + j, ch * 128:(ch + 1) * 128], identity=ident[:]),
                          reads=[a_.b, ident.b], writes=[pvt.b])
                kk.op("dve", lambda e: e.tensor_copy(out=vtok[:].rearrange("p a b -> p (a b)"), in_=pvt[:].rearrange("p a b -> p (a b)")),
                      reads=[pvt.b], writes=[vtok.b])
                kk.op("dve", lambda e: e.tensor_tensor(out=vsq[:], in0=vtok[:], in1=vtok[:], op=ALU.mult), reads=[vtok.b], writes=[vsq.b])
                kk.op("dve", lambda e: e.tensor_reduce(out=vss[:], in_=vsq[:], axis=AX.X, op=ALU.add), reads=[vsq.b], writes=[vss.b])
                kk.op("dve", lambda e: e.tensor_scalar(out=vss[:], in0=vss[:], scalar1=1.0 / 64, scalar2=EPS, op0=ALU.mult, op1=ALU.add),
                      reads=[vss.b], writes=[vss.b])
                kk.op("act", lambda e: e.activation(out=vss[:], in_=vss[:], func=AF.Ln), reads=[vss.b], writes=[vss.b])
                kk.op("act", lambda e: e.activation(out=vss[:], in_=vss[:], func=AF.Exp, scale=-0.5), reads=[vss.b], writes=[vss.b])
                for h in range(4):
                    kk.op("dve", lambda e, vn_=vn_, h=h: e.tensor_scalar(
                        out=vn_[:, h, (h % 2) * 64:(h % 2) * 64 + 64], in0=vtok[:, h, :], scalar1=vss[:, h:h + 1], scalar2=1.0,
                        op0=ALU.mult, op1=ALU.mult), reads=[vtok.b, vss.b], writes=[vn_.b])
                for h in range(4):
                    kk.op("pe", lambda e, vn_=vn_, h=h, ch=ch, pm_=pm_: e.matmul(
                        pm_[:, h // 2, ch * 128:(ch + 1) * 128], vn_[:, h, :], sw[:, h, :], start=(h % 2 == 0), stop=(h % 2 == 1)),
                        reads=[vn_.b, sw.b], writes=[pm_.b])
            m_ = ma[s]
            kk.op("dve", lambda e, pm_=pm_: e.tensor_tensor(
                out=ta[:].rearrange("p a (c i) -> p a c i", c=4), in0=pm_[:].rearrange("p a (c i) -> p a c i", c=4),
                in1=sbT[:].unsqueeze(2).to_broadcast([128, 2, 4, 128]), op=ALU.add), reads=[pm_.b, sbT.b], writes=[ta.b])
            kk.op("act", lambda e, a_=a_: e.activation(out=sz[:], in_=a_[:, 4:6, :], func=AF.Silu), reads=[a_.b], writes=[sz.b])
            kk.op("pool", lambda e, a_=a_: e.tensor_tensor(out=ta[:], in0=ta[:], in1=a_[:, 0:2, :], op=ALU.mult), reads=[ta.b, a_.b], writes=[ta.b])
            kk.op("dve", lambda e, m_=m_: e.tensor_tensor(out=m_[:], in0=ta[:], in1=sz[:], op=ALU.mult), reads=[ta.b, sz.b], writes=[m_.b])
            kk.op("pool", lambda e, m_=m_, c0=c0: e.dma_start(out=dr[f"mixT{si}"][0:256, c0:c0 + 512].rearrange("(g p) s -> p g s", p=128), in_=m_[:]),
                  reads=[m_.b], dma=f"st{s}")

            X = d_
            kk.op("dve", lambda e, X=X: e.tensor_tensor(out=s2[:, :, 1:528], in0=X[:, 0:2, 0:527], in1=X[:, 0:2, 1:528], op=ALU.add),
                  reads=[X.b], writes=[s2.b])
            kk.op("pool", lambda e: e.tensor_tensor(out=s4[:, :, 2:527], in0=s2[:, :, 1:526], in1=s2[:, :, 3:528], op=ALU.add),
                  reads=[s2.b], writes=[s4.b])
            kk.op("dve", lambda e: e.tensor_tensor(out=s8[:, 4:525], in0=s4[:, 1, 2:523], in1=s4[:, 1, 6:527], op=ALU.add),
                  reads=[s4.b], writes=[s8.b])
            kk.op("pool", lambda e: e.tensor_copy(out=ss[0:64, 0, 8:520], in_=s2[0:64, 0, 8:520]), reads=[s2.b], writes=[ss.b])
            kk.op("pool", lambda e: e.tensor_copy(out=ss[64:128, 0, 8:520], in_=s4[64:128, 0, 8:520]), reads=[s4.b], writes=[ss.b])
            kk.op("dve", lambda e: e.tensor_copy(out=ss[0:64, 1, 8:520], in_=s8[0:64, 8:520]), reads=[s8.b], writes=[ss.b])
            kk.op("dve", lambda e: e.tensor_tensor(out=ss[64:128, 1, 8:520], in0=s8[64:128, 4:516], in1=s8[64:128, 12:524], op=ALU.add),
                  reads=[s8.b], writes=[ss.b])
            kk.op("dve", lambda e, i_=i_: e.tensor_tensor(out=ss[:, :, 8:520], in0=ss[:, :, 8:520], in1=i_[:], op=ALU.mult),
                  reads=[ss.b, i_.b], writes=[ss.b])
            kk.op("dve", lambda e, X=X: e.tensor_tensor(out=dfb[:], in0=ss[:, :, 8:520], in1=X[:, 0:2, 8:520], op=ALU.subtract),
                  reads=[ss.b, X.b], writes=[dfb.b])
            for ch in range(2):
                kk.op("pe", lambda e, ch=ch: e.matmul(ppl[:, ch, :], pw[:, ch, :], dfb[:, ch, :], start=True, stop=True),
                      reads=[pw.b, dfb.b], writes=[ppl.b])
            kk.op("act", lambda e, X=X: e.activation(out=szd[:], in_=X[:, 2:4, 0:512], func=AF.Silu), reads=[X.b], writes=[szd.b])
            o_ = md[s]
            for ch in range(2):
                kk.op("dve", lambda e, ch=ch, o_=o_: e.scalar_tensor_tensor(
                    out=o_[:, ch, :], in0=ppl[:, ch, :], scalar=psc[:, ch:ch + 1], in1=szd[:, ch, :], op0=ALU.mult, op1=ALU.mult),
                    reads=[ppl.b, psc.b, szd.b], writes=[o_.b])
            kk.op("pool", lambda e, o_=o_, c0=c0: e.dma_start(out=dr[f"mixT{si}"][768:1024, c0:c0 + 512].rearrange("(g p) s -> p g s", p=128), in_=o_[:]),
                  reads=[o_.b], dma=f"st{s}")

            co = cvo[s]
            for ch in range(6):
                kk.op("dve", lambda e, b_=b_, ch=ch: e.tensor_scalar(out=cacc[:], in0=b_[:, ch, 0:512], scalar1=cw[:, ch, 0:1], scalar2=1.0,
                                                                    op0=ALU.mult, op1=ALU.mult), reads=[b_.b, cw.b], writes=[cacc.b])
                for i in range(1, 5):
                    kk.op("dve", lambda e, b_=b_, ch=ch, i=i: e.scalar_tensor_tensor(
                        out=cacc[:], in0=b_[:, ch, i:i + 512], scalar=cw[:, ch, i:i + 1], in1=cacc[:], op0=ALU.mult, op1=ALU.add),
                        reads=[b_.b, cw.b, cacc.b], writes=[cacc.b])
                kk.op("act", lambda e, co=co, ch=ch: e.activation(out=co[:, ch, :], in_=cacc[:], func=AF.Silu), reads=[cacc.b], writes=[co.b])
            kk.op("pool", lambda e, co=co, c0=c0: e.dma_start(out=dr[f"convT{si}"][:, c0:c0 + 512].rearrange("(g p) s -> p g s", p=128), in_=co[:]),
                  reads=[co.b], dma=f"st{s}")


def bc(ap, shape, axis):
    return ap.unsqueeze(axis).to_broadcast(shape)


def phase_dn(nc, kk, dr, l, si, S, ident, ident_f, ones_f):
    with ExitStack() as st:
        NCH = S // 128
        tri = [sb(nc, st, f"tri{d}", [128, 128], F32) for d in range(2)]
        nmI = [sb(nc, st, f"nmI{d}", [128, 128], F32) for d in range(2)]
        nmS = [sb(nc, st, f"nmS{d}", [128, 128], F32) for d in range(2)]
        offd = sb(nc, st, "offd", [128, 128], F32)
        dnw = sb(nc, st, "dnw", [128, 4, 64], F32)
        kk.op("sp", lambda e: e.dma_start(out=tri[0][:], in_=dr["triU"][:, :]), writes=[tri[0].b], dma="c0")
        kk.op("sp", lambda e: e.dma_start(out=tri[1][:], in_=dr["triL"][:, :]), writes=[tri[1].b], dma="c0")
        kk.op("sp", lambda e: e.dma_start(out=dnw[:], in_=dr["dn_w"][l]), writes=[dnw.b], dma="c0")
        kk.op("dve", lambda e: e.tensor_scalar(out=offd[:], in0=ident_f[:], scalar1=-1.0, scalar2=1.0, op0=ALU.mult, op1=ALU.add),
              reads=[ident_f.b], writes=[offd.b])
        for d in range(2):
            kk.op("dve", lambda e, d=d: e.tensor_scalar(out=nmI[d][:], in0=tri[d][:], scalar1=-1.0, scalar2=1e30, op0=ALU.add, op1=ALU.mult),
                  reads=[tri[d].b], writes=[nmI[d].b])
            kk.op("dve", lambda e, d=d: e.tensor_tensor(out=nmS[d][:], in0=tri[d][:], in1=offd[:], op=ALU.mult),
                  reads=[tri[d].b, offd.b], writes=[nmS[d].b])
            kk.op("dve", lambda e, d=d: e.tensor_scalar(out=nmS[d][:], in0=nmS[d][:], scalar1=-1.0, scalar2=1e30, op0=ALU.add, op1=ALU.mult),
                  reads=[nmS[d].b], writes=[nmS[d].b])

        bkb = ps(nc, st, "bkb", [128, 1024], BF16)
        bk1 = ps(nc, st, "bk1", [128, 512], F32)
        bk2 = ps(nc, st, "bk2", [128, 512], F32)
        bk3 = ps(nc, st, "bk3", [128, 512], F32)
        bk4 = ps(nc, st, "bk4", [128, 512], F32)
        bk5 = ps(nc, st, "bk5", [128, 512], F32)
        bka = ps(nc, st, "bka", [128, 1024], F32)

        def v4(ap, n=4):
            return ap.rearrange("p (c i) -> p c i", c=n)

        cT = [sb(nc, st, f"cT{i}", [128, 6, 128], BF16) for i in range(2)]
        bgc = [sb(nc, st, f"bgc{i}", [128, 16], F32) for i in range(2)]
        ofl = [sb(nc, st, f"ofl{i}", [128, 256], F32) for i in range(2)]
        zb = [sb(nc, st, f"zb{i}", [128, 2, 128], BF16) for i in range(2)]
        tokL = [sb(nc, st, f"tok{i}", [128, 768], F32) for i in range(2)]
        sq = sb(nc, st, "dsq", [128, 512], F32)
        rs = sb(nc, st, "drs", [128, 8], F32)
        qkn = sb(nc, st, "qkn", [128, 8, 64], BF16)
        qkn32 = sb(nc, st, "qkn32", [128, 8, 64], F32)
        ek32 = sb(nc, st, "ek32", [128, 4, 64], F32)
        kd32L = [sb(nc, st, f"kd32{i}", [128, 4, 64], F32) for i in range(2)]
        ekT32L = [sb(nc, st, f"ekT32{i}", [64, 4, 128], F32) for i in range(2)]
        rb = sb(nc, st, "rb", [128, 4, 64], BF16)
        vnew32 = sb(nc, st, "vnew32", [128, 4, 64], F32)
        vb = sb(nc, st, "vb", [128, 256], BF16)
        qkT = sb(nc, st, "dqkT", [64, 8, 128], BF16)
        gs = sb(nc, st, "gs", [128, 8], F32)
        eg = sb(nc, st, "eg", [128, 4], F32)
        ekd = sb(nc, st, "ekd", [128, 4], F32)
        eglL = [sb(nc, st, f"egl{i}", [128, 4], F32) for i in range(2)]
        rg = sb(nc, st, "rg", [128, 4, 128], F32)
        dm = sb(nc, st, "dm", [128, 4, 128], F32)
        dmI = sb(nc, st, "dmI", [128, 4, 128], F32)
        dmS = sb(nc, st, "dmS", [128, 4, 128], F32)
        X = sb(nc, st, "X", [128, 4, 128], F32R)
        ATL = [sb(nc, st, f"AT{i}", [128, 4, 128], BF16) for i in range(2)]
        YS = [sb(nc, st, f"YS{i}", [128, 4, 256], F32R) for i in range(2)]
        YT = [sb(nc, st, f"YT{i}", [128, 4, 128], F32R) for i in range(2)]
        db = sb(nc, st, "db", [128, 4, 128], F32)
        TTbL = [sb(nc, st, f"TTb{i}", [128, 4, 128], BF16) for i in range(2)]
        usb = sb(nc, st, "usb", [128, 4, 64], F32)
        ek = sb(nc, st, "ek", [128, 4, 64], BF16)
        kd = sb(nc, st, "kd", [128, 4, 64], BF16)
        qd = sb(nc, st, "qd", [128, 4, 64], BF16)
        wT = sb(nc, st, "wT", [64, 4, 128], BF16)
        qdTL = [sb(nc, st, f"qdT{i}", [64, 4, 128], BF16) for i in range(2)]
        vnew = sb(nc, st, "vnew", [128, 4, 64], BF16)
        Sf = sb(nc, st, "Sf", [64, 4, 64], F32)
        Sb = sb(nc, st, "Sb", [64, 4, 64], BF16)
        osb = [sb(nc, st, f"dosb{i}", [128, 256], F32) for i in range(2)]
        osq = sb(nc, st, "osq", [128, 256], F32)
        oss = sb(nc, st, "oss", [128, 4], F32)
        onb = sb(nc, st, "onb", [128, 256], BF16)
        omx = [sb(nc, st, f"omx{i}", [128, 2, 128], BF16) for i in range(2)]

        for d in range(2):
            if d == 1:
                kk.barrier()
            kk.op("pool", lambda e: e.memset(Sf[:], 0.0), writes=[Sf.b])
            kk.op("pool", lambda e: e.memset(Sb[:], 0.0), writes=[Sb.b])
            order = list(range(NCH)) if d == 0 else list(range(NCH - 1, -1, -1))

            def prep(n, ci, d=d):
                s = n % 2
                tok, AT, TTb, qdT, egl, kd32, ekT32 = tokL[s], ATL[s], TTbL[s], qdTL[s], eglL[s], kd32L[s], ekT32L[s]
                r0 = ci * 128
                c_, g_ = cT[s], bgc[s]
                yield
                kk.op("sp", lambda e, c_=c_, r0=r0: e.dma_start(out=c_[:], in_=dr[f"convT{si}"][:, r0:r0 + 128].rearrange("(g p) s -> p g s", p=128)),
                      writes=[c_.b], dma=f"x{s}")
                yield
                kk.op("sp", lambda e, g_=g_, r0=r0: e.dma_start(out=g_[:], in_=dr[f"bg{si}"][r0:r0 + 128, :]), writes=[g_.b], dma=f"x{s}")
                if d == 1:
                    kk.op("sp", lambda e, o_=ofl[s], r0=r0: e.dma_start(out=o_[:], in_=dr[f"of{si}"][r0:r0 + 128, :]), writes=[ofl[s].b], dma=f"x{s}")
                    kk.op("sp", lambda e, z_=zb[s], r0=r0: e.dma_start(
                        out=z_[:], in_=dr[f"projT{si}"][G_BZ * 128:(G_BZ + 2) * 128, r0:r0 + 128].rearrange("(g p) s -> p g s", p=128)),
                        writes=[zb[s].b], dma=f"x{s}")
                gd = g_[:, 8 + 4 * d:12 + 4 * d]
                bd = g_[:, 4 * d:4 * d + 4]
                yield
                for j in range(6):
                    kk.op("pe", lambda e, c_=c_, j=j: e.transpose(out=bkb[:, j * 128:(j + 1) * 128], in_=c_[:, j, :], identity=ident[:]),
                          reads=[c_.b, ident.b], writes=[bkb.b])
                yield
                kk.op("dve", lambda e: e.tensor_copy(out=tok[:], in_=bkb[:, 0:768]), reads=[bkb.b], writes=[tok.b])
                yield
                kk.op("dve", lambda e: e.tensor_tensor(out=sq[:], in0=tok[:, 0:512], in1=tok[:, 0:512], op=ALU.mult), reads=[tok.b], writes=[sq.b])
                yield
                kk.op("dve", lambda e: e.tensor_reduce(out=rs[:], in_=v4(sq[:], 8), axis=AX.X, op=ALU.add), reads=[sq.b], writes=[rs.b])
                yield
                kk.op("dve", lambda e: e.tensor_scalar(out=rs[:], in0=rs[:], scalar1=EPS, scalar2=1.0, op0=ALU.add, op1=ALU.mult),
                      reads=[rs.b], writes=[rs.b])
                yield
                kk.op("act", lambda e: e.activation(out=rs[:], in_=rs[:], func=AF.Ln), reads=[rs.b], writes=[rs.b])
                yield
                kk.op("act", lambda e: e.activation(out=rs[:], in_=rs[:], func=AF.Exp, scale=-0.5), reads=[rs.b], writes=[rs.b])
                yield
                kk.op("dve", lambda e: e.tensor_scalar(out=rs[:, 0:4], in0=rs[:, 0:4], scalar1=0.125, scalar2=1.0, op0=ALU.mult, op1=ALU.mult),
                      reads=[rs.b], writes=[rs.b])
                yield
                kk.op("dve", lambda e: e.tensor_tensor(out=qkn32[:], in0=v4(tok[:, 0:512], 8), in1=bc(rs[:], [128, 8, 64], 2), op=ALU.mult),
                      reads=[tok.b, rs.b], writes=[qkn32.b])
                yield
                kk.op("pool", lambda e: e.tensor_copy(out=qkn[:], in_=qkn32[:]), reads=[qkn32.b], writes=[qkn.b])
                yield
                for j in range(8):
                    kk.op("pe", lambda e, j=j: e.transpose(out=bkb[0:64, j * 128:(j + 1) * 128], in_=qkn[:, j, :], identity=ident[:]),
                          reads=[qkn.b, ident.b], writes=[bkb.b])
                yield
                kk.op("act", lambda e: e.activation(out=qkT[:].rearrange("p a b -> p (a b)"), in_=bkb[0:64, 0:1024], func=AF.Identity),
                      reads=[bkb.b], writes=[qkT.b])
                yield
                kk.op("pe", lambda e, gd=gd, d=d: e.matmul(bk4[:, 0:4], tri[d][:], gd, start=True, stop=True), reads=[tri[d].b, g_.b], writes=[bk4.b])
                yield
                kk.op("pe", lambda e, gd=gd: e.matmul(bk4[:, 4:8], ones_f[:], gd, start=True, stop=True), reads=[ones_f.b, g_.b], writes=[bk4.b])
                yield
                kk.op("dve", lambda e: e.tensor_copy(out=gs[:], in_=bk4[:, 0:8]), reads=[bk4.b], writes=[gs.b])
                yield
                kk.op("act", lambda e: e.activation(out=eg[:], in_=gs[:, 0:4], func=AF.Exp), reads=[gs.b], writes=[eg.b])
                yield
                kk.op("act", lambda e: e.activation(out=egl[:], in_=gs[:, 4:8], func=AF.Exp), reads=[gs.b], writes=[egl.b])
                yield
                kk.op("dve", lambda e: e.tensor_tensor(out=ekd[:], in0=gs[:, 4:8], in1=gs[:, 0:4], op=ALU.subtract), reads=[gs.b], writes=[ekd.b])
                yield
                kk.op("act", lambda e: e.activation(out=ekd[:], in_=ekd[:], func=AF.Exp), reads=[ekd.b], writes=[ekd.b])
                yield
                kk.op("dve", lambda e, gd=gd, d=d: e.tensor_tensor(out=rg[:], in0=bc(tri[d][:], [128, 4, 128], 1), in1=bc(gd, [128, 4, 128], 2), op=ALU.mult),
                      reads=[tri[d].b, g_.b], writes=[rg.b])
                yield
                kk.op("pe", lambda e: e.matmul(bk3[:], ones_f[:], rg[:].rearrange("p a b -> p (a b)"), start=True, stop=True),
                      reads=[ones_f.b, rg.b], writes=[bk3.b])
                yield
                kk.op("dve", lambda e: e.tensor_tensor(out=dm[:], in0=v4(bk3[:]), in1=bc(gs[:, 0:4], [128, 4, 128], 2), op=ALU.subtract),
                      reads=[bk3.b, gs.b], writes=[dm.b])
                yield
                kk.op("dve", lambda e, d=d: e.tensor_tensor(out=dmI[:], in0=dm[:], in1=bc(nmI[d][:], [128, 4, 128], 1), op=ALU.add),
                      reads=[dm.b, nmI[d].b], writes=[dmI.b])
                yield
                kk.op("pool", lambda e, d=d: e.tensor_tensor(out=dmS[:], in0=dm[:], in1=bc(nmS[d][:], [128, 4, 128], 1), op=ALU.add),
                      reads=[dm.b, nmS[d].b], writes=[dmS.b])
                yield
                kk.op("act", lambda e: e.activation(out=dmI[:], in_=dmI[:], func=AF.Exp), reads=[dmI.b], writes=[dmI.b])
                yield
                kk.op("act", lambda e: e.activation(out=dmS[:], in_=dmS[:], func=AF.Exp), reads=[dmS.b], writes=[dmS.b])
                yield
                for h in range(4):
                    kTh = qkT[:, 4 + h, :]
                    qTh = qkT[:, h, :]
                    kk.op("pe", lambda e, h=h, kTh=kTh: e.matmul(bk4[:, h * 128:(h + 1) * 128], kTh, kTh, start=True, stop=True),
                          reads=[qkT.b], writes=[bk4.b])
                    kk.op("pe", lambda e, h=h, kTh=kTh, qTh=qTh: e.matmul(bk5[:, h * 128:(h + 1) * 128], kTh, qTh, start=True, stop=True),
                          reads=[qkT.b], writes=[bk5.b])
                yield
                for h in range(4):
                    kk.op("dve", lambda e, h=h, bd=bd: e.scalar_tensor_tensor(
                        out=X[:, h, :], in0=bk4[:, h * 128:(h + 1) * 128], scalar=bd[:, h:h + 1], in1=dmS[:, h, :], op0=ALU.mult, op1=ALU.mult),
                        reads=[bk4.b, g_.b, dmS.b], writes=[X.b])
                yield
                kk.op("dve", lambda e: e.tensor_tensor(out=AT[:], in0=v4(bk5[:]), in1=dmI[:], op=ALU.mult), reads=[bk5.b, dmI.b], writes=[AT.b])
                pbv = v4(bk4[:])
                yield
                for h in range(4):
                    kk.op("pe", lambda e, h=h: e.transpose(out=pbv[:, h, :], in_=X[:, h, :].bitcast(F32), identity=ident_f[:]),
                          reads=[X.b, ident_f.b], writes=[bk4.b])
                yield
                kk.op("dve", lambda e: e.tensor_copy(out=YT[1][:], in_=pbv), reads=[bk4.b], writes=[YT[1].b])
                pa = bka[:].rearrange("p (c i) -> p c i", c=4)
                yield
                for h in range(4):
                    kk.op("pe", lambda e, h=h: e.matmul(pa[:, h, 0:128], YT[1][:, h, :], X[:, h, :], start=True, stop=True),
                          reads=[YT[1].b, X.b], writes=[bka.b])
                    kk.op("pe", lambda e, h=h: e.matmul(pbv[:, h, :], X[:, h, :], YT[1][:, h, :], start=True, stop=True),
                          reads=[YT[1].b, X.b], writes=[bk4.b])
                yield
                kk.op("dve", lambda e: e.tensor_tensor(out=YS[0][:, :, 128:256], in0=bc(ident_f[:], [128, 4, 128], 1), in1=X[:], op=ALU.subtract),
                      reads=[ident_f.b, X.b], writes=[YS[0].b])
                yield
                kk.op("act", lambda e: e.activation(out=YS[0][:, :, 0:128], in_=pa[:, :, 0:128], func=AF.Identity), reads=[bka.b], writes=[YS[0].b])
                yield
                kk.op("dve", lambda e: e.tensor_copy(out=YT[0][:], in_=pbv), reads=[bk4.b], writes=[YT[0].b])
                cur = 0
                yield
                for k in range(1, 7):
                    ys, yt = YS[cur], YT[cur]
                    nys, nyt = YS[1 - cur], YT[1 - cur]
                    for h in range(4):
                        kk.op("pe", lambda e, h=h, ys=ys, yt=yt: e.matmul(pa[:, h, :], yt[:, h, :], ys[:, h, :], start=True, stop=True),
                              reads=[ys.b, yt.b], writes=[bka.b])
                        if k < 6:
                            kk.op("pe", lambda e, h=h, ys=ys, yt=yt: e.matmul(pbv[:, h, :], ys[:, h, 0:128], yt[:, h, :], start=True, stop=True),
                                  reads=[ys.b, yt.b], writes=[bk4.b])
                    kk.op("dve", lambda e, ys=ys, nys=nys: e.tensor_tensor(out=nys[:, :, 128:256], in0=pa[:, :, 128:256], in1=ys[:, :, 128:256], op=ALU.add),
                          reads=[bka.b, ys.b], writes=[nys.b])
                    if k < 6:
                        kk.op("act", lambda e, nys=nys: e.activation(out=nys[:, :, 0:128], in_=pa[:, :, 0:128], func=AF.Identity),
                              reads=[bka.b], writes=[nys.b])
                        kk.op("dve", lambda e, nyt=nyt: e.tensor_copy(out=nyt[:], in_=pbv), reads=[bk4.b], writes=[nyt.b])
                    cur = 1 - cur
                TT = YS[cur]
                yield
                kk.op("pool", lambda e, bd=bd: e.tensor_tensor(out=db[:], in0=bc(ident_f[:], [128, 4, 128], 1), in1=bc(bd, [128, 4, 128], 2), op=ALU.mult),
                      reads=[ident_f.b, g_.b], writes=[db.b])
                yield
                kk.op("pe", lambda e: e.matmul(bk3[:], ones_f[:], db[:].rearrange("p a b -> p (a b)"), start=True, stop=True),
                      reads=[ones_f.b, db.b], writes=[bk3.b])
                yield
                kk.op("dve", lambda e, TT=TT: e.tensor_tensor(out=TTb[:], in0=v4(bk3[:]), in1=TT[:, :, 128:256], op=ALU.mult),
                      reads=[bk3.b, TT.b], writes=[TTb.b])
                yield
                kk.op("dve", lambda e: e.tensor_tensor(out=ek32[:], in0=qkn32[:, 4:8, :], in1=bc(eg[:], [128, 4, 64], 2), op=ALU.mult),
                      reads=[qkn32.b, eg.b], writes=[ek32.b])
                yield
                kk.op("pool", lambda e: e.tensor_tensor(out=kd32[:], in0=qkn32[:, 4:8, :], in1=bc(ekd[:], [128, 4, 64], 2), op=ALU.mult),
                      reads=[qkn32.b, ekd.b], writes=[kd32.b])
                yield
                kk.op("pool", lambda e: e.tensor_tensor(out=qd[:], in0=qkn32[:, 0:4, :], in1=bc(eg[:], [128, 4, 64], 2), op=ALU.mult),
                      reads=[qkn32.b, eg.b], writes=[qd.b])
                pek = bk5[0:64, :].rearrange("p (c i) -> p c i", c=4)
                yield
                for h in range(4):
                    kk.op("pe", lambda e, h=h: e.transpose(out=pek[:, h, :], in_=ek32[:, h, :], identity=ident_f[:]),
                          reads=[ek32.b, ident_f.b], writes=[bk5.b])
                yield
                kk.op("dve", lambda e: e.tensor_copy(out=ekT32[:], in_=pek), reads=[bk5.b], writes=[ekT32.b])
                yield
                for h in range(4):
                    kk.op("pe", lambda e, h=h: e.transpose(out=bkb[0:64, h * 128:(h + 1) * 128], in_=qd[:, h, :], identity=ident[:]),
                          reads=[qd.b, ident.b], writes=[bkb.b])
                yield
                kk.op("dve", lambda e: e.tensor_copy(out=qdT[:].rearrange("p a b -> p (a b)"), in_=bkb[0:64, 0:512]), reads=[bkb.b], writes=[qdT.b])

            def scan(n, ci, d=d):
                s = n % 2
                r0 = ci * 128
                tok, AT, TTb, qdT, egl, kd32, ekT32 = tokL[s], ATL[s], TTbL[s], qdTL[s], eglL[s], kd32L[s], ekT32L[s]
                pws = bk1[:, 0:256].rearrange("p (c i) -> p c i", c=4)
                pout = bk1[:, 256:512].rearrange("p (c i) -> p c i", c=4)
                pds = bk2[0:64, 0:256].rearrange("p (c i) -> p c i", c=4)
                pv2 = bk2[:, 256:512].rearrange("p (c i) -> p c i", c=4)
                yield
                for h in range(4):
                    kk.op("pe", lambda e, h=h: e.matmul(pws[:, h, :], ekT32[:, h, :], Sf[:, h, :], start=True, stop=True),
                          reads=[ekT32.b, Sf.b], writes=[bk1.b])
                yield
                kk.op("dve", lambda e: e.tensor_tensor(out=rb[:], in0=v4(tok[:, 512:768]), in1=pws, op=ALU.subtract),
                      reads=[tok.b, bk1.b], writes=[rb.b])
                yield
                for h in range(4):
                    kk.op("pe", lambda e, h=h: e.matmul(pv2[:, h, :], TTb[:, h, :], rb[:, h, :], start=True, stop=True),
                          reads=[TTb.b, rb.b], writes=[bk2.b])
                yield
                kk.op("dve", lambda e: e.tensor_copy(out=vnew32[:], in_=pv2), reads=[bk2.b], writes=[vnew32.b])
                yield
                kk.op("dve", lambda e: e.tensor_copy(out=vnew[:], in_=pv2), reads=[bk2.b], writes=[vnew.b])
                yield
                for h in range(4):
                    kk.op("pe", lambda e, h=h: e.matmul(pout[:, h, :], qdT[:, h, :], Sb[:, h, :], start=True, stop=False),
                          reads=[qdT.b, Sb.b], writes=[bk1.b])
                    kk.op("pe", lambda e, h=h: e.matmul(pout[:, h, :], AT[:, h, :], vnew[:, h, :], start=False, stop=True),
                          reads=[AT.b, vnew.b], writes=[bk1.b])
                yield
                for h in range(4):
                    kk.op("pe", lambda e, h=h: e.matmul(pds[:, h, :], kd32[:, h, :], vnew32[:, h, :], start=True, stop=True),
                          reads=[kd32.b, vnew32.b], writes=[bk2.b])
                yield
                kk.op("dve", lambda e: e.tensor_tensor(out=Sf[:], in0=Sf[:], in1=bc(egl[0:64, :], [64, 4, 64], 2), op=ALU.mult),
                      reads=[Sf.b, egl.b], writes=[Sf.b])
                yield
                kk.op("dve", lambda e: e.tensor_tensor(out=Sf[:], in0=Sf[:], in1=pds, op=ALU.add), reads=[Sf.b, bk2.b], writes=[Sf.b])
                yield
                kk.op("act", lambda e: e.activation(out=Sb[:], in_=Sf[:], func=AF.Identity), reads=[Sf.b], writes=[Sb.b])
                o_ = osb[s]
                if d == 0:
                    kk.op("dve", lambda e, o_=o_: e.tensor_copy(out=o_[:], in_=bk1[:, 256:512]), reads=[bk1.b], writes=[o_.b])
                    kk.op("pool", lambda e, o_=o_, r0=r0: e.dma_start(out=dr[f"of{si}"][r0:r0 + 128, :], in_=o_[:]), reads=[o_.b], dma=f"st{s}")
                else:
                    kk.op("dve", lambda e, o_=o_, f_=ofl[s]: e.tensor_tensor(out=o_[:], in0=bk1[:, 256:512], in1=f_[:], op=ALU.add),
                          reads=[bk1.b, ofl[s].b], writes=[o_.b])
                    if f"osum{si}" in dr:
                        kk.op("pool", lambda e, o_=o_, r0=r0: e.dma_start(out=dr[f"osum{si}"][r0:r0 + 128, :], in_=o_[:]), reads=[o_.b], dma=f"st{s}")
                    kk.op("pool", lambda e, o_=o_: e.tensor_tensor(out=osq[:], in0=o_[:], in1=o_[:], op=ALU.mult), reads=[o_.b], writes=[osq.b])
                    kk.op("dve", lambda e: e.tensor_reduce(out=oss[:], in_=v4(osq[:]), axis=AX.X, op=ALU.add), reads=[osq.b], writes=[oss.b])
                    kk.op("dve", lambda e: e.tensor_scalar(out=oss[:], in0=oss[:], scalar1=1.0 / 64, scalar2=EPS, op0=ALU.mult, op1=ALU.add),
                          reads=[oss.b], writes=[oss.b])
                    kk.op("act", lambda e: e.activation(out=oss[:], in_=oss[:], func=AF.Ln), reads=[oss.b], writes=[oss.b])
                    kk.op("act", lambda e: e.activation(out=oss[:], in_=oss[:], func=AF.Exp, scale=-0.5), reads=[oss.b], writes=[oss.b])
                    kk.op("dve", lambda e, o_=o_: e.tensor_tensor(out=v4(o_[:]), in0=v4(o_[:]), in1=bc(oss[:], [128, 4, 64], 2), op=ALU.mult),
                          reads=[o_.b, oss.b], writes=[o_.b])
                    kk.op("pool", lambda e, o_=o_: e.tensor_tensor(out=onb[:], in0=o_[:], in1=dnw[:].rearrange("p a b -> p (a b)"), op=ALU.mult),
                          reads=[o_.b, dnw.b], writes=[onb.b])
                    for j in range(2):
                        kk.op("pe", lambda e, j=j: e.transpose(out=bkb[:, j * 128:(j + 1) * 128], in_=onb[:, j * 128:(j + 1) * 128], identity=ident[:]),
                              reads=[onb.b, ident.b], writes=[bkb.b])
                    z_ = zb[s]
                    m_ = omx[s]
                    kk.op("act", lambda e, z_=z_: e.activation(out=z_[:], in_=z_[:], func=AF.Silu), reads=[z_.b], writes=[z_.b])
                    kk.op("dve", lambda e, z_=z_, m_=m_: e.tensor_tensor(out=m_[:], in0=bkb[:, 0:256].rearrange("p (a b) -> p a b", a=2), in1=z_[:], op=ALU.mult),
                          reads=[bkb.b, z_.b], writes=[m_.b])
                    kk.op("pool", lambda e, m_=m_, r0=r0: e.dma_start(
                        out=dr[f"mixT{si}"][256:512, r0:r0 + 128].rearrange("(g p) s -> p g s", p=128), in_=m_[:]), reads=[m_.b], dma=f"st{s}")


            for _ in prep(0, order[0]):
                pass
            for n in range(NCH):
                gsc = scan(n, order[n])
                gpr = prep(n + 1, order[n + 1]) if n + 1 < NCH else iter(())
                alive_s = alive_p = True
                while alive_s or alive_p:
                    for _ in range(3):
                        if alive_p and next(gpr, "END") == "END":
                            alive_p = False
                    if alive_s and next(gsc, "END") == "END":
                        alive_s = False


POOL_WINDOWS = (2, 4, 8, 16)


def host_layout(seqs, depth, norm_w, w_in, sgu_w, sgu_b, conv_w, a_log, dt_bias, dn_norm_w,
                q_norm_w, k_norm_w, pool_w, pool_scale, w_out):
    f = np.float32
    m = {}
    m["w_in"] = np.ascontiguousarray(np.asarray(w_in, f).reshape(depth, 8, 128, NCOL).transpose(0, 2, 1, 3)[..., COL_PERM])
    m["norm_w"] = np.ascontiguousarray(np.asarray(norm_w, f).reshape(depth, 8, 128).transpose(0, 2, 1))
    m["w_out"] = np.ascontiguousarray(np.asarray(w_out, f).reshape(depth, 8, 128, D).transpose(0, 2, 1, 3))
    qk = np.concatenate([np.repeat(np.asarray(q_norm_w, f)[:, None, :], 4, 1), np.repeat(np.asarray(k_norm_w, f)[:, None, :], 2, 1)], 1)
    m["qkw"] = np.ascontiguousarray(np.broadcast_to(qk[:, None], (depth, 128, 6, 64)))
    m["a_log"] = np.ascontiguousarray(np.broadcast_to(np.asarray(a_log, f).reshape(depth, 1, 8), (depth, 128, 8)))
    m["dt_bias"] = np.ascontiguousarray(np.broadcast_to(np.asarray(dt_bias, f).reshape(depth, 1, 8), (depth, 128, 8)))
    m["ident"] = np.eye(128, dtype=f)
    m["sgu_wT"] = np.ascontiguousarray(np.asarray(sgu_w, f).transpose(0, 3, 1, 2))
    sb_ = np.asarray(sgu_b, f)
    sbT = np.zeros((depth, 128, 2, 128), f)
    for hp in range(2):
        for h2 in range(2):
            sbT[:, h2 * 64:(h2 + 1) * 64, hp, :] = sb_[:, hp * 2 + h2, None, :]
    m["sgu_bT"] = sbT
    m["conv_w"] = np.ascontiguousarray(np.asarray(conv_w, f).reshape(depth, 5, 6, 128).transpose(0, 3, 2, 1))
    pw = np.zeros((depth, 128, 2, 128), f)
    pwi = np.asarray(pool_w, f)
    for ch in range(2):
        for g2 in range(2):
            pw[:, g2 * 64:(g2 + 1) * 64, ch, g2 * 64:(g2 + 1) * 64] = pwi[:, ch * 2 + g2]
    m["pool_w"] = pw
    m["pool_s"] = np.ascontiguousarray(np.asarray(pool_scale, f).reshape(depth, 2, 128).transpose(0, 2, 1))
    m["dn_w"] = np.ascontiguousarray(np.broadcast_to(np.asarray(dn_norm_w, f)[:, None, None, :], (depth, 128, 4, 64)))
    k_ = np.arange(128)
    m["triU"] = (k_[:, None] <= k_[None, :]).astype(f)
    m["triL"] = (k_[:, None] >= k_[None, :]).astype(f)
    for i, S in enumerate(seqs):
        c, s_ = rope_tables(S)
        m[f"cos{i}"] = c
        m[f"sin{i}"] = s_
        t = np.arange(S)
        ic = np.zeros((128, 2, S), f)
        for g, win in enumerate(POOL_WINDOWS):
            lo = np.clip(t - win // 2, 0, S)
            hi = np.clip(t + win // 2, 0, S)
            ic[(g % 2) * 64:(g % 2) * 64 + 64, g // 2, :] = (1.0 / (hi - lo).astype(f))[None, :]
        m[f"icnt{i}"] = ic
    return m


_NC_CACHE = {}


def kernel(x_prompt, x_sample, norm_w, w_in, sgu_w, sgu_b, conv_w, a_log, dt_bias, dn_norm_w,
           q_norm_w, k_norm_w, pool_w, pool_scale, w_out):
    x_prompt = np.asarray(x_prompt, np.float32)
    x_sample = np.asarray(x_sample, np.float32)
    depth = int(np.asarray(w_in).shape[0])
    seqs = [x_prompt.shape[1], x_sample.shape[1]]
    key = (tuple(seqs), depth)
    if key not in _NC_CACHE:
        _NC_CACHE[key] = build(seqs, depth, divs=(x_prompt.shape[0], x_sample.shape[0]),
                               groups=(8 // x_prompt.shape[0], 8 // x_sample.shape[0]))
    nc = _NC_CACHE[key]
    common = host_layout(seqs, depth, norm_w, w_in, sgu_w, sgu_b, conv_w, a_log, dt_bias, dn_norm_w,
                         q_norm_w, k_norm_w, pool_w, pool_scale, w_out)
    nb_p, nb_s = x_prompt.shape[0], x_sample.shape[0]
    in_maps = []
    for c in range(8):
        mm = dict(common)
        mm["x0"] = np.ascontiguousarray(x_prompt[c % nb_p])
        mm["x1"] = np.ascontiguousarray(x_sample[c % nb_s])
        in_maps.append(mm)
    res = run_bass_kernel_spmd(nc, in_maps, core_ids=list(range(8)))
    yp = np.zeros(x_prompt.shape, np.float32)
    ys = np.zeros(x_sample.shape, np.float32)
    gp, gs = 8 // nb_p, 8 // nb_s
    np_, ns_ = seqs[0] // gp, seqs[1] // gs
    for c in range(8):
        yp[c % nb_p, (c // nb_p) * np_:(c // nb_p + 1) * np_] = np.asarray(res.results[c]["y0"], np.float32)
        ys[c % nb_s, (c // nb_s) * ns_:(c // nb_s + 1) * ns_] = np.asarray(res.results[c]["y1"], np.float32)
    return (yp, ys)
```

```python
import numpy as np
import ml_dtypes
from contextlib import ExitStack
import concourse.bass as bass
import concourse.mybir as mybir
from concourse.bass_utils import run_bass_kernel_spmd

F32 = mybir.dt.float32
F32R = mybir.dt.float32r
BF16 = mybir.dt.bfloat16
AF = mybir.ActivationFunctionType
ALU = mybir.AluOpType
AX = mybir.AxisListType

D = 1024
NCOL = 3088
EPS = 1e-6
G_AU, G_AV, G_AZ, G_BQ, G_BK, G_BV, G_BZ, G_CQ, G_CK, G_CV, G_CZ, G_DX, G_DZ = \
    0, 2, 4, 6, 8, 10, 12, 14, 16, 17, 18, 20, 22
ORIG_OFF = dict(au=0, av=256, az=512, bq=768, bk=1024, bv=1280, bz=1536, bb=1792, ba=1800,
                cq=1808, ck=2064, cv=2192, cz=2320, dx=2576, dz=2832)
COL_PERM = np.concatenate([
    np.arange(0, 1792), np.arange(1808, 3088), np.arange(1792, 1808)])


class Buf:
    __slots__ = ("name", "w", "r", "sem", "epoch")

    def __init__(self, name):
        self.name = name
        self.w = None
        self.r = {}
        self.sem = None
        self.epoch = -1


class K:
    ENG = ("pe", "act", "dve", "pool", "sp")
    NDMA = 48

    def __init__(self, nc, stack):
        self.nc = nc
        self.sems = {}
        self.cnt = {}
        self.stack = stack
        for e in self.ENG:
            self._newsem(e)
        self.prog = {e: [] for e in self.ENG}
        self.waited = {e: {} for e in self.ENG}
        self.ninstr = 0
        self.pool_idx = 0
        self.epoch = 0
        for i in range(self.NDMA):
            self._newsem(f"d{i}")

    def _newsem(self, key):
        self.sems[key] = self.stack.enter_context(self.nc.semaphore("s_" + key))
        self.cnt[key] = 0

    LIMIT = None

    def op(self, e, fn, reads=(), writes=(), dma=None):
        if K.LIMIT is not None and self.ninstr >= K.LIMIT:
            return
        need = {}

        def want(tok):
            if tok is None:
                return
            k, v = tok
            if k == "pe" and e == "pe" and dma is None:
                return
            if k not in self.ENG:
                v = self.cnt[k]
            if need.get(k, 0) < v:
                need[k] = v
        for b in reads:
            want(b.w)
        for b in writes:
            want(b.w)
            for k, v in b.r.items():
                want((k, v))
        waits = []
        wd = self.waited[e]
        for k, v in need.items():
            if wd.get(k, 0) < v:
                wd[k] = v
                waits.append((k, v))
        if dma is not None:
            b = (list(writes) + list(reads))[0]
            if b.sem is None or b.epoch != self.epoch:
                b.sem = f"d{self.pool_idx}"
                b.epoch = self.epoch
                self.pool_idx += 1
                assert self.pool_idx <= self.NDMA, "DMA semaphore pool exhausted"
            key, inc = b.sem, 16
        else:
            key, inc = e, 1
        self.cnt[key] += inc
        tok = (key, self.cnt[key])
        for b in reads:
            if b.r.get(key, 0) < tok[1]:
                b.r[key] = tok[1]
        for b in writes:
            b.w = tok
            b.r = {}
        self.prog[e].append((waits, fn, key, inc))
        self.ninstr += 1

    def barrier(self):
        self.pool_idx = 0
        self.epoch += 1
        tot = dict(self.cnt)
        for e in self.ENG:
            waits = []
            for k, v in tot.items():
                if v > 0 and self.waited[e].get(k, 0) < v:
                    self.waited[e][k] = v
                    waits.append((k, v))
            if waits:
                self.prog[e].append((waits, None, None, 0))

    def flush(self):
        nc = self.nc
        _DYN.clear()
        with nc.Block() as block:
            def run(e, eng):
                for waits, fn, key, inc in self.prog[e]:
                    for k, v in waits:
                        eng.wait_ge(self.sems[k], v)
                    if fn is not None:
                        fn(eng).then_inc(self.sems[key], inc)

            @block.tensor
            def _(eng):
                run("pe", eng)

            @block.scalar
            def _(eng):
                run("act", eng)

            @block.vector
            def _(eng):
                run("dve", eng)

            @block.gpsimd
            def _(eng):
                run("pool", eng)

            @block.sync
            def _(eng):
                run("sp", eng)
        self.prog = {e: [] for e in self.ENG}


class T:
    def __init__(self, t, name):
        self.t = t
        self.b = Buf(name)

    def __getitem__(self, idx):
        return self.t[idx]


_UID = [0]


def sb(nc, st, name, shape, dt):
    _UID[0] += 1
    nm = f"sb{_UID[0]}_{name}"
    return T(st.enter_context(nc.sbuf_tensor(nm, list(shape), dt)), nm)


def ps(nc, st, name, shape, dt):
    _UID[0] += 1
    nm = f"ps{_UID[0]}_{name}"
    return T(st.enter_context(nc.psum_tensor(nm, list(shape), dt)), nm)


def rope_tables(S):
    rows = np.repeat(np.arange(S // 64), 64)
    cols = np.tile(np.arange(64), S // 64)
    inv = np.power(np.float32(10000.0), -2.0 * np.arange(16, dtype=np.float32) / 32).astype(np.float32)
    ang = np.stack([rows, cols], -1).astype(np.float32)[:, :, None] * inv
    return np.cos(ang).astype(np.float32).reshape(S, 32), np.sin(ang).astype(np.float32).reshape(S, 32)


def build(seqs, depth, debug=False, phases="pfdao", divs=(2, 4), groups=(4, 2)):
    nc = bass.Bass("TRN2", target_bir_lowering=False)
    dr = {}

    def din(name, shape, dt=F32):
        dr[name] = nc.dram_tensor(name, list(shape), dt, kind="ExternalInput").ap()
        return dr[name]

    def dscr(name, shape, dt, out=False):
        kind = "ExternalOutput" if out else "Internal"
        dr[name] = nc.dram_tensor(name, list(shape), dt, kind=kind).ap()
        return dr[name]

    nseq = len(seqs)
    for i, S in enumerate(seqs):
        din(f"x{i}", [S, D])
        din(f"cos{i}", [S, 32])
        din(f"sin{i}", [S, 32])
        dscr(f"y{i}", [S // groups[i], D], F32, out=True)
        dscr(f"y1_{i}", [S, D], F32)
        dscr(f"projT{i}", [3072, S], BF16, out=debug)
        dscr(f"qT{i}", [256, S], BF16, out=debug)
        dscr(f"kT{i}", [128, S], BF16, out=debug)
        dscr(f"va{i}", [S, 130], BF16, out=debug)
        dscr(f"bg{i}", [S, 16], F32, out=debug)
        dscr(f"mixT{i}", [1024, S], BF16, out=debug)
        dscr(f"convT{i}", [768, S], BF16, out=debug)
        dscr(f"of{i}", [S, 256], F32, out=debug)
        npi = S // groups[i]
        dscr(f"qTp{i}", [256, npi], BF16)
        dscr(f"zTp{i}", [256, npi], BF16)
        dscr(f"mixTp{i}", [1024, npi], BF16)
        dscr(f"xp{i}", [npi, D], F32)
        if debug:
            dscr(f"osum{i}", [S, 256], F32, out=True)
    din("w_in", [depth, 128, 8, NCOL])
    din("norm_w", [depth, 128, 8])
    din("w_out", [depth, 128, 8, D])
    din("qkw", [depth, 128, 6, 64])
    din("a_log", [depth, 128, 8])
    din("dt_bias", [depth, 128, 8])
    din("ident", [128, 128])
    din("sgu_wT", [depth, 128, 4, 128])
    din("sgu_bT", [depth, 128, 2, 128])
    din("conv_w", [depth, 128, 6, 5])
    din("pool_w", [depth, 128, 2, 128])
    din("pool_s", [depth, 128, 2])
    din("dn_w", [depth, 128, 4, 64])
    din("triU", [128, 128])
    din("triL", [128, 128])
    for i, S in enumerate(seqs):
        din(f"icnt{i}", [128, 2, S])

    with ExitStack() as top:
        kk = K(nc, top)
        with nc.Block() as blk0:
            @blk0.vector
            def _(eng):
                for key in kk.sems:
                    eng.sem_clear(kk.sems[key])
        ident_f = sb(nc, top, "ident_f", [128, 128], F32)
        ident = sb(nc, top, "ident_b", [128, 128], BF16)
        ones_f = sb(nc, top, "ones_f", [128, 128], F32)
        kk.op("sp", lambda e: e.dma_start(out=ident_f[:], in_=dr["ident"][:, :]), writes=[ident_f.b], dma="c0")
        kk.op("dve", lambda e: e.tensor_copy(out=ident[:], in_=ident_f[:]), reads=[ident_f.b], writes=[ident.b])
        kk.op("pool", lambda e: e.memset(ones_f[:], 1.0), writes=[ones_f.b])

        for l in range(depth):
            for si, S in enumerate(seqs):
                xin = dr[f"x{si}"] if l == 0 else dr[f"y1_{si}"]
                last = (l == depth - 1)
                yout = dr[f"y{si}"] if last else dr[f"y1_{si}"]
                part = (S // groups[si], divs[si]) if last else None
                for ph in phases:
                    if ph == "p":
                        phase_proj(nc, kk, dr, l, si, S, xin, ident, ident_f)
                    elif ph == "f":
                        phase_fm(nc, kk, dr, l, si, S, ident)
                    elif ph == "d":
                        phase_dn(nc, kk, dr, l, si, S, ident, ident_f, ones_f)
                    elif ph == "a":
                        if part is not None:
                            phase_part(nc, kk, dr, si, S, xin, part)
                            kk.barrier()
                            kk.flush()
                        phase_attn(nc, kk, dr, l, si, S, ones_f, part)
                    elif ph == "o":
                        phase_out(nc, kk, dr, l, si, S, xin, yout, part)
                    kk.barrier()
                    kk.flush()
    return nc


def load_w_bf16(nc, kk, st, name, src_ap, ncols, scale_t=None, chunk=1024):
    w = sb(nc, st, name, [128, 8, ncols], BF16)
    stg = [sb(nc, st, f"{name}_stg{i}", [128, chunk], F32) for i in range(2)]
    n = 0
    for kc in range(8):
        for c0 in range(0, ncols, chunk):
            cw = min(chunk, ncols - c0)
            s = stg[n % 2]
            kk.op("sp", lambda e, s=s, kc=kc, c0=c0, cw=cw: e.dma_start(out=s[:, 0:cw], in_=src_ap[:, kc, c0:c0 + cw]),
                  writes=[s.b], dma=f"wl{n % 2}")
            if scale_t is not None:
                kk.op("dve", lambda e, s=s, kc=kc, c0=c0, cw=cw: e.tensor_scalar(
                    out=w[:, kc, c0:c0 + cw], in0=s[:, 0:cw], scalar1=scale_t[:, kc:kc + 1], scalar2=1.0,
                    op0=ALU.mult, op1=ALU.mult), reads=[s.b, scale_t.b], writes=[w.b])
            else:
                eng = "dve" if n % 2 == 0 else "pool"
                kk.op(eng, lambda e, s=s, kc=kc, c0=c0, cw=cw: e.tensor_copy(out=w[:, kc, c0:c0 + cw], in_=s[:, 0:cw]),
                      reads=[s.b], writes=[w.b])
            n += 1
    return w


def phase_proj(nc, kk, dr, l, si, S, xin, ident, ident_f):
    with ExitStack() as st:
        nw = sb(nc, st, "nw", [128, 8], F32)
        kk.op("sp", lambda e: e.dma_start(out=nw[:], in_=dr["norm_w"][l]), writes=[nw.b], dma="c0")
        w = load_w_bf16(nc, kk, st, "w_in_sb", dr["w_in"][l], NCOL, scale_t=nw)
        qkw = sb(nc, st, "qkw", [128, 6, 64], F32)
        kk.op("sp", lambda e: e.dma_start(out=qkw[:], in_=dr["qkw"][l]), writes=[qkw.b], dma="c0")
        alog = sb(nc, st, "alog", [128, 8], F32)
        nA = sb(nc, st, "nA", [128, 8], F32)
        dtb = sb(nc, st, "dtb", [128, 8], F32)
        kk.op("sp", lambda e: e.dma_start(out=alog[:], in_=dr["a_log"][l]), writes=[alog.b], dma="c0")
        kk.op("sp", lambda e: e.dma_start(out=dtb[:], in_=dr["dt_bias"][l]), writes=[dtb.b], dma="c0")
        kk.op("act", lambda e: e.activation(out=nA[:], in_=alog[:], func=AF.Exp), reads=[alog.b], writes=[nA.b])
        kk.op("dve", lambda e: e.tensor_scalar(out=nA[:], in0=nA[:], scalar1=-1.0, scalar2=1.0, op0=ALU.mult, op1=ALU.mult),
              reads=[nA.b], writes=[nA.b])

        NS = 2
        xt = [sb(nc, st, f"xt{i}", [128, D], F32) for i in range(NS)]
        junk = sb(nc, st, "junk", [128, D], BF16)
        hb = [sb(nc, st, f"hb{i}", [128, D], BF16) for i in range(NS)]
        ssq = [sb(nc, st, f"ssq{i}", [128, 1], F32) for i in range(NS)]
        rstd = [sb(nc, st, f"rstd{i}", [128, 1], F32) for i in range(NS)]
        pT = [ps(nc, st, f"pT{i}", [128, 8, 128], BF16) for i in range(2)]
        hT = [sb(nc, st, f"hT{i}", [128, 8, 512], BF16) for i in range(2)]
        pacc = [ps(nc, st, f"pacc{i}", [128, 512], F32) for i in range(3)]
        ptk = [ps(nc, st, f"ptk{i}", [128, 512], F32) for i in range(1)]
        pbg = [ps(nc, st, f"pbg{i}", [128, 512], F32) for i in range(1)]
        ptr = ps(nc, st, "ptr", [128, 3, 128], BF16)
        stage = [sb(nc, st, f"stage{i}", [128, 24, 512], BF16) for i in range(2)]
        cs = [sb(nc, st, f"cs{i}", [128, 2, 32], F32) for i in range(NS)]
        qk = [sb(nc, st, f"qk{i}", [128, 6, 64], F32) for i in range(NS)]
        qsq = sb(nc, st, "qsq", [128, 6, 64], F32)
        qss = [sb(nc, st, f"qss{i}", [128, 6], F32) for i in range(NS)]
        qr = [sb(nc, st, f"qr{i}", [128, 6, 64], F32) for i in range(NS)]
        tmpa = sb(nc, st, "tmpa", [128, 6, 2, 16], F32)
        tmpb = sb(nc, st, "tmpb", [128, 6, 2, 16], F32)
        qkb = [sb(nc, st, f"qkb{i}", [128, 384], BF16) for i in range(NS)]
        qkT = [sb(nc, st, f"qkT{i}", [128, 3, 512], BF16) for i in range(2)]
        va = [sb(nc, st, f"va{i}", [128, 130], BF16) for i in range(NS)]
        bgt = [sb(nc, st, f"bgt{i}", [128, 16], F32) for i in range(NS)]
        t8 = [sb(nc, st, f"t8{i}", [128, 16], F32) for i in range(NS)]
        for v_ in va:
            kk.op("pool", lambda e, v_=v_: e.memset(v_[:], 1.0), writes=[v_.b])

        projT = dr[f"projT{si}"]
        nblk = S // 512
        ev = 0
        for tb in range(nblk):
            hTb = hT[tb % 2]
            qkTb = qkT[tb % 2]
            for t4 in range(4):
                ti = tb * 4 + t4
                s = ti % NS
                r0 = ti * 128
                x_, h_, sq_, rs_ = xt[s], hb[s], ssq[s], rstd[s]
                kk.op("sp", lambda e, x_=x_, r0=r0: e.dma_start(out=x_[:], in_=xin[r0:r0 + 128, :]),
                      writes=[x_.b], dma=f"x{s}")
                kk.op("sp", lambda e, c_=cs[s], r0=r0: e.dma_start(out=c_[:, 0, :], in_=dr[f"cos{si}"][r0:r0 + 128, :]),
                      writes=[cs[s].b], dma=f"x{s}")
                kk.op("sp", lambda e, c_=cs[s], r0=r0: e.dma_start(out=c_[:, 1, :], in_=dr[f"sin{si}"][r0:r0 + 128, :]),
                      writes=[cs[s].b], dma=f"x{s}")
                kk.op("act", lambda e, x_=x_, sq_=sq_: e.activation(out=junk[:], in_=x_[:], func=AF.Square, accum_out=sq_[:]),
                      reads=[x_.b], writes=[junk.b, sq_.b])
                kk.op("dve", lambda e, sq_=sq_, rs_=rs_: e.tensor_scalar(out=rs_[:], in0=sq_[:], scalar1=1.0 / D, scalar2=EPS,
                                                                         op0=ALU.mult, op1=ALU.add), reads=[sq_.b], writes=[rs_.b])
                kk.op("act", lambda e, rs_=rs_: e.activation(out=rs_[:], in_=rs_[:], func=AF.Ln), reads=[rs_.b], writes=[rs_.b])
                kk.op("act", lambda e, rs_=rs_: e.activation(out=rs_[:], in_=rs_[:], func=AF.Exp, scale=-0.5), reads=[rs_.b], writes=[rs_.b])
                kk.op("dve", lambda e, x_=x_, h_=h_, rs_=rs_: e.tensor_scalar(out=h_[:], in0=x_[:], scalar1=rs_[:, 0:1], scalar2=1.0,
                                                                              op0=ALU.mult, op1=ALU.mult),
                      reads=[x_.b, rs_.b], writes=[h_.b])
                p_ = pT[ti % 2]
                for kc in range(8):
                    kk.op("pe", lambda e, p_=p_, h_=h_, kc=kc: e.transpose(out=p_[:, kc, :], in_=h_[:, kc * 128:(kc + 1) * 128],
                                                                          identity=ident[:]),
                          reads=[h_.b, ident.b], writes=[p_.b])
                kk.op("act", lambda e, p_=p_, hTb=hTb, t4=t4: e.activation(out=hTb[:, :, t4 * 128:(t4 + 1) * 128], in_=p_[:],
                                                                           func=AF.Identity),
                      reads=[p_.b], writes=[hTb.b])
                pk = ptk[0]
                for kc in range(8):
                    kk.op("pe", lambda e, pk=pk, hTb=hTb, t4=t4, kc=kc: e.matmul(
                        pk[:], hTb[:, kc, t4 * 128:(t4 + 1) * 128], w[:, kc, G_CQ * 128:G_CQ * 128 + 512],
                        start=(kc == 0), stop=(kc == 7)), reads=[hTb.b, w.b], writes=[pk.b])
                pb_ = pbg[0]
                for kc in range(8):
                    kk.op("pe", lambda e, pb_=pb_, hTb=hTb, t4=t4, kc=kc: e.matmul(
                        pb_[:, 0:16], hTb[:, kc, t4 * 128:(t4 + 1) * 128], w[:, kc, 3072:3088],
                        start=(kc == 0), stop=(kc == 7)), reads=[hTb.b, w.b], writes=[pb_.b])
                q_, ss_, r_, qb_, va_, c_ = qk[s], qss[s], qr[s], qkb[s], va[s], cs[s]
                kk.op("dve", lambda e, q_=q_, pk=pk: e.tensor_copy(out=q_[:].rearrange("p a b -> p (a b)"), in_=pk[:, 0:384]),
                      reads=[pk.b], writes=[q_.b])
                kk.op("dve", lambda e, va_=va_, pk=pk: e.tensor_copy(
                    out=va_[:].rearrange("p (a b) -> p a b", a=2)[:, :, 0:64],
                    in_=pk[:, 384:512].rearrange("p (a b) -> p a b", a=2)), reads=[pk.b], writes=[va_.b])
                kk.op("pool", lambda e, va_=va_, r0=r0: e.dma_start(out=dr[f"va{si}"][r0:r0 + 128, :], in_=va_[:]),
                      reads=[va_.b], dma=f"st{s}")
                kk.op("dve", lambda e, q_=q_: e.tensor_tensor(out=qsq[:], in0=q_[:], in1=q_[:], op=ALU.mult),
                      reads=[q_.b], writes=[qsq.b])
                kk.op("dve", lambda e, ss_=ss_: e.tensor_reduce(out=ss_[:], in_=qsq[:], axis=AX.X, op=ALU.add),
                      reads=[qsq.b], writes=[ss_.b])
                kk.op("dve", lambda e, ss_=ss_: e.tensor_scalar(out=ss_[:], in0=ss_[:], scalar1=1.0 / 64, scalar2=EPS,
                                                                op0=ALU.mult, op1=ALU.add), reads=[ss_.b], writes=[ss_.b])
                kk.op("act", lambda e, ss_=ss_: e.activation(out=ss_[:], in_=ss_[:], func=AF.Ln), reads=[ss_.b], writes=[ss_.b])
                kk.op("act", lambda e, ss_=ss_: e.activation(out=ss_[:], in_=ss_[:], func=AF.Exp, scale=-0.5), reads=[ss_.b], writes=[ss_.b])
                kk.op("dve", lambda e, q_=q_, ss_=ss_: e.tensor_tensor(
                    out=q_[:], in0=q_[:], in1=ss_[:].unsqueeze(2).to_broadcast([128, 6, 64]), op=ALU.mult),
                    reads=[q_.b, ss_.b], writes=[q_.b])
                kk.op("pool", lambda e, q_=q_: e.tensor_tensor(out=q_[:], in0=q_[:], in1=qkw[:], op=ALU.mult),
                      reads=[q_.b, qkw.b], writes=[q_.b])
                def v5(t):
                    return t[:].rearrange("p h (a b f) -> p h a b f", a=2, b=2)
                cosb = lambda c_: c_[:, 0, :].rearrange("p (a f) -> p a f", a=2).unsqueeze(1).to_broadcast([128, 6, 2, 16])
                sinb = lambda c_: c_[:, 1, :].rearrange("p (a f) -> p a f", a=2).unsqueeze(1).to_broadcast([128, 6, 2, 16])
                kk.op("dve", lambda e, q_=q_, c_=c_: e.tensor_tensor(out=tmpa[:], in0=v5(q_)[:, :, :, 1, :], in1=sinb(c_), op=ALU.mult),
                      reads=[q_.b, c_.b], writes=[tmpa.b])
                kk.op("pool", lambda e, q_=q_, c_=c_: e.tensor_tensor(out=tmpb[:], in0=v5(q_)[:, :, :, 0, :], in1=sinb(c_), op=ALU.mult),
                      reads=[q_.b, c_.b], writes=[tmpb.b])
                kk.op("dve", lambda e, q_=q_, r_=r_, c_=c_: e.tensor_tensor(out=v5(r_)[:, :, :, 0, :], in0=v5(q_)[:, :, :, 0, :], in1=cosb(c_), op=ALU.mult),
                      reads=[q_.b, c_.b], writes=[r_.b])
                kk.op("pool", lambda e, q_=q_, r_=r_, c_=c_: e.tensor_tensor(out=v5(r_)[:, :, :, 1, :], in0=v5(q_)[:, :, :, 1, :], in1=cosb(c_), op=ALU.mult),
                      reads=[q_.b, c_.b], writes=[r_.b])
                kk.op("dve", lambda e, r_=r_: e.tensor_tensor(out=v5(r_)[:, :, :, 0, :], in0=v5(r_)[:, :, :, 0, :], in1=tmpa[:], op=ALU.subtract),
                      reads=[r_.b, tmpa.b], writes=[r_.b])
                kk.op("dve", lambda e, r_=r_: e.tensor_tensor(out=v5(r_)[:, :, :, 1, :], in0=v5(r_)[:, :, :, 1, :], in1=tmpb[:], op=ALU.add),
                      reads=[r_.b, tmpb.b], writes=[r_.b])
                kk.op("act", lambda e, r_=r_, qb_=qb_: e.activation(out=qb_[:], in_=r_[:].rearrange("p a b -> p (a b)"), func=AF.Identity),
                      reads=[r_.b], writes=[qb_.b])
                for j in range(3):
                    kk.op("pe", lambda e, qb_=qb_, j=j: e.transpose(out=ptr[:, j, :], in_=qb_[:, j * 128:(j + 1) * 128], identity=ident[:]),
                          reads=[qb_.b, ident.b], writes=[ptr.b])
                kk.op("dve", lambda e, qkTb=qkTb, t4=t4: e.tensor_copy(out=qkTb[:, :, t4 * 128:(t4 + 1) * 128], in_=ptr[:]),
                      reads=[ptr.b], writes=[qkTb.b])
                b_, t_ = bgt[s], t8[s]
                kk.op("dve", lambda e, t_=t_, pb_=pb_: e.tensor_tensor(out=t_[:, 8:16], in0=pb_[:, 8:16], in1=dtb[:], op=ALU.add),
                      reads=[pb_.b, dtb.b], writes=[t_.b])
                kk.op("act", lambda e, t_=t_, pb_=pb_: e.activation(out=t_[:, 0:8], in_=pb_[:, 0:8], func=AF.Exp, scale=-1.0),
                      reads=[pb_.b], writes=[t_.b])
                kk.op("act", lambda e, t_=t_: e.activation(out=t_[:, 8:16], in_=t_[:, 8:16], func=AF.Exp),
                      reads=[t_.b], writes=[t_.b])
                kk.op("act", lambda e, t_=t_: e.activation(out=t_[:, 8:16], in_=t_[:, 8:16], func=AF.Ln, bias=1.0),
                      reads=[t_.b], writes=[t_.b])
                kk.op("dve", lambda e, t_=t_: e.tensor_scalar(out=t_[:, 0:8], in0=t_[:, 0:8], scalar1=1.0, scalar2=1.0, op0=ALU.add, op1=ALU.mult),
                      reads=[t_.b], writes=[t_.b])
                kk.op("dve", lambda e, t_=t_, b_=b_: e.reciprocal(out=b_[:, 0:8], in_=t_[:, 0:8]), reads=[t_.b], writes=[b_.b])
                kk.op("dve", lambda e, t_=t_, b_=b_: e.tensor_tensor(out=b_[:, 8:16], in0=t_[:, 8:16], in1=nA[:], op=ALU.mult),
                      reads=[t_.b, nA.b], writes=[b_.b])
                kk.op("pool", lambda e, b_=b_, r0=r0: e.dma_start(out=dr[f"bg{si}"][r0:r0 + 128, :], in_=b_[:]),
                      reads=[b_.b], dma=f"st{s}")
            c0 = tb * 512
            kk.op("pool", lambda e, qkTb=qkTb, c0=c0: e.dma_start(
                out=dr[f"qT{si}"][:, c0:c0 + 512].rearrange("(j p) s -> p j s", p=128), in_=qkTb[:, 0:2, :]),
                reads=[qkTb.b], dma=f"sq{tb % 2}")
            kk.op("pool", lambda e, qkTb=qkTb, c0=c0: e.dma_start(out=dr[f"kT{si}"][:, c0:c0 + 512], in_=qkTb[:, 2, :]),
                  reads=[qkTb.b], dma=f"sq{tb % 2}")
            stg = stage[tb % 2]
            for g in range(24):
                pa = pacc[g % 3]
                for kc in range(8):
                    kk.op("pe", lambda e, pa=pa, g=g, kc=kc, hTb=hTb: e.matmul(
                        pa[:], w[:, kc, g * 128:(g + 1) * 128], hTb[:, kc, :], start=(kc == 0), stop=(kc == 7)),
                        reads=[w.b, hTb.b], writes=[pa.b])
                if ev % 2 == 0:
                    kk.op("act", lambda e, pa=pa, g=g, stg=stg: e.activation(out=stg[:, g, :], in_=pa[:], func=AF.Identity),
                          reads=[pa.b], writes=[stg.b])
                else:
                    kk.op("dve", lambda e, pa=pa, g=g, stg=stg: e.tensor_copy(out=stg[:, g, :], in_=pa[:]),
                          reads=[pa.b], writes=[stg.b])
                ev += 1
            for h2 in range(2):
                kk.op("pool", lambda e, stg=stg, c0=c0, h2=h2: e.dma_start(
                    out=projT[h2 * 1536:(h2 + 1) * 1536, c0:c0 + 512].rearrange("(g p) s -> p g s", p=128),
                    in_=stg[:, h2 * 12:(h2 + 1) * 12, :]), reads=[stg.b], dma=f"sp{tb % 2}")


_DYN = {}


def dyn(e, part, base, n):
    if part is None:
        return slice(base, base + n)
    npart, div = part
    key = (id(e), div, npart)
    if key not in _DYN:
        _DYN[key] = e.snap((e.partition_id() // div) * npart)
    return bass.ds(_DYN[key] + base, n)


def phase_part(nc, kk, dr, si, S, xin, part):
    npart, div = part
    dummy = Buf("partcopy")

    def off(e):
        return e.snap((e.partition_id() // div) * npart)

    if si == 0:
        qe, xe, me = "sp", "sp", "pool"
    else:
        qe, xe, me = "pool", "act", "pool"
    kk.op(qe, lambda e: e.dma_start(out=dr[f"qTp{si}"][:, :], in_=dr[f"qT{si}"][:, bass.ds(off(e), npart)]), writes=[dummy], dma="c0")
    kk.op(qe, lambda e: e.dma_start(out=dr[f"zTp{si}"][:, :], in_=dr[f"projT{si}"][G_CZ * 128:(G_CZ + 2) * 128, bass.ds(off(e), npart)]),
          writes=[dummy], dma="c0")
    kk.op(me, lambda e: e.dma_start(out=dr[f"mixTp{si}"][0:512, :], in_=dr[f"mixT{si}"][0:512, bass.ds(off(e), npart)]), writes=[dummy], dma="c0")
    kk.op(me, lambda e: e.dma_start(out=dr[f"mixTp{si}"][768:1024, :], in_=dr[f"mixT{si}"][768:1024, bass.ds(off(e), npart)]),
          writes=[dummy], dma="c0")
    step = 1024
    for r in range(0, npart, step):
        n = min(step, npart - r)
        kk.op(xe, lambda e, r=r, n=n: e.dma_start(out=dr[f"xp{si}"][r:r + n, :], in_=xin[bass.ds(off(e) + r, n), :]), writes=[dummy], dma="c0")


def phase_out(nc, kk, dr, l, si, S, xin, yout, part=None):
    with ExitStack() as st:
        ntok = S if part is None else part[0]
        if part is not None:
            xin = dr[f"xp{si}"]
        wo = load_w_bf16(nc, kk, st, "w_out_sb", dr["w_out"][l], D)
        NS = 2
        mt = [sb(nc, st, f"mt{i}", [128, 8, 128], BF16) for i in range(NS)]
        xt = [sb(nc, st, f"xo{i}", [128, D], F32) for i in range(NS)]
        yt = [sb(nc, st, f"yo{i}", [128, D], F32) for i in range(NS)]
        po = [ps(nc, st, f"po{i}", [128, 512], F32) for i in range(4)]
        mixT = dr[f"mixT{si}"] if part is None else dr[f"mixTp{si}"]
        for ti in range(ntok // 128):
            s = ti % NS
            r0 = ti * 128
            m_, x_, y_ = mt[s], xt[s], yt[s]
            kk.op("sp", lambda e, m_=m_, r0=r0: e.dma_start(out=m_[:], in_=mixT[:, r0:r0 + 128].rearrange("(k p) s -> p k s", p=128)),
                  writes=[m_.b], dma=f"x{s}")
            kk.op("sp", lambda e, x_=x_, r0=r0: e.dma_start(out=x_[:], in_=xin[r0:r0 + 128, :]), writes=[x_.b], dma=f"x{s}")
            for g in range(2):
                p_ = po[(ti * 2 + g) % 4]
                for kc in range(8):
                    kk.op("pe", lambda e, p_=p_, m_=m_, kc=kc, g=g: e.matmul(
                        p_[:], m_[:, kc, :], wo[:, kc, g * 512:(g + 1) * 512], start=(kc == 0), stop=(kc == 7)),
                        reads=[m_.b, wo.b], writes=[p_.b])
                kk.op("dve", lambda e, p_=p_, x_=x_, y_=y_, g=g: e.tensor_tensor(
                    out=y_[:, g * 512:(g + 1) * 512], in0=p_[:], in1=x_[:, g * 512:(g + 1) * 512], op=ALU.add),
                    reads=[p_.b, x_.b], writes=[y_.b])
            kk.op("pool", lambda e, y_=y_, r0=r0: e.dma_start(out=yout[r0:r0 + 128, :], in_=y_[:]), reads=[y_.b], dma=f"st{s}")


def phase_attn(nc, kk, dr, l, si, S, ones_f, part=None):
    with ExitStack() as st:
        nkt = S // 128
        KT = sb(nc, st, "KT", [128, S], BF16)
        VA = sb(nc, st, "VA", [128, nkt, 130], BF16)
        for c in range(0, S, 2048):
            ce = min(c + 2048, S)
            kk.op("sp", lambda e, c=c, ce=ce: e.dma_start(out=KT[:, c:ce], in_=dr[f"kT{si}"][:, c:ce]), writes=[KT.b], dma="c0")
        for c in range(0, nkt, 16):
            ce = min(c + 16, nkt)
            kk.op("sp", lambda e, c=c, ce=ce: e.dma_start(out=VA[:, c:ce, :],
                                                          in_=dr[f"va{si}"][c * 128:ce * 128, :].rearrange("(t p) c -> p t c", p=128)),
                  writes=[VA.b], dma="c0")
        qt = [sb(nc, st, f"qt{i}", [128, 2, 512], BF16) for i in range(2)]
        zt = [sb(nc, st, f"zt{i}", [64, 4, 512], BF16) for i in range(2)]
        pss = [ps(nc, st, f"pss{i}", [128, 2, 512], F32) for i in range(2)]
        pex = [sb(nc, st, f"pex{i}", [128, 2, 512], BF16) for i in range(2)]
        pov = [ps(nc, st, f"pov{i}", [128, 512], F32) for i in range(2)]
        pbc = ps(nc, st, "pbc", [64, 512], F32)
        osb = [sb(nc, st, f"osb{i}", [128, 512], F32) for i in range(2)]
        rc = [sb(nc, st, f"rc{i}", [128, 512], F32) for i in range(2)]
        og = [sb(nc, st, f"og{i}", [64, 4, 512], BF16) for i in range(2)]
        it = 0
        nq = S if part is None else part[0]
        qsrc = dr[f"qT{si}"] if part is None else dr[f"qTp{si}"]
        zsrc = dr[f"projT{si}"][G_CZ * 128:(G_CZ + 2) * 128, :] if part is None else dr[f"zTp{si}"]
        mixdst = dr[f"mixT{si}"] if part is None else dr[f"mixTp{si}"]
        for qb in range(nq // 512):
            c0 = qb * 512
            q_ = qt[qb % 2]
            z_ = zt[qb % 2]
            og_ = og[qb % 2]
            for h in range(4):
                kv = h // 2
                kk.op("sp", lambda e, q_=q_, h=h, kv=kv, c0=c0: e.dma_start(
                    out=q_[kv * 64:(kv + 1) * 64, h % 2, :], in_=qsrc[h * 64:(h + 1) * 64, c0:c0 + 512]),
                    writes=[q_.b], dma=f"x{qb % 2}")
            kk.op("sp", lambda e, z_=z_, c0=c0: e.dma_start(
                out=z_[:], in_=zsrc[:, c0:c0 + 512].rearrange("(h p) s -> p h s", p=64)),
                writes=[z_.b], dma=f"x{qb % 2}")
            kk.op("act", lambda e, z_=z_: e.activation(out=z_[:], in_=z_[:], func=AF.Silu), reads=[z_.b], writes=[z_.b])
            for h in range(4):
                kv = h // 2
                po_ = pov[h % 2]
                npair = nkt // 2
                slots = []
                for kp in range(npair):
                    slots.append((pss[it % 2], pex[it % 2]))
                    it += 1

                def emit_qk(kp):
                    ps_ = slots[kp][0]
                    for j in range(2):
                        kt = kp * 2 + j
                        kk.op("pe", lambda e, ps_=ps_, j=j, kt=kt, kv=kv, q_=q_, h=h: e.matmul(
                            ps_[:, j, :], KT[kv * 64:(kv + 1) * 64, kt * 128:(kt + 1) * 128], q_[kv * 64:(kv + 1) * 64, h % 2, :],
                            start=True, stop=True), reads=[KT.b, q_.b], writes=[ps_.b])

                emit_qk(0)
                for kp in range(npair):
                    ps_, pe_ = slots[kp]
                    if kp + 1 < npair:
                        emit_qk(kp + 1)
                    kk.op("act", lambda e, ps_=ps_, pe_=pe_: e.activation(out=pe_[:], in_=ps_[:], func=AF.Exp, scale=0.125),
                          reads=[ps_.b], writes=[pe_.b])
                    for j in range(2):
                        kt = kp * 2 + j
                        kk.op("pe", lambda e, po_=po_, pe_=pe_, j=j, kt=kt, kv=kv: e.matmul(
                            po_[0:65, :], VA[:, kt, kv * 65:(kv + 1) * 65], pe_[:, j, :],
                            start=(kt == 0), stop=(kt == nkt - 1)), reads=[VA.b, pe_.b], writes=[po_.b])
                o_ = osb[h % 2]
                r_ = rc[h % 2]
                kk.op("dve", lambda e, o_=o_, po_=po_: e.tensor_copy(out=o_[0:65, :], in_=po_[0:65, :]), reads=[po_.b], writes=[o_.b])
                kk.op("dve", lambda e, o_=o_, r_=r_: e.reciprocal(out=r_[64:65, :], in_=o_[64:65, :]), reads=[o_.b], writes=[r_.b])
                kk.op("pe", lambda e, r_=r_: e.matmul(pbc[:], ones_f[64:65, 0:64], r_[64:65, :], start=True, stop=True),
                      reads=[r_.b, ones_f.b], writes=[pbc.b])
                kk.op("dve", lambda e, o_=o_: e.tensor_tensor(out=o_[0:64, :], in0=o_[0:64, :], in1=pbc[:], op=ALU.mult),
                      reads=[o_.b, pbc.b], writes=[o_.b])
                kk.op("dve", lambda e, o_=o_, og_=og_, h=h, z_=z_: e.tensor_tensor(out=og_[:, h, :], in0=o_[0:64, :], in1=z_[:, h, :], op=ALU.mult),
                      reads=[o_.b, z_.b], writes=[og_.b])
            kk.op("pool", lambda e, og_=og_, c0=c0: e.dma_start(
                out=mixdst[512:768, c0:c0 + 512].rearrange("(h p) s -> p h s", p=64), in_=og_[:]),
                reads=[og_.b], dma=f"st{qb % 2}")


def phase_fm(nc, kk, dr, l, si, S, ident):
    with ExitStack() as st:
        projT = dr[f"projT{si}"]
        swf = sb(nc, st, "swf", [128, 4, 128], F32)
        sw = sb(nc, st, "sw", [128, 4, 128], BF16)
        sbT = sb(nc, st, "sbT", [128, 2, 128], F32)
        cw = sb(nc, st, "cw", [128, 6, 5], F32)
        pwf = sb(nc, st, "pwf", [128, 2, 128], F32)
        pw = sb(nc, st, "pw", [128, 2, 128], BF16)
        psc = sb(nc, st, "psc", [128, 2], F32)
        kk.op("sp", lambda e: e.dma_start(out=swf[:], in_=dr["sgu_wT"][l]), writes=[swf.b], dma="c0")
        kk.op("sp", lambda e: e.dma_start(out=sbT[:], in_=dr["sgu_bT"][l]), writes=[sbT.b], dma="c0")
        kk.op("sp", lambda e: e.dma_start(out=cw[:], in_=dr["conv_w"][l]), writes=[cw.b], dma="c0")
        kk.op("sp", lambda e: e.dma_start(out=pwf[:], in_=dr["pool_w"][l]), writes=[pwf.b], dma="c0")
        kk.op("sp", lambda e: e.dma_start(out=psc[:], in_=dr["pool_s"][l]), writes=[psc.b], dma="c0")
        kk.op("dve", lambda e: e.tensor_copy(out=sw[:], in_=swf[:]), reads=[swf.b], writes=[sw.b])
        kk.op("dve", lambda e: e.tensor_copy(out=pw[:], in_=pwf[:]), reads=[pwf.b], writes=[pw.b])

        NS = 2
        av = [sb(nc, st, f"av{i}", [128, 6, 512], BF16) for i in range(NS)]
        dxz = [sb(nc, st, f"dxz{i}", [128, 4, 528], BF16) for i in range(NS)]
        icn = [sb(nc, st, f"icn{i}", [128, 2, 512], F32) for i in range(NS)]
        bx = [sb(nc, st, f"bx{i}", [128, 6, 516], BF16) for i in range(NS)]
        pvt = ps(nc, st, "pvt", [128, 2, 128], BF16)
        vtok = sb(nc, st, "vtok", [128, 4, 64], F32)
        vsq = sb(nc, st, "vsq", [128, 4, 64], F32)
        vss = sb(nc, st, "vss", [128, 4], F32)
        vnm = [sb(nc, st, f"vnm{i}", [128, 4, 128], BF16) for i in range(2)]
        for v_ in vnm:
            kk.op("pool", lambda e, v_=v_: e.memset(v_[:], 0.0), writes=[v_.b])
        pm = [ps(nc, st, f"pm{i}", [128, 2, 512], F32) for i in range(1)]
        ta = sb(nc, st, "ta", [128, 2, 512], F32)
        sz = sb(nc, st, "sz", [128, 2, 512], BF16)
        ma = [sb(nc, st, f"ma{i}", [128, 2, 512], BF16) for i in range(NS)]
        ss = sb(nc, st, "ss", [128, 2, 528], F32)
        s2 = sb(nc, st, "s2", [128, 2, 528], F32)
        s4 = sb(nc, st, "s4", [128, 2, 528], F32)
        s8 = sb(nc, st, "s8", [128, 528], F32)
        dfb = sb(nc, st, "dfb", [128, 2, 512], BF16)
        ppl = ps(nc, st, "ppl", [128, 2, 512], F32)
        md = [sb(nc, st, f"md{i}", [128, 2, 512], BF16) for i in range(NS)]
        szd = sb(nc, st, "szd", [128, 2, 512], BF16)
        cacc = sb(nc, st, "cacc", [128, 512], F32)
        cvo = [sb(nc, st, f"cvo{i}", [128, 6, 512], BF16) for i in range(NS)]

        nblk = S // 512
        for tb in range(nblk):
            s = tb % NS
            c0 = tb * 512
            a_, d_, i_, b_ = av[s], dxz[s], icn[s], bx[s]
            kk.op("sp", lambda e, a_=a_, c0=c0: e.dma_start(out=a_[:], in_=projT[0:768, c0:c0 + 512].rearrange("(g p) s -> p g s", p=128)),
                  writes=[a_.b], dma=f"x{s}")
            lo = max(c0 - 8, 0)
            hi = min(c0 + 520, S)
            if lo > c0 - 8:
                kk.op("pool", lambda e, d_=d_: e.memset(d_[:, 0:2, 0:8], 0.0), writes=[d_.b])
            if hi < c0 + 520:
                kk.op("pool", lambda e, d_=d_: e.memset(d_[:, 0:2, 520:528], 0.0), writes=[d_.b])
            kk.op("sp", lambda e, d_=d_, lo=lo, hi=hi, c0=c0: e.dma_start(
                out=d_[:, 0:2, lo - (c0 - 8):hi - (c0 - 8)], in_=projT[G_DX * 128:(G_DX + 2) * 128, lo:hi].rearrange("(g p) s -> p g s", p=128)),
                writes=[d_.b], dma=f"x{s}")
            kk.op("sp", lambda e, d_=d_, c0=c0: e.dma_start(
                out=d_[:, 2:4, 0:512], in_=projT[G_DZ * 128:(G_DZ + 2) * 128, c0:c0 + 512].rearrange("(g p) s -> p g s", p=128)),
                writes=[d_.b], dma=f"x{s}")
            kk.op("sp", lambda e, i_=i_, c0=c0: e.dma_start(out=i_[:], in_=dr[f"icnt{si}"][:, :, c0:c0 + 512]), writes=[i_.b], dma=f"x{s}")
            lo2 = max(c0 - 2, 0)
            hi2 = min(c0 + 514, S)
            if lo2 > c0 - 2:
                kk.op("pool", lambda e, b_=b_: e.memset(b_[:, :, 0:2], 0.0), writes=[b_.b])
            if hi2 < c0 + 514:
                kk.op("pool", lambda e, b_=b_: e.memset(b_[:, :, 514:516], 0.0), writes=[b_.b])
            kk.op("sp", lambda e, b_=b_, lo2=lo2, hi2=hi2, c0=c0: e.dma_start(
                out=b_[:, :, lo2 - (c0 - 2):hi2 - (c0 - 2)], in_=projT[G_BQ * 128:(G_BQ + 6) * 128, lo2:hi2].rearrange("(g p) s -> p g s", p=128)),
                writes=[b_.b], dma=f"x{s}")

            pm_ = pm[0]
            for ch in range(4):
                vn_ = vnm[ch % 2]
                for j in range(2):
                    kk.op("pe", lambda e, a_=a_, j=j, ch=ch: e.transpose(out=pvt[:, j, :], in_=a_[:, 2 + j, ch * 128:(ch + 1) * 128], identity=ident[:]),
                          reads=[a_.b, ident.b], writes=[pvt.b])
                kk.op("dve", lambda e: e.tensor_copy(out=vtok[:].rearrange("p a b -> p (a b)"), in_=pvt[:].rearrange("p a b -> p (a b)")),
                      reads=[pvt.b], writes=[vtok.b])
                kk.op("dve", lambda e: e.tensor_tensor(out=vsq[:], in0=vtok[:], in1=vtok[:], op=ALU.mult), reads=[vtok.b], writes=[vsq.b])
                kk.op("dve", lambda e: e.tensor_reduce(out=vss[:], in_=vsq[:], axis=AX.X, op=ALU.add), reads=[vsq.b], writes=[vss.b])
                kk.op("dve", lambda e: e.tensor_scalar(out=vss[:], in0=vss[:], scalar1=1.0 / 64, scalar2=EPS, op0=ALU.mult, op1=ALU.add),
                      reads=[vss.b], writes=[vss.b])
                kk.op("act", lambda e: e.activation(out=vss[:], in_=vss[:], func=AF.Ln), reads=[vss.b], writes=[vss.b])
                kk.op("act", lambda e: e.activation(out=vss[:], in_=vss[:], func=AF.Exp, scale=-0.5), reads=[vss.b], writes=[vss.b])
                for h in range(4):
                    kk.op("dve", lambda e, vn_=vn_, h=h: e.tensor_scalar(
                        out=vn_[:, h, (h % 2) * 64:(h % 2) * 64 + 64], in0=vtok[:, h, :], scalar1=vss[:, h:h + 1], scalar2=1.0,
                        op0=ALU.mult, op1=ALU.mult), reads=[vtok.b, vss.b], writes=[vn_.b])
                for h in range(4):
                    kk.op("pe", lambda e, vn_=vn_, h=h, ch=ch, pm_=pm_: e.matmul(
                        pm_[:, h // 2, ch * 128:(ch + 1) * 128], vn_[:, h, :], sw[:, h, :], start=(h % 2 == 0), stop=(h % 2 == 1)),
                        reads=[vn_.b, sw.b], writes=[pm_.b])
            m_ = ma[s]
            kk.op("dve", lambda e, pm_=pm_: e.tensor_tensor(
                out=ta[:].rearrange("p a (c i) -> p a c i", c=4), in0=pm_[:].rearrange("p a (c i) -> p a c i", c=4),
                in1=sbT[:].unsqueeze(2).to_broadcast([128, 2, 4, 128]), op=ALU.add), reads=[pm_.b, sbT.b], writes=[ta.b])
            kk.op("act", lambda e, a_=a_: e.activation(out=sz[:], in_=a_[:, 4:6, :], func=AF.Silu), reads=[a_.b], writes=[sz.b])
            kk.op("pool", lambda e, a_=a_: e.tensor_tensor(out=ta[:], in0=ta[:], in1=a_[:, 0:2, :], op=ALU.mult), reads=[ta.b, a_.b], writes=[ta.b])
            kk.op("dve", lambda e, m_=m_: e.tensor_tensor(out=m_[:], in0=ta[:], in1=sz[:], op=ALU.mult), reads=[ta.b, sz.b], writes=[m_.b])
            kk.op("pool", lambda e, m_=m_, c0=c0: e.dma_start(out=dr[f"mixT{si}"][0:256, c0:c0 + 512].rearrange("(g p) s -> p g s", p=128), in_=m_[:]),
                  reads=[m_.b], dma=f"st{s}")

            X = d_
            kk.op("dve", lambda e, X=X: e.tensor_tensor(out=s2[:, :, 1:528], in0=X[:, 0:2, 0:527], in1=X[:, 0:2, 1:528], op=ALU.add),
                  reads=[X.b], writes=[s2.b])
            kk.op("pool", lambda e: e.tensor_tensor(out=s4[:, :, 2:527], in0=s2[:, :, 1:526], in1=s2[:, :, 3:528], op=ALU.add),
                  reads=[s2.b], writes=[s4.b])
            kk.op("dve", lambda e: e.tensor_tensor(out=s8[:, 4:525], in0=s4[:, 1, 2:523], in1=s4[:, 1, 6:527], op=ALU.add),
                  reads=[s4.b], writes=[s8.b])
            kk.op("pool", lambda e: e.tensor_copy(out=ss[0:64, 0, 8:520], in_=s2[0:64, 0, 8:520]), reads=[s2.b], writes=[ss.b])
            kk.op("pool", lambda e: e.tensor_copy(out=ss[64:128, 0, 8:520], in_=s4[64:128, 0, 8:520]), reads=[s4.b], writes=[ss.b])
            kk.op("dve", lambda e: e.tensor_copy(out=ss[0:64, 1, 8:520], in_=s8[0:64, 8:520]), reads=[s8.b], writes=[ss.b])
            kk.op("dve", lambda e: e.tensor_tensor(out=ss[64:128, 1, 8:520], in0=s8[64:128, 4:516], in1=s8[64:128, 12:524], op=ALU.add),
                  reads=[s8.b], writes=[ss.b])
            kk.op("dve", lambda e, i_=i_: e.tensor_tensor(out=ss[:, :, 8:520], in0=ss[:, :, 8:520], in1=i_[:], op=ALU.mult),
                  reads=[ss.b, i_.b], writes=[ss.b])
            kk.op("dve", lambda e, X=X: e.tensor_tensor(out=dfb[:], in0=ss[:, :, 8:520], in1=X[:, 0:2, 8:520], op=ALU.subtract),
                  reads=[ss.b, X.b], writes=[dfb.b])
            for ch in range(2):
                kk.op("pe", lambda e, ch=ch: e.matmul(ppl[:, ch, :], pw[:, ch, :], dfb[:, ch, :], start=True, stop=True),
                      reads=[pw.b, dfb.b], writes=[ppl.b])
            kk.op("act", lambda e, X=X: e.activation(out=szd[:], in_=X[:, 2:4, 0:512], func=AF.Silu), reads=[X.b], writes=[szd.b])
            o_ = md[s]
            for ch in range(2):
                kk.op("dve", lambda e, ch=ch, o_=o_: e.scalar_tensor_tensor(
                    out=o_[:, ch, :], in0=ppl[:, ch, :], scalar=psc[:, ch:ch + 1], in1=szd[:, ch, :], op0=ALU.mult, op1=ALU.mult),
                    reads=[ppl.b, psc.b, szd.b], writes=[o_.b])
            kk.op("pool", lambda e, o_=o_, c0=c0: e.dma_start(out=dr[f"mixT{si}"][768:1024, c0:c0 + 512].rearrange("(g p) s -> p g s", p=128), in_=o_[:]),
                  reads=[o_.b], dma=f"st{s}")

            co = cvo[s]
            for ch in range(6):
                kk.op("dve", lambda e, b_=b_, ch=ch: e.tensor_scalar(out=cacc[:], in0=b_[:, ch, 0:512], scalar1=cw[:, ch, 0:1], scalar2=1.0,
                                                                    op0=ALU.mult, op1=ALU.mult), reads=[b_.b, cw.b], writes=[cacc.b])
                for i in range(1, 5):
                    kk.op("dve", lambda e, b_=b_, ch=ch, i=i: e.scalar_tensor_tensor(
                        out=cacc[:], in0=b_[:, ch, i:i + 512], scalar=cw[:, ch, i:i + 1], in1=cacc[:], op0=ALU.mult, op1=ALU.add),
                        reads=[b_.b, cw.b, cacc.b], writes=[cacc.b])
                kk.op("act", lambda e, co=co, ch=ch: e.activation(out=co[:, ch, :], in_=cacc[:], func=AF.Silu), reads=[cacc.b], writes=[co.b])
            kk.op("pool", lambda e, co=co, c0=c0: e.dma_start(out=dr[f"convT{si}"][:, c0:c0 + 512].rearrange("(g p) s -> p g s", p=128), in_=co[:]),
                  reads=[co.b], dma=f"st{s}")


def bc(ap, shape, axis):
    return ap.unsqueeze(axis).to_broadcast(shape)


def phase_dn(nc, kk, dr, l, si, S, ident, ident_f, ones_f):
    with ExitStack() as st:
        NCH = S // 128
        tri = [sb(nc, st, f"tri{d}", [128, 128], F32) for d in range(2)]
        nmI = [sb(nc, st, f"nmI{d}", [128, 128], F32) for d in range(2)]
        nmS = [sb(nc, st, f"nmS{d}", [128, 128], F32) for d in range(2)]
        offd = sb(nc, st, "offd", [128, 128], F32)
        dnw = sb(nc, st, "dnw", [128, 4, 64], F32)
        kk.op("sp", lambda e: e.dma_start(out=tri[0][:], in_=dr["triU"][:, :]), writes=[tri[0].b], dma="c0")
        kk.op("sp", lambda e: e.dma_start(out=tri[1][:], in_=dr["triL"][:, :]), writes=[tri[1].b], dma="c0")
        kk.op("sp", lambda e: e.dma_start(out=dnw[:], in_=dr["dn_w"][l]), writes=[dnw.b], dma="c0")
        kk.op("dve", lambda e: e.tensor_scalar(out=offd[:], in0=ident_f[:], scalar1=-1.0, scalar2=1.0, op0=ALU.mult, op1=ALU.add),
              reads=[ident_f.b], writes=[offd.b])
        for d in range(2):
            kk.op("dve", lambda e, d=d: e.tensor_scalar(out=nmI[d][:], in0=tri[d][:], scalar1=-1.0, scalar2=1e30, op0=ALU.add, op1=ALU.mult),
                  reads=[tri[d].b], writes=[nmI[d].b])
            kk.op("dve", lambda e, d=d: e.tensor_tensor(out=nmS[d][:], in0=tri[d][:], in1=offd[:], op=ALU.mult),
                  reads=[tri[d].b, offd.b], writes=[nmS[d].b])
            kk.op("dve", lambda e, d=d: e.tensor_scalar(out=nmS[d][:], in0=nmS[d][:], scalar1=-1.0, scalar2=1e30, op0=ALU.add, op1=ALU.mult),
                  reads=[nmS[d].b], writes=[nmS[d].b])

        bkb = ps(nc, st, "bkb", [128, 1024], BF16)
        bk1 = ps(nc, st, "bk1", [128, 512], F32)
        bk2 = ps(nc, st, "bk2", [128, 512], F32)
        bk3 = ps(nc, st, "bk3", [128, 512], F32)
        bk4 = ps(nc, st, "bk4", [128, 512], F32)
        bk5 = ps(nc, st, "bk5", [128, 512], F32)
        bka = ps(nc, st, "bka", [128, 1024], F32)

        def v4(ap, n=4):
            return ap.rearrange("p (c i) -> p c i", c=n)

        cT = [sb(nc, st, f"cT{i}", [128, 6, 128], BF16) for i in range(2)]
        bgc = [sb(nc, st, f"bgc{i}", [128, 16], F32) for i in range(2)]
        ofl = [sb(nc, st, f"ofl{i}", [128, 256], F32) for i in range(2)]
        zb = [sb(nc, st, f"zb{i}", [128, 2, 128], BF16) for i in range(2)]
        tokL = [sb(nc, st, f"tok{i}", [128, 768], F32) for i in range(2)]
        sq = sb(nc, st, "dsq", [128, 512], F32)
        rs = sb(nc, st, "drs", [128, 8], F32)
        qkn = sb(nc, st, "qkn", [128, 8, 64], BF16)
        qkn32 = sb(nc, st, "qkn32", [128, 8, 64], F32)
        ek32 = sb(nc, st, "ek32", [128, 4, 64], F32)
        kd32L = [sb(nc, st, f"kd32{i}", [128, 4, 64], F32) for i in range(2)]
        ekT32L = [sb(nc, st, f"ekT32{i}", [64, 4, 128], F32) for i in range(2)]
        rb = sb(nc, st, "rb", [128, 4, 64], BF16)
        vnew32 = sb(nc, st, "vnew32", [128, 4, 64], F32)
        vb = sb(nc, st, "vb", [128, 256], BF16)
        qkT = sb(nc, st, "dqkT", [64, 8, 128], BF16)
        gs = sb(nc, st, "gs", [128, 8], F32)
        eg = sb(nc, st, "eg", [128, 4], F32)
        ekd = sb(nc, st, "ekd", [128, 4], F32)
        eglL = [sb(nc, st, f"egl{i}", [128, 4], F32) for i in range(2)]
        rg = sb(nc, st, "rg", [128, 4, 128], F32)
        dm = sb(nc, st, "dm", [128, 4, 128], F32)
        dmI = sb(nc, st, "dmI", [128, 4, 128], F32)
        dmS = sb(nc, st, "dmS", [128, 4, 128], F32)
        X = sb(nc, st, "X", [128, 4, 128], F32R)
        ATL = [sb(nc, st, f"AT{i}", [128, 4, 128], BF16) for i in range(2)]
        YS = [sb(nc, st, f"YS{i}", [128, 4, 256], F32R) for i in range(2)]
        YT = [sb(nc, st, f"YT{i}", [128, 4, 128], F32R) for i in range(2)]
        db = sb(nc, st, "db", [128, 4, 128], F32)
        TTbL = [sb(nc, st, f"TTb{i}", [128, 4, 128], BF16) for i in range(2)]
        usb = sb(nc, st, "usb", [128, 4, 64], F32)
        ek = sb(nc, st, "ek", [128, 4, 64], BF16)
        kd = sb(nc, st, "kd", [128, 4, 64], BF16)
        qd = sb(nc, st, "qd", [128, 4, 64], BF16)
        wT = sb(nc, st, "wT", [64, 4, 128], BF16)
        qdTL = [sb(nc, st, f"qdT{i}", [64, 4, 128], BF16) for i in range(2)]
        vnew = sb(nc, st, "vnew", [128, 4, 64], BF16)
        Sf = sb(nc, st, "Sf", [64, 4, 64], F32)
        Sb = sb(nc, st, "Sb", [64, 4, 64], BF16)
        osb = [sb(nc, st, f"dosb{i}", [128, 256], F32) for i in range(2)]
        osq = sb(nc, st, "osq", [128, 256], F32)
        oss = sb(nc, st, "oss", [128, 4], F32)
        onb = sb(nc, st, "onb", [128, 256], BF16)
        omx = [sb(nc, st, f"omx{i}", [128, 2, 128], BF16) for i in range(2)]

        for d in range(2):
            if d == 1:
                kk.barrier()
            kk.op("pool", lambda e: e.memset(Sf[:], 0.0), writes=[Sf.b])
            kk.op("pool", lambda e: e.memset(Sb[:], 0.0), writes=[Sb.b])
            order = list(range(NCH)) if d == 0 else list(range(NCH - 1, -1, -1))

            def prep(n, ci, d=d):
                s = n % 2
                tok, AT, TTb, qdT, egl, kd32, ekT32 = tokL[s], ATL[s], TTbL[s], qdTL[s], eglL[s], kd32L[s], ekT32L[s]
                r0 = ci * 128
                c_, g_ = cT[s], bgc[s]
                yield
                kk.op("sp", lambda e, c_=c_, r0=r0: e.dma_start(out=c_[:], in_=dr[f"convT{si}"][:, r0:r0 + 128].rearrange("(g p) s -> p g s", p=128)),
                      writes=[c_.b], dma=f"x{s}")
                yield
                kk.op("sp", lambda e, g_=g_, r0=r0: e.dma_start(out=g_[:], in_=dr[f"bg{si}"][r0:r0 + 128, :]), writes=[g_.b], dma=f"x{s}")
                if d == 1:
                    kk.op("sp", lambda e, o_=ofl[s], r0=r0: e.dma_start(out=o_[:], in_=dr[f"of{si}"][r0:r0 + 128, :]), writes=[ofl[s].b], dma=f"x{s}")
                    kk.op("sp", lambda e, z_=zb[s], r0=r0: e.dma_start(
                        out=z_[:], in_=dr[f"projT{si}"][G_BZ * 128:(G_BZ + 2) * 128, r0:r0 + 128].rearrange("(g p) s -> p g s", p=128)),
                        writes=[zb[s].b], dma=f"x{s}")
                gd = g_[:, 8 + 4 * d:12 + 4 * d]
                bd = g_[:, 4 * d:4 * d + 4]
                yield
                for j in range(6):
                    kk.op("pe", lambda e, c_=c_, j=j: e.transpose(out=bkb[:, j * 128:(j + 1) * 128], in_=c_[:, j, :], identity=ident[:]),
                          reads=[c_.b, ident.b], writes=[bkb.b])
                yield
                kk.op("dve", lambda e: e.tensor_copy(out=tok[:], in_=bkb[:, 0:768]), reads=[bkb.b], writes=[tok.b])
                yield
                kk.op("dve", lambda e: e.tensor_tensor(out=sq[:], in0=tok[:, 0:512], in1=tok[:, 0:512], op=ALU.mult), reads=[tok.b], writes=[sq.b])
                yield
                kk.op("dve", lambda e: e.tensor_reduce(out=rs[:], in_=v4(sq[:], 8), axis=AX.X, op=ALU.add), reads=[sq.b], writes=[rs.b])
                yield
                kk.op("dve", lambda e: e.tensor_scalar(out=rs[:], in0=rs[:], scalar1=EPS, scalar2=1.0, op0=ALU.add, op1=ALU.mult),
                      reads=[rs.b], writes=[rs.b])
                yield
                kk.op("act", lambda e: e.activation(out=rs[:], in_=rs[:], func=AF.Ln), reads=[rs.b], writes=[rs.b])
                yield
                kk.op("act", lambda e: e.activation(out=rs[:], in_=rs[:], func=AF.Exp, scale=-0.5), reads=[rs.b], writes=[rs.b])
                yield
                kk.op("dve", lambda e: e.tensor_scalar(out=rs[:, 0:4], in0=rs[:, 0:4], scalar1=0.125, scalar2=1.0, op0=ALU.mult, op1=ALU.mult),
                      reads=[rs.b], writes=[rs.b])
                yield
                kk.op("dve", lambda e: e.tensor_tensor(out=qkn32[:], in0=v4(tok[:, 0:512], 8), in1=bc(rs[:], [128, 8, 64], 2), op=ALU.mult),
                      reads=[tok.b, rs.b], writes=[qkn32.b])
                yield
                kk.op("pool", lambda e: e.tensor_copy(out=qkn[:], in_=qkn32[:]), reads=[qkn32.b], writes=[qkn.b])
                yield
                for j in range(8):
                    kk.op("pe", lambda e, j=j: e.transpose(out=bkb[0:64, j * 128:(j + 1) * 128], in_=qkn[:, j, :], identity=ident[:]),
                          reads=[qkn.b, ident.b], writes=[bkb.b])
                yield
                kk.op("act", lambda e: e.activation(out=qkT[:].rearrange("p a b -> p (a b)"), in_=bkb[0:64, 0:1024], func=AF.Identity),
                      reads=[bkb.b], writes=[qkT.b])
                yield
                kk.op("pe", lambda e, gd=gd, d=d: e.matmul(bk4[:, 0:4], tri[d][:], gd, start=True, stop=True), reads=[tri[d].b, g_.b], writes=[bk4.b])
                yield
                kk.op("pe", lambda e, gd=gd: e.matmul(bk4[:, 4:8], ones_f[:], gd, start=True, stop=True), reads=[ones_f.b, g_.b], writes=[bk4.b])
                yield
                kk.op("dve", lambda e: e.tensor_copy(out=gs[:], in_=bk4[:, 0:8]), reads=[bk4.b], writes=[gs.b])
                yield
                kk.op("act", lambda e: e.activation(out=eg[:], in_=gs[:, 0:4], func=AF.Exp), reads=[gs.b], writes=[eg.b])
                yield
                kk.op("act", lambda e: e.activation(out=egl[:], in_=gs[:, 4:8], func=AF.Exp), reads=[gs.b], writes=[egl.b])
                yield
                kk.op("dve", lambda e: e.tensor_tensor(out=ekd[:], in0=gs[:, 4:8], in1=gs[:, 0:4], op=ALU.subtract), reads=[gs.b], writes=[ekd.b])
                yield
                kk.op("act", lambda e: e.activation(out=ekd[:], in_=ekd[:], func=AF.Exp), reads=[ekd.b], writes=[ekd.b])
                yield
                kk.op("dve", lambda e, gd=gd, d=d: e.tensor_tensor(out=rg[:], in0=bc(tri[d][:], [128, 4, 128], 1), in1=bc(gd, [128, 4, 128], 2), op=ALU.mult),
                      reads=[tri[d].b, g_.b], writes=[rg.b])
                yield
                kk.op("pe", lambda e: e.matmul(bk3[:], ones_f[:], rg[:].rearrange("p a b -> p (a b)"), start=True, stop=True),
                      reads=[ones_f.b, rg.b], writes=[bk3.b])
                yield
                kk.op("dve", lambda e: e.tensor_tensor(out=dm[:], in0=v4(bk3[:]), in1=bc(gs[:, 0:4], [128, 4, 128], 2), op=ALU.subtract),
                      reads=[bk3.b, gs.b], writes=[dm.b])
                yield
                kk.op("dve", lambda e, d=d: e.tensor_tensor(out=dmI[:], in0=dm[:], in1=bc(nmI[d][:], [128, 4, 128], 1), op=ALU.add),
                      reads=[dm.b, nmI[d].b], writes=[dmI.b])
                yield
                kk.op("pool", lambda e, d=d: e.tensor_tensor(out=dmS[:], in0=dm[:], in1=bc(nmS[d][:], [128, 4, 128], 1), op=ALU.add),
                      reads=[dm.b, nmS[d].b], writes=[dmS.b])
                yield
                kk.op("act", lambda e: e.activation(out=dmI[:], in_=dmI[:], func=AF.Exp), reads=[dmI.b], writes=[dmI.b])
                yield
                kk.op("act", lambda e: e.activation(out=dmS[:], in_=dmS[:], func=AF.Exp), reads=[dmS.b], writes=[dmS.b])
                yield
                for h in range(4):
                    kTh = qkT[:, 4 + h, :]
                    qTh = qkT[:, h, :]
                    kk.op("pe", lambda e, h=h, kTh=kTh: e.matmul(bk4[:, h * 128:(h + 1) * 128], kTh, kTh, start=True, stop=True),
                          reads=[qkT.b], writes=[bk4.b])
                    kk.op("pe", lambda e, h=h, kTh=kTh, qTh=qTh: e.matmul(bk5[:, h * 128:(h + 1) * 128], kTh, qTh, start=True, stop=True),
                          reads=[qkT.b], writes=[bk5.b])
                yield
                for h in range(4):
                    kk.op("dve", lambda e, h=h, bd=bd: e.scalar_tensor_tensor(
                        out=X[:, h, :], in0=bk4[:, h * 128:(h + 1) * 128], scalar=bd[:, h:h + 1], in1=dmS[:, h, :], op0=ALU.mult, op1=ALU.mult),
                        reads=[bk4.b, g_.b, dmS.b], writes=[X.b])
                yield
                kk.op("dve", lambda e: e.tensor_tensor(out=AT[:], in0=v4(bk5[:]), in1=dmI[:], op=ALU.mult), reads=[bk5.b, dmI.b], writes=[AT.b])
                pbv = v4(bk4[:])
                yield
                for h in range(4):
                    kk.op("pe", lambda e, h=h: e.transpose(out=pbv[:, h, :], in_=X[:, h, :].bitcast(F32), identity=ident_f[:]),
                          reads=[X.b, ident_f.b], writes=[bk4.b])
                yield
                kk.op("dve", lambda e: e.tensor_copy(out=YT[1][:], in_=pbv), reads=[bk4.b], writes=[YT[1].b])
                pa = bka[:].rearrange("p (c i) -> p c i", c=4)
                yield
                for h in range(4):
                    kk.op("pe", lambda e, h=h: e.matmul(pa[:, h, 0:128], YT[1][:, h, :], X[:, h, :], start=True, stop=True),
                          reads=[YT[1].b, X.b], writes=[bka.b])
                    kk.op("pe", lambda e, h=h: e.matmul(pbv[:, h, :], X[:, h, :], YT[1][:, h, :], start=True, stop=True),
                          reads=[YT[1].b, X.b], writes=[bk4.b])
                yield
                kk.op("dve", lambda e: e.tensor_tensor(out=YS[0][:, :, 128:256], in0=bc(ident_f[:], [128, 4, 128], 1), in1=X[:], op=ALU.subtract),
                      reads=[ident_f.b, X.b], writes=[YS[0].b])
                yield
                kk.op("act", lambda e: e.activation(out=YS[0][:, :, 0:128], in_=pa[:, :, 0:128], func=AF.Identity), reads=[bka.b], writes=[YS[0].b])
                yield
                kk.op("dve", lambda e: e.tensor_copy(out=YT[0][:], in_=pbv), reads=[bk4.b], writes=[YT[0].b])
                cur = 0
                yield
                for k in range(1, 7):
                    ys, yt = YS[cur], YT[cur]
                    nys, nyt = YS[1 - cur], YT[1 - cur]
                    for h in range(4):
                        kk.op("pe", lambda e, h=h, ys=ys, yt=yt: e.matmul(pa[:, h, :], yt[:, h, :], ys[:, h, :], start=True, stop=True),
                              reads=[ys.b, yt.b], writes=[bka.b])
                        if k < 6:
                            kk.op("pe", lambda e, h=h, ys=ys, yt=yt: e.matmul(pbv[:, h, :], ys[:, h, 0:128], yt[:, h, :], start=True, stop=True),
                                  reads=[ys.b, yt.b], writes=[bk4.b])
                    kk.op("dve", lambda e, ys=ys, nys=nys: e.tensor_tensor(out=nys[:, :, 128:256], in0=pa[:, :, 128:256], in1=ys[:, :, 128:256], op=ALU.add),
                          reads=[bka.b, ys.b], writes=[nys.b])
                    if k < 6:
                        kk.op("act", lambda e, nys=nys: e.activation(out=nys[:, :, 0:128], in_=pa[:, :, 0:128], func=AF.Identity),
                              reads=[bka.b], writes=[nys.b])
                        kk.op("dve", lambda e, nyt=nyt: e.tensor_copy(out=nyt[:], in_=pbv), reads=[bk4.b], writes=[nyt.b])
                    cur = 1 - cur
                TT = YS[cur]
                yield
                kk.op("pool", lambda e, bd=bd: e.tensor_tensor(out=db[:], in0=bc(ident_f[:], [128, 4, 128], 1), in1=bc(bd, [128, 4, 128], 2), op=ALU.mult),
                      reads=[ident_f.b, g_.b], writes=[db.b])
                yield
                kk.op("pe", lambda e: e.matmul(bk3[:], ones_f[:], db[:].rearrange("p a b -> p (a b)"), start=True, stop=True),
                      reads=[ones_f.b, db.b], writes=[bk3.b])
                yield
                kk.op("dve", lambda e, TT=TT: e.tensor_tensor(out=TTb[:], in0=v4(bk3[:]), in1=TT[:, :, 128:256], op=ALU.mult),
                      reads=[bk3.b, TT.b], writes=[TTb.b])
                yield
                kk.op("dve", lambda e: e.tensor_tensor(out=ek32[:], in0=qkn32[:, 4:8, :], in1=bc(eg[:], [128, 4, 64], 2), op=ALU.mult),
                      reads=[qkn32.b, eg.b], writes=[ek32.b])
                yield
                kk.op("pool", lambda e: e.tensor_tensor(out=kd32[:], in0=qkn32[:, 4:8, :], in1=bc(ekd[:], [128, 4, 64], 2), op=ALU.mult),
                      reads=[qkn32.b, ekd.b], writes=[kd32.b])
                yield
                kk.op("pool", lambda e: e.tensor_tensor(out=qd[:], in0=qkn32[:, 0:4, :], in1=bc(eg[:], [128, 4, 64], 2), op=ALU.mult),
                      reads=[qkn32.b, eg.b], writes=[qd.b])
                pek = bk5[0:64, :].rearrange("p (c i) -> p c i", c=4)
                yield
                for h in range(4):
                    kk.op("pe", lambda e, h=h: e.transpose(out=pek[:, h, :], in_=ek32[:, h, :], identity=ident_f[:]),
                          reads=[ek32.b, ident_f.b], writes=[bk5.b])
                yield
                kk.op("dve", lambda e: e.tensor_copy(out=ekT32[:], in_=pek), reads=[bk5.b], writes=[ekT32.b])
                yield
                for h in range(4):
                    kk.op("pe", lambda e, h=h: e.transpose(out=bkb[0:64, h * 128:(h + 1) * 128], in_=qd[:, h, :], identity=ident[:]),
                          reads=[qd.b, ident.b], writes=[bkb.b])
                yield
                kk.op("dve", lambda e: e.tensor_copy(out=qdT[:].rearrange("p a b -> p (a b)"), in_=bkb[0:64, 0:512]), reads=[bkb.b], writes=[qdT.b])

            def scan(n, ci, d=d):
                s = n % 2
                r0 = ci * 128
                tok, AT, TTb, qdT, egl, kd32, ekT32 = tokL[s], ATL[s], TTbL[s], qdTL[s], eglL[s], kd32L[s], ekT32L[s]
                pws = bk1[:, 0:256].rearrange("p (c i) -> p c i", c=4)
                pout = bk1[:, 256:512].rearrange("p (c i) -> p c i", c=4)
                pds = bk2[0:64, 0:256].rearrange("p (c i) -> p c i", c=4)
                pv2 = bk2[:, 256:512].rearrange("p (c i) -> p c i", c=4)
                yield
                for h in range(4):
                    kk.op("pe", lambda e, h=h: e.matmul(pws[:, h, :], ekT32[:, h, :], Sf[:, h, :], start=True, stop=True),
                          reads=[ekT32.b, Sf.b], writes=[bk1.b])
                yield
                kk.op("dve", lambda e: e.tensor_tensor(out=rb[:], in0=v4(tok[:, 512:768]), in1=pws, op=ALU.subtract),
                      reads=[tok.b, bk1.b], writes=[rb.b])
                yield
                for h in range(4):
                    kk.op("pe", lambda e, h=h: e.matmul(pv2[:, h, :], TTb[:, h, :], rb[:, h, :], start=True, stop=True),
                          reads=[TTb.b, rb.b], writes=[bk2.b])
                yield
                kk.op("dve", lambda e: e.tensor_copy(out=vnew32[:], in_=pv2), reads=[bk2.b], writes=[vnew32.b])
                yield
                kk.op("dve", lambda e: e.tensor_copy(out=vnew[:], in_=pv2), reads=[bk2.b], writes=[vnew.b])
                yield
                for h in range(4):
                    kk.op("pe", lambda e, h=h: e.matmul(pout[:, h, :], qdT[:, h, :], Sb[:, h, :], start=True, stop=False),
                          reads=[qdT.b, Sb.b], writes=[bk1.b])
                    kk.op("pe", lambda e, h=h: e.matmul(pout[:, h, :], AT[:, h, :], vnew[:, h, :], start=False, stop=True),
                          reads=[AT.b, vnew.b], writes=[bk1.b])
                yield
                for h in range(4):
                    kk.op("pe", lambda e, h=h: e.matmul(pds[:, h, :], kd32[:, h, :], vnew32[:, h, :], start=True, stop=True),
                          reads=[kd32.b, vnew32.b], writes=[bk2.b])
                yield
                kk.op("dve", lambda e: e.tensor_tensor(out=Sf[:], in0=Sf[:], in1=bc(egl[0:64, :], [64, 4, 64], 2), op=ALU.mult),
                      reads=[Sf.b, egl.b], writes=[Sf.b])
                yield
                kk.op("dve", lambda e: e.tensor_tensor(out=Sf[:], in0=Sf[:], in1=pds, op=ALU.add), reads=[Sf.b, bk2.b], writes=[Sf.b])
                yield
                kk.op("act", lambda e: e.activation(out=Sb[:], in_=Sf[:], func=AF.Identity), reads=[Sf.b], writes=[Sb.b])
                o_ = osb[s]
                if d == 0:
                    kk.op("dve", lambda e, o_=o_: e.tensor_copy(out=o_[:], in_=bk1[:, 256:512]), reads=[bk1.b], writes=[o_.b])
                    kk.op("pool", lambda e, o_=o_, r0=r0: e.dma_start(out=dr[f"of{si}"][r0:r0 + 128, :], in_=o_[:]), reads=[o_.b], dma=f"st{s}")
                else:
                    kk.op("dve", lambda e, o_=o_, f_=ofl[s]: e.tensor_tensor(out=o_[:], in0=bk1[:, 256:512], in1=f_[:], op=ALU.add),
                          reads=[bk1.b, ofl[s].b], writes=[o_.b])
                    if f"osum{si}" in dr:
                        kk.op("pool", lambda e, o_=o_, r0=r0: e.dma_start(out=dr[f"osum{si}"][r0:r0 + 128, :], in_=o_[:]), reads=[o_.b], dma=f"st{s}")
                    kk.op("pool", lambda e, o_=o_: e.tensor_tensor(out=osq[:], in0=o_[:], in1=o_[:], op=ALU.mult), reads=[o_.b], writes=[osq.b])
                    kk.op("dve", lambda e: e.tensor_reduce(out=oss[:], in_=v4(osq[:]), axis=AX.X, op=ALU.add), reads=[osq.b], writes=[oss.b])
                    kk.op("dve", lambda e: e.tensor_scalar(out=oss[:], in0=oss[:], scalar1=1.0 / 64, scalar2=EPS, op0=ALU.mult, op1=ALU.add),
                          reads=[oss.b], writes=[oss.b])
                    kk.op("act", lambda e: e.activation(out=oss[:], in_=oss[:], func=AF.Ln), reads=[oss.b], writes=[oss.b])
                    kk.op("act", lambda e: e.activation(out=oss[:], in_=oss[:], func=AF.Exp, scale=-0.5), reads=[oss.b], writes=[oss.b])
                    kk.op("dve", lambda e, o_=o_: e.tensor_tensor(out=v4(o_[:]), in0=v4(o_[:]), in1=bc(oss[:], [128, 4, 64], 2), op=ALU.mult),
                          reads=[o_.b, oss.b], writes=[o_.b])
                    kk.op("pool", lambda e, o_=o_: e.tensor_tensor(out=onb[:], in0=o_[:], in1=dnw[:].rearrange("p a b -> p (a b)"), op=ALU.mult),
                          reads=[o_.b, dnw.b], writes=[onb.b])
                    for j in range(2):
                        kk.op("pe", lambda e, j=j: e.transpose(out=bkb[:, j * 128:(j + 1) * 128], in_=onb[:, j * 128:(j + 1) * 128], identity=ident[:]),
                              reads=[onb.b, ident.b], writes=[bkb.b])
                    z_ = zb[s]
                    m_ = omx[s]
                    kk.op("act", lambda e, z_=z_: e.activation(out=z_[:], in_=z_[:], func=AF.Silu), reads=[z_.b], writes=[z_.b])
                    kk.op("dve", lambda e, z_=z_, m_=m_: e.tensor_tensor(out=m_[:], in0=bkb[:, 0:256].rearrange("p (a b) -> p a b", a=2), in1=z_[:], op=ALU.mult),
                          reads=[bkb.b, z_.b], writes=[m_.b])
                    kk.op("pool", lambda e, m_=m_, r0=r0: e.dma_start(
                        out=dr[f"mixT{si}"][256:512, r0:r0 + 128].rearrange("(g p) s -> p g s", p=128), in_=m_[:]), reads=[m_.b], dma=f"st{s}")


            for _ in prep(0, order[0]):
                pass
            for n in range(NCH):
                gsc = scan(n, order[n])
                gpr = prep(n + 1, order[n + 1]) if n + 1 < NCH else iter(())
                alive_s = alive_p = True
                while alive_s or alive_p:
                    for _ in range(3):
                        if alive_p and next(gpr, "END") == "END":
                            alive_p = False
                    if alive_s and next(gsc, "END") == "END":
                        alive_s = False


POOL_WINDOWS = (2, 4, 8, 16)


def host_layout(seqs, depth, norm_w, w_in, sgu_w, sgu_b, conv_w, a_log, dt_bias, dn_norm_w,
                q_norm_w, k_norm_w, pool_w, pool_scale, w_out):
    f = np.float32
    m = {}
    m["w_in"] = np.ascontiguousarray(np.asarray(w_in, f).reshape(depth, 8, 128, NCOL).transpose(0, 2, 1, 3)[..., COL_PERM])
    m["norm_w"] = np.ascontiguousarray(np.asarray(norm_w, f).reshape(depth, 8, 128).transpose(0, 2, 1))
    m["w_out"] = np.ascontiguousarray(np.asarray(w_out, f).reshape(depth, 8, 128, D).transpose(0, 2, 1, 3))
    qk = np.concatenate([np.repeat(np.asarray(q_norm_w, f)[:, None, :], 4, 1), np.repeat(np.asarray(k_norm_w, f)[:, None, :], 2, 1)], 1)
    m["qkw"] = np.ascontiguousarray(np.broadcast_to(qk[:, None], (depth, 128, 6, 64)))
    m["a_log"] = np.ascontiguousarray(np.broadcast_to(np.asarray(a_log, f).reshape(depth, 1, 8), (depth, 128, 8)))
    m["dt_bias"] = np.ascontiguousarray(np.broadcast_to(np.asarray(dt_bias, f).reshape(depth, 1, 8), (depth, 128, 8)))
    m["ident"] = np.eye(128, dtype=f)
    m["sgu_wT"] = np.ascontiguousarray(np.asarray(sgu_w, f).transpose(0, 3, 1, 2))
    sb_ = np.asarray(sgu_b, f)
    sbT = np.zeros((depth, 128, 2, 128), f)
    for hp in range(2):
        for h2 in range(2):
            sbT[:, h2 * 64:(h2 + 1) * 64, hp, :] = sb_[:, hp * 2 + h2, None, :]
    m["sgu_bT"] = sbT
    m["conv_w"] = np.ascontiguousarray(np.asarray(conv_w, f).reshape(depth, 5, 6, 128).transpose(0, 3, 2, 1))
    pw = np.zeros((depth, 128, 2, 128), f)
    pwi = np.asarray(pool_w, f)
    for ch in range(2):
        for g2 in range(2):
            pw[:, g2 * 64:(g2 + 1) * 64, ch, g2 * 64:(g2 + 1) * 64] = pwi[:, ch * 2 + g2]
    m["pool_w"] = pw
    m["pool_s"] = np.ascontiguousarray(np.asarray(pool_scale, f).reshape(depth, 2, 128).transpose(0, 2, 1))
    m["dn_w"] = np.ascontiguousarray(np.broadcast_to(np.asarray(dn_norm_w, f)[:, None, None, :], (depth, 128, 4, 64)))
    k_ = np.arange(128)
    m["triU"] = (k_[:, None] <= k_[None, :]).astype(f)
    m["triL"] = (k_[:, None] >= k_[None, :]).astype(f)
    for i, S in enumerate(seqs):
        c, s_ = rope_tables(S)
        m[f"cos{i}"] = c
        m[f"sin{i}"] = s_
        t = np.arange(S)
        ic = np.zeros((128, 2, S), f)
        for g, win in enumerate(POOL_WINDOWS):
            lo = np.clip(t - win // 2, 0, S)
            hi = np.clip(t + win // 2, 0, S)
            ic[(g % 2) * 64:(g % 2) * 64 + 64, g // 2, :] = (1.0 / (hi - lo).astype(f))[None, :]
        m[f"icnt{i}"] = ic
    return m


_NC_CACHE = {}


def kernel(x_prompt, x_sample, norm_w, w_in, sgu_w, sgu_b, conv_w, a_log, dt_bias, dn_norm_w,
           q_norm_w, k_norm_w, pool_w, pool_scale, w_out):
    x_prompt = np.asarray(x_prompt, np.float32)
    x_sample = np.asarray(x_sample, np.float32)
    depth = int(np.asarray(w_in).shape[0])
    seqs = [x_prompt.shape[1], x_sample.shape[1]]
    key = (tuple(seqs), depth)
    if key not in _NC_CACHE:
        _NC_CACHE[key] = build(seqs, depth, divs=(x_prompt.shape[0], x_sample.shape[0]),
                               groups=(8 // x_prompt.shape[0], 8 // x_sample.shape[0]))
    nc = _NC_CACHE[key]
    common = host_layout(seqs, depth, norm_w, w_in, sgu_w, sgu_b, conv_w, a_log, dt_bias, dn_norm_w,
                         q_norm_w, k_norm_w, pool_w, pool_scale, w_out)
    nb_p, nb_s = x_prompt.shape[0], x_sample.shape[0]
    in_maps = []
    for c in range(8):
        mm = dict(common)
        mm["x0"] = np.ascontiguousarray(x_prompt[c % nb_p])
        mm["x1"] = np.ascontiguousarray(x_sample[c % nb_s])
        in_maps.append(mm)
    res = run_bass_kernel_spmd(nc, in_maps, core_ids=list(range(8)))
    yp = np.zeros(x_prompt.shape, np.float32)
    ys = np.zeros(x_sample.shape, np.float32)
    gp, gs = 8 // nb_p, 8 // nb_s
    np_, ns_ = seqs[0] // gp, seqs[1] // gs
    for c in range(8):
        yp[c % nb_p, (c // nb_p) * np_:(c // nb_p + 1) * np_] = np.asarray(res.results[c]["y0"], np.float32)
        ys[c % nb_s, (c // nb_s) * ns_:(c // nb_s + 1) * ns_] = np.asarray(res.results[c]["y1"], np.float32)
    return (yp, ys)
```

```python
import numpy as np
import ml_dtypes
from contextlib import ExitStack
import concourse.bass as bass
import concourse.mybir as mybir
from concourse.bass_utils import run_bass_kernel_spmd

F32 = mybir.dt.float32
F32R = mybir.dt.float32r
BF16 = mybir.dt.bfloat16
AF = mybir.ActivationFunctionType
ALU = mybir.AluOpType
AX = mybir.AxisListType

D = 1024
NCOL = 3088
EPS = 1e-6
G_AU, G_AV, G_AZ, G_BQ, G_BK, G_BV, G_BZ, G_CQ, G_CK, G_CV, G_CZ, G_DX, G_DZ = \
    0, 2, 4, 6, 8, 10, 12, 14, 16, 17, 18, 20, 22
ORIG_OFF = dict(au=0, av=256, az=512, bq=768, bk=1024, bv=1280, bz=1536, bb=1792, ba=1800,
                cq=1808, ck=2064, cv=2192, cz=2320, dx=2576, dz=2832)
COL_PERM = np.concatenate([
    np.arange(0, 1792), np.arange(1808, 3088), np.arange(1792, 1808)])


class Buf:
    __slots__ = ("name", "w", "r", "sem", "epoch")

    def __init__(self, name):
        self.name = name
        self.w = None
        self.r = {}
        self.sem = None
        self.epoch = -1


class K:
    ENG = ("pe", "act", "dve", "pool", "sp")
    NDMA = 48

    def __init__(self, nc, stack):
        self.nc = nc
        self.sems = {}
        self.cnt = {}
        self.stack = stack
        for e in self.ENG:
            self._newsem(e)
        self.prog = {e: [] for e in self.ENG}
        self.waited = {e: {} for e in self.ENG}
        self.ninstr = 0
        self.pool_idx = 0
        self.epoch = 0
        for i in range(self.NDMA):
            self._newsem(f"d{i}")

    def _newsem(self, key):
        self.sems[key] = self.stack.enter_context(self.nc.semaphore("s_" + key))
        self.cnt[key] = 0

    LIMIT = None

    def op(self, e, fn, reads=(), writes=(), dma=None):
        if K.LIMIT is not None and self.ninstr >= K.LIMIT:
            return
        need = {}

        def want(tok):
            if tok is None:
                return
            k, v = tok
            if k == "pe" and e == "pe" and dma is None:
                return
            if k not in self.ENG:
                v = self.cnt[k]
            if need.get(k, 0) < v:
                need[k] = v
        for b in reads:
            want(b.w)
        for b in writes:
            want(b.w)
            for k, v in b.r.items():
                want((k, v))
        waits = []
        wd = self.waited[e]
        for k, v in need.items():
            if wd.get(k, 0) < v:
                wd[k] = v
                waits.append((k, v))
        if dma is not None:
            b = (list(writes) + list(reads))[0]
            if b.sem is None or b.epoch != self.epoch:
                b.sem = f"d{self.pool_idx}"
                b.epoch = self.epoch
                self.pool_idx += 1
                assert self.pool_idx <= self.NDMA, "DMA semaphore pool exhausted"
            key, inc = b.sem, 16
        else:
            key, inc = e, 1
        self.cnt[key] += inc
        tok = (key, self.cnt[key])
        for b in reads:
            if b.r.get(key, 0) < tok[1]:
                b.r[key] = tok[1]
        for b in writes:
            b.w = tok
            b.r = {}
        self.prog[e].append((waits, fn, key, inc))
        self.ninstr += 1

    def barrier(self):
        self.pool_idx = 0
        self.epoch += 1
        tot = dict(self.cnt)
        for e in self.ENG:
            waits = []
            for k, v in tot.items():
                if v > 0 and self.waited[e].get(k, 0) < v:
                    self.waited[e][k] = v
                    waits.append((k, v))
            if waits:
                self.prog[e].append((waits, None, None, 0))

    def flush(self):
        nc = self.nc
        _DYN.clear()
        with nc.Block() as block:
            def run(e, eng):
                for waits, fn, key, inc in self.prog[e]:
                    for k, v in waits:
                        eng.wait_ge(self.sems[k], v)
                    if fn is not None:
                        fn(eng).then_inc(self.sems[key], inc)

            @block.tensor
            def _(eng):
                run("pe", eng)

            @block.scalar
            def _(eng):
                run("act", eng)

            @block.vector
            def _(eng):
                run("dve", eng)

            @block.gpsimd
            def _(eng):
                run("pool", eng)

            @block.sync
            def _(eng):
                run("sp", eng)
        self.prog = {e: [] for e in self.ENG}


class T:
    def __init__(self, t, name):
        self.t = t
        self.b = Buf(name)

    def __getitem__(self, idx):
        return self.t[idx]


_UID = [0]


def sb(nc, st, name, shape, dt):
    _UID[0] += 1
    nm = f"sb{_UID[0]}_{name}"
    return T(st.enter_context(nc.sbuf_tensor(nm, list(shape), dt)), nm)


def ps(nc, st, name, shape, dt):
    _UID[0] += 1
    nm = f"ps{_UID[0]}_{name}"
    return T(st.enter_context(nc.psum_tensor(nm, list(shape), dt)), nm)


def rope_tables(S):
    rows = np.repeat(np.arange(S // 64), 64)
    cols = np.tile(np.arange(64), S // 64)
    inv = np.power(np.float32(10000.0), -2.0 * np.arange(16, dtype=np.float32) / 32).astype(np.float32)
    ang = np.stack([rows, cols], -1).astype(np.float32)[:, :, None] * inv
    return np.cos(ang).astype(np.float32).reshape(S, 32), np.sin(ang).astype(np.float32).reshape(S, 32)


def build(seqs, depth, debug=False, phases="pfdao", divs=(2, 4), groups=(4, 2)):
    nc = bass.Bass("TRN2", target_bir_lowering=False)
    dr = {}

    def din(name, shape, dt=F32):
        dr[name] = nc.dram_tensor(name, list(shape), dt, kind="ExternalInput").ap()
        return dr[name]

    def dscr(name, shape, dt, out=False):
        kind = "ExternalOutput" if out else "Internal"
        dr[name] = nc.dram_tensor(name, list(shape), dt, kind=kind).ap()
        return dr[name]

    nseq = len(seqs)
    for i, S in enumerate(seqs):
        din(f"x{i}", [S, D])
        din(f"cos{i}", [S, 32])
        din(f"sin{i}", [S, 32])
        dscr(f"y{i}", [S // groups[i], D], F32, out=True)
        dscr(f"y1_{i}", [S, D], F32)
        dscr(f"projT{i}", [3072, S], BF16, out=debug)
        dscr(f"qT{i}", [256, S], BF16, out=debug)
        dscr(f"kT{i}", [128, S], BF16, out=debug)
        dscr(f"va{i}", [S, 130], BF16, out=debug)
        dscr(f"bg{i}", [S, 16], F32, out=debug)
        dscr(f"mixT{i}", [1024, S], BF16, out=debug)
        dscr(f"convT{i}", [768, S], BF16, out=debug)
        dscr(f"of{i}", [S, 256], F32, out=debug)
        npi = S // groups[i]
        dscr(f"qTp{i}", [256, npi], BF16)
        dscr(f"zTp{i}", [256, npi], BF16)
        dscr(f"mixTp{i}", [1024, npi], BF16)
        dscr(f"xp{i}", [npi, D], F32)
        if debug:
            dscr(f"osum{i}", [S, 256], F32, out=True)
    din("w_in", [depth, 128, 8, NCOL])
    din("norm_w", [depth, 128, 8])
    din("w_out", [depth, 128, 8, D])
    din("qkw", [depth, 128, 6, 64])
    din("a_log", [depth, 128, 8])
    din("dt_bias", [depth, 128, 8])
    din("ident", [128, 128])
    din("sgu_wT", [depth, 128, 4, 128])
    din("sgu_bT", [depth, 128, 2, 128])
    din("conv_w", [depth, 128, 6, 5])
    din("pool_w", [depth, 128, 2, 128])
    din("pool_s", [depth, 128, 2])
    din("dn_w", [depth, 128, 4, 64])
    din("triU", [128, 128])
    din("triL", [128, 128])
    for i, S in enumerate(seqs):
        din(f"icnt{i}", [128, 2, S])

    with ExitStack() as top:
        kk = K(nc, top)
        with nc.Block() as blk0:
            @blk0.vector
            def _(eng):
                for key in kk.sems:
                    eng.sem_clear(kk.sems[key])
        ident_f = sb(nc, top, "ident_f", [128, 128], F32)
        ident = sb(nc, top, "ident_b", [128, 128], BF16)
        ones_f = sb(nc, top, "ones_f", [128, 128], F32)
        kk.op("sp", lambda e: e.dma_start(out=ident_f[:], in_=dr["ident"][:, :]), writes=[ident_f.b], dma="c0")
        kk.op("dve", lambda e: e.tensor_copy(out=ident[:], in_=ident_f[:]), reads=[ident_f.b], writes=[ident.b])
        kk.op("pool", lambda e: e.memset(ones_f[:], 1.0), writes=[ones_f.b])

        for l in range(depth):
            for si, S in enumerate(seqs):
                xin = dr[f"x{si}"] if l == 0 else dr[f"y1_{si}"]
                last = (l == depth - 1)
                yout = dr[f"y{si}"] if last else dr[f"y1_{si}"]
                part = (S // groups[si], divs[si]) if last else None
                for ph in phases:
                    if ph == "p":
                        phase_proj(nc, kk, dr, l, si, S, xin, ident, ident_f)
                    elif ph == "f":
                        phase_fm(nc, kk, dr, l, si, S, ident)
                    elif ph == "d":
                        phase_dn(nc, kk, dr, l, si, S, ident, ident_f, ones_f)
                    elif ph == "a":
                        if part is not None:
                            phase_part(nc, kk, dr, si, S, xin, part)
                            kk.barrier()
                            kk.flush()
                        phase_attn(nc, kk, dr, l, si, S, ones_f, part)
                    elif ph == "o":
                        phase_out(nc, kk, dr, l, si, S, xin, yout, part)
                    kk.barrier()
                    kk.flush()
    return nc


def load_w_bf16(nc, kk, st, name, src_ap, ncols, scale_t=None, chunk=1024):
    w = sb(nc, st, name, [128, 8, ncols], BF16)
    stg = [sb(nc, st, f"{name}_stg{i}", [128, chunk], F32) for i in range(2)]
    n = 0
    for kc in range(8):
        for c0 in range(0, ncols, chunk):
            cw = min(chunk, ncols - c0)
            s = stg[n % 2]
            kk.op("sp", lambda e, s=s, kc=kc, c0=c0, cw=cw: e.dma_start(out=s[:, 0:cw], in_=src_ap[:, kc, c0:c0 + cw]),
                  writes=[s.b], dma=f"wl{n % 2}")
            if scale_t is not None:
                kk.op("dve", lambda e, s=s, kc=kc, c0=c0, cw=cw: e.tensor_scalar(
                    out=w[:, kc, c0:c0 + cw], in0=s[:, 0:cw], scalar1=scale_t[:, kc:kc + 1], scalar2=1.0,
                    op0=ALU.mult, op1=ALU.mult), reads=[s.b, scale_t.b], writes=[w.b])
            else:
                eng = "dve" if n % 2 == 0 else "pool"
                kk.op(eng, lambda e, s=s, kc=kc, c0=c0, cw=cw: e.tensor_copy(out=w[:, kc, c0:c0 + cw], in_=s[:, 0:cw]),
                      reads=[s.b], writes=[w.b])
            n += 1
    return w


def phase_proj(nc, kk, dr, l, si, S, xin, ident, ident_f):
    with ExitStack() as st:
        nw = sb(nc, st, "nw", [128, 8], F32)
        kk.op("sp", lambda e: e.dma_start(out=nw[:], in_=dr["norm_w"][l]), writes=[nw.b], dma="c0")
        w = load_w_bf16(nc, kk, st, "w_in_sb", dr["w_in"][l], NCOL, scale_t=nw)
        qkw = sb(nc, st, "qkw", [128, 6, 64], F32)
        kk.op("sp", lambda e: e.dma_start(out=qkw[:], in_=dr["qkw"][l]), writes=[qkw.b], dma="c0")
        alog = sb(nc, st, "alog", [128, 8], F32)
        nA = sb(nc, st, "nA", [128, 8], F32)
        dtb = sb(nc, st, "dtb", [128, 8], F32)
        kk.op("sp", lambda e: e.dma_start(out=alog[:], in_=dr["a_log"][l]), writes=[alog.b], dma="c0")
        kk.op("sp", lambda e: e.dma_start(out=dtb[:], in_=dr["dt_bias"][l]), writes=[dtb.b], dma="c0")
        kk.op("act", lambda e: e.activation(out=nA[:], in_=alog[:], func=AF.Exp), reads=[alog.b], writes=[nA.b])
        kk.op("dve", lambda e: e.tensor_scalar(out=nA[:], in0=nA[:], scalar1=-1.0, scalar2=1.0, op0=ALU.mult, op1=ALU.mult),
              reads=[nA.b], writes=[nA.b])

        NS = 2
        xt = [sb(nc, st, f"xt{i}", [128, D], F32) for i in range(NS)]
        junk = sb(nc, st, "junk", [128, D], BF16)
        hb = [sb(nc, st, f"hb{i}", [128, D], BF16) for i in range(NS)]
        ssq = [sb(nc, st, f"ssq{i}", [128, 1], F32) for i in range(NS)]
        rstd = [sb(nc, st, f"rstd{i}", [128, 1], F32) for i in range(NS)]
        pT = [ps(nc, st, f"pT{i}", [128, 8, 128], BF16) for i in range(2)]
        hT = [sb(nc, st, f"hT{i}", [128, 8, 512], BF16) for i in range(2)]
        pacc = [ps(nc, st, f"pacc{i}", [128, 512], F32) for i in range(3)]
        ptk = [ps(nc, st, f"ptk{i}", [128, 512], F32) for i in range(1)]
        pbg = [ps(nc, st, f"pbg{i}", [128, 512], F32) for i in range(1)]
        ptr = ps(nc, st, "ptr", [128, 3, 128], BF16)
        stage = [sb(nc, st, f"stage{i}", [128, 24, 512], BF16) for i in range(2)]
        cs = [sb(nc, st, f"cs{i}", [128, 2, 32], F32) for i in range(NS)]
        qk = [sb(nc, st, f"qk{i}", [128, 6, 64], F32) for i in range(NS)]
        qsq = sb(nc, st, "qsq", [128, 6, 64], F32)
        qss = [sb(nc, st, f"qss{i}", [128, 6], F32) for i in range(NS)]
        qr = [sb(nc, st, f"qr{i}", [128, 6, 64], F32) for i in range(NS)]
        tmpa = sb(nc, st, "tmpa", [128, 6, 2, 16], F32)
        tmpb = sb(nc, st, "tmpb", [128, 6, 2, 16], F32)
        qkb = [sb(nc, st, f"qkb{i}", [128, 384], BF16) for i in range(NS)]
        qkT = [sb(nc, st, f"qkT{i}", [128, 3, 512], BF16) for i in range(2)]
        va = [sb(nc, st, f"va{i}", [128, 130], BF16) for i in range(NS)]
        bgt = [sb(nc, st, f"bgt{i}", [128, 16], F32) for i in range(NS)]
        t8 = [sb(nc, st, f"t8{i}", [128, 16], F32) for i in range(NS)]
        for v_ in va:
            kk.op("pool", lambda e, v_=v_: e.memset(v_[:], 1.0), writes=[v_.b])

        projT = dr[f"projT{si}"]
        nblk = S // 512
        ev = 0
        for tb in range(nblk):
            hTb = hT[tb % 2]
            qkTb = qkT[tb % 2]
            for t4 in range(4):
                ti = tb * 4 + t4
                s = ti % NS
                r0 = ti * 128
                x_, h_, sq_, rs_ = xt[s], hb[s], ssq[s], rstd[s]
                kk.op("sp", lambda e, x_=x_, r0=r0: e.dma_start(out=x_[:], in_=xin[r0:r0 + 128, :]),
                      writes=[x_.b], dma=f"x{s}")
                kk.op("sp", lambda e, c_=cs[s], r0=r0: e.dma_start(out=c_[:, 0, :], in_=dr[f"cos{si}"][r0:r0 + 128, :]),
                      writes=[cs[s].b], dma=f"x{s}")
                kk.op("sp", lambda e, c_=cs[s], r0=r0: e.dma_start(out=c_[:, 1, :], in_=dr[f"sin{si}"][r0:r0 + 128, :]),
                      writes=[cs[s].b], dma=f"x{s}")
                kk.op("act", lambda e, x_=x_, sq_=sq_: e.activation(out=junk[:], in_=x_[:], func=AF.Square, accum_out=sq_[:]),
                      reads=[x_.b], writes=[junk.b, sq_.b])
                kk.op("dve", lambda e, sq_=sq_, rs_=rs_: e.tensor_scalar(out=rs_[:], in0=sq_[:], scalar1=1.0 / D, scalar2=EPS,
                                                                         op0=ALU.mult, op1=ALU.add), reads=[sq_.b], writes=[rs_.b])
                kk.op("act", lambda e, rs_=rs_: e.activation(out=rs_[:], in_=rs_[:], func=AF.Ln), reads=[rs_.b], writes=[rs_.b])
                kk.op("act", lambda e, rs_=rs_: e.activation(out=rs_[:], in_=rs_[:], func=AF.Exp, scale=-0.5), reads=[rs_.b], writes=[rs_.b])
                kk.op("dve", lambda e, x_=x_, h_=h_, rs_=rs_: e.tensor_scalar(out=h_[:], in0=x_[:], scalar1=rs_[:, 0:1], scalar2=1.0,
                                                                              op0=ALU.mult, op1=ALU.mult),
                      reads=[x_.b, rs_.b], writes=[h_.b])
                p_ = pT[ti % 2]
                for kc in range(8):
                    kk.op("pe", lambda e, p_=p_, h_=h_, kc=kc: e.transpose(out=p_[:, kc, :], in_=h_[:, kc * 128:(kc + 1) * 128],
                                                                          identity=ident[:]),
                          reads=[h_.b, ident.b], writes=[p_.b])
                kk.op("act", lambda e, p_=p_, hTb=hTb, t4=t4: e.activation(out=hTb[:, :, t4 * 128:(t4 + 1) * 128], in_=p_[:],
                                                                           func=AF.Identity),
                      reads=[p_.b], writes=[hTb.b])
                pk = ptk[0]
                for kc in range(8):
                    kk.op("pe", lambda e, pk=pk, hTb=hTb, t4=t4, kc=kc: e.matmul(
                        pk[:], hTb[:, kc, t4 * 128:(t4 + 1) * 128], w[:, kc, G_CQ * 128:G_CQ * 128 + 512],
                        start=(kc == 0), stop=(kc == 7)), reads=[hTb.b, w.b], writes=[pk.b])
                pb_ = pbg[0]
                for kc in range(8):
                    kk.op("pe", lambda e, pb_=pb_, hTb=hTb, t4=t4, kc=kc: e.matmul(
                        pb_[:, 0:16], hTb[:, kc, t4 * 128:(t4 + 1) * 128], w[:, kc, 3072:3088],
                        start=(kc == 0), stop=(kc == 7)), reads=[hTb.b, w.b], writes=[pb_.b])
                q_, ss_, r_, qb_, va_, c_ = qk[s], qss[s], qr[s], qkb[s], va[s], cs[s]
                kk.op("dve", lambda e, q_=q_, pk=pk: e.tensor_copy(out=q_[:].rearrange("p a b -> p (a b)"), in_=pk[:, 0:384]),
                      reads=[pk.b], writes=[q_.b])
                kk.op("dve", lambda e, va_=va_, pk=pk: e.tensor_copy(
                    out=va_[:].rearrange("p (a b) -> p a b", a=2)[:, :, 0:64],
                    in_=pk[:, 384:512].rearrange("p (a b) -> p a b", a=2)), reads=[pk.b], writes=[va_.b])
                kk.op("pool", lambda e, va_=va_, r0=r0: e.dma_start(out=dr[f"va{si}"][r0:r0 + 128, :], in_=va_[:]),
                      reads=[va_.b], dma=f"st{s}")
                kk.op("dve", lambda e, q_=q_: e.tensor_tensor(out=qsq[:], in0=q_[:], in1=q_[:], op=ALU.mult),
                      reads=[q_.b], writes=[qsq.b])
                kk.op("dve", lambda e, ss_=ss_: e.tensor_reduce(out=ss_[:], in_=qsq[:], axis=AX.X, op=ALU.add),
                      reads=[qsq.b], writes=[ss_.b])
                kk.op("dve", lambda e, ss_=ss_: e.tensor_scalar(out=ss_[:], in0=ss_[:], scalar1=1.0 / 64, scalar2=EPS,
                                                                op0=ALU.mult, op1=ALU.add), reads=[ss_.b], writes=[ss_.b])
                kk.op("act", lambda e, ss_=ss_: e.activation(out=ss_[:], in_=ss_[:], func=AF.Ln), reads=[ss_.b], writes=[ss_.b])
                kk.op("act", lambda e, ss_=ss_: e.activation(out=ss_[:], in_=ss_[:], func=AF.Exp, scale=-0.5), reads=[ss_.b], writes=[ss_.b])
                kk.op("dve", lambda e, q_=q_, ss_=ss_: e.tensor_tensor(
                    out=q_[:], in0=q_[:], in1=ss_[:].unsqueeze(2).to_broadcast([128, 6, 64]), op=ALU.mult),
                    reads=[q_.b, ss_.b], writes=[q_.b])
                kk.op("pool", lambda e, q_=q_: e.tensor_tensor(out=q_[:], in0=q_[:], in1=qkw[:], op=ALU.mult),
                      reads=[q_.b, qkw.b], writes=[q_.b])
                def v5(t):
                    return t[:].rearrange("p h (a b f) -> p h a b f", a=2, b=2)
                cosb = lambda c_: c_[:, 0, :].rearrange("p (a f) -> p a f", a=2).unsqueeze(1).to_broadcast([128, 6, 2, 16])
                sinb = lambda c_: c_[:, 1, :].rearrange("p (a f) -> p a f", a=2).unsqueeze(1).to_broadcast([128, 6, 2, 16])
                kk.op("dve", lambda e, q_=q_, c_=c_: e.tensor_tensor(out=tmpa[:], in0=v5(q_)[:, :, :, 1, :], in1=sinb(c_), op=ALU.mult),
                      reads=[q_.b, c_.b], writes=[tmpa.b])
                kk.op("pool", lambda e, q_=q_, c_=c_: e.tensor_tensor(out=tmpb[:], in0=v5(q_)[:, :, :, 0, :], in1=sinb(c_), op=ALU.mult),
                      reads=[q_.b, c_.b], writes=[tmpb.b])
                kk.op("dve", lambda e, q_=q_, r_=r_, c_=c_: e.tensor_tensor(out=v5(r_)[:, :, :, 0, :], in0=v5(q_)[:, :, :, 0, :], in1=cosb(c_), op=ALU.mult),
                      reads=[q_.b, c_.b], writes=[r_.b])
                kk.op("pool", lambda e, q_=q_, r_=r_, c_=c_: e.tensor_tensor(out=v5(r_)[:, :, :, 1, :], in0=v5(q_)[:, :, :, 1, :], in1=cosb(c_), op=ALU.mult),
                      reads=[q_.b, c_.b], writes=[r_.b])
                kk.op("dve", lambda e, r_=r_: e.tensor_tensor(out=v5(r_)[:, :, :, 0, :], in0=v5(r_)[:, :, :, 0, :], in1=tmpa[:], op=ALU.subtract),
                      reads=[r_.b, tmpa.b], writes=[r_.b])
                kk.op("dve", lambda e, r_=r_: e.tensor_tensor(out=v5(r_)[:, :, :, 1, :], in0=v5(r_)[:, :, :, 1, :], in1=tmpb[:], op=ALU.add),
                      reads=[r_.b, tmpb.b], writes=[r_.b])
                kk.op("act", lambda e, r_=r_, qb_=qb_: e.activation(out=qb_[:], in_=r_[:].rearrange("p a b -> p (a b)"), func=AF.Identity),
                      reads=[r_.b], writes=[qb_.b])
                for j in range(3):
                    kk.op("pe", lambda e, qb_=qb_, j=j: e.transpose(out=ptr[:, j, :], in_=qb_[:, j * 128:(j + 1) * 128], identity=ident[:]),
                          reads=[qb_.b, ident.b], writes=[ptr.b])
                kk.op("dve", lambda e, qkTb=qkTb, t4=t4: e.tensor_copy(out=qkTb[:, :, t4 * 128:(t4 + 1) * 128], in_=ptr[:]),
                      reads=[ptr.b], writes=[qkTb.b])
                b_, t_ = bgt[s], t8[s]
                kk.op("dve", lambda e, t_=t_, pb_=pb_: e.tensor_tensor(out=t_[:, 8:16], in0=pb_[:, 8:16], in1=dtb[:], op=ALU.add),
                      reads=[pb_.b, dtb.b], writes=[t_.b])
                kk.op("act", lambda e, t_=t_, pb_=pb_: e.activation(out=t_[:, 0:8], in_=pb_[:, 0:8], func=AF.Exp, scale=-1.0),
                      reads=[pb_.b], writes=[t_.b])
                kk.op("act", lambda e, t_=t_: e.activation(out=t_[:, 8:16], in_=t_[:, 8:16], func=AF.Exp),
                      reads=[t_.b], writes=[t_.b])
                kk.op("act", lambda e, t_=t_: e.activation(out=t_[:, 8:16], in_=t_[:, 8:16], func=AF.Ln, bias=1.0),
                      reads=[t_.b], writes=[t_.b])
                kk.op("dve", lambda e, t_=t_: e.tensor_scalar(out=t_[:, 0:8], in0=t_[:, 0:8], scalar1=1.0, scalar2=1.0, op0=ALU.add, op1=ALU.mult),
                      reads=[t_.b], writes=[t_.b])
                kk.op("dve", lambda e, t_=t_, b_=b_: e.reciprocal(out=b_[:, 0:8], in_=t_[:, 0:8]), reads=[t_.b], writes=[b_.b])
                kk.op("dve", lambda e, t_=t_, b_=b_: e.tensor_tensor(out=b_[:, 8:16], in0=t_[:, 8:16], in1=nA[:], op=ALU.mult),
                      reads=[t_.b, nA.b], writes=[b_.b])
                kk.op("pool", lambda e, b_=b_, r0=r0: e.dma_start(out=dr[f"bg{si}"][r0:r0 + 128, :], in_=b_[:]),
                      reads=[b_.b], dma=f"st{s}")
            c0 = tb * 512
            kk.op("pool", lambda e, qkTb=qkTb, c0=c0: e.dma_start(
                out=dr[f"qT{si}"][:, c0:c0 + 512].rearrange("(j p) s -> p j s", p=128), in_=qkTb[:, 0:2, :]),
                reads=[qkTb.b], dma=f"sq{tb % 2}")
            kk.op("pool", lambda e, qkTb=qkTb, c0=c0: e.dma_start(out=dr[f"kT{si}"][:, c0:c0 + 512], in_=qkTb[:, 2, :]),
                  reads=[qkTb.b], dma=f"sq{tb % 2}")
            stg = stage[tb % 2]
            for g in range(24):
                pa = pacc[g % 3]
                for kc in range(8):
                    kk.op("pe", lambda e, pa=pa, g=g, kc=kc, hTb=hTb: e.matmul(
                        pa[:], w[:, kc, g * 128:(g + 1) * 128], hTb[:, kc, :], start=(kc == 0), stop=(kc == 7)),
                        reads=[w.b, hTb.b], writes=[pa.b])
                if ev % 2 == 0:
                    kk.op("act", lambda e, pa=pa, g=g, stg=stg: e.activation(out=stg[:, g, :], in_=pa[:], func=AF.Identity),
                          reads=[pa.b], writes=[stg.b])
                else:
                    kk.op("dve", lambda e, pa=pa, g=g, stg=stg: e.tensor_copy(out=stg[:, g, :], in_=pa[:]),
                          reads=[pa.b], writes=[stg.b])
                ev += 1
            for h2 in range(2):
                kk.op("pool", lambda e, stg=stg, c0=c0, h2=h2: e.dma_start(
                    out=projT[h2 * 1536:(h2 + 1) * 1536, c0:c0 + 512].rearrange("(g p) s -> p g s", p=128),
                    in_=stg[:, h2 * 12:(h2 + 1) * 12, :]), reads=[stg.b], dma=f"sp{tb % 2}")


_DYN = {}


def dyn(e, part, base, n):
    if part is None:
        return slice(base, base + n)
    npart, div = part
    key = (id(e), div, npart)
    if key not in _DYN:
        _DYN[key] = e.snap((e.partition_id() // div) * npart)
    return bass.ds(_DYN[key] + base, n)


def phase_part(nc, kk, dr, si, S, xin, part):
    npart, div = part
    dummies = {e_: Buf("partcopy_" + e_) for e_ in ("sp", "act", "pool")}

    def off(e):
        return e.snap((e.partition_id() // div) * npart)

    if si == 0:
        qe, xe, me = "sp", "sp", "pool"
    else:
        qe, xe, me = "pool", "act", "pool"
    kk.op(qe, lambda e: e.dma_start(out=dr[f"qTp{si}"][:, :], in_=dr[f"qT{si}"][:, bass.ds(off(e), npart)]), writes=[dummies[qe]], dma="c0")
    kk.op(qe, lambda e: e.dma_start(out=dr[f"zTp{si}"][:, :], in_=dr[f"projT{si}"][G_CZ * 128:(G_CZ + 2) * 128, bass.ds(off(e), npart)]),
          writes=[dummies[qe]], dma="c0")
    kk.op(me, lambda e: e.dma_start(out=dr[f"mixTp{si}"][0:512, :], in_=dr[f"mixT{si}"][0:512, bass.ds(off(e), npart)]), writes=[dummies[me]], dma="c0")
    kk.op(me, lambda e: e.dma_start(out=dr[f"mixTp{si}"][768:1024, :], in_=dr[f"mixT{si}"][768:1024, bass.ds(off(e), npart)]),
          writes=[dummies[me]], dma="c0")
    step = 1024
    for r in range(0, npart, step):
        n = min(step, npart - r)
        kk.op(xe, lambda e, r=r, n=n: e.dma_start(out=dr[f"xp{si}"][r:r + n, :], in_=xin[bass.ds(off(e) + r, n), :]), writes=[dummies[xe]], dma="c0")


def phase_out(nc, kk, dr, l, si, S, xin, yout, part=None):
    with ExitStack() as st:
        ntok = S if part is None else part[0]
        if part is not None:
            xin = dr[f"xp{si}"]
        wo = load_w_bf16(nc, kk, st, "w_out_sb", dr["w_out"][l], D)
        NS = 2
        mt = [sb(nc, st, f"mt{i}", [128, 8, 128], BF16) for i in range(NS)]
        xt = [sb(nc, st, f"xo{i}", [128, D], F32) for i in range(NS)]
        yt = [sb(nc, st, f"yo{i}", [128, D], F32) for i in range(NS)]
        po = [ps(nc, st, f"po{i}", [128, 512], F32) for i in range(4)]
        mixT = dr[f"mixT{si}"] if part is None else dr[f"mixTp{si}"]
        for ti in range(ntok // 128):
            s = ti % NS
            r0 = ti * 128
            m_, x_, y_ = mt[s], xt[s], yt[s]
            kk.op("sp", lambda e, m_=m_, r0=r0: e.dma_start(out=m_[:], in_=mixT[:, r0:r0 + 128].rearrange("(k p) s -> p k s", p=128)),
                  writes=[m_.b], dma=f"x{s}")
            kk.op("sp", lambda e, x_=x_, r0=r0: e.dma_start(out=x_[:], in_=xin[r0:r0 + 128, :]), writes=[x_.b], dma=f"x{s}")
            for g in range(2):
                p_ = po[(ti * 2 + g) % 4]
                for kc in range(8):
                    kk.op("pe", lambda e, p_=p_, m_=m_, kc=kc, g=g: e.matmul(
                        p_[:], m_[:, kc, :], wo[:, kc, g * 512:(g + 1) * 512], start=(kc == 0), stop=(kc == 7)),
                        reads=[m_.b, wo.b], writes=[p_.b])
                kk.op("dve", lambda e, p_=p_, x_=x_, y_=y_, g=g: e.tensor_tensor(
                    out=y_[:, g * 512:(g + 1) * 512], in0=p_[:], in1=x_[:, g * 512:(g + 1) * 512], op=ALU.add),
                    reads=[p_.b, x_.b], writes=[y_.b])
            kk.op("pool", lambda e, y_=y_, r0=r0: e.dma_start(out=yout[r0:r0 + 128, :], in_=y_[:]), reads=[y_.b], dma=f"st{s}")


def phase_attn(nc, kk, dr, l, si, S, ones_f, part=None):
    with ExitStack() as st:
        nkt = S // 128
        KT = sb(nc, st, "KT", [128, S], BF16)
        VA = sb(nc, st, "VA", [128, nkt, 130], BF16)
        for c in range(0, S, 2048):
            ce = min(c + 2048, S)
            kk.op("sp", lambda e, c=c, ce=ce: e.dma_start(out=KT[:, c:ce], in_=dr[f"kT{si}"][:, c:ce]), writes=[KT.b], dma="c0")
        for c in range(0, nkt, 16):
            ce = min(c + 16, nkt)
            kk.op("sp", lambda e, c=c, ce=ce: e.dma_start(out=VA[:, c:ce, :],
                                                          in_=dr[f"va{si}"][c * 128:ce * 128, :].rearrange("(t p) c -> p t c", p=128)),
                  writes=[VA.b], dma="c0")
        qt = [sb(nc, st, f"qt{i}", [128, 2, 512], BF16) for i in range(2)]
        zt = [sb(nc, st, f"zt{i}", [64, 4, 512], BF16) for i in range(2)]
        pss = [ps(nc, st, f"pss{i}", [128, 2, 512], F32) for i in range(2)]
        pex = [sb(nc, st, f"pex{i}", [128, 2, 512], BF16) for i in range(2)]
        pov = [ps(nc, st, f"pov{i}", [128, 512], F32) for i in range(2)]
        pbc = ps(nc, st, "pbc", [64, 512], F32)
        osb = [sb(nc, st, f"osb{i}", [128, 512], F32) for i in range(2)]
        rc = [sb(nc, st, f"rc{i}", [128, 512], F32) for i in range(2)]
        og = [sb(nc, st, f"og{i}", [64, 4, 512], BF16) for i in range(2)]
        it = 0
        nq = S if part is None else part[0]
        qsrc = dr[f"qT{si}"] if part is None else dr[f"qTp{si}"]
        zsrc = dr[f"projT{si}"][G_CZ * 128:(G_CZ + 2) * 128, :] if part is None else dr[f"zTp{si}"]
        mixdst = dr[f"mixT{si}"] if part is None else dr[f"mixTp{si}"]
        for qb in range(nq // 512):
            c0 = qb * 512
            q_ = qt[qb % 2]
            z_ = zt[qb % 2]
            og_ = og[qb % 2]
            for h in range(4):
                kv = h // 2
                kk.op("sp", lambda e, q_=q_, h=h, kv=kv, c0=c0: e.dma_start(
                    out=q_[kv * 64:(kv + 1) * 64, h % 2, :], in_=qsrc[h * 64:(h + 1) * 64, c0:c0 + 512]),
                    writes=[q_.b], dma=f"x{qb % 2}")
            kk.op("sp", lambda e, z_=z_, c0=c0: e.dma_start(
                out=z_[:], in_=zsrc[:, c0:c0 + 512].rearrange("(h p) s -> p h s", p=64)),
                writes=[z_.b], dma=f"x{qb % 2}")
            kk.op("act", lambda e, z_=z_: e.activation(out=z_[:], in_=z_[:], func=AF.Silu), reads=[z_.b], writes=[z_.b])
            for h in range(4):
                kv = h // 2
                po_ = pov[h % 2]
                npair = nkt // 2
                slots = []
                for kp in range(npair):
                    slots.append((pss[it % 2], pex[it % 2]))
                    it += 1

                def emit_qk(kp):
                    ps_ = slots[kp][0]
                    for j in range(2):
                        kt = kp * 2 + j
                        kk.op("pe", lambda e, ps_=ps_, j=j, kt=kt, kv=kv, q_=q_, h=h: e.matmul(
                            ps_[:, j, :], KT[kv * 64:(kv + 1) * 64, kt * 128:(kt + 1) * 128], q_[kv * 64:(kv + 1) * 64, h % 2, :],
                            start=True, stop=True), reads=[KT.b, q_.b], writes=[ps_.b])

                emit_qk(0)
                for kp in range(npair):
                    ps_, pe_ = slots[kp]
                    if kp + 1 < npair:
                        emit_qk(kp + 1)
                    kk.op("act", lambda e, ps_=ps_, pe_=pe_: e.activation(out=pe_[:], in_=ps_[:], func=AF.Exp, scale=0.125),
                          reads=[ps_.b], writes=[pe_.b])
                    for j in range(2):
                        kt = kp * 2 + j
                        kk.op("pe", lambda e, po_=po_, pe_=pe_, j=j, kt=kt, kv=kv: e.matmul(
                            po_[0:65, :], VA[:, kt, kv * 65:(kv + 1) * 65], pe_[:, j, :],
                            start=(kt == 0), stop=(kt == nkt - 1)), reads=[VA.b, pe_.b], writes=[po_.b])
                o_ = osb[h % 2]
                r_ = rc[h % 2]
                kk.op("dve", lambda e, o_=o_, po_=po_: e.tensor_copy(out=o_[0:65, :], in_=po_[0:65, :]), reads=[po_.b], writes=[o_.b])
                kk.op("dve", lambda e, o_=o_, r_=r_: e.reciprocal(out=r_[64:65, :], in_=o_[64:65, :]), reads=[o_.b], writes=[r_.b])
                kk.op("pe", lambda e, r_=r_: e.matmul(pbc[:], ones_f[64:65, 0:64], r_[64:65, :], start=True, stop=True),
                      reads=[r_.b, ones_f.b], writes=[pbc.b])
                kk.op("dve", lambda e, o_=o_: e.tensor_tensor(out=o_[0:64, :], in0=o_[0:64, :], in1=pbc[:], op=ALU.mult),
                      reads=[o_.b, pbc.b], writes=[o_.b])
                kk.op("dve", lambda e, o_=o_, og_=og_, h=h, z_=z_: e.tensor_tensor(out=og_[:, h, :], in0=o_[0:64, :], in1=z_[:, h, :], op=ALU.mult),
                      reads=[o_.b, z_.b], writes=[og_.b])
            kk.op("pool", lambda e, og_=og_, c0=c0: e.dma_start(
                out=mixdst[512:768, c0:c0 + 512].rearrange("(h p) s -> p h s", p=64), in_=og_[:]),
                reads=[og_.b], dma=f"st{qb % 2}")


def phase_fm(nc, kk, dr, l, si, S, ident):
    with ExitStack() as st:
        projT = dr[f"projT{si}"]
        swf = sb(nc, st, "swf", [128, 4, 128], F32)
        sw = sb(nc, st, "sw", [128, 4, 128], BF16)
        sbT = sb(nc, st, "sbT", [128, 2, 128], F32)
        cw = sb(nc, st, "cw", [128, 6, 5], F32)
        pwf = sb(nc, st, "pwf", [128, 2, 128], F32)
        pw = sb(nc, st, "pw", [128, 2, 128], BF16)
        psc = sb(nc, st, "psc", [128, 2], F32)
        kk.op("sp", lambda e: e.dma_start(out=swf[:], in_=dr["sgu_wT"][l]), writes=[swf.b], dma="c0")
        kk.op("sp", lambda e: e.dma_start(out=sbT[:], in_=dr["sgu_bT"][l]), writes=[sbT.b], dma="c0")
        kk.op("sp", lambda e: e.dma_start(out=cw[:], in_=dr["conv_w"][l]), writes=[cw.b], dma="c0")
        kk.op("sp", lambda e: e.dma_start(out=pwf[:], in_=dr["pool_w"][l]), writes=[pwf.b], dma="c0")
        kk.op("sp", lambda e: e.dma_start(out=psc[:], in_=dr["pool_s"][l]), writes=[psc.b], dma="c0")
        kk.op("dve", lambda e: e.tensor_copy(out=sw[:], in_=swf[:]), reads=[swf.b], writes=[sw.b])
        kk.op("dve", lambda e: e.tensor_copy(out=pw[:], in_=pwf[:]), reads=[pwf.b], writes=[pw.b])

        NS = 2
        av = [sb(nc, st, f"av{i}", [128, 6, 512], BF16) for i in range(NS)]
        dxz = [sb(nc, st, f"dxz{i}", [128, 4, 528], BF16) for i in range(NS)]
        icn = [sb(nc, st, f"icn{i}", [128, 2, 512], F32) for i in range(NS)]
        bx = [sb(nc, st, f"bx{i}", [128, 6, 516], BF16) for i in range(NS)]
        pvt = ps(nc, st, "pvt", [128, 2, 128], BF16)
        vtok = sb(nc, st, "vtok", [128, 4, 64], F32)
        vsq = sb(nc, st, "vsq", [128, 4, 64], F32)
        vss = sb(nc, st, "vss", [128, 4], F32)
        vnm = [sb(nc, st, f"vnm{i}", [128, 4, 128], BF16) for i in range(2)]
        for v_ in vnm:
            kk.op("pool", lambda e, v_=v_: e.memset(v_[:], 0.0), writes=[v_.b])
        pm = [ps(nc, st, f"pm{i}", [128, 2, 512], F32) for i in range(1)]
        ta = sb(nc, st, "ta", [128, 2, 512], F32)
        sz = sb(nc, st, "sz", [128, 2, 512], BF16)
        ma = [sb(nc, st, f"ma{i}", [128, 2, 512], BF16) for i in range(NS)]
        ss = sb(nc, st, "ss", [128, 2, 528], F32)
        s2 = sb(nc, st, "s2", [128, 2, 528], F32)
        s4 = sb(nc, st, "s4", [128, 2, 528], F32)
        s8 = sb(nc, st, "s8", [128, 528], F32)
        dfb = sb(nc, st, "dfb", [128, 2, 512], BF16)
        ppl = ps(nc, st, "ppl", [128, 2, 512], F32)
        md = [sb(nc, st, f"md{i}", [128, 2, 512], BF16) for i in range(NS)]
        szd = sb(nc, st, "szd", [128, 2, 512], BF16)
        cacc = sb(nc, st, "cacc", [128, 512], F32)
        cvo = [sb(nc, st, f"cvo{i}", [128, 6, 512], BF16) for i in range(NS)]

        nblk = S // 512
        for tb in range(nblk):
            s = tb % NS
            c0 = tb * 512
            a_, d_, i_, b_ = av[s], dxz[s], icn[s], bx[s]
            kk.op("sp", lambda e, a_=a_, c0=c0: e.dma_start(out=a_[:], in_=projT[0:768, c0:c0 + 512].rearrange("(g p) s -> p g s", p=128)),
                  writes=[a_.b], dma=f"x{s}")
            lo = max(c0 - 8, 0)
            hi = min(c0 + 520, S)
            if lo > c0 - 8:
                kk.op("pool", lambda e, d_=d_: e.memset(d_[:, 0:2, 0:8], 0.0), writes=[d_.b])
            if hi < c0 + 520:
                kk.op("pool", lambda e, d_=d_: e.memset(d_[:, 0:2, 520:528], 0.0), writes=[d_.b])
            kk.op("sp", lambda e, d_=d_, lo=lo, hi=hi, c0=c0: e.dma_start(
                out=d_[:, 0:2, lo - (c0 - 8):hi - (c0 - 8)], in_=projT[G_DX * 128:(G_DX + 2) * 128, lo:hi].rearrange("(g p) s -> p g s", p=128)),
                writes=[d_.b], dma=f"x{s}")
            kk.op("sp", lambda e, d_=d_, c0=c0: e.dma_start(
                out=d_[:, 2:4, 0:512], in_=projT[G_DZ * 128:(G_DZ + 2) * 128, c0:c0 + 512].rearrange("(g p) s -> p g s", p=128)),
                writes=[d_.b], dma=f"x{s}")
            kk.op("sp", lambda e, i_=i_, c0=c0: e.dma_start(out=i_[:], in_=dr[f"icnt{si}"][:, :, c0:c0 + 512]), writes=[i_.b], dma=f"x{s}")
            lo2 = max(c0 - 2, 0)
            hi2 = min(c0 + 514, S)
            if lo2 > c0 - 2:
                kk.op("pool", lambda e, b_=b_: e.memset(b_[:, :, 0:2], 0.0), writes=[b_.b])
            if hi2 < c0 + 514:
                kk.op("pool", lambda e, b_=b_: e.memset(b_[:, :, 514:516], 0.0), writes=[b_.b])
            kk.op("sp", lambda e, b_=b_, lo2=lo2, hi2=hi2, c0=c0: e.dma_start(
                out=b_[:, :, lo2 - (c0 - 2):hi2 - (c0 - 2)], in_=projT[G_BQ * 128:(G_BQ + 6) * 128, lo2:hi2].rearrange("(g p) s -> p g s", p=128)),
                writes=[b_.b], dma=f"x{s}")

            pm_ = pm[0]
            for ch in range(4):
                vn_ = vnm[ch % 2]
                for j in range(2):
                    kk.op("pe", lambda e, a_=a_, j=j, ch=ch: e.transpose(out=pvt[:, j, :], in_=a_[:, 2 + j, ch * 128:(ch + 1) * 128], identity=ident[:]),
                          reads=[a_.b, ident.b], writes=[pvt.b])
                kk.op("dve", lambda e: e.tensor_copy(out=vtok[:].rearrange("p a b -> p (a b)"), in_=pvt[:].rearrange("p a b -> p (a b)")),
                      reads=[pvt.b], writes=[vtok.b])
                kk.op("dve", lambda e: e.tensor_tensor(out=vsq[:], in0=vtok[:], in1=vtok[:], op=ALU.mult), reads=[vtok.b], writes=[vsq.b])
                kk.op("dve", lambda e: e.tensor_reduce(out=vss[:], in_=vsq[:], axis=AX.X, op=ALU.add), reads=[vsq.b], writes=[vss.b])
                kk.op("dve", lambda e: e.tensor_scalar(out=vss[:], in0=vss[:], scalar1=1.0 / 64, scalar2=EPS, op0=ALU.mult, op1=ALU.add),
                      reads=[vss.b], writes=[vss.b])
                kk.op("act", lambda e: e.activation(out=vss[:], in_=vss[:], func=AF.Ln), reads=[vss.b], writes=[vss.b])
                kk.op("act", lambda e: e.activation(out=vss[:], in_=vss[:], func=AF.Exp, scale=-0.5), reads=[vss.b], writes=[vss.b])
                for h in range(4):
                    kk.op("dve", lambda e, vn_=vn_, h=h: e.tensor_scalar(
                        out=vn_[:, h, (h % 2) * 64:(h % 2) * 64 + 64], in0=vtok[:, h, :], scalar1=vss[:, h:h + 1], scalar2=1.0,
                        op0=ALU.mult, op1=ALU.mult), reads=[vtok.b, vss.b], writes=[vn_.b])
                for h in range(4):
                    kk.op("pe", lambda e, vn_=vn_, h=h, ch=ch, pm_=pm_: e.matmul(
                        pm_[:, h // 2, ch * 128:(ch + 1) * 128], vn_[:, h, :], sw[:, h, :], start=(h % 2 == 0), stop=(h % 2 == 1)),
                        reads=[vn_.b, sw.b], writes=[pm_.b])
            m_ = ma[s]
            kk.op("dve", lambda e, pm_=pm_: e.tensor_tensor(
                out=ta[:].rearrange("p a (c i) -> p a c i", c=4), in0=pm_[:].rearrange("p a (c i) -> p a c i", c=4),
                in1=sbT[:].unsqueeze(2).to_broadcast([128, 2, 4, 128]), op=ALU.add), reads=[pm_.b, sbT.b], writes=[ta.b])
            kk.op("act", lambda e, a_=a_: e.activation(out=sz[:], in_=a_[:, 4:6, :], func=AF.Silu), reads=[a_.b], writes=[sz.b])
            kk.op("pool", lambda e, a_=a_: e.tensor_tensor(out=ta[:], in0=ta[:], in1=a_[:, 0:2, :], op=ALU.mult), reads=[ta.b, a_.b], writes=[ta.b])
            kk.op("dve", lambda e, m_=m_: e.tensor_tensor(out=m_[:], in0=ta[:], in1=sz[:], op=ALU.mult), reads=[ta.b, sz.b], writes=[m_.b])
            kk.op("pool", lambda e, m_=m_, c0=c0: e.dma_start(out=dr[f"mixT{si}"][0:256, c0:c0 + 512].rearrange("(g p) s -> p g s", p=128), in_=m_[:]),
                  reads=[m_.b], dma=f"st{s}")

            X = d_
            kk.op("dve", lambda e, X=X: e.tensor_tensor(out=s2[:, :, 1:528], in0=X[:, 0:2, 0:527], in1=X[:, 0:2, 1:528], op=ALU.add),
                  reads=[X.b], writes=[s2.b])
            kk.op("pool", lambda e: e.tensor_tensor(out=s4[:, :, 2:527], in0=s2[:, :, 1:526], in1=s2[:, :, 3:528], op=ALU.add),
                  reads=[s2.b], writes=[s4.b])
            kk.op("dve", lambda e: e.tensor_tensor(out=s8[:, 4:525], in0=s4[:, 1, 2:523], in1=s4[:, 1, 6:527], op=ALU.add),
                  reads=[s4.b], writes=[s8.b])
            kk.op("pool", lambda e: e.tensor_copy(out=ss[0:64, 0, 8:520], in_=s2[0:64, 0, 8:520]), reads=[s2.b], writes=[ss.b])
            kk.op("pool", lambda e: e.tensor_copy(out=ss[64:128, 0, 8:520], in_=s4[64:128, 0, 8:520]), reads=[s4.b], writes=[ss.b])
            kk.op("dve", lambda e: e.tensor_copy(out=ss[0:64, 1, 8:520], in_=s8[0:64, 8:520]), reads=[s8.b], writes=[ss.b])
            kk.op("dve", lambda e: e.tensor_tensor(out=ss[64:128, 1, 8:520], in0=s8[64:128, 4:516], in1=s8[64:128, 12:524], op=ALU.add),
                  reads=[s8.b], writes=[ss.b])
            kk.op("dve", lambda e, i_=i_: e.tensor_tensor(out=ss[:, :, 8:520], in0=ss[:, :, 8:520], in1=i_[:], op=ALU.mult),
                  reads=[ss.b, i_.b], writes=[ss.b])
            kk.op("dve", lambda e, X=X: e.tensor_tensor(out=dfb[:], in0=ss[:, :, 8:520], in1=X[:, 0:2, 8:520], op=ALU.subtract),
                  reads=[ss.b, X.b], writes=[dfb.b])
            for ch in range(2):
                kk.op("pe", lambda e, ch=ch: e.matmul(ppl[:, ch, :], pw[:, ch, :], dfb[:, ch, :], start=True, stop=True),
                      reads=[pw.b, dfb.b], writes=[ppl.b])
            kk.op("act", lambda e, X=X: e.activation(out=szd[:], in_=X[:, 2:4, 0:512], func=AF.Silu), reads=[X.b], writes=[szd.b])
            o_ = md[s]
            for ch in range(2):
                kk.op("dve", lambda e, ch=ch, o_=o_: e.scalar_tensor_tensor(
                    out=o_[:, ch, :], in0=ppl[:, ch, :], scalar=psc[:, ch:ch + 1], in1=szd[:, ch, :], op0=ALU.mult, op1=ALU.mult),
                    reads=[ppl.b, psc.b, szd.b], writes=[o_.b])
            kk.op("pool", lambda e, o_=o_, c0=c0: e.dma_start(out=dr[f"mixT{si}"][768:1024, c0:c0 + 512].rearrange("(g p) s -> p g s", p=128), in_=o_[:]),
                  reads=[o_.b], dma=f"st{s}")

            co = cvo[s]
            for ch in range(6):
                kk.op("dve", lambda e, b_=b_, ch=ch: e.tensor_scalar(out=cacc[:], in0=b_[:, ch, 0:512], scalar1=cw[:, ch, 0:1], scalar2=1.0,
                                                                    op0=ALU.mult, op1=ALU.mult), reads=[b_.b, cw.b], writes=[cacc.b])
                for i in range(1, 5):
                    kk.op("dve", lambda e, b_=b_, ch=ch, i=i: e.scalar_tensor_tensor(
                        out=cacc[:], in0=b_[:, ch, i:i + 512], scalar=cw[:, ch, i:i + 1], in1=cacc[:], op0=ALU.mult, op1=ALU.add),
                        reads=[b_.b, cw.b, cacc.b], writes=[cacc.b])
                kk.op("act", lambda e, co=co, ch=ch: e.activation(out=co[:, ch, :], in_=cacc[:], func=AF.Silu), reads=[cacc.b], writes=[co.b])
            kk.op("pool", lambda e, co=co, c0=c0: e.dma_start(out=dr[f"convT{si}"][:, c0:c0 + 512].rearrange("(g p) s -> p g s", p=128), in_=co[:]),
                  reads=[co.b], dma=f"st{s}")


def bc(ap, shape, axis):
    return ap.unsqueeze(axis).to_broadcast(shape)


def phase_dn(nc, kk, dr, l, si, S, ident, ident_f, ones_f):
    with ExitStack() as st:
        NCH = S // 128
        tri = [sb(nc, st, f"tri{d}", [128, 128], F32) for d in range(2)]
        nmI = [sb(nc, st, f"nmI{d}", [128, 128], F32) for d in range(2)]
        nmS = [sb(nc, st, f"nmS{d}", [128, 128], F32) for d in range(2)]
        offd = sb(nc, st, "offd", [128, 128], F32)
        dnw = sb(nc, st, "dnw", [128, 4, 64], F32)
        kk.op("sp", lambda e: e.dma_start(out=tri[0][:], in_=dr["triU"][:, :]), writes=[tri[0].b], dma="c0")
        kk.op("sp", lambda e: e.dma_start(out=tri[1][:], in_=dr["triL"][:, :]), writes=[tri[1].b], dma="c0")
        kk.op("sp", lambda e: e.dma_start(out=dnw[:], in_=dr["dn_w"][l]), writes=[dnw.b], dma="c0")
        kk.op("dve", lambda e: e.tensor_scalar(out=offd[:], in0=ident_f[:], scalar1=-1.0, scalar2=1.0, op0=ALU.mult, op1=ALU.add),
              reads=[ident_f.b], writes=[offd.b])
        for d in range(2):
            kk.op("dve", lambda e, d=d: e.tensor_scalar(out=nmI[d][:], in0=tri[d][:], scalar1=-1.0, scalar2=1e30, op0=ALU.add, op1=ALU.mult),
                  reads=[tri[d].b], writes=[nmI[d].b])
            kk.op("dve", lambda e, d=d: e.tensor_tensor(out=nmS[d][:], in0=tri[d][:], in1=offd[:], op=ALU.mult),
                  reads=[tri[d].b, offd.b], writes=[nmS[d].b])
            kk.op("dve", lambda e, d=d: e.tensor_scalar(out=nmS[d][:], in0=nmS[d][:], scalar1=-1.0, scalar2=1e30, op0=ALU.add, op1=ALU.mult),
                  reads=[nmS[d].b], writes=[nmS[d].b])

        bkb = ps(nc, st, "bkb", [128, 1024], BF16)
        bk1 = ps(nc, st, "bk1", [128, 512], F32)
        bk2 = ps(nc, st, "bk2", [128, 512], F32)
        bk3 = ps(nc, st, "bk3", [128, 512], F32)
        bk4 = ps(nc, st, "bk4", [128, 512], F32)
        bk5 = ps(nc, st, "bk5", [128, 512], F32)
        bka = ps(nc, st, "bka", [128, 1024], F32)

        def v4(ap, n=4):
            return ap.rearrange("p (c i) -> p c i", c=n)

        cT = [sb(nc, st, f"cT{i}", [128, 6, 128], BF16) for i in range(2)]
        bgc = [sb(nc, st, f"bgc{i}", [128, 16], F32) for i in range(2)]
        ofl = [sb(nc, st, f"ofl{i}", [128, 256], F32) for i in range(2)]
        zb = [sb(nc, st, f"zb{i}", [128, 2, 128], BF16) for i in range(2)]
        tokL = [sb(nc, st, f"tok{i}", [128, 768], F32) for i in range(2)]
        sq = sb(nc, st, "dsq", [128, 512], F32)
        rs = sb(nc, st, "drs", [128, 8], F32)
        qkn = sb(nc, st, "qkn", [128, 8, 64], BF16)
        qkn32 = sb(nc, st, "qkn32", [128, 8, 64], F32)
        ek32 = sb(nc, st, "ek32", [128, 4, 64], F32)
        kd32L = [sb(nc, st, f"kd32{i}", [128, 4, 64], F32) for i in range(2)]
        ekT32L = [sb(nc, st, f"ekT32{i}", [64, 4, 128], F32) for i in range(2)]
        rb = sb(nc, st, "rb", [128, 4, 64], BF16)
        vnew32 = sb(nc, st, "vnew32", [128, 4, 64], F32)
        vb = sb(nc, st, "vb", [128, 256], BF16)
        qkT = sb(nc, st, "dqkT", [64, 8, 128], BF16)
        gs = sb(nc, st, "gs", [128, 8], F32)
        eg = sb(nc, st, "eg", [128, 4], F32)
        ekd = sb(nc, st, "ekd", [128, 4], F32)
        eglL = [sb(nc, st, f"egl{i}", [128, 4], F32) for i in range(2)]
        rg = sb(nc, st, "rg", [128, 4, 128], F32)
        dm = sb(nc, st, "dm", [128, 4, 128], F32)
        dmI = sb(nc, st, "dmI", [128, 4, 128], F32)
        dmS = sb(nc, st, "dmS", [128, 4, 128], F32)
        X = sb(nc, st, "X", [128, 4, 128], F32R)
        ATL = [sb(nc, st, f"AT{i}", [128, 4, 128], BF16) for i in range(2)]
        YS = [sb(nc, st, f"YS{i}", [128, 4, 256], F32R) for i in range(2)]
        YT = [sb(nc, st, f"YT{i}", [128, 4, 128], F32R) for i in range(2)]
        db = sb(nc, st, "db", [128, 4, 128], F32)
        TTbL = [sb(nc, st, f"TTb{i}", [128, 4, 128], BF16) for i in range(2)]
        usb = sb(nc, st, "usb", [128, 4, 64], F32)
        ek = sb(nc, st, "ek", [128, 4, 64], BF16)
        kd = sb(nc, st, "kd", [128, 4, 64], BF16)
        qd = sb(nc, st, "qd", [128, 4, 64], BF16)
        wT = sb(nc, st, "wT", [64, 4, 128], BF16)
        qdTL = [sb(nc, st, f"qdT{i}", [64, 4, 128], BF16) for i in range(2)]
        vnew = sb(nc, st, "vnew", [128, 4, 64], BF16)
        Sf = sb(nc, st, "Sf", [64, 4, 64], F32)
        Sb = sb(nc, st, "Sb", [64, 4, 64], BF16)
        osb = [sb(nc, st, f"dosb{i}", [128, 256], F32) for i in range(2)]
        osq = sb(nc, st, "osq", [128, 256], F32)
        oss = sb(nc, st, "oss", [128, 4], F32)
        onb = sb(nc, st, "onb", [128, 256], BF16)
        omx = [sb(nc, st, f"omx{i}", [128, 2, 128], BF16) for i in range(2)]

        for d in range(2):
            if d == 1:
                kk.barrier()
            kk.op("pool", lambda e: e.memset(Sf[:], 0.0), writes=[Sf.b])
            kk.op("pool", lambda e: e.memset(Sb[:], 0.0), writes=[Sb.b])
            order = list(range(NCH)) if d == 0 else list(range(NCH - 1, -1, -1))

            def prep(n, ci, d=d):
                s = n % 2
                tok, AT, TTb, qdT, egl, kd32, ekT32 = tokL[s], ATL[s], TTbL[s], qdTL[s], eglL[s], kd32L[s], ekT32L[s]
                r0 = ci * 128
                c_, g_ = cT[s], bgc[s]
                yield
                kk.op("sp", lambda e, c_=c_, r0=r0: e.dma_start(out=c_[:], in_=dr[f"convT{si}"][:, r0:r0 + 128].rearrange("(g p) s -> p g s", p=128)),
                      writes=[c_.b], dma=f"x{s}")
                yield
                kk.op("sp", lambda e, g_=g_, r0=r0: e.dma_start(out=g_[:], in_=dr[f"bg{si}"][r0:r0 + 128, :]), writes=[g_.b], dma=f"x{s}")
                if d == 1:
                    kk.op("sp", lambda e, o_=ofl[s], r0=r0: e.dma_start(out=o_[:], in_=dr[f"of{si}"][r0:r0 + 128, :]), writes=[ofl[s].b], dma=f"x{s}")
                    kk.op("sp", lambda e, z_=zb[s], r0=r0: e.dma_start(
                        out=z_[:], in_=dr[f"projT{si}"][G_BZ * 128:(G_BZ + 2) * 128, r0:r0 + 128].rearrange("(g p) s -> p g s", p=128)),
                        writes=[zb[s].b], dma=f"x{s}")
                gd = g_[:, 8 + 4 * d:12 + 4 * d]
                bd = g_[:, 4 * d:4 * d + 4]
                yield
                for j in range(6):
                    kk.op("pe", lambda e, c_=c_, j=j: e.transpose(out=bkb[:, j * 128:(j + 1) * 128], in_=c_[:, j, :], identity=ident[:]),
                          reads=[c_.b, ident.b], writes=[bkb.b])
                yield
                kk.op("dve", lambda e: e.tensor_copy(out=tok[:], in_=bkb[:, 0:768]), reads=[bkb.b], writes=[tok.b])
                yield
                kk.op("dve", lambda e: e.tensor_tensor(out=sq[:], in0=tok[:, 0:512], in1=tok[:, 0:512], op=ALU.mult), reads=[tok.b], writes=[sq.b])
                yield
                kk.op("dve", lambda e: e.tensor_reduce(out=rs[:], in_=v4(sq[:], 8), axis=AX.X, op=ALU.add), reads=[sq.b], writes=[rs.b])
                yield
                kk.op("dve", lambda e: e.tensor_scalar(out=rs[:], in0=rs[:], scalar1=EPS, scalar2=1.0, op0=ALU.add, op1=ALU.mult),
                      reads=[rs.b], writes=[rs.b])
                yield
                kk.op("act", lambda e: e.activation(out=rs[:], in_=rs[:], func=AF.Ln), reads=[rs.b], writes=[rs.b])
                yield
                kk.op("act", lambda e: e.activation(out=rs[:], in_=rs[:], func=AF.Exp, scale=-0.5), reads=[rs.b], writes=[rs.b])
                yield
                kk.op("dve", lambda e: e.tensor_scalar(out=rs[:, 0:4], in0=rs[:, 0:4], scalar1=0.125, scalar2=1.0, op0=ALU.mult, op1=ALU.mult),
                      reads=[rs.b], writes=[rs.b])
                yield
                kk.op("dve", lambda e: e.tensor_tensor(out=qkn32[:], in0=v4(tok[:, 0:512], 8), in1=bc(rs[:], [128, 8, 64], 2), op=ALU.mult),
                      reads=[tok.b, rs.b], writes=[qkn32.b])
                yield
                kk.op("pool", lambda e: e.tensor_copy(out=qkn[:], in_=qkn32[:]), reads=[qkn32.b], writes=[qkn.b])
                yield
                for j in range(8):
                    kk.op("pe", lambda e, j=j: e.transpose(out=bkb[0:64, j * 128:(j + 1) * 128], in_=qkn[:, j, :], identity=ident[:]),
                          reads=[qkn.b, ident.b], writes=[bkb.b])
                yield
                kk.op("act", lambda e: e.activation(out=qkT[:].rearrange("p a b -> p (a b)"), in_=bkb[0:64, 0:1024], func=AF.Identity),
                      reads=[bkb.b], writes=[qkT.b])
                yield
                kk.op("pe", lambda e, gd=gd, d=d: e.matmul(bk4[:, 0:4], tri[d][:], gd, start=True, stop=True), reads=[tri[d].b, g_.b], writes=[bk4.b])
                yield
                kk.op("pe", lambda e, gd=gd: e.matmul(bk4[:, 4:8], ones_f[:], gd, start=True, stop=True), reads=[ones_f.b, g_.b], writes=[bk4.b])
                yield
                kk.op("dve", lambda e: e.tensor_copy(out=gs[:], in_=bk4[:, 0:8]), reads=[bk4.b], writes=[gs.b])
                yield
                kk.op("act", lambda e: e.activation(out=eg[:], in_=gs[:, 0:4], func=AF.Exp), reads=[gs.b], writes=[eg.b])
                yield
                kk.op("act", lambda e: e.activation(out=egl[:], in_=gs[:, 4:8], func=AF.Exp), reads=[gs.b], writes=[egl.b])
                yield
                kk.op("dve", lambda e: e.tensor_tensor(out=ekd[:], in0=gs[:, 4:8], in1=gs[:, 0:4], op=ALU.subtract), reads=[gs.b], writes=[ekd.b])
                yield
                kk.op("act", lambda e: e.activation(out=ekd[:], in_=ekd[:], func=AF.Exp), reads=[ekd.b], writes=[ekd.b])
                yield
                kk.op("dve", lambda e, gd=gd, d=d: e.tensor_tensor(out=rg[:], in0=bc(tri[d][:], [128, 4, 128], 1), in1=bc(gd, [128, 4, 128], 2), op=ALU.mult),
                      reads=[tri[d].b, g_.b], writes=[rg.b])
                yield
                kk.op("pe", lambda e: e.matmul(bk3[:], ones_f[:], rg[:].rearrange("p a b -> p (a b)"), start=True, stop=True),
                      reads=[ones_f.b, rg.b], writes=[bk3.b])
                yield
                kk.op("dve", lambda e: e.tensor_tensor(out=dm[:], in0=v4(bk3[:]), in1=bc(gs[:, 0:4], [128, 4, 128], 2), op=ALU.subtract),
                      reads=[bk3.b, gs.b], writes=[dm.b])
                yield
                kk.op("dve", lambda e, d=d: e.tensor_tensor(out=dmI[:], in0=dm[:], in1=bc(nmI[d][:], [128, 4, 128], 1), op=ALU.add),
                      reads=[dm.b, nmI[d].b], writes=[dmI.b])
                yield
                kk.op("pool", lambda e, d=d: e.tensor_tensor(out=dmS[:], in0=dm[:], in1=bc(nmS[d][:], [128, 4, 128], 1), op=ALU.add),
                      reads=[dm.b, nmS[d].b], writes=[dmS.b])
                yield
                kk.op("act", lambda e: e.activation(out=dmI[:], in_=dmI[:], func=AF.Exp), reads=[dmI.b], writes=[dmI.b])
                yield
                kk.op("act", lambda e: e.activation(out=dmS[:], in_=dmS[:], func=AF.Exp), reads=[dmS.b], writes=[dmS.b])
                yield
                for h in range(4):
                    kTh = qkT[:, 4 + h, :]
                    qTh = qkT[:, h, :]
                    kk.op("pe", lambda e, h=h, kTh=kTh: e.matmul(bk4[:, h * 128:(h + 1) * 128], kTh, kTh, start=True, stop=True),
                          reads=[qkT.b], writes=[bk4.b])
                    kk.op("pe", lambda e, h=h, kTh=kTh, qTh=qTh: e.matmul(bk5[:, h * 128:(h + 1) * 128], kTh, qTh, start=True, stop=True),
                          reads=[qkT.b], writes=[bk5.b])
                yield
                for h in range(4):
                    kk.op("dve", lambda e, h=h, bd=bd: e.scalar_tensor_tensor(
                        out=X[:, h, :], in0=bk4[:, h * 128:(h + 1) * 128], scalar=bd[:, h:h + 1], in1=dmS[:, h, :], op0=ALU.mult, op1=ALU.mult),
                        reads=[bk4.b, g_.b, dmS.b], writes=[X.b])
                yield
                kk.op("dve", lambda e: e.tensor_tensor(out=AT[:], in0=v4(bk5[:]), in1=dmI[:], op=ALU.mult), reads=[bk5.b, dmI.b], writes=[AT.b])
                pbv = v4(bk4[:])
                yield
                for h in range(4):
                    kk.op("pe", lambda e, h=h: e.transpose(out=pbv[:, h, :], in_=X[:, h, :].bitcast(F32), identity=ident_f[:]),
                          reads=[X.b, ident_f.b], writes=[bk4.b])
                yield
                kk.op("dve", lambda e: e.tensor_copy(out=YT[1][:], in_=pbv), reads=[bk4.b], writes=[YT[1].b])
                pa = bka[:].rearrange("p (c i) -> p c i", c=4)
                yield
                for h in range(4):
                    kk.op("pe", lambda e, h=h: e.matmul(pa[:, h, 0:128], YT[1][:, h, :], X[:, h, :], start=True, stop=True),
                          reads=[YT[1].b, X.b], writes=[bka.b])
                    kk.op("pe", lambda e, h=h: e.matmul(pbv[:, h, :], X[:, h, :], YT[1][:, h, :], start=True, stop=True),
                          reads=[YT[1].b, X.b], writes=[bk4.b])
                yield
                kk.op("dve", lambda e: e.tensor_tensor(out=YS[0][:, :, 128:256], in0=bc(ident_f[:], [128, 4, 128], 1), in1=X[:], op=ALU.subtract),
                      reads=[ident_f.b, X.b], writes=[YS[0].b])
                yield
                kk.op("act", lambda e: e.activation(out=YS[0][:, :, 0:128], in_=pa[:, :, 0:128], func=AF.Identity), reads=[bka.b], writes=[YS[0].b])
                yield
                kk.op("dve", lambda e: e.tensor_copy(out=YT[0][:], in_=pbv), reads=[bk4.b], writes=[YT[0].b])
                cur = 0
                yield
                for k in range(1, 7):
                    ys, yt = YS[cur], YT[cur]
                    nys, nyt = YS[1 - cur], YT[1 - cur]
                    for h in range(4):
                        kk.op("pe", lambda e, h=h, ys=ys, yt=yt: e.matmul(pa[:, h, :], yt[:, h, :], ys[:, h, :], start=True, stop=True),
                              reads=[ys.b, yt.b], writes=[bka.b])
                        if k < 6:
                            kk.op("pe", lambda e, h=h, ys=ys, yt=yt: e.matmul(pbv[:, h, :], ys[:, h, 0:128], yt[:, h, :], start=True, stop=True),
                                  reads=[ys.b, yt.b], writes=[bk4.b])
                    kk.op("dve", lambda e, ys=ys, nys=nys: e.tensor_tensor(out=nys[:, :, 128:256], in0=pa[:, :, 128:256], in1=ys[:, :, 128:256], op=ALU.add),
                          reads=[bka.b, ys.b], writes=[nys.b])
                    if k < 6:
                        kk.op("act", lambda e, nys=nys: e.activation(out=nys[:, :, 0:128], in_=pa[:, :, 0:128], func=AF.Identity),
                              reads=[bka.b], writes=[nys.b])
                        kk.op("dve", lambda e, nyt=nyt: e.tensor_copy(out=nyt[:], in_=pbv), reads=[bk4.b], writes=[nyt.b])
                    cur = 1 - cur
                TT = YS[cur]
                yield
                kk.op("pool", lambda e, bd=bd: e.tensor_tensor(out=db[:], in0=bc(ident_f[:], [128, 4, 128], 1), in1=bc(bd, [128, 4, 128], 2), op=ALU.mult),
                      reads=[ident_f.b, g_.b], writes=[db.b])
                yield
                kk.op("pe", lambda e: e.matmul(bk3[:], ones_f[:], db[:].rearrange("p a b -> p (a b)"), start=True, stop=True),
                      reads=[ones_f.b, db.b], writes=[bk3.b])
                yield
                kk.op("dve", lambda e, TT=TT: e.tensor_tensor(out=TTb[:], in0=v4(bk3[:]), in1=TT[:, :, 128:256], op=ALU.mult),
                      reads=[bk3.b, TT.b], writes=[TTb.b])
                yield
                kk.op("dve", lambda e: e.tensor_tensor(out=ek32[:], in0=qkn32[:, 4:8, :], in1=bc(eg[:], [128, 4, 64], 2), op=ALU.mult),
                      reads=[qkn32.b, eg.b], writes=[ek32.b])
                yield
                kk.op("pool", lambda e: e.tensor_tensor(out=kd32[:], in0=qkn32[:, 4:8, :], in1=bc(ekd[:], [128, 4, 64], 2), op=ALU.mult),
                      reads=[qkn32.b, ekd.b], writes=[kd32.b])
                yield
                kk.op("pool", lambda e: e.tensor_tensor(out=qd[:], in0=qkn32[:, 0:4, :], in1=bc(eg[:], [128, 4, 64], 2), op=ALU.mult),
                      reads=[qkn32.b, eg.b], writes=[qd.b])
                pek = bk5[0:64, :].rearrange("p (c i) -> p c i", c=4)
                yield
                for h in range(4):
                    kk.op("pe", lambda e, h=h: e.transpose(out=pek[:, h, :], in_=ek32[:, h, :], identity=ident_f[:]),
                          reads=[ek32.b, ident_f.b], writes=[bk5.b])
                yield
                kk.op("dve", lambda e: e.tensor_copy(out=ekT32[:], in_=pek), reads=[bk5.b], writes=[ekT32.b])
                yield
                for h in range(4):
                    kk.op("pe", lambda e, h=h: e.transpose(out=bkb[0:64, h * 128:(h + 1) * 128], in_=qd[:, h, :], identity=ident[:]),
                          reads=[qd.b, ident.b], writes=[bkb.b])
                yield
                kk.op("dve", lambda e: e.tensor_copy(out=qdT[:].rearrange("p a b -> p (a b)"), in_=bkb[0:64, 0:512]), reads=[bkb.b], writes=[qdT.b])

            def scan(n, ci, d=d):
                s = n % 2
                r0 = ci * 128
                tok, AT, TTb, qdT, egl, kd32, ekT32 = tokL[s], ATL[s], TTbL[s], qdTL[s], eglL[s], kd32L[s], ekT32L[s]
                pws = bk1[:, 0:256].rearrange("p (c i) -> p c i", c=4)
                pout = bk1[:, 256:512].rearrange("p (c i) -> p c i", c=4)
                pds = bk2[0:64, 0:256].rearrange("p (c i) -> p c i", c=4)
                pv2 = bk2[:, 256:512].rearrange("p (c i) -> p c i", c=4)
                yield
                for h in range(4):
                    kk.op("pe", lambda e, h=h: e.matmul(pws[:, h, :], ekT32[:, h, :], Sf[:, h, :], start=True, stop=True),
                          reads=[ekT32.b, Sf.b], writes=[bk1.b])
                yield
                kk.op("dve", lambda e: e.tensor_tensor(out=rb[:], in0=v4(tok[:, 512:768]), in1=pws, op=ALU.subtract),
                      reads=[tok.b, bk1.b], writes=[rb.b])
                yield
                for h in range(4):
                    kk.op("pe", lambda e, h=h: e.matmul(pv2[:, h, :], TTb[:, h, :], rb[:, h, :], start=True, stop=True),
                          reads=[TTb.b, rb.b], writes=[bk2.b])
                yield
                kk.op("dve", lambda e: e.tensor_copy(out=vnew32[:], in_=pv2), reads=[bk2.b], writes=[vnew32.b])
                yield
                kk.op("dve", lambda e: e.tensor_copy(out=vnew[:], in_=pv2), reads=[bk2.b], writes=[vnew.b])
                yield
                for h in range(4):
                    kk.op("pe", lambda e, h=h: e.matmul(pout[:, h, :], qdT[:, h, :], Sb[:, h, :], start=True, stop=False),
                          reads=[qdT.b, Sb.b], writes=[bk1.b])
                    kk.op("pe", lambda e, h=h: e.matmul(pout[:, h, :], AT[:, h, :], vnew[:, h, :], start=False, stop=True),
                          reads=[AT.b, vnew.b], writes=[bk1.b])
                yield
                for h in range(4):
                    kk.op("pe", lambda e, h=h: e.matmul(pds[:, h, :], kd32[:, h, :], vnew32[:, h, :], start=True, stop=True),
                          reads=[kd32.b, vnew32.b], writes=[bk2.b])
                yield
                kk.op("dve", lambda e: e.tensor_tensor(out=Sf[:], in0=Sf[:], in1=bc(egl[0:64, :], [64, 4, 64], 2), op=ALU.mult),
                      reads=[Sf.b, egl.b], writes=[Sf.b])
                yield
                kk.op("dve", lambda e: e.tensor_tensor(out=Sf[:], in0=Sf[:], in1=pds, op=ALU.add), reads=[Sf.b, bk2.b], writes=[Sf.b])
                yield
                kk.op("act", lambda e: e.activation(out=Sb[:], in_=Sf[:], func=AF.Identity), reads=[Sf.b], writes=[Sb.b])
                o_ = osb[s]
                if d == 0:
                    kk.op("dve", lambda e, o_=o_: e.tensor_copy(out=o_[:], in_=bk1[:, 256:512]), reads=[bk1.b], writes=[o_.b])
                    kk.op("pool", lambda e, o_=o_, r0=r0: e.dma_start(out=dr[f"of{si}"][r0:r0 + 128, :], in_=o_[:]), reads=[o_.b], dma=f"st{s}")
                else:
                    kk.op("dve", lambda e, o_=o_, f_=ofl[s]: e.tensor_tensor(out=o_[:], in0=bk1[:, 256:512], in1=f_[:], op=ALU.add),
                          reads=[bk1.b, ofl[s].b], writes=[o_.b])
                    if f"osum{si}" in dr:
                        kk.op("pool", lambda e, o_=o_, r0=r0: e.dma_start(out=dr[f"osum{si}"][r0:r0 + 128, :], in_=o_[:]), reads=[o_.b], dma=f"st{s}")
                    kk.op("pool", lambda e, o_=o_: e.tensor_tensor(out=osq[:], in0=o_[:], in1=o_[:], op=ALU.mult), reads=[o_.b], writes=[osq.b])
                    kk.op("dve", lambda e: e.tensor_reduce(out=oss[:], in_=v4(osq[:]), axis=AX.X, op=ALU.add), reads=[osq.b], writes=[oss.b])
                    kk.op("dve", lambda e: e.tensor_scalar(out=oss[:], in0=oss[:], scalar1=1.0 / 64, scalar2=EPS, op0=ALU.mult, op1=ALU.add),
                          reads=[oss.b], writes=[oss.b])
                    kk.op("act", lambda e: e.activation(out=oss[:], in_=oss[:], func=AF.Ln), reads=[oss.b], writes=[oss.b])
                    kk.op("act", lambda e: e.activation(out=oss[:], in_=oss[:], func=AF.Exp, scale=-0.5), reads=[oss.b], writes=[oss.b])
                    kk.op("dve", lambda e, o_=o_: e.tensor_tensor(out=v4(o_[:]), in0=v4(o_[:]), in1=bc(oss[:], [128, 4, 64], 2), op=ALU.mult),
                          reads=[o_.b, oss.b], writes=[o_.b])
                    kk.op("pool", lambda e, o_=o_: e.tensor_tensor(out=onb[:], in0=o_[:], in1=dnw[:].rearrange("p a b -> p (a b)"), op=ALU.mult),
                          reads=[o_.b, dnw.b], writes=[onb.b])
                    for j in range(2):
                        kk.op("pe", lambda e, j=j: e.transpose(out=bkb[:, j * 128:(j + 1) * 128], in_=onb[:, j * 128:(j + 1) * 128], identity=ident[:]),
                              reads=[onb.b, ident.b], writes=[bkb.b])
                    z_ = zb[s]
                    m_ = omx[s]
                    kk.op("act", lambda e, z_=z_: e.activation(out=z_[:], in_=z_[:], func=AF.Silu), reads=[z_.b], writes=[z_.b])
                    kk.op("dve", lambda e, z_=z_, m_=m_: e.tensor_tensor(out=m_[:], in0=bkb[:, 0:256].rearrange("p (a b) -> p a b", a=2), in1=z_[:], op=ALU.mult),
                          reads=[bkb.b, z_.b], writes=[m_.b])
                    kk.op("pool", lambda e, m_=m_, r0=r0: e.dma_start(
                        out=dr[f"mixT{si}"][256:512, r0:r0 + 128].rearrange("(g p) s -> p g s", p=128), in_=m_[:]), reads=[m_.b], dma=f"st{s}")


            for _ in prep(0, order[0]):
                pass
            for n in range(NCH):
                gsc = scan(n, order[n])
                gpr = prep(n + 1, order[n + 1]) if n + 1 < NCH else iter(())
                alive_s = alive_p = True
                while alive_s or alive_p:
                    for _ in range(3):
                        if alive_p and next(gpr, "END") == "END":
                            alive_p = False
                    if alive_s and next(gsc, "END") == "END":
                        alive_s = False


POOL_WINDOWS = (2, 4, 8, 16)


def host_layout(seqs, depth, norm_w, w_in, sgu_w, sgu_b, conv_w, a_log, dt_bias, dn_norm_w,
                q_norm_w, k_norm_w, pool_w, pool_scale, w_out):
    f = np.float32
    m = {}
    m["w_in"] = np.ascontiguousarray(np.asarray(w_in, f).reshape(depth, 8, 128, NCOL).transpose(0, 2, 1, 3)[..., COL_PERM])
    m["norm_w"] = np.ascontiguousarray(np.asarray(norm_w, f).reshape(depth, 8, 128).transpose(0, 2, 1))
    m["w_out"] = np.ascontiguousarray(np.asarray(w_out, f).reshape(depth, 8, 128, D).transpose(0, 2, 1, 3))
    qk = np.concatenate([np.repeat(np.asarray(q_norm_w, f)[:, None, :], 4, 1), np.repeat(np.asarray(k_norm_w, f)[:, None, :], 2, 1)], 1)
    m["qkw"] = np.ascontiguousarray(np.broadcast_to(qk[:, None], (depth, 128, 6, 64)))
    m["a_log"] = np.ascontiguousarray(np.broadcast_to(np.asarray(a_log, f).reshape(depth, 1, 8), (depth, 128, 8)))
    m["dt_bias"] = np.ascontiguousarray(np.broadcast_to(np.asarray(dt_bias, f).reshape(depth, 1, 8), (depth, 128, 8)))
    m["ident"] = np.eye(128, dtype=f)
    m["sgu_wT"] = np.ascontiguousarray(np.asarray(sgu_w, f).transpose(0, 3, 1, 2))
    sb_ = np.asarray(sgu_b, f)
    sbT = np.zeros((depth, 128, 2, 128), f)
    for hp in range(2):
        for h2 in range(2):
            sbT[:, h2 * 64:(h2 + 1) * 64, hp, :] = sb_[:, hp * 2 + h2, None, :]
    m["sgu_bT"] = sbT
    m["conv_w"] = np.ascontiguousarray(np.asarray(conv_w, f).reshape(depth, 5, 6, 128).transpose(0, 3, 2, 1))
    pw = np.zeros((depth, 128, 2, 128), f)
    pwi = np.asarray(pool_w, f)
    for ch in range(2):
        for g2 in range(2):
            pw[:, g2 * 64:(g2 + 1) * 64, ch, g2 * 64:(g2 + 1) * 64] = pwi[:, ch * 2 + g2]
    m["pool_w"] = pw
    m["pool_s"] = np.ascontiguousarray(np.asarray(pool_scale, f).reshape(depth, 2, 128).transpose(0, 2, 1))
    m["dn_w"] = np.ascontiguousarray(np.broadcast_to(np.asarray(dn_norm_w, f)[:, None, None, :], (depth, 128, 4, 64)))
    k_ = np.arange(128)
    m["triU"] = (k_[:, None] <= k_[None, :]).astype(f)
    m["triL"] = (k_[:, None] >= k_[None, :]).astype(f)
    for i, S in enumerate(seqs):
        c, s_ = rope_tables(S)
        m[f"cos{i}"] = c
        m[f"sin{i}"] = s_
        t = np.arange(S)
        ic = np.zeros((128, 2, S), f)
        for g, win in enumerate(POOL_WINDOWS):
            lo = np.clip(t - win // 2, 0, S)
            hi = np.clip(t + win // 2, 0, S)
            ic[(g % 2) * 64:(g % 2) * 64 + 64, g // 2, :] = (1.0 / (hi - lo).astype(f))[None, :]
        m[f"icnt{i}"] = ic
    return m


_NC_CACHE = {}


def kernel(x_prompt, x_sample, norm_w, w_in, sgu_w, sgu_b, conv_w, a_log, dt_bias, dn_norm_w,
           q_norm_w, k_norm_w, pool_w, pool_scale, w_out):
    x_prompt = np.asarray(x_prompt, np.float32)
    x_sample = np.asarray(x_sample, np.float32)
    depth = int(np.asarray(w_in).shape[0])
    seqs = [x_prompt.shape[1], x_sample.shape[1]]
    key = (tuple(seqs), depth)
    if key not in _NC_CACHE:
        _NC_CACHE[key] = build(seqs, depth, divs=(x_prompt.shape[0], x_sample.shape[0]),
                               groups=(8 // x_prompt.shape[0], 8 // x_sample.shape[0]))
    nc = _NC_CACHE[key]
    common = host_layout(seqs, depth, norm_w, w_in, sgu_w, sgu_b, conv_w, a_log, dt_bias, dn_norm_w,
                         q_norm_w, k_norm_w, pool_w, pool_scale, w_out)
    nb_p, nb_s = x_prompt.shape[0], x_sample.shape[0]
    in_maps = []
    for c in range(8):
        mm = dict(common)
        mm["x0"] = np.ascontiguousarray(x_prompt[c % nb_p])
        mm["x1"] = np.ascontiguousarray(x_sample[c % nb_s])
        in_maps.append(mm)
    res = run_bass_kernel_spmd(nc, in_maps, core_ids=list(range(8)))
    yp = np.zeros(x_prompt.shape, np.float32)
    ys = np.zeros(x_sample.shape, np.float32)
    gp, gs = 8 // nb_p, 8 // nb_s
    np_, ns_ = seqs[0] // gp, seqs[1] // gs
    for c in range(8):
        yp[c % nb_p, (c // nb_p) * np_:(c // nb_p + 1) * np_] = np.asarray(res.results[c]["y0"], np.float32)
        ys[c % nb_s, (c // nb_s) * ns_:(c // nb_s + 1) * ns_] = np.asarray(res.results[c]["y1"], np.float32)
    return (yp, ys)
```
